# Optimizing a Trainium2 kernel written in Bass

```python
import math
import jax
import jax.numpy as jnp
from jax import lax
import numpy as np

D_MODEL = 1024
BATCH = 8
SEQ = 2048
DEPTH = 4
DEC_BATCH = 128
DEC_SEQ = 1
PAST_LEN = 16384
PAGE_SIZE = 128

BRANCH_WIDTH = D_MODEL // 2
SSM_WIDTH = BRANCH_WIDTH
SSM_GROUP = 16
SSM_GROUPS = SSM_WIDTH // SSM_GROUP
SSM_STATE = 64
POOL_WIDTH = BRANCH_WIDTH
POOL_WINDOWS = (2, 4, 8, 16)
POOL_GROUPS = len(POOL_WINDOWS)
POOL_GROUP_WIDTH = POOL_WIDTH // POOL_GROUPS
POOL_HIST = max(POOL_WINDOWS) - 1
N_MEM = 256
MEM_HEADS = 4
MEM_HEAD_DIM = BRANCH_WIDTH // MEM_HEADS
MEM_WIDTH = MEM_HEADS * MEM_HEAD_DIM
N_BRANCH = 3
IN_WIDTH = SSM_WIDTH + POOL_WIDTH + MEM_WIDTH + N_BRANCH * D_MODEL
D_FF = -(-(8 * D_MODEL) // (3 * 256)) * 256
RMS_EPS = 1e-6
DT_MIN = 1e-3
DT_MAX = 1e-1

kernel_name = "hybrid_s5_pool_memxattn_step"


def rmsnorm(x, g):
    xf = x.astype(jnp.float32)
    y = xf * lax.rsqrt(jnp.mean(xf * xf, axis=-1, keepdims=True) + RMS_EPS)
    return (y * g.astype(jnp.float32)).astype(x.dtype)


def s5_discretise(lam_re, lam_im, log_dt, b_re, b_im):
    dt = jnp.exp(log_dt.astype(jnp.float32))[:, None]
    lr = lam_re.astype(jnp.float32)
    li = lam_im.astype(jnp.float32)
    zr, zi = lr * dt, li * dt
    mag = jnp.exp(zr)
    ar, ai = mag * jnp.cos(zi), mag * jnp.sin(zi)
    den = lr * lr + li * li
    fr = ((ar - 1.0) * lr + ai * li) / den
    fi = (ai * lr - (ar - 1.0) * li) / den
    br, bi = b_re.astype(jnp.float32), b_im.astype(jnp.float32)
    bbr = fr[..., None] * br - fi[..., None] * bi
    bbi = fr[..., None] * bi + fi[..., None] * br
    return zr, zi, ar, ai, bbr, bbi


def _complex_affine_combine(e1, e2):
    a1r, a1i, b1r, b1i = e1
    a2r, a2i, b2r, b2i = e2
    return (a2r * a1r - a2i * a1i, a2r * a1i + a2i * a1r,
            a2r * b1r - a2i * b1i + b2r, a2r * b1i + a2i * b1r + b2i)


def s5_branch(u, h0, lam_re, lam_im, log_dt, b_re, b_im, c_re, c_im, d_skip, w_glu, b_glu):
    n, t, _ = u.shape
    uf = u.astype(jnp.float32).reshape(n, t, SSM_GROUPS, SSM_GROUP)
    zr, zi, ar, ai, bbr, bbi = s5_discretise(lam_re, lam_im, log_dt, b_re, b_im)
    bu_re = jnp.einsum('ntgh,gph->ntgp', uf, bbr)
    bu_im = jnp.einsum('ntgh,gph->ntgp', uf, bbi)
    a_re = jnp.broadcast_to(ar, bu_re.shape)
    a_im = jnp.broadcast_to(ai, bu_im.shape)
    _, _, h_re, h_im = lax.associative_scan(
        _complex_affine_combine, (a_re, a_im, bu_re, bu_im), axis=1)
    if h0 is not None:
        h0r = h0[0].astype(jnp.float32)[:, None]
        h0i = h0[1].astype(jnp.float32)[:, None]
        steps = jnp.arange(1, t + 1, dtype=jnp.float32)[:, None, None]
        mag = jnp.exp(zr * steps)
        pr, pim = mag * jnp.cos(zi * steps), mag * jnp.sin(zi * steps)
        h_re, h_im = h_re + pr * h0r - pim * h0i, h_im + pr * h0i + pim * h0r
    y = (jnp.einsum('ntgp,ghp->ntgh', h_re, c_re.astype(jnp.float32))
         - jnp.einsum('ntgp,ghp->ntgh', h_im, c_im.astype(jnp.float32)))
    y = y.reshape(n, t, SSM_WIDTH) + d_skip.astype(jnp.float32) * uf.reshape(n, t, SSM_WIDTH)
    y = jax.nn.gelu(y)
    y = y * jax.nn.sigmoid(y @ w_glu.astype(jnp.float32) + b_glu.astype(jnp.float32))
    return y.astype(u.dtype), h_re[:, -1].astype(u.dtype), h_im[:, -1].astype(u.dtype)


def pool_branch(u, hist, pos0, pool_w, pool_scale):
    n, t, _ = u.shape
    uf = u.astype(jnp.float32)
    if hist is None:
        hist_f = jnp.zeros((n, POOL_HIST, POOL_WIDTH), jnp.float32)
    else:
        hist_f = hist.astype(jnp.float32)
    full = jnp.concatenate([hist_f, uf], axis=1)
    cs = jnp.concatenate([jnp.zeros((n, 1, POOL_WIDTH), jnp.float32),
                          jnp.cumsum(full, axis=1)], axis=1)
    pos = pos0 + jnp.arange(t, dtype=jnp.int32)
    outs = []
    for gi, w in enumerate(POOL_WINDOWS):
        sl = slice(gi * POOL_GROUP_WIDTH, (gi + 1) * POOL_GROUP_WIDTH)
        end = cs[:, POOL_HIST + 1:POOL_HIST + 1 + t, sl]
        start = cs[:, POOL_HIST + 1 - w:POOL_HIST + 1 - w + t, sl]
        cnt = jnp.minimum(pos + 1, w).astype(jnp.float32)[None, :, None]
        outs.append((end - start) / cnt)
    pooled = jnp.concatenate(outs, axis=-1) - uf
    mixed = jnp.einsum('ntgc,gcd->ntgd',
                       pooled.reshape(n, t, POOL_GROUPS, POOL_GROUP_WIDTH),
                       pool_w.astype(jnp.float32)).reshape(n, t, POOL_WIDTH)
    mixed = mixed * pool_scale.astype(jnp.float32)
    return mixed.astype(u.dtype), full[:, -POOL_HIST:].astype(u.dtype)


def mem_kv(mem, g_mem, w_kv):
    kv = rmsnorm(mem, g_mem) @ w_kv
    n = mem.shape[0]
    k = kv[..., :MEM_WIDTH].reshape(n, N_MEM, MEM_HEADS, MEM_HEAD_DIM)
    v = kv[..., MEM_WIDTH:].reshape(n, N_MEM, MEM_HEADS, MEM_HEAD_DIM)
    return k, v


def mem_attention(q, k, v):
    n, t, _ = q.shape
    qh = q.reshape(n, t, MEM_HEADS, MEM_HEAD_DIM)
    s = jnp.einsum('nthd,nmhd->nhtm', qh, k).astype(jnp.float32) * (MEM_HEAD_DIM ** -0.5)
    p = jax.nn.softmax(s, axis=-1).astype(v.dtype)
    o = jnp.einsum('nhtm,nmhd->nthd', p, v)
    return o.reshape(n, t, MEM_WIDTH)


def trunk_layer(x, p, carried, pos0, k_mem, v_mem):
    n, t, _ = x.shape
    h = rmsnorm(x, p['g_mix_pre'])
    proj = h @ p['w_in']
    o1 = SSM_WIDTH
    o2 = o1 + POOL_WIDTH
    o3 = o2 + MEM_WIDTH
    u_ssm, u_pool, q_mem, gate_logits = proj[..., :o1], proj[..., o1:o2], proj[..., o2:o3], proj[..., o3:]
    h0 = None if carried is None else (carried[0], carried[1])
    hist = None if carried is None else carried[2]
    o_ssm, h_re, h_im = s5_branch(u_ssm, h0, p['lam_re'], p['lam_im'], p['log_dt'], p['b_re'], p['b_im'],
                                  p['c_re'], p['c_im'], p['d_skip'], p['w_glu'], p['b_glu'])
    o_pool, new_hist = pool_branch(u_pool, hist, pos0, p['pool_w'], p['pool_scale'])
    o_mem = mem_attention(q_mem, k_mem, v_mem)
    branches = jnp.stack([o_ssm, o_pool, o_mem], axis=2)
    up = jnp.einsum('ntbc,bcd->ntbd', branches, p['w_branch_up'])
    gates = jax.nn.sigmoid(gate_logits.astype(jnp.float32).reshape(n, t, N_BRANCH, D_MODEL))
    merged = jnp.sum(gates * up.astype(jnp.float32), axis=2).astype(x.dtype)
    x = x + rmsnorm(merged @ p['w_out'], p['g_mix_post'])
    hf = rmsnorm(x, p['g_ffn_pre']) @ p['w_ffn_in']
    f = (jax.nn.silu(hf[..., :D_FF]) * hf[..., D_FF:]) @ p['w_ffn_out']
    x = x + rmsnorm(f, p['g_ffn_post'])
    return x, h_re, h_im, new_hist


def setup_inputs(seed: int = 0) -> dict:
    key = jax.random.key(seed)
    ks = iter(jax.random.split(key, 40))
    f32 = jnp.float32

    def nrm(shape, scale):
        return jax.random.normal(next(ks), shape, f32) * scale

    def gain(shape):
        return 1.0 + nrm(shape, 0.05)

    lam_re = -0.5 + nrm((DEPTH, SSM_GROUPS, SSM_STATE), 0.01)
    lam_im = (jnp.pi * jnp.arange(SSM_STATE, dtype=f32))[None, None, :] + nrm((DEPTH, SSM_GROUPS, SSM_STATE), 0.01)
    log_dt = jax.random.uniform(next(ks), (DEPTH, SSM_GROUPS), f32, math.log(DT_MIN), math.log(DT_MAX))
    return {
        'x_prompt': nrm((BATCH, SEQ, D_MODEL), 1.0),
        'x_sample': nrm((DEC_BATCH, DEC_SEQ, D_MODEL), 1.0),
        'mem_prompt': nrm((BATCH, N_MEM, D_MODEL), 1.0),
        'cache_mem_k': nrm((DEPTH, DEC_BATCH, N_MEM, MEM_HEADS, MEM_HEAD_DIM), 1.0),
        'cache_mem_v': nrm((DEPTH, DEC_BATCH, N_MEM, MEM_HEADS, MEM_HEAD_DIM), 1.0),
        'state_ssm_re': nrm((DEPTH, DEC_BATCH, SSM_GROUPS, SSM_STATE), 0.5),
        'state_ssm_im': nrm((DEPTH, DEC_BATCH, SSM_GROUPS, SSM_STATE), 0.5),
        'state_pool': nrm((DEPTH, DEC_BATCH, POOL_HIST, POOL_WIDTH), 1.0),
        'g_mix_pre': gain((DEPTH, D_MODEL)),
        'g_mix_post': gain((DEPTH, D_MODEL)),
        'g_ffn_pre': gain((DEPTH, D_MODEL)),
        'g_ffn_post': gain((DEPTH, D_MODEL)),
        'g_mem': gain((DEPTH, D_MODEL)),
        'w_in': nrm((DEPTH, D_MODEL, IN_WIDTH), D_MODEL ** -0.5),
        'w_kv': nrm((DEPTH, D_MODEL, 2 * MEM_WIDTH), D_MODEL ** -0.5),
        'ssm_lam_re': lam_re,
        'ssm_lam_im': lam_im,
        'ssm_log_dt': log_dt,
        'ssm_b_re': nrm((DEPTH, SSM_GROUPS, SSM_STATE, SSM_GROUP), (2 * SSM_GROUP) ** -0.5),
        'ssm_b_im': nrm((DEPTH, SSM_GROUPS, SSM_STATE, SSM_GROUP), (2 * SSM_GROUP) ** -0.5),
        'ssm_c_re': nrm((DEPTH, SSM_GROUPS, SSM_GROUP, SSM_STATE), (2 * SSM_STATE) ** -0.5),
        'ssm_c_im': nrm((DEPTH, SSM_GROUPS, SSM_GROUP, SSM_STATE), (2 * SSM_STATE) ** -0.5),
        'ssm_d': nrm((DEPTH, SSM_WIDTH), 1.0),
        'ssm_w_glu': nrm((DEPTH, SSM_WIDTH, SSM_WIDTH), SSM_WIDTH ** -0.5),
        'ssm_b_glu': nrm((DEPTH, SSM_WIDTH), 0.02),
        'pool_w': nrm((DEPTH, POOL_GROUPS, POOL_GROUP_WIDTH, POOL_GROUP_WIDTH), POOL_GROUP_WIDTH ** -0.5),
        'pool_scale': 1.0 + nrm((DEPTH, POOL_WIDTH), 0.1),
        'w_branch_up': nrm((DEPTH, N_BRANCH, BRANCH_WIDTH, D_MODEL), BRANCH_WIDTH ** -0.5),
        'w_out': nrm((DEPTH, D_MODEL, D_MODEL), D_MODEL ** -0.5),
        'w_ffn_in': nrm((DEPTH, D_MODEL, 2 * D_FF), D_MODEL ** -0.5),
        'w_ffn_out': nrm((DEPTH, D_FF, D_MODEL), D_FF ** -0.5),
    }


def reference(x_prompt, x_sample, mem_prompt, cache_mem_k, cache_mem_v, state_ssm_re, state_ssm_im, state_pool,
              g_mix_pre, g_mix_post, g_ffn_pre, g_ffn_post, g_mem, w_in, w_kv,
              ssm_lam_re, ssm_lam_im, ssm_log_dt, ssm_b_re, ssm_b_im, ssm_c_re, ssm_c_im, ssm_d,
              ssm_w_glu, ssm_b_glu, pool_w, pool_scale, w_branch_up, w_out, w_ffn_in, w_ffn_out):
    yp, ys = x_prompt, x_sample
    re_p, im_p, pool_p, mk_p, mv_p = [], [], [], [], []
    re_s, im_s, pool_s = [], [], []
    for l in range(DEPTH):
        p = {
            'g_mix_pre': g_mix_pre[l], 'g_mix_post': g_mix_post[l],
            'g_ffn_pre': g_ffn_pre[l], 'g_ffn_post': g_ffn_post[l],
            'w_in': w_in[l], 'lam_re': ssm_lam_re[l], 'lam_im': ssm_lam_im[l], 'log_dt': ssm_log_dt[l],
            'b_re': ssm_b_re[l], 'b_im': ssm_b_im[l], 'c_re': ssm_c_re[l], 'c_im': ssm_c_im[l],
            'd_skip': ssm_d[l], 'w_glu': ssm_w_glu[l], 'b_glu': ssm_b_glu[l],
            'pool_w': pool_w[l], 'pool_scale': pool_scale[l],
            'w_branch_up': w_branch_up[l], 'w_out': w_out[l],
            'w_ffn_in': w_ffn_in[l], 'w_ffn_out': w_ffn_out[l],
        }
        k_m, v_m = mem_kv(mem_prompt, g_mem[l], w_kv[l])
        yp, hr, hi, hist = trunk_layer(yp, p, None, 0, k_m, v_m)
        re_p.append(hr)
        im_p.append(hi)
        pool_p.append(hist)
        mk_p.append(k_m)
        mv_p.append(v_m)
        carried = (state_ssm_re[l], state_ssm_im[l], state_pool[l])
        ys, hr, hi, hist = trunk_layer(ys, p, carried, PAST_LEN, cache_mem_k[l], cache_mem_v[l])
        re_s.append(hr)
        im_s.append(hi)
        pool_s.append(hist)
    return (yp, ys,
            jnp.stack(re_p), jnp.stack(im_p), jnp.stack(pool_p), jnp.stack(mk_p), jnp.stack(mv_p),
            jnp.stack(re_s), jnp.stack(im_s), jnp.stack(pool_s))
```

```python
import numpy as np
import concourse.bass as bass
import concourse.mybir as mybir
from concourse.bass_utils import run_bass_kernel_spmd

F32 = mybir.dt.float32
BF16 = mybir.dt.bfloat16
ALU = mybir.AluOpType
AF = mybir.ActivationFunctionType
AX = mybir.AxisListType

ENGS = ("pe", "act", "dve", "pool", "sp")
import os as _os
SKIP = _os.environ.get("KSKIP", "").split(",")
TAPS2D = bool(int(_os.environ.get("TAPS2D", "0")))
KSS = int(_os.environ.get("KSS", "9"))
KSTOP = int(_os.environ.get("KSTOP", "99"))
FFNBAR = bool(int(_os.environ.get("FFNBAR", "0")))
DQ = _os.environ.get("DQ", "sp")

DEPTH = 4
D = 1024
KT = 8
SEQ = 2048
HALF = 1024
NS = 16
NT = HALF + NS
LCH = 8
NCH = HALF // LCH
OUT_LAYOUT = {}
_off = 0
for _nm, _shp in (("yp", (2048, 1024)), ("ys", (16, 1024)), ("re_p", (4, 2048)), ("im_p", (4, 2048)), ("pool_p", (4, 15, 512)), ("mk_p", (4, 256, 512)),
                  ("mv_p", (4, 256, 512)), ("re_s", (4, 16, 2048)), ("im_s", (4, 16, 2048)), ("pool_s", (4, 16, 15, 512))):
    OUT_LAYOUT[_nm] = (_off, _shp)
    _off += int(np.prod(_shp))
OUT_TOTAL = _off
DFF = 2816
FT = 22
NMEM = 256
EPS = 1e-6
PAST = 16384


class Prog:
    def __init__(self, nc):
        self.nc = nc
        self.ops = {e: [] for e in ENGS}
        self.count = {e: 0 for e in ENGS}
        self.waited = {e: {} for e in ENGS}
        self.last_w = {}
        self.readers = {}
        self.dma_cum = {}
        self.sems = {}
        self.final_tokens = []
        self._ctx = []
        self.barrier_tok = {e: [] for e in ENGS}

    def sb(self, name, shape, dt):
        g = self.nc.sbuf_tensor("sb_" + name, list(shape), dt)
        t = g.__enter__()
        self._ctx.append(g)
        return t

    def ps(self, name, shape, dt=F32):
        g = self.nc.psum_tensor("ps_" + name, list(shape), dt)
        t = g.__enter__()
        self._ctx.append(g)
        return t

    def _need(self, eng, tok, waits):
        if tok is None:
            return
        key, val = tok
        if key == ("eng", eng) and eng in ("pe", "sp"):
            return
        if self.waited[eng].get(key, 0) >= val:
            return
        self.waited[eng][key] = val
        waits.append((key, val))

    def _deps(self, eng, r, w):
        waits = []
        for t in self.barrier_tok[eng]:
            self._need(eng, t, waits)
        self.barrier_tok[eng] = []
        for x in r:
            self._need(eng, self.last_w.get(x), waits)
        for x in w:
            self._need(eng, self.last_w.get(x), waits)
            for t in self.readers.get(x, ()):
                self._need(eng, t, waits)
        return waits

    def _commit(self, tok, r, w):
        for x in r:
            lst = self.readers.setdefault(x, [])
            lst[:] = [t for t in lst if t[0] != tok[0]] + [tok]
        for x in w:
            self.last_w[x] = tok
            self.readers[x] = []

    def op(self, eng, fn, r=(), w=()):
        psr = [x for x in r if isinstance(x, tuple) and x[0] == "ps" and x not in w]
        if psr:
            w = list(w) + psr
        waits = self._deps(eng, r, w)
        self.count[eng] += 1
        tok = (("eng", eng), self.count[eng])
        self.ops[eng].append((waits, fn, "op", None))
        self._commit(tok, r, w)
        return tok

    def dma(self, eng, out, in_, r=(), w=(), sem="dma", final=False, **kw):
        waits = self._deps(eng, r, w)
        key = ("dma", sem)
        self.dma_cum[key] = self.dma_cum.get(key, 0) + 16
        tok = (key, self.dma_cum[key])
        self.ops[eng].append((waits, lambda e: e.dma_start(out=out, in_=in_, **kw), "dma", key))
        self._commit(tok, r, w)
        if final:
            self.final_tokens.append(tok)
        return tok

    def barrier(self):
        toks = [(("eng", o), self.count[o]) for o in ENGS if o != "sp" and self.count[o] > 0]
        toks += [(k, v) for k, v in self.dma_cum.items()]
        for e in ENGS:
            self.barrier_tok[e] = list(toks)

    def finalize(self):
        nc = self.nc
        keys = [("eng", e) for e in ENGS if e != "sp"] + list(self.dma_cum.keys())
        guards = []
        for k in keys:
            g = nc.semaphore("s_" + "_".join(str(x) for x in k))
            self.sems[k] = g.__enter__()
            guards.append(g)
        seen = {}
        for key, val in self.final_tokens:
            seen[key] = max(seen.get(key, 0), val)
        final_waits = list(seen.items())
        blk = nc.Block()
        block = blk.__enter__()

        def runner(ename):
            def run(e):
                mysem = self.sems.get(("eng", ename))
                for waits, fn, kind, key in self.ops[ename]:
                    for k, v in waits:
                        e.wait_ge(self.sems[k], v)
                    inst = fn(e)
                    if kind == "dma":
                        inst.then_inc(self.sems[key], 16)
                    elif mysem is not None:
                        inst.then_inc(mysem, 1)
                if ename == "sp":
                    for k, v in final_waits:
                        e.wait_ge(self.sems[k], v)
            return run

        block.tensor(runner("pe"))
        block.scalar(runner("act"))
        block.vector(runner("dve"))
        block.gpsimd(runner("pool"))
        block.sync(runner("sp"))
        blk.__exit__(None, None, None)
        for g in reversed(guards):
            g.__exit__(None, None, None)
        for g in reversed(self._ctx):
            g.__exit__(None, None, None)


class DummyProg:
    def op(self, *a, **k):
        return None

    def dma(self, *a, **k):
        return None

    def barrier(self):
        pass


class Builder:
    def __init__(self, nlayers=DEPTH, nhalves=2, dbg=None):
        self.nl = nlayers
        self.nh = nhalves
        self.dbg = dbg
        nc = bass.Bass("TRN2", target_bir_lowering=False)
        self.nc = nc
        self.P = Prog(nc)
        self.din = {}
        self.dout = {}
        self.nsem = 0
        self._q = None
        self._xr = ()
        self._xw = ()

    def inp(self, name, shape):
        self.din[name] = self.nc.dram_tensor(name, list(shape), F32, kind="ExternalInput").ap()
        return self.din[name]

    def outp(self, name, shape):
        self.dout[name] = self.nc.dram_tensor(name, list(shape), F32, kind="ExternalOutput").ap()
        return self.dout[name]

    def build(self):
        P = self.P
        nc = self.nc
        i = self.inp
        xp = i("xp", [SEQ, D]); xs = i("xs", [NS, D]); mem = i("mem", [NMEM, D])
        ck = i("ck", [DEPTH, NS, NMEM, 512]); cv = i("cv", [DEPTH, NS, NMEM, 512])
        sre = i("sre", [DEPTH, NS, 2048]); sim = i("sim", [DEPTH, NS, 2048])
        spool = i("spool", [DEPTH, NS, 15, 512])
        self.w_in = i("w_in", [DEPTH, D, 4608]); self.w_kv = i("w_kv", [DEPTH, D, 1024])
        self.w_glu = i("w_glu", [DEPTH, 512, 512]); self.pool_w = i("pool_w", [DEPTH, 4, 128, 128])
        self.w_up = i("w_up", [DEPTH, 3, 512, D]); self.w_out = i("w_out", [DEPTH, D, D])
        self.w_f1 = i("w_f1", [DEPTH, D, 2 * DFF]); self.w_f2 = i("w_f2", [DEPTH, DFF, D])
        call_d = i("call", [128, 216 + 160 + 48])
        cst_d = call_d[:, 0:216]; gvec_d = call_d[:, 216:376]; vec4_d = call_d[:, 376:424]
        sprm_d = i("sprm", [DEPTH, 128, 2352])
        out_d = self.outp("out", [OUT_TOTAL])
        ov = {}
        for nm, (off, shp) in OUT_LAYOUT.items():
            n = int(np.prod(shp))
            v = out_d[off:off + n]
            if len(shp) == 2:
                v = v.rearrange("(a b) -> a b", b=shp[1])
            elif len(shp) == 3:
                v = v.rearrange("(a b c) -> a b c", b=shp[1], c=shp[2])
            elif len(shp) == 4:
                v = v.rearrange("(a b c d) -> a b c d", b=shp[1], c=shp[2], d=shp[3])
            ov[nm] = v
        yp, ys, re_p, im_p, pool_p, mk_p, mv_p, re_s, im_s, pool_s = [ov[k] for k in ("yp", "ys", "re_p", "im_p", "pool_p", "mk_p", "mv_p", "re_s", "im_s", "pool_s")]
        self.o = dict(yp=yp, ys=ys, re_p=re_p, im_p=im_p, pool_p=pool_p, mk_p=mk_p, mv_p=mv_p,
                      re_s=re_s, im_s=im_s, pool_s=pool_s)
        self.i = dict(xp=xp, xs=xs, mem=mem, ck=ck, cv=cv, sre=sre, sim=sim, spool=spool,
                      sprm=sprm_d)

        self.xT = P.sb("xT", [128, KT, NT], F32)
        self.hT = P.sb("hT", [128, KT, NT], BF16)
        self.A = P.sb("arena", [128, FT * NT], BF16)
        A = self.A
        self.aT = A[:, 0:FT * NT].rearrange("p (k n) -> p k n", n=NT)
        self.mg = A[:, 0:8 * NT].rearrange("p (k n) -> p k n", n=NT)
        self.uT = A[:, 8 * NT:12 * NT].rearrange("p (k n) -> p k n", n=NT)
        self.oT = A[:, 12 * NT:16 * NT].rearrange("p (k n) -> p k n", n=NT)
        self.Hbf = A[:, 16 * NT:16 * NT + 16 * 2 * (NCH + 1)].rearrange("p (a r c) -> p a r c", a=16, r=2)
        self.Wsb = A[:, 0:8192].rearrange("p (g k r c) -> p g k r c", g=4, k=8, r=2)
        self.Tc = A[:, 12 * NT:16 * NT].bitcast(F32)[:, 0:2048].rearrange("p (a c) -> p a c", c=NCH)
        self.Ts = A[:, 16 * NT:16 * NT + 4128].bitcast(F32)[:, 0:2048].rearrange("p (a c) -> p a c", c=NCH)
        self.Hst = P.sb("Hst", [128, 16, 2, NCH + 2], F32)
        self.Wco = P.sb("Wco", [128, 16, 9, 2, 32], BF16)
        self.Wtp = P.sb("Wtp", [128, 4, 8, 128], BF16)
        self.zb = P.sb("zb", [128, KT, 512], F32)
        self.stg = [P.sb("stg%d" % j, [128, 2048], F32) for j in range(2)]
        self.wbf = [P.sb("wbf%d" % j, [128, 2048], BF16) for j in range(3)]
        self.gvec = P.sb("gvec", [128, 5, DEPTH, KT], F32)
        self.vec4 = P.sb("vec4", [128, 3, DEPTH, 4], F32)
        self.cst = P.sb("cst", [128, 128 + 8 + 64 + 16], F32)
        self.identb = P.sb("identb", [128, 128], BF16)
        self.onesb = P.sb("onesb", [128, 128], BF16)
        self.onesf = P.sb("onesf", [128, 128], F32)
        self.rstd = P.sb("rstd", [128, 512], F32)
        self.sq = [P.sb("sq%d" % j, [128, 512], BF16) for j in range(2)]
        self.tmpA = [P.sb("tmpA%d" % j, [128, 512], F32) for j in range(2)]
        self.tmpB = [P.sb("tmpB%d" % j, [128, 512], F32) for j in range(2)]
        self.carry = P.sb("carry", [128, DEPTH, 16, 2], F32)
        self.phist = P.sb("phist", [128, DEPTH, 4, 16], F32)
        wtf = self.Wtp[:].rearrange("p g k c -> p (g k c)")
        self.kT = wtf[:, 0:1024].rearrange("p (h m) -> p h m", m=NMEM)
        self.vv = wtf[:, 1024:2048].rearrange("p (t c) -> p t c", c=512)
        self.qtok = wtf[0:NS, 2048:3072].bitcast(F32)
        self.memn = P.sb("memn", [128, KT, NMEM], BF16)
        self.memh = A[:, 16 * NT + 4128:16 * NT + 4128 + 2048].rearrange("p (k m) -> p k m", m=NMEM)
        self.epsb = P.sb("epsb", [128, 2], F32)
        self.lastu = P.sb("lastu", [128, 4, 16], F32)
        self.usam = P.sb("usam", [128, 4, 16], F32)
        self.small = P.sb("small", [128, 768], F32)
        self.sA = P.sb("sA", [128, 512], F32)
        self.sB = P.sb("sB", [128, 512], F32)
        self.a1rho = P.sb("a1rho", [128, 48], F32)
        self.rho = self.a1rho[:, 32:48]
        self.A8 = P.sb("A8", [128, 2, 16], F32)
        self.A1 = self.a1rho[:, 0:32].rearrange("p (r a) -> p r a", r=2)
        self.pw = self.zb[:].rearrange("p k c -> p (k c)")[:, 1024:1024 + 2080]
        self.prm = self.Hst[:].rearrange("p a r c -> p (a r c)")[:, 0:2400]
        self.PSB = [P.ps("psb%d" % j, [128, 512]) for j in range(8)]
        self.ident = self.cst[:, 0:128]
        dt_ = lambda nm, shp, dt: self.nc.dram_tensor(nm, shp, dt, kind="Internal").ap()
        self.scr = dict(sb=dt_("scr_sb", [DEPTH, 128, 8192], BF16), co=dt_("scr_co", [DEPTH, 128, 9216], BF16),
                        tp=dt_("scr_tp", [DEPTH, 128, 4096], BF16), t=dt_("scr_t", [DEPTH, 128, 4144], F32),
                        kv=dt_("scr_kv", [DEPTH, 128, 2048], BF16))
        self.dd = dict(cst_d=cst_d, gvec_d=gvec_d, vec4_d=vec4_d)
        realP = self.P
        self.P = DummyProg()
        self.specs = None
        self.rec = []
        self.emit()
        self.specs = self.rec
        self.P = realP
        self.emit()
        self.P.finalize()
        return nc

    def emit(self):
        P = self.P
        self.rot = 0
        self.cnt = 0
        self.stg_i = 0
        self.wbf_i = 0
        self.wl_i = 0
        self.wl_issued = 0
        ident = self.ident
        cst_d, gvec_d, vec4_d = self.dd["cst_d"], self.dd["gvec_d"], self.dd["vec4_d"]
        P.dma("sp", self.cst[:], cst_d, w=["cst"], sem="c0")
        P.dma("sp", self.gvec[:].rearrange("p a l k -> p (a l k)"), gvec_d, w=["gvec"], sem="c1")
        P.dma("sp", self.vec4[:].rearrange("p a l k -> p (a l k)"), vec4_d, w=["vec4"], sem="c2")
        P.op("dve", lambda e: e.tensor_copy(out=self.identb[:], in_=ident), r=["cst"], w=["identb"])
        P.op("pool", lambda e: e.memset(self.onesb[:], 1.0), w=["onesb"])
        P.op("pool", lambda e: e.memset(self.onesf[:], 1.0), w=["onesf"])
        P.op("pool", lambda e: e.memset(self.epsb[:, 0:1], EPS), w=["epsb"])
        P.op("pool", lambda e: e.memset(self.epsb[:, 1:2], float(np.pi / 2)), w=["epsb"])
        P.op("pool", lambda e: e.memset(self.carry[:], 0.0), w=["carry"])
        P.op("pool", lambda e: e.memset(self.phist[:], 0.0), w=["phist"])

        self.prep_mem()
        P.barrier()
        kvst = self.A[:, 8 * NT:8 * NT + 2048]
        save_kv = (self.kT, self.vv)
        self.kT = kvst[:, 0:1024].rearrange("p (h m) -> p h m", m=NMEM)
        self.vv = kvst[:, 1024:2048].rearrange("p (t c) -> p t c", c=512)
        for l in range(self.nl):
            self.kv(l, 0)
            P.dma("sp", self.scr["kv"][l], kvst, r=["kT", "vv"], sem="kst")
            self.ssm_prep(l)
        self.kT, self.vv = save_kv
        P.barrier()
        P.op("dve", lambda e: e.memset(self.small[:, 760:761], 0.0), w=["scr"])
        for half in range(self.nh):
            self.half = half
            self.ntok = HALF + (NS if half == 1 else 0)
            self.tbs = [(0, 512), (512, 512)] + ([(HALF, NS)] if half == 1 else [])
            self.load_x(half)
            P.barrier()
            for l in range(self.nl):
                self.layer(l, half)
            P.barrier()
            if not (half == 1 and KSTOP <= 6):
                self.store_y(half)

    def bank(self):
        b = self.rot
        self.rot = (self.rot + 1) % 6
        return b

    def uid(self):
        self.cnt += 1
        return self.cnt

    def cast_eng(self):
        return "pool"

    def _issue(self, idx):
        P = self.P
        pieces, kt, ncols = self.specs[idx]
        si = idx % 2
        bi = idx % 3
        st = self.stg[si][:, 0:kt * ncols].rearrange("p (k c) -> p k c", c=ncols)
        wb = self.wbf[bi][:, 0:kt * ncols].rearrange("p (k c) -> p k c", c=ncols)
        c0 = 0
        for ap in pieces:
            c = ap.shape[-1]
            P.dma("sp", st[:, :, c0:c0 + c], ap.rearrange("(k p) c -> p k c", p=128), w=[("stg", si)], sem="stg%d" % si)
            c0 += c
        if idx % 2 == 0:
            P.op("act", lambda e: e.activation(out=wb, in_=st, func=AF.Copy), r=[("stg", si)], w=[("wbf", bi)])
        else:
            P.op("pool", lambda e: e.tensor_copy(out=wb, in_=st), r=[("stg", si)], w=[("wbf", bi)])

    def wload(self, pieces, kt, ncols):
        idx = self.wl_i
        self.wl_i += 1
        bi = idx % 3
        wb = self.wbf[bi][:, 0:kt * ncols].rearrange("p (k c) -> p k c", c=ncols)
        if self.specs is None:
            self.rec.append((pieces, kt, ncols))
            return wb, ("wbf", bi)
        while self.wl_issued <= min(idx + 1, len(self.specs) - 1):
            self._issue(self.wl_issued)
            self.wl_issued += 1
        return wb, ("wbf", bi)

    def norm_rstd(self, src_fn, src_res, ntiles, w, tag):
        P = self.P
        pb = 6 + (self.uid() % 2)
        ps = self.PSB[pb]
        for k in range(ntiles):
            sq = self.sq[k % 2]
            P.op("act", lambda e, k=k, sq=sq: e.activation(out=sq[:, :w], in_=src_fn(k), func=AF.Square),
                 r=[src_res(k)], w=[("sq", k % 2)])
            P.op("pe", lambda e, k=k, sq=sq: e.matmul(ps[:, :w], lhsT=self.onesb[:], rhs=sq[:, :w], start=(k == 0), stop=(k == ntiles - 1)),
                 r=[("sq", k % 2), "onesb"], w=[("ps", pb)])
        P.op("act", lambda e: e.activation(out=self.rstd[:, :w], in_=ps[:, :w], func=AF.Sqrt, scale=1.0 / D, bias=self.epsb[:, 0:1]),
             r=[("ps", pb), "epsb"], w=["rstd"])
        P.op("dve", lambda e: e.reciprocal(out=self.rstd[:, :w], in_=self.rstd[:, :w]), r=["rstd"], w=["rstd"])

    def dump(self, name, ap, l=0, half=0):
        if not self.dbg or l != 0 or half != 0 or isinstance(self.P, DummyProg):
            return
        shp = list(ap.shape)
        d = self.nc.dram_tensor("dbg_" + name, shp, ap.dtype, kind="ExternalOutput").ap()
        self.P.barrier()
        self.P.dma("sp", d, ap, sem="dbg_" + name, final=True)
        self.P.barrier()

    def load_x(self, half):
        P = self.P
        xp = self.i["xp"]
        zb4 = self.zb[:].rearrange("p k c -> p (k c)").rearrange("p (j c) -> p j c", c=D)
        for blk in range(2):
            t0 = blk * 512
            r0 = half * HALF + t0
            P.dma("sp", zb4, xp[r0:r0 + 512, :].rearrange("(j p) c -> p j c", p=128), w=["zb"], sem="zb")
            for k in range(KT):
                b = self.bank(); ps = self.PSB[b]
                for j in range(4):
                    P.op("pe", lambda e, j=j, k=k, ps=ps: e.transpose(ps[:, j * 128:(j + 1) * 128], zb4[:, j, k * 128:(k + 1) * 128], self.ident),
                         r=["zb", "cst"], w=[("ps", b)])
                eng = "act" if k % 2 else "dve"
                if eng == "act":
                    P.op("act", lambda e, k=k, ps=ps, t0=t0: e.activation(out=self.xT[:, k, t0:t0 + 512], in_=ps[:, :], func=AF.Copy), r=[("ps", b)], w=[("xT", t0)])
                else:
                    P.op("dve", lambda e, k=k, ps=ps, t0=t0: e.tensor_copy(out=self.xT[:, k, t0:t0 + 512], in_=ps[:, :]), r=[("ps", b)], w=[("xT", t0)])
        if half == 1:
            xs = self.i["xs"]
            P.dma("sp", zb4[0:NS, 0, :], xs, w=["zb"], sem="zb")
            b = self.bank(); ps = self.PSB[b]
            for k in range(KT):
                P.op("pe", lambda e, k=k, ps=ps: e.transpose(ps[:, k * NS:(k + 1) * NS], zb4[0:NS, 0, k * 128:(k + 1) * 128], self.ident[0:NS, 0:NS]),
                     r=["zb", "cst"], w=[("ps", b)])
            P.op("dve", lambda e, ps=ps: e.tensor_copy(out=self.xT[:, :, HALF:HALF + NS], in_=ps[:, 0:KT * NS].rearrange("p (k n) -> p k n", n=NS)),
                 r=[("ps", b)], w=[("xT", HALF)])

    def store_y(self, half):
        P = self.P
        yp = self.o["yp"]
        zb4 = self.zb[:].rearrange("p k c -> p (k c)").rearrange("p (j c) -> p j c", c=D)
        for blk in range(2):
            t0 = blk * 512
            r0 = half * HALF + t0
            for j in range(4):
                for kk in range(2):
                    b = self.bank(); ps = self.PSB[b]
                    for k4 in range(4):
                        k = kk * 4 + k4
                        P.op("pe", lambda e, j=j, k=k, k4=k4, ps=ps, t0=t0: e.transpose(ps[:, k4 * 128:(k4 + 1) * 128], self.xT[:, k, t0 + j * 128:t0 + (j + 1) * 128], self.ident),
                             r=[("xT", t0), "cst"], w=[("ps", b)])
                    if kk:
                        P.op("act", lambda e, j=j, kk=kk, ps=ps: e.activation(out=zb4[:, j, kk * 512:(kk + 1) * 512], in_=ps[:, :], func=AF.Copy), r=[("ps", b)], w=["zb"])
                    else:
                        P.op("dve", lambda e, j=j, kk=kk, ps=ps: e.tensor_copy(out=zb4[:, j, kk * 512:(kk + 1) * 512], in_=ps[:, :]), r=[("ps", b)], w=["zb"])
            P.dma("sp", yp[r0:r0 + 512, :].rearrange("(j p) c -> p j c", p=128), zb4, r=["zb"], sem="yout", final=True)
        if half == 1:
            ys = self.o["ys"]
            for kk in range(2):
                b = self.bank(); ps = self.PSB[b]
                for k4 in range(4):
                    k = kk * 4 + k4
                    P.op("pe", lambda e, k=k, k4=k4, ps=ps: e.transpose(ps[0:NS, k4 * 128:(k4 + 1) * 128], self.xT[:, k, HALF:HALF + NS], self.ident),
                         r=[("xT", HALF), "cst"], w=[("ps", b)])
                P.op("dve", lambda e, kk=kk, ps=ps: e.tensor_copy(out=zb4[0:NS, 0, kk * 512:(kk + 1) * 512], in_=ps[0:NS, :]), r=[("ps", b)], w=["zb"])
            P.dma("sp", ys, zb4[0:NS, 0, :], r=["zb"], sem="yout", final=True)

    def prep_mem(self):
        P = self.P
        mem = self.i["mem"]
        zb2 = self.zb[:].rearrange("p k c -> p (k c)").rearrange("p (j c) -> p j c", c=D)
        P.dma("sp", zb2[:, 0:2, :], mem.rearrange("(j p) c -> p j c", p=128), w=["zb"], sem="zb")
        ss = self.small[:, 0:2]
        for j in range(2):
            P.op("act", lambda e, j=j: e.activation(out=zb2[:, 2 + j, :], in_=zb2[:, j, :], func=AF.Square, accum_out=ss[:, j:j + 1]), r=["zb"], w=["zb", "small"])
        P.op("act", lambda e: e.activation(out=ss, in_=ss, func=AF.Sqrt, scale=1.0 / D, bias=self.epsb[:, 0:1]), r=["small", "epsb"], w=["small"])
        P.op("dve", lambda e: e.reciprocal(out=ss, in_=ss), r=["small"], w=["small"])
        for j in range(2):
            P.op("dve", lambda e, j=j: e.tensor_scalar(out=zb2[:, j, :], in0=zb2[:, j, :], scalar1=ss[:, j:j + 1], scalar2=None, op0=ALU.mult), r=["zb", "small"], w=["zb"])
        for k in range(KT):
            b = self.bank(); ps = self.PSB[b]
            for j in range(2):
                P.op("pe", lambda e, j=j, k=k, ps=ps: e.transpose(ps[:, j * 128:(j + 1) * 128], zb2[:, j, k * 128:(k + 1) * 128], self.ident), r=["zb", "cst"], w=[("ps", b)])
            P.op("dve", lambda e, k=k, ps=ps: e.tensor_copy(out=self.memn[:, k, :], in_=ps[:, 0:NMEM]), r=[("ps", b)], w=["memn"])

    def kv(self, l, half):
        P = self.P
        kT_, vv_ = self.kT, self.vv
        for k in range(KT):
            P.op("dve", lambda e, k=k: e.tensor_scalar(out=self.memh[:, k, :], in0=self.memn[:, k, :], scalar1=self.gvec[:, 4, l, k:k + 1], scalar2=None, op0=ALU.mult),
                 r=["memn", "gvec"], w=["memh"])
        for c in range(4):
            wb, wres = self.wload([self.w_kv[l][:, c * 256:(c + 1) * 256]], KT, 256)
            if c < 2:
                for hh in range(2):
                    b = self.bank(); ps = self.PSB[b]
                    for k in range(KT):
                        P.op("pe", lambda e, k=k, hh=hh, ps=ps, wb=wb: e.matmul(ps[:, 0:NMEM], lhsT=wb[:, k, hh * 128:(hh + 1) * 128], rhs=self.memh[:, k, :], start=(k == 0), stop=(k == KT - 1)),
                             r=[wres, "memh"], w=[("ps", b)])
                    P.op("act", lambda e, hh=hh, ps=ps, c=c: e.activation(out=kT_[:, 2 * c + hh, :], in_=ps[:, 0:NMEM], func=AF.Copy), r=[("ps", b)], w=["kT"])
            if c >= 2 or half == 0:
                for mt in range(2):
                    b = self.bank(); ps = self.PSB[b]
                    for k in range(KT):
                        P.op("pe", lambda e, k=k, mt=mt, ps=ps, wb=wb: e.matmul(ps[:, 0:256], lhsT=self.memh[:, k, mt * 128:(mt + 1) * 128], rhs=wb[:, k, :], start=(k == 0), stop=(k == KT - 1)),
                             r=[wres, "memh"], w=[("ps", b)])
                    if c >= 2:
                        P.op("act", lambda e, mt=mt, ps=ps, c=c: e.activation(out=vv_[:, mt, (c - 2) * 256:(c - 1) * 256], in_=ps[:, 0:256], func=AF.Copy), r=[("ps", b)], w=["vv"])
                    if half == 0:
                        t = self.tmpA[mt]
                        P.op("dve", lambda e, ps=ps, t=t: e.tensor_copy(out=t[:, 0:256], in_=ps[:, 0:256]), r=[("ps", b)], w=[("tmpA", mt)])
                        dst = self.o["mk_p"] if c < 2 else self.o["mv_p"]
                        cc = c % 2
                        P.dma("sp", dst[l][mt * 128:(mt + 1) * 128, cc * 256:(cc + 1) * 256], t[:, 0:256], r=[("tmpA", mt)], sem="kvout%d" % mt, final=True)

    def norm_to_h(self, l, kind):
        P = self.P
        for (t0, w) in self.tbs:
            self.norm_rstd(lambda k, t0=t0, w=w: self.xT[:, k, t0:t0 + w], lambda k, t0=t0: ("xT", t0), KT, w, "n")
            for k in range(KT):
                eng = "dve"
                if eng == "dve":
                    P.op("dve", lambda e, k=k, t0=t0, w=w: e.scalar_tensor_tensor(out=self.hT[:, k, t0:t0 + w], in0=self.xT[:, k, t0:t0 + w], scalar=self.gvec[:, kind, l, k:k + 1],
                                                                                 in1=self.rstd[:, :w], op0=ALU.mult, op1=ALU.mult),
                         r=[("xT", t0), "rstd", "gvec"], w=[("hT", t0)])
                else:
                    P.op("pool", lambda e, k=k, t0=t0, w=w: e.scalar_tensor_tensor(out=self.hT[:, k, t0:t0 + w], in0=self.xT[:, k, t0:t0 + w], scalar=self.gvec[:, kind, l, k:k + 1],
                                                                                  in1=self.rstd[:, :w], op0=ALU.mult, op1=ALU.mult),
                         r=[("xT", t0), "rstd", "gvec"], w=[("hT", t0)])

    def mm_fm(self, wb, wres, kt, nm, src, src_name, evac, m0=0):
        P = self.P
        for mi in range(nm):
            for (t0, w) in self.tbs:
                b = self.bank(); ps = self.PSB[b]
                for k in range(kt):
                    P.op("pe", lambda e, k=k, mi=mi, ps=ps, t0=t0, w=w: e.matmul(ps[:, :w], lhsT=wb[:, k, mi * 128:(mi + 1) * 128], rhs=src[:, k, t0:t0 + w],
                                                                                start=(k == 0), stop=(k == kt - 1)),
                         r=[wres, (src_name, t0)], w=[("ps", b)])
                evac(ps, b, m0 + mi, t0, w)

    def proj_u(self, l, col0, extra=None):
        P = self.P
        for c in range(2):
            wb, wres = self.wload([self.w_in[l][:, col0 + c * 256: col0 + (c + 1) * 256]], KT, 256)

            def evac(ps, b, m, t0, w):
                if (m + (t0 // 512)) % 2 == 0:
                    P.op("act", lambda e: e.activation(out=self.uT[:, m, t0:t0 + w], in_=ps[:, :w], func=AF.Copy), r=[("ps", b)], w=[("uT", t0)])
                else:
                    P.op("dve", lambda e: e.tensor_copy(out=self.uT[:, m, t0:t0 + w], in_=ps[:, :w]), r=[("ps", b)], w=[("uT", t0)])
                if extra is not None:
                    extra(ps, b, m, t0, w)
            self.mm_fm(wb, wres, KT, 2, self.hT, "hT", evac, m0=2 * c)
            if extra is not None and hasattr(extra, "chunk"):
                extra.chunk(wb, wres, c)

    def merge(self, l, br, src, src_name):
        P = self.P
        for c2 in range(2):
            for cg in range(2):
                ucol = c2 * 512 + cg * 256
                wu, wures = self.wload([self.w_up[l, br][:, ucol:ucol + 256]], 4, 256)
                gcol = 1536 + br * 1024 + ucol
                wg, wgres = self.wload([self.w_in[l][:, gcol:gcol + 256]], KT, 256)
                for mi in range(2):
                    m = c2 * 4 + cg * 2 + mi
                    for (t0, w) in self.tbs:
                        bg = self.bank(); pg = self.PSB[bg]
                        for k in range(KT):
                            P.op("pe", lambda e, k=k, mi=mi, pg=pg, t0=t0, w=w, wg=wg: e.matmul(pg[:, :w], lhsT=wg[:, k, mi * 128:(mi + 1) * 128], rhs=self.hT[:, k, t0:t0 + w],
                                                                                                start=(k == 0), stop=(k == KT - 1)),
                                 r=[wgres, ("hT", t0)], w=[("ps", bg)])
                        si = self.uid() % 2
                        sg = self.tmpB[si]
                        P.op("act", lambda e, pg=pg, sg=sg, w=w: e.activation(out=sg[:, :w], in_=pg[:, :w], func=AF.Sigmoid), r=[("ps", bg)], w=[("tmpB", si)])
                        bu = self.bank(); pu = self.PSB[bu]
                        mu = mi
                        for k in range(4):
                            P.op("pe", lambda e, k=k, mu=mu, pu=pu, t0=t0, w=w, wu=wu: e.matmul(pu[:, :w], lhsT=wu[:, k, mu * 128:(mu + 1) * 128], rhs=src[:, k, t0:t0 + w],
                                                                                                start=(k == 0), stop=(k == 3)),
                                 r=[wures, (src_name, t0)], w=[("ps", bu)])
                        if br == 0:
                            P.op("dve", lambda e, pu=pu, sg=sg, m=m, t0=t0, w=w: e.tensor_tensor(out=self.mg[:, m, t0:t0 + w], in0=pu[:, :w], in1=sg[:, :w], op=ALU.mult),
                                 r=[("ps", bu), ("tmpB", si)], w=[("mg", t0)])
                        else:
                            P.op("dve", lambda e, pu=pu, sg=sg, w=w: e.tensor_tensor(out=sg[:, :w], in0=pu[:, :w], in1=sg[:, :w], op=ALU.mult),
                                 r=[("ps", bu), ("tmpB", si)], w=[("tmpB", si)])
                            P.op("pool", lambda e, sg=sg, m=m, t0=t0, w=w: e.tensor_tensor(out=self.mg[:, m, t0:t0 + w], in0=self.mg[:, m, t0:t0 + w], in1=sg[:, :w], op=ALU.add),
                                 r=[("tmpB", si), ("mg", t0)], w=[("mg", t0)])

    def post_norm_add(self, l, kind, zT, zname):
        P = self.P
        for (t0, w) in self.tbs:
            self.norm_rstd(lambda k, t0=t0, w=w: zT[:, k, t0:t0 + w], lambda k, t0=t0: (zname, t0), KT, w, "p")
            for k in range(KT):
                si = self.uid() % 2
                t = self.tmpA[si]
                P.op("dve", lambda e, k=k, t=t, t0=t0, w=w: e.scalar_tensor_tensor(out=t[:, :w], in0=zT[:, k, t0:t0 + w], scalar=self.gvec[:, kind, l, k:k + 1], in1=self.rstd[:, :w],
                                                                                  op0=ALU.mult, op1=ALU.mult),
                     r=[(zname, t0), "rstd", "gvec"], w=[("tmpA", si)])
                P.op("pool", lambda e, k=k, t=t, t0=t0, w=w: e.tensor_tensor(out=self.xT[:, k, t0:t0 + w], in0=self.xT[:, k, t0:t0 + w], in1=t[:, :w], op=ALU.add),
                     r=[("tmpA", si), ("xT", t0)], w=[("xT", t0)])

    def out_proj(self, l):
        P = self.P
        zT = self.A[:, 8 * NT:16 * NT].rearrange("p (k n) -> p k n", n=NT)
        for c in range(4):
            wb, wres = self.wload([self.w_out[l][:, c * 256:(c + 1) * 256]], KT, 256)

            def evac(ps, b, m, t0, w):
                if (m + t0 // 512) % 2 == 0:
                    P.op("act", lambda e: e.activation(out=zT[:, m, t0:t0 + w], in_=ps[:, :w], func=AF.Copy), r=[("ps", b)], w=[("zT", t0), ("uT", t0), ("oT", t0)])
                else:
                    P.op("dve", lambda e: e.tensor_copy(out=zT[:, m, t0:t0 + w], in_=ps[:, :w]), r=[("ps", b)], w=[("zT", t0), ("uT", t0), ("oT", t0)])
            self.mm_fm(wb, wres, KT, 2, self.mg, "mg", evac, m0=2 * c)
        self.post_norm_add(l, 1, zT, "zT")

    def ffn(self, l):
        P = self.P
        self.norm_to_h(l, 2)
        for j in range(FT):
            wb, wres = self.wload([self.w_f1[l][:, j * 128:(j + 1) * 128], self.w_f1[l][:, DFF + j * 128:DFF + (j + 1) * 128]], KT, 256)
            for (t0, w) in self.tbs:
                ba = self.bank(); pa = self.PSB[ba]
                bb = self.bank(); pb = self.PSB[bb]
                for mi, (bx, px) in enumerate(((ba, pa), (bb, pb))):
                    for k in range(KT):
                        P.op("pe", lambda e, k=k, mi=mi, px=px, t0=t0, w=w, wb=wb: e.matmul(px[:, :w], lhsT=wb[:, k, mi * 128:(mi + 1) * 128], rhs=self.hT[:, k, t0:t0 + w],
                                                                                            start=(k == 0), stop=(k == KT - 1)),
                             r=[wres, ("hT", t0)], w=[("ps", bx)])
                si = self.uid() % 2
                sg = self.tmpB[si]
                P.op("act", lambda e, pa=pa, sg=sg, w=w: e.activation(out=sg[:, :w], in_=pa[:, :w], func=AF.Silu), r=[("ps", ba)], w=[("tmpB", si)])
                if j < 8:
                    al = [("mg", t0)]
                elif j < 16:
                    al = [("zT", t0), ("uT", t0), ("oT", t0)]
                else:
                    al = [("Hbf", 0), ("Hbf", 1), ("Hbf", 2), ("Hbf", 3), "memh", "ssmw"]
                P.op("dve", lambda e, pb=pb, sg=sg, j=j, t0=t0, w=w: e.tensor_tensor(out=self.aT[:, j, t0:t0 + w], in0=pb[:, :w], in1=sg[:, :w], op=ALU.mult),
                     r=[("ps", bb), ("tmpB", si)], w=[("aT", t0)] + al)
        zT = self.Hst[:].rearrange("p a r c -> p (a r c)").bitcast(BF16).rearrange("p (k n) -> p k n", n=NT)
        for m in range(KT):
            banks = [self.bank() for _ in self.tbs]
            for kh in range(2):
                wb, wres = self.wload([self.w_f2[l][kh * 1408:(kh + 1) * 1408, m * 128:(m + 1) * 128]], 11, 128)
                for ti, (t0, w) in enumerate(self.tbs):
                    b = banks[ti]; ps = self.PSB[b]
                    for k in range(11):
                        kk = kh * 11 + k
                        P.op("pe", lambda e, k=k, kk=kk, ps=ps, t0=t0, w=w, wb=wb: e.matmul(ps[:, :w], lhsT=wb[:, k, :], rhs=self.aT[:, kk, t0:t0 + w],
                                                                                            start=(kk == 0), stop=(kk == FT - 1)),
                             r=[wres, ("aT", t0)], w=[("ps", b)])
            for ti, (t0, w) in enumerate(self.tbs):
                b = banks[ti]; ps = self.PSB[b]
                if ti % 2 == 0:
                    P.op("act", lambda e, ps=ps, m=m, t0=t0, w=w: e.activation(out=zT[:, m, t0:t0 + w], in_=ps[:, :w], func=AF.Copy), r=[("ps", b)], w=[("zF", t0), "carry"] + [("Hst", g) for g in range(4)])
                else:
                    P.op("dve", lambda e, ps=ps, m=m, t0=t0, w=w: e.tensor_copy(out=zT[:, m, t0:t0 + w], in_=ps[:, :w]), r=[("ps", b)], w=[("zF", t0), "carry"] + [("Hst", g) for g in range(4)])
        self.post_norm_add(l, 3, zT, "zF")
        if FFNBAR:
            P.barrier()

    def _emit(self, eng, fn, r, w):
        r = list(r) + list(self._xr)
        w = list(w) + list(self._xw)
        if self._q is not None:
            self._q.append((eng, fn, r, w))
        else:
            self.P.op(eng, fn, r=r, w=w)

    def _flush(self, qa, qb):
        i = j = 0
        while i < len(qa) or j < len(qb):
            if i < len(qa):
                eng, fn, r, w = qa[i]; self.P.op(eng, fn, r=r, w=w); i += 1
            if j < len(qb):
                eng, fn, r, w = qb[j]; self.P.op(eng, fn, r=r, w=w); j += 1

    def _tt(self, out, a, b, op):
        ch = self._ch
        self._emit(self._eng, lambda e: e.tensor_tensor(out=out, in0=a, in1=b, op=op), [ch], [ch])

    def _ts(self, out, a, s1, op0, s2=None, op1=None, extra_r=()):
        ch = self._ch
        if op1 is None:
            self._emit(self._eng, lambda e: e.tensor_scalar(out=out, in0=a, scalar1=s1, scalar2=None, op0=op0), [ch] + list(extra_r), [ch])
        else:
            self._emit(self._eng, lambda e: e.tensor_scalar(out=out, in0=a, scalar1=s1, scalar2=s2, op0=op0, op1=op1), [ch] + list(extra_r), [ch])

    def _mask(self, out, a, mcol):
        ch = self._ch
        if self._eng == "pool":
            shp = list(out.shape)
            mc = mcol
            for d in range(2, len(shp)):
                mc = mc.unsqueeze(d)
            mb = mc.to_broadcast(shp)
            self._emit("pool", lambda e: e.tensor_tensor(out=out, in0=a, in1=mb, op=ALU.mult), [ch, "cst"], [ch])
        else:
            self._emit(self._eng, lambda e: e.tensor_scalar(out=out, in0=a, scalar1=mcol, scalar2=None, op0=ALU.mult), [ch, "cst"], [ch])

    def _cp(self, out, a):
        ch = self._ch
        self._emit(self._eng, lambda e: e.tensor_copy(out=out, in_=a), [ch], [ch])

    def _ms(self, out, val):
        ch = self._ch
        self._emit(self._eng, lambda e: e.memset(out, val), [ch], [ch])

    def _act(self, out, a, func, scale=1.0, bias=None):
        ch = self._ch
        if bias is None:
            self._emit("act", lambda e: e.activation(out=out, in_=a, func=func, scale=scale), [ch], [ch])
        else:
            self._emit("act", lambda e: e.activation(out=out, in_=a, func=func, scale=scale, bias=bias), [ch, "epsb"], [ch])

    def _rcp(self, out, a):
        ch = self._ch
        self._emit("dve", lambda e: e.reciprocal(out=out, in_=a), [ch], [ch])

    def _disc(self, F, lr, li, ldt, S, pre_rden=None):
        tt, ts, act = self._tt, self._ts, self._act
        s0, s1, s2, s3, s4, s5, s6 = S[:7]
        if pre_rden is not None:
            ch = self._ch
            self.P.op("dve", lambda e: e.tensor_tensor(out=pre_rden, in0=lr, in1=lr, op=ALU.mult), r=[ch], w=[ch])
            self.P.op("dve", lambda e: e.tensor_tensor(out=s6, in0=li, in1=li, op=ALU.mult), r=[ch], w=[ch])
            self.P.op("dve", lambda e: e.tensor_tensor(out=pre_rden, in0=pre_rden, in1=s6, op=ALU.add), r=[ch], w=[ch])
            self.P.op("dve", lambda e: e.reciprocal(out=pre_rden, in_=pre_rden), r=[ch], w=[ch])
        act(s0, ldt, AF.Exp)
        tt(s1, lr, s0, ALU.mult)
        tt(s2, li, s0, ALU.mult)
        act(s0, s1, AF.Exp)
        act(s1, s2, AF.Sin, scale=1.0 / 32, bias=self.epsb[:, 1:2])
        act(s3, s2, AF.Sin, scale=1.0 / 32)
        for _ in range(5):
            tt(s2, s1, s1, ALU.mult)
            tt(s4, s3, s3, ALU.mult)
            tt(s5, s1, s3, ALU.mult)
            tt(s1, s2, s4, ALU.subtract)
            ts(s3, s5, 2.0, ALU.mult)
        tt(s2, s0, s1, ALU.mult)
        tt(s4, s0, s3, ALU.mult)
        if pre_rden is None:
            tt(s0, lr, lr, ALU.mult)
            tt(s1, li, li, ALU.mult)
            tt(s0, s0, s1, ALU.add)
            self._rcp(s0, s0)
        else:
            s0 = pre_rden
        ts(s1, s2, -1.0, ALU.add)
        tt(s3, s1, lr, ALU.mult)
        tt(s5, s4, li, ALU.mult)
        tt(s3, s3, s5, ALU.add)
        tt(s3, s3, s0, ALU.mult)
        tt(s5, s4, lr, ALU.mult)
        tt(s6, s1, li, ALU.mult)
        tt(s5, s5, s6, ALU.subtract)
        tt(s5, s5, s0, ALU.mult)
        return s2, s4, s3, s5

    def ssm_prep(self, l):
        P = self.P
        tt, ts = self._tt, self._ts
        pw = self.prm
        P.dma("sp", pw[:, 0:2352], self.i["sprm"][l], w=["pP", "pL"], sem="prm")
        zf = self.zb[:].rearrange("p k c -> p (k c)")
        self._eng, self._ch = "dve", "pP"
        mE = self.cst[:, 128:130]; mL = self.cst[:, 130:132]; mM = self.cst[:, 132:136]; nmE = self.cst[:, 200:202]
        F = 256
        self._eng, self._ch = "pool", "pL"
        zf = self.hT[:].rearrange("p k n -> p (k n)").bitcast(F32)
        S = [zf[:, j * F:(j + 1) * F] for j in range(7)]
        ar, ai, fr, fi = self._disc(F, pw[:, 1072:1328], pw[:, 1328:1584], pw[:, 1584:1840], S, pre_rden=zf[:, 3840:4096])
        o = 7 * F
        brL = pw[:, 1840:2096]; biL = pw[:, 2096:2352]

        def t1(j):
            return zf[:, o + j * 256:o + (j + 1) * 256]
        bb_r, bb_i, cur_r, cur_i, n_r, n_i, u1, u2 = [t1(j) for j in range(8)]
        x1, x2 = S[0], S[1]
        tt(u1, brL, fr, ALU.mult); tt(u2, biL, fi, ALU.mult); tt(bb_r, u1, u2, ALU.subtract)
        tt(u1, biL, fr, ALU.mult); tt(u2, brL, fi, ALU.mult); tt(bb_i, u1, u2, ALU.add)
        self._xw = [("curL", 0), ("curL", 1), "pLa", "pLb"]
        self._ms(cur_r, 1.0)
        self._ms(cur_i, 0.0)
        self._xw = ()
        for k in range(8):
            s = 7 - k
            par = k % 2
            self._q = qa = []
            self._ch = "pLa"; self._xr = [("curL", par), "pL"]; self._xw = ()
            tt(u1, cur_r, bb_r, ALU.mult); tt(u2, cur_i, bb_i, ALU.mult); tt(u1, u1, u2, ALU.subtract)
            for e2 in range(2):
                self._mask(self.Wsb[:, :, s, 0, e2 * 64:(e2 + 1) * 64], u1.rearrange("p (g q) -> p g q", q=64), mL[:, e2:e2 + 1])
            tt(u1, cur_r, bb_i, ALU.mult); tt(u2, cur_i, bb_r, ALU.mult); tt(u1, u1, u2, ALU.add)
            for e2 in range(2):
                self._mask(self.Wsb[:, :, s, 1, e2 * 64:(e2 + 1) * 64], u1.rearrange("p (g q) -> p g q", q=64), mL[:, e2:e2 + 1])
            self._q = qb = []
            if k < 7:
                self._ch = "pLb"; self._xr = [("curL", par), "pL"]; self._xw = ()
                tt(x1, cur_r, ar, ALU.mult); tt(x2, cur_i, ai, ALU.mult)
                self._xw = [("curL", 1 - par)]
                tt(n_r, x1, x2, ALU.subtract)
                self._xw = ()
                tt(x1, cur_r, ai, ALU.mult); tt(x2, cur_i, ar, ALU.mult)
                self._xw = [("curL", 1 - par)]
                tt(n_i, x1, x2, ALU.add)
                self._xw = ()
            self._q = None
            self._flush(qa, qb)
            if k < 7:
                cur_r, n_r = n_r, cur_r
                cur_i, n_i = n_i, cur_i
        self._xr = (); self._xw = ()
        self._ch = "pL"
        self.P.op("pool", lambda e: e.memset(self.small[:, 761:762], 0.0), r=["pLa", "pLb", ("curL", 0), ("curL", 1)], w=["pL"])
        zf = self.zb[:].rearrange("p k c -> p (k c)")
        self._eng, self._ch = "dve", "pP"
        F = 16
        S = [zf[:, j * F:(j + 1) * F] for j in range(7)]
        ar, ai, fr, fi = self._disc(F, pw[:, 0:16], pw[:, 16:32], pw[:, 32:48], S)
        o = 7 * F
        br = pw[:, 48:304].rearrange("p (a h) -> p a h", h=16); bi = pw[:, 304:560].rearrange("p (a h) -> p a h", h=16)
        cr = pw[:, 560:816].rearrange("p (a h) -> p a h", h=16); ci = pw[:, 816:1072].rearrange("p (a h) -> p a h", h=16)

        def t3(j):
            return zf[:, o + j * 256:o + (j + 1) * 256].rearrange("p (a h) -> p a h", h=16)
        bb_r, bb_i, u1, u2 = t3(0), t3(1), t3(2), t3(3)
        frb = fr.unsqueeze(2).to_broadcast([128, 16, 16]); fib = fi.unsqueeze(2).to_broadcast([128, 16, 16])
        tt(u1, br, frb, ALU.mult); tt(u2, bi, fib, ALU.mult); tt(bb_r, u1, u2, ALU.subtract)
        tt(u1, bi, frb, ALU.mult); tt(u2, br, fib, ALU.mult); tt(bb_i, u1, u2, ALU.add)
        Bp = self.small[:, 0:512].bitcast(BF16)[:, 0:1024].rearrange("p (a r c) -> p a r c", a=16, r=2)
        for ri, bb in enumerate((bb_r, bb_i)):
            for e2 in range(2):
                self._mask(Bp[:, :, ri, e2 * 16:(e2 + 1) * 16], bb, mE[:, e2:e2 + 1])
        o2 = o + 4 * 256
        cur_r = zf[:, o2:o2 + 16]; cur_i = zf[:, o2 + 16:o2 + 32]; n_r = zf[:, o2 + 32:o2 + 48]; n_i = zf[:, o2 + 48:o2 + 64]
        w1 = zf[:, o2 + 64:o2 + 80]; w2 = zf[:, o2 + 80:o2 + 96]
        self._xw = [("curP", 0), ("curP", 1), "pPa", "pPb"]
        self._ms(cur_r, 1.0)
        self._ms(cur_i, 0.0)
        self._xw = ()
        for k in range(9):
            par = k % 2
            crb = cur_r.unsqueeze(2).to_broadcast([128, 16, 16]); cib = cur_i.unsqueeze(2).to_broadcast([128, 16, 16])
            self._q = qa = []
            self._ch = "pPa"; self._xr = [("curP", par), "pP"]; self._xw = ()
            tt(u1, cr, crb, ALU.mult); tt(u2, ci, cib, ALU.mult); tt(u1, u1, u2, ALU.subtract)
            for e2 in range(2):
                self._mask(self.Wco[:, :, k, 0, e2 * 16:(e2 + 1) * 16], u1, mE[:, e2:e2 + 1])
            tt(u1, cr, cib, ALU.mult); tt(u2, ci, crb, ALU.mult); tt(u1, u1, u2, ALU.add)
            for e2 in range(2):
                self._mask(self.Wco[:, :, k, 1, e2 * 16:(e2 + 1) * 16], u1, nmE[:, e2:e2 + 1])
            if k == 1:
                self._cp(self.A1[:, 0, :], cur_r)
                self._cp(self.A1[:, 1, :], cur_i)
            if k == 8:
                self._cp(self.A8[:, 0, :], cur_r)
                self._cp(self.A8[:, 1, :], cur_i)
            self._q = qb = []
            if k < 8:
                self._ch = "pPb"; self._xr = [("curP", par), "pP"]; self._xw = ()
                tt(w1, cur_r, ar, ALU.mult); tt(w2, cur_i, ai, ALU.mult)
                self._xw = [("curP", 1 - par)]
                tt(n_r, w1, w2, ALU.subtract)
                self._xw = ()
                tt(w1, cur_r, ai, ALU.mult); tt(w2, cur_i, ar, ALU.mult)
                self._xw = [("curP", 1 - par)]
                tt(n_i, w1, w2, ALU.add)
                self._xw = ()
            self._q = None
            self._flush(qa, qb)
            if k < 8:
                cur_r, n_r = n_r, cur_r
                cur_i, n_i = n_i, cur_i
        self._xr = (); self._xw = ()
        self._ch = "pP"
        self.P.op("dve", lambda e: e.memset(self.small[:, 762:763], 0.0), r=["pPa", "pPb", ("curP", 0), ("curP", 1)], w=["pP"])
        A8r, A8i = self.A8[:, 0, :], self.A8[:, 1, :]
        r2 = zf[:, o2 + 96:o2 + 112]; r3 = zf[:, o2 + 112:o2 + 128]; Ur = zf[:, o2 + 128:o2 + 144]; Ui = zf[:, o2 + 144:o2 + 160]
        tt(r2, A8r, A8r, ALU.mult); tt(r3, A8i, A8i, ALU.mult); tt(r2, r2, r3, ALU.add)
        self._act(self.rho, r2, AF.Sqrt)
        self._rcp(r3, self.rho)
        tt(Ur, A8r, r3, ALU.mult); tt(Ui, A8i, r3, ALU.mult)
        Tc, Ts = self.Tc, self.Ts
        self._cp(Tc[:, :, 0], Ur); self._cp(Ts[:, :, 0], Ui)
        g1 = zf[:, 2048:3072]; g2 = zf[:, 3072:4096]
        Pr, Pi = Ur, Ui
        q1 = zf[:, o2 + 160:o2 + 176]; q2 = zf[:, o2 + 176:o2 + 192]; q3 = zf[:, o2 + 192:o2 + 208]; q4 = zf[:, o2 + 208:o2 + 224]
        nxt = [(q1, q2), (q3, q4)]
        n = 1
        lvl = 0
        while n < NCH:
            a1 = g1[:, 0:16 * n].rearrange("p (a j) -> p a j", j=n); a2 = g2[:, 0:16 * n].rearrange("p (a j) -> p a j", j=n)
            Prb = Pr.unsqueeze(2).to_broadcast([128, 16, n]); Pib = Pi.unsqueeze(2).to_broadcast([128, 16, n])
            pres = "pP" if lvl == 0 else ("Pq", (lvl - 1) % 2)
            self._q = qa = []
            self._ch = "pTa"; self._xr = [pres, "pP"]; self._xw = ()
            tt(a1, Tc[:, :, 0:n], Prb, ALU.mult); tt(a2, Ts[:, :, 0:n], Pib, ALU.mult); tt(Tc[:, :, n:2 * n], a1, a2, ALU.subtract)
            tt(a1, Tc[:, :, 0:n], Pib, ALU.mult); tt(a2, Ts[:, :, 0:n], Prb, ALU.mult); tt(Ts[:, :, n:2 * n], a1, a2, ALU.add)
            self._q = qb = []
            if 2 * n < NCH:
                nr, ni = nxt[lvl % 2]
                b1 = zf[:, o2 + 224:o2 + 240]; b2 = zf[:, o2 + 240:o2 + 256]
                self._ch = "pTb"; self._xr = [pres, "pP"]; self._xw = ()
                tt(b1, Pr, Pr, ALU.mult); tt(b2, Pi, Pi, ALU.mult); tt(b1, b1, b2, ALU.subtract)
                tt(b2, Pr, Pi, ALU.mult)
                self._xw = [("Pq", lvl % 2)]
                ts(ni, b2, 2.0, ALU.mult); self._cp(nr, b1)
                self._xw = ()
            self._q = None
            self._flush(qa, qb)
            if 2 * n < NCH:
                Pr, Pi = nr, ni
            n *= 2
            lvl += 1
        self._xr = (); self._xw = ()
        self._ch = "pP"
        self.P.op("dve", lambda e: e.memset(self.small[:, 763:764], 0.0), r=["pTa", "pTb", ("Pq", 0), ("Pq", 1)], w=["pP"])
        for gt in range(4):
            pb = 6 + gt % 2
            ps = self.PSB[pb]
            for m in range(4):
                pair = 4 * gt + m
                for ri in range(2):
                    P.op("pe", lambda e, m=m, pair=pair, ri=ri, ps=ps: e.matmul(ps[32 * m:32 * m + 32, 0:256], lhsT=Bp[:, pair, ri, :],
                                                                              rhs=self.Wco[:, pair, 0:8, ri, :], start=(ri == 0), stop=(ri == 1),
                                                                              skip_group_check=True, tile_position=(0, 32 * m)),
                         r=["pP"], w=[("ps", pb)])
            for m2 in range(4):
                P.op("dve", lambda e, gt=gt, m2=m2, ps=ps: e.tensor_scalar(out=self.Wtp[:, gt, :, 32 * m2:32 * m2 + 32], in0=ps[:, 0:256].rearrange("p (k c) -> p k c", c=32),
                                                                          scalar1=mM[:, m2:m2 + 1], scalar2=None, op0=ALU.mult),
                     r=[("ps", pb), "cst"], w=["Wtp"])
        sc = self.scr
        P.dma("sp", sc["sb"][l], self.Wsb.rearrange("p g k r c -> p (g k r c)"), r=["pL"], sem="pst1")
        P.dma("sp", sc["co"][l], self.Wco[:].rearrange("p a k r c -> p (a k r c)"), r=["pP"], sem="pst2")
        P.dma("sp", sc["tp"][l], self.Wtp[:].rearrange("p g k c -> p (g k c)"), r=["Wtp"], sem="pst3")
        P.dma("sp", sc["t"][l][:, 0:2048], self.Tc.rearrange("p a c -> p (a c)"), r=["pP"], sem="pst4")
        P.dma("sp", sc["t"][l][:, 2048:4096], self.Ts.rearrange("p a c -> p (a c)"), r=["pP"], sem="pst5")
        P.dma("sp", sc["t"][l][:, 4096:4144], self.a1rho[:, :], r=["pP"], sem="pst6")

    def load_prep(self, l):
        P = self.P
        sc = self.scr
        T3 = (0, 512, HALF)
        aA = [("aT", t) for t in T3]
        P.dma("sp", self.Wsb.rearrange("p g k r c -> p (g k r c)"), sc["sb"][l], r=["scr"], w=["ssmw"] + aA + [("mg", t) for t in T3], sem="pld")
        P.dma("sp", self.Wco[:].rearrange("p a k r c -> p (a k r c)"), sc["co"][l], r=["scr"], w=["ssmw", ("kvs", 0), ("kvs", 1)], sem="pld")
        P.dma("sp", self.Wtp[:].rearrange("p g k c -> p (g k c)"), sc["tp"][l], r=["scr"], w=["ssmw", "kT", "vv", "qtok"], sem="pld")
        P.dma("sp", self.Tc.rearrange("p a c -> p (a c)"), sc["t"][l][:, 0:2048], r=["scr"], w=["ssmw"] + aA + [("oT", t) for t in T3] + [("zT", t) for t in T3], sem="pld")
        P.dma("sp", self.Ts.rearrange("p a c -> p (a c)"), sc["t"][l][:, 2048:4096], r=["scr"], w=["ssmw"] + aA + [("Hbf", g) for g in range(4)], sem="pld")
        P.dma("sp", self.a1rho[:, :], sc["t"][l][:, 4096:4144], r=["scr"], w=["ssmw"], sem="pld")

    def gelu_to(self, y, yres, dst, dres, w):
        P = self.P
        si = self.uid() % 2
        t = self.tmpB[si]
        P.op("pool", lambda e: e.tensor_tensor(out=t[:, :w], in0=y, in1=y, op=ALU.mult), r=[yres], w=[("tmpB", si)])
        P.op("pool", lambda e: e.tensor_scalar(out=t[:, :w], in0=t[:, :w], scalar1=0.044715, scalar2=1.0, op0=ALU.mult, op1=ALU.add), r=[("tmpB", si)], w=[("tmpB", si)])
        P.op("pool", lambda e: e.tensor_tensor(out=t[:, :w], in0=t[:, :w], in1=y, op=ALU.mult), r=[("tmpB", si), yres], w=[("tmpB", si)])
        P.op("act", lambda e: e.activation(out=t[:, :w], in_=t[:, :w], func=AF.Sigmoid, scale=1.5957691216057308), r=[("tmpB", si)], w=[("tmpB", si)])
        P.op("dve", lambda e: e.tensor_tensor(out=dst, in0=y, in1=t[:, :w], op=ALU.mult), r=[("tmpB", si), yres], w=[dres])

    def ssm_core(self, l, half):
        P = self.P
        Hst, Hbf, Wsb, Wco, Wtp, uT = self.Hst, self.Hbf, self.Wsb, self.Wco, self.Wtp, self.uT
        u8 = uT[:, :, 0:HALF].rearrange("p g (c j) -> p g c j", j=LCH)
        HG = [("Hst", g) for g in range(4)]
        P.op("dve", lambda e: e.tensor_copy(out=Hst[:, :, :, 0], in_=self.carry[:, l]), r=["carry"], w=HG)
        if half == 1 and "ss" not in SKIP:
            self.ssm_sample_h0(l)
        Tc, Ts = self.Tc, self.Ts
        tmps = [(self.tmpA[0], ("tmpA", 0)), (self.tmpA[1], ("tmpA", 1)), (self.tmpB[0], ("tmpB", 0)), (self.tmpB[1], ("tmpB", 1))]
        v = lambda t: t[:, :].rearrange("p (a c) -> p a c", c=NCH)

        def rot(sign, qd):
            hres = ("Hst", qd)
            sl = slice(4 * qd, 4 * qd + 4)
            Xr = Hst[:, sl, 0, 1:NCH + 1]; Xi = Hst[:, sl, 1, 1:NCH + 1]
            C = Tc[:, sl, :]; S_ = Ts[:, sl, :]
            (t1, r1), (t2, r2), (t3, r3), (t4, r4) = tmps
            P.op("dve", lambda e: e.tensor_tensor(out=v(t1), in0=Xr, in1=C, op=ALU.mult), r=[hres, "ssmw"], w=[r1])
            P.op("pool", lambda e: e.tensor_tensor(out=v(t2), in0=Xi, in1=S_, op=ALU.mult), r=[hres, "ssmw"], w=[r2])
            P.op("dve", lambda e: e.tensor_tensor(out=v(t3), in0=Xi, in1=C, op=ALU.mult), r=[hres, "ssmw"], w=[r3])
            P.op("pool", lambda e: e.tensor_tensor(out=v(t4), in0=Xr, in1=S_, op=ALU.mult), r=[hres, "ssmw"], w=[r4])
            if sign < 0:
                P.op("dve", lambda e: e.tensor_tensor(out=Xr, in0=v(t1), in1=v(t2), op=ALU.add), r=[r1, r2], w=[hres])
                P.op("pool", lambda e: e.tensor_tensor(out=Xi, in0=v(t3), in1=v(t4), op=ALU.subtract), r=[r3, r4], w=[hres])
            else:
                P.op("dve", lambda e: e.tensor_tensor(out=Xr, in0=v(t1), in1=v(t2), op=ALU.subtract), r=[r1, r2], w=[hres])
                P.op("pool", lambda e: e.tensor_tensor(out=Xi, in0=v(t3), in1=v(t4), op=ALU.add), r=[r3, r4], w=[hres])
        def state_build(gt):
            off = (gt % 2) * 256
            for ri in range(2):
                for k in range(LCH):
                    for m in range(4):
                        ps = self.PSB[m]
                        P.op("pe", lambda e, m=m, ri=ri, k=k, ps=ps: e.matmul(ps[:, off + ri * 128: off + (ri + 1) * 128], lhsT=Wsb[32 * m:32 * m + 32, gt, k, ri, :],
                                                                               rhs=u8[32 * m:32 * m + 32, gt, :, k], start=(k == 0), stop=(k == LCH - 1),
                                                                               skip_group_check=True, tile_position=(32 * m, 0)),
                             r=["ssmw", ("uT", 0), ("uT", 512)], w=[("ps", m)])
            for m in range(4):
                pair = 4 * gt + m
                ps = self.PSB[m]
                if pair % 2:
                    P.op("act", lambda e, pair=pair, ps=ps: e.activation(out=Hst[:, pair, :, 1:NCH + 1], in_=ps[:, off:off + 256].rearrange("p (r c) -> p r c", r=2), func=AF.Copy),
                         r=[("ps", m)], w=[("Hst", gt)])
                else:
                    P.op("dve", lambda e, pair=pair, ps=ps: e.tensor_copy(out=Hst[:, pair, :, 1:NCH + 1], in_=ps[:, off:off + 256].rearrange("p (r c) -> p r c", r=2)),
                         r=[("ps", m)], w=[("Hst", gt)])

        for qd in range(4):
            state_build(qd)
            rot(-1, qd)
            for pair in range(4 * qd, 4 * qd + 4):
                for ri in range(2):
                    P.op("dve", lambda e, pair=pair, ri=ri: e.tensor_tensor_scan(out=Hst[:, pair, ri, 1:NCH + 1], data0=self.rho[:, pair:pair + 1].to_broadcast([128, NCH]),
                                                                                data1=Hst[:, pair, ri, 1:NCH + 1], initial=Hst[:, pair, ri, 0:1], op0=ALU.mult, op1=ALU.add),
                         r=[("Hst", qd), "ssmw"], w=[("Hst", qd)])
            rot(+1, qd)
        for qd in range(4):
            P.op("act", lambda e, qd=qd: e.activation(out=Hbf[:, 4 * qd:4 * qd + 4, :, 0:NCH], in_=Hst[:, 4 * qd:4 * qd + 4, :, 0:NCH], func=AF.Copy), r=HG, w=[("Hbf", qd)])
        P.op("dve", lambda e: e.tensor_copy(out=self.carry[:, l], in_=Hst[:, :, :, NCH]), r=HG, w=["carry"])
        if half == self.nh - 1:
            for ri, nm in enumerate(("re_p", "im_p")):
                t = self.small[:, 640 + 16 * ri: 656 + 16 * ri]
                P.op("act", lambda e, ri=ri, t=t: e.activation(out=t, in_=Hst[:, :, ri, NCH], func=AF.Copy), r=HG, w=[("fin", ri)])
                P.dma("sp", self.o[nm][l].rearrange("(a q) -> q a", q=128), t, r=[("fin", ri)], sem="fin%d" % ri, final=True, allow_slow_non_contiguous=True)
        for gt in range(4):
            for bi_, (t0, w) in enumerate(self.tbs[:2]):
                b = self.bank(); ps = self.PSB[b]
                y3 = ps[:, :].rearrange("p (c j) -> p c j", j=LCH)
                u3 = uT[:, gt, t0:t0 + 512].rearrange("p (c j) -> p c j", j=LCH)
                if TAPS2D:
                    for k in range(LCH):
                        for j2 in range(k, LCH):
                            P.op("pe", lambda e, gt=gt, k=k, j2=j2, y3=y3, u3=u3: e.matmul(y3[:, :, j2], lhsT=Wtp[:, gt, k, :], rhs=u3[:, :, j2 - k], start=(k == 0 and j2 == 0), stop=False, skip_group_check=True),
                                 r=["ssmw", ("uT", t0)], w=[("ps", b)])
                else:
                    for k in range(LCH):
                        P.op("pe", lambda e, gt=gt, k=k, y3=y3, u3=u3: e.matmul(y3[:, :, k:LCH], lhsT=Wtp[:, gt, k, :], rhs=u3[:, :, 0:LCH - k], start=(k == 0), stop=False, skip_group_check=True),
                             r=["ssmw", ("uT", t0)], w=[("ps", b)])
                c0 = bi_ * 64
                for m in range(4):
                    pair = 4 * gt + m
                    for j in range(LCH):
                        for ri in range(2):
                            last = (m == 3 and j == LCH - 1 and ri == 1)
                            P.op("pe", lambda e, m=m, pair=pair, j=j, ri=ri, y3=y3, c0=c0, last=last: e.matmul(y3[32 * m:32 * m + 32, :, j], lhsT=Wco[:, pair, j + 1, ri, :],
                                                                                                           rhs=Hbf[:, pair, ri, c0:c0 + 64], start=False, stop=last,
                                                                                                           skip_group_check=True, tile_position=(0, 32 * m)),
                                 r=["ssmw", ("Hbf", gt)], w=[("ps", b)])
                self.ssm_post(l, gt, ps, b, t0, 512)
        if half == 1 and "ss" not in SKIP:
            self.ssm_sample(l)

    def ssm_post(self, l, gt, ps, b, t0, w):
        P = self.P
        si = self.uid() % 2
        y = self.tmpA[si]
        P.op("dve", lambda e: e.scalar_tensor_tensor(out=y[:, :w], in0=self.uT[:, gt, t0:t0 + w], scalar=self.vec4[:, 0, l, gt:gt + 1], in1=ps[:, :w], op0=ALU.mult, op1=ALU.add),
             r=[("ps", b), ("uT", t0), "vec4"], w=[("tmpA", si)])
        self.gelu_to(y[:, :w], ("tmpA", si), self.oT[:, gt, t0:t0 + w], ("oT", t0), w)

    def glu(self, l):
        P = self.P
        wb, wres = self.wload([self.w_glu[l]], 4, 512)

        def evac(ps, b, m, t0, w):
            si = self.uid() % 2
            sg = self.tmpB[si]
            P.op("act", lambda e: e.activation(out=sg[:, :w], in_=ps[:, :w], func=AF.Sigmoid, bias=self.vec4[:, 1, l, m:m + 1]), r=[("ps", b), "vec4"], w=[("tmpB", si)])
            P.op("dve", lambda e: e.tensor_tensor(out=self.uT[:, m, t0:t0 + w], in0=self.oT[:, m, t0:t0 + w], in1=sg[:, :w], op=ALU.mult), r=[("tmpB", si), ("oT", t0)], w=[("uT", t0)])
        self.mm_fm(wb, wres, 4, 4, self.oT, "oT", evac)

    def ssm_sample_load(self, l):
        P = self.P
        zs = self.zb[:].rearrange("p k c -> p (k c)")[0:NS, :].rearrange("p (r c) -> p r c", r=2)
        P.dma("sp", zs[:, 0, :], self.i["sre"][l], w=["zb", "pw0", "pw1", "hist", ("zbs", 0)], sem="zbs0")
        P.dma("sp", zs[:, 1, :], self.i["sim"][l], w=[("zbs", 1)], sem="zbs1")

    def ssm_sample_h0(self, l):
        P = self.P
        zs = self.zb[:].rearrange("p k c -> p (k c)")[0:NS, :].rearrange("p (r c) -> p r c", r=2)
        b = self.bank(); ps = self.PSB[b]
        ps4 = ps[:, :].rearrange("p (a r n) -> p a r n", a=16, r=2)
        for pair in range(16):
            for ri in range(2):
                P.op("pe", lambda e, pair=pair, ri=ri, ps4=ps4: e.transpose(ps4[:, pair, ri, :], zs[:, ri, pair * 128:(pair + 1) * 128], self.ident[0:NS, 0:NS]),
                     r=["zb", ("zbs", 0), ("zbs", 1), "cst"], w=[("ps", b)])
        H0 = self.small[:, 0:512].rearrange("p (a r n) -> p a r n", a=16, r=2)
        P.op("dve", lambda e: e.tensor_copy(out=H0, in_=ps4), r=[("ps", b)], w=["H0"])

    def ssm_sample(self, l):
        P = self.P
        Wsb, Wco, uT = self.Wsb, self.Wco, self.uT
        H0 = self.small[:, 0:512].rearrange("p (a r n) -> p a r n", a=16, r=2)
        if KSS < 2:
            return
        BU = self.sA[:, :].rearrange("p (a r n) -> p a r n", a=16, r=2)
        BUg = self.sA[:, :].rearrange("p (g m r n) -> p g m r n", g=4, m=4, r=2)
        for gt in range(4):
            for ri in range(2):
                c0 = (gt * 2 + ri) * NS
                for m in range(4):
                    psm = self.PSB[m]
                    P.op("pe", lambda e, m=m, gt=gt, ri=ri, psm=psm, c0=c0: e.matmul(psm[:, c0:c0 + NS], lhsT=Wsb[32 * m:32 * m + 32, gt, 7, ri, :], rhs=uT[32 * m:32 * m + 32, gt, HALF:HALF + NS],
                                                                                   start=True, stop=True, skip_group_check=True, tile_position=(32 * m, 0)),
                         r=["ssmw", ("uT", HALF)], w=[("ps", m)])
        for m in range(4):
            psm = self.PSB[m]
            P.op("dve" if m % 2 else "act", (lambda e, m=m, psm=psm: e.tensor_copy(out=BUg[:, :, m], in_=psm[:, 0:128].rearrange("p (g r n) -> p g r n", g=4, r=2))) if m % 2 else
                 (lambda e, m=m, psm=psm: e.activation(out=BUg[:, :, m], in_=psm[:, 0:128].rearrange("p (g r n) -> p g r n", g=4, r=2), func=AF.Copy)), r=[("ps", m)], w=["BU"])
        if KSS < 3:
            return
        A1r = self.A1[:, 0, :].unsqueeze(2).to_broadcast([128, 16, NS]); A1i = self.A1[:, 1, :].unsqueeze(2).to_broadcast([128, 16, NS])
        T = self.sB[:, 0:256].rearrange("p (a n) -> p a n", n=NS)
        seq = [(0, A1r, 0, ALU.add), (1, A1i, 0, ALU.subtract), (0, A1i, 1, ALU.add), (1, A1r, 1, ALU.add)]
        for (hs, Ax, dst, op) in seq:
            P.op("pool", lambda e, hs=hs, Ax=Ax: e.tensor_tensor(out=T, in0=H0[:, :, hs, :], in1=Ax, op=ALU.mult), r=["H0", "ssmw"], w=["sT"])
            P.op("pool", lambda e, dst=dst, op=op: e.tensor_tensor(out=BU[:, :, dst, :], in0=BU[:, :, dst, :], in1=T, op=op), r=["sT", "BU"], w=["BU"])
        if KSS < 4:
            return
        hb = self.sB[:, 256:512].bitcast(BF16).rearrange("p (a r n) -> p a r n", a=16, r=2)
        P.op("act", lambda e: e.activation(out=hb, in_=BU, func=AF.Copy), r=["BU"], w=["hb"])
        for gt in range(4):
            b = self.bank(); ps = self.PSB[b]
            for m in range(4):
                pair = 4 * gt + m
                for ri in range(2):
                    P.op("pe", lambda e, m=m, pair=pair, ri=ri, ps=ps: e.matmul(ps[32 * m:32 * m + 32, 0:NS], lhsT=Wco[:, pair, 0, ri, :], rhs=hb[:, pair, ri, :],
                                                                              start=(ri == 0), stop=(ri == 1), skip_group_check=True, tile_position=(0, 32 * m)),
                         r=["ssmw", "hb"], w=[("ps", b)])
            self.ssm_post(l, gt, ps, b, HALF, NS)
        if KSS < 5:
            return
        for ri, nm in enumerate(("re_s", "im_s")):
            for q in range(4):
                psq = self.PSB[q]
                for a4 in range(4):
                    pair = 4 * q + a4
                    P.op("pe", lambda e, pair=pair, ri=ri, a4=a4, psq=psq: e.transpose(psq[0:NS, a4 * 128:(a4 + 1) * 128], BU[:, pair, ri, :], self.ident),
                         r=["BU", "cst"], w=[("ps", q)])
                si = self.uid() % 2
                t = self.tmpA[si]
                P.op("dve", lambda e, psq=psq, t=t: e.tensor_copy(out=t[0:NS, :], in_=psq[0:NS, :]), r=[("ps", q)], w=[("tmpA", si)])
                if KSS == 7:
                    if not hasattr(self, "dbgres"):
                        self.dbgres = self.nc.dram_tensor("dbg_res", [2, NS, 2048], F32, kind="ExternalOutput").ap()
                    P.dma("sp", self.dbgres[ri][:, q * 512:(q + 1) * 512], t[0:NS, :], r=[("tmpA", si)], sem="sso%d" % si, final=True)
                elif KSS >= 6:
                    P.dma(DQ, self.o[nm][l][:, q * 512:(q + 1) * 512], t[0:NS, :], r=[("tmpA", si)], sem="sso%d" % si, final=True)

    def pool_core(self, l, half):
        P = self.P
        uT = self.uT
        wb, wres = self.wload([self.pool_w[l, gi] for gi in range(4)], 1, 512)
        W = HALF + 16
        bufs = [(self.pw[:, 0:W], "pw0"), (self.pw[:, W:2 * W], "pw1")]
        invc = self.cst[:, 136:200].rearrange("p (g t) -> p g t", t=16)
        for gi, win in enumerate((2, 4, 8, 16)):
            (src, sres), (dst, dres) = bufs
            weng = "pool" if gi in (0, 3) else "dve"
            P.op(weng, lambda e, a=src, gi=gi: e.tensor_copy(out=a[:, 0:16], in_=self.phist[:, l, gi, :]), r=["phist"], w=[sres, "zb", "hist"])
            P.op(weng, lambda e, a=src, gi=gi: e.tensor_copy(out=a[:, 16:W], in_=uT[:, gi, 0:HALF]), r=[("uT", 0), ("uT", 512)], w=[sres])
            P.op(weng, lambda e, gi=gi: e.tensor_copy(out=self.phist[:, l, gi, :], in_=uT[:, gi, HALF - 16:HALF]), r=[("uT", 512), sres], w=["phist"])
            step = 1
            while step < win:
                P.op(weng, lambda e, src=src, dst=dst, step=step: e.tensor_tensor(out=dst[:, step:W], in0=src[:, step:W], in1=src[:, 0:W - step], op=ALU.add), r=[sres], w=[dres])
                P.op(weng, lambda e, src=src, dst=dst, step=step: e.tensor_copy(out=dst[:, 0:step], in_=src[:, 0:step]), r=[sres], w=[dres])
                src, dst = dst, src
                sres, dres = dres, sres
                step *= 2
            pl = dst[:, 0:HALF // 2 + 8].bitcast(BF16)[:, 0:HALF]
            P.op("dve", lambda e, src=src, pl=pl, gi=gi, win=win: e.scalar_tensor_tensor(out=pl, in0=src[:, 16:W], scalar=1.0 / win, in1=uT[:, gi, 0:HALF], op0=ALU.mult, op1=ALU.subtract),
                 r=[sres, ("uT", 0), ("uT", 512)], w=[dres])
            if half == 0:
                t = self.small[:, 672:688]
                P.op("dve", lambda e, src=src, gi=gi, t=t: e.tensor_tensor(out=t, in0=src[:, 16:32], in1=invc[:, gi, :], op=ALU.mult), r=[sres, "cst"], w=["pfix"])
                P.op("dve", lambda e, pl=pl, gi=gi, t=t: e.tensor_tensor(out=pl[:, 0:16], in0=t, in1=uT[:, gi, 0:16], op=ALU.subtract), r=["pfix", ("uT", 0), dres], w=[dres])
            for (t0, w) in self.tbs[:2]:
                b = self.bank(); ps = self.PSB[b]
                P.op("pe", lambda e, gi=gi, ps=ps, pl=pl, t0=t0, w=w: e.matmul(ps[:, :w], lhsT=wb[:, 0, gi * 128:(gi + 1) * 128], rhs=pl[:, t0:t0 + w], start=True, stop=True),
                     r=[wres, dres], w=[("ps", b)])
                P.op("act", lambda e, gi=gi, ps=ps, t0=t0, w=w: e.activation(out=self.oT[:, gi, t0:t0 + w], in_=ps[:, :w], func=AF.Copy, scale=self.vec4[:, 2, l, gi:gi + 1]),
                     r=[("ps", b), "vec4"], w=[("oT", t0)])
        if half == 1 and "ps" not in SKIP:
            self.pool_sample(l, wb, wres)
        if half == self.nh - 1:
            b = self.bank(); ps = self.PSB[b]
            for gi in range(4):
                P.op("pe", lambda e, gi=gi, ps=ps: e.transpose(ps[0:16, gi * 128:(gi + 1) * 128], self.lastu[:, gi, :], self.ident), r=["lastu", "cst"], w=[("ps", b)])
            t = self.sB
            P.op("dve", lambda e, ps=ps: e.tensor_copy(out=t[0:16, :], in_=ps[0:16, :]), r=[("ps", b)], w=["sBo", "sT", "hb", "den"])
            P.dma("sp", self.o["pool_p"][l], t[1:16, :], r=["sBo"], sem="sBo", final=True)

    def pool_sample(self, l, wb, wres):
        P = self.P
        uT = self.uT
        sp = self.i["spool"][l].rearrange("n r c -> (n r) c")
        zt = self.zb[:].rearrange("p k c -> p (k c)")
        hist = self.pw[:, 0:960].rearrange("p (g x) -> p g x", g=4)
        for j in range(2):
            P.dma("sp", zt[0:120, j * 512:(j + 1) * 512], sp[j * 120:(j + 1) * 120, :], w=["zb"], sem="zb")
        P.dma("sp", self.o["pool_s"][l][:, 0:14, :], self.i["spool"][l][:, 1:15, :], sem="pcopy", final=True)
        for gi in range(4):
            b = self.bank(); ps = self.PSB[b]
            for j in range(2):
                P.op("pe", lambda e, gi=gi, j=j, ps=ps: e.transpose(ps[:, j * 120:(j + 1) * 120], zt[0:120, j * 512 + gi * 128: j * 512 + (gi + 1) * 128], self.ident[0:120, 0:120]),
                     r=["zb", "cst"], w=[("ps", b)])
            P.op("dve", lambda e, gi=gi, ps=ps: e.tensor_copy(out=hist[:, gi, :], in_=ps[:, 0:240]), r=[("ps", b)], w=["hist", "pw0"])
        h4 = self.pw[:, 0:960].rearrange("p (g n r) -> p g n r", g=4, r=15)
        red = self.small[:, 688:704]
        plb = self.small[:, 720:752].bitcast(BF16).rearrange("p (g n) -> p g n", g=4)
        for gi, win in enumerate((2, 4, 8, 16)):
            P.op("dve", lambda e, gi=gi, win=win: e.tensor_reduce(out=red, in_=h4[:, gi, :, 16 - win:15], axis=AX.X, op=ALU.add), r=["hist"], w=["red"])
            P.op("dve", lambda e, gi=gi: e.tensor_tensor(out=red, in0=red, in1=self.usam[:, gi, :], op=ALU.add), r=["red", "usam"], w=["red"])
            P.op("dve", lambda e, gi=gi, win=win: e.scalar_tensor_tensor(out=plb[:, gi, :], in0=red, scalar=1.0 / win, in1=self.usam[:, gi, :], op0=ALU.mult, op1=ALU.subtract),
                 r=["red", "usam"], w=["plb"])
            b = self.bank(); ps = self.PSB[b]
            P.op("pe", lambda e, gi=gi, ps=ps: e.matmul(ps[:, 0:NS], lhsT=wb[:, 0, gi * 128:(gi + 1) * 128], rhs=plb[:, gi, :], start=True, stop=True), r=[wres, "plb"], w=[("ps", b)])
            P.op("act", lambda e, gi=gi, ps=ps: e.activation(out=self.oT[:, gi, HALF:HALF + NS], in_=ps[:, 0:NS], func=AF.Copy, scale=self.vec4[:, 2, l, gi:gi + 1]),
                 r=[("ps", b), "vec4"], w=[("oT", HALF)])
        b = self.bank(); ps = self.PSB[b]
        for gi in range(4):
            P.op("pe", lambda e, gi=gi, ps=ps: e.transpose(ps[0:NS, gi * 128:(gi + 1) * 128], self.usam[:, gi, :], self.ident), r=["usam", "cst"], w=[("ps", b)])
        t = self.sA
        P.op("dve", lambda e, ps=ps: e.tensor_copy(out=t[0:NS, :], in_=ps[0:NS, :]), r=[("ps", b)], w=["sAo", "BU"])
        P.dma("sp", self.o["pool_s"][l][:, 14, :], t[0:NS, :], r=["sAo"], sem="sAo", final=True)

    def attn_core(self, l, half):
        P = self.P
        qT = self.uT
        sc = 128 ** -0.5
        for (t0, w) in self.tbs[:2]:
            for h in range(4):
                pts = []
                for mt in range(2):
                    b = self.bank(); ps = self.PSB[b]
                    P.op("pe", lambda e, h=h, mt=mt, ps=ps, t0=t0, w=w: e.matmul(ps[:, :w], lhsT=self.kT[:, h, mt * 128:(mt + 1) * 128], rhs=qT[:, h, t0:t0 + w], start=True, stop=True),
                         r=["kT", ("uT", t0)], w=[("ps", b)])
                    pt = self.sq[mt]
                    P.op("act", lambda e, ps=ps, pt=pt, w=w: e.activation(out=pt[:, :w], in_=ps[:, :w], func=AF.Exp, scale=sc), r=[("ps", b)], w=[("sq", mt)])
                    pts.append(pt)
                bd = self.bank(); pd = self.PSB[bd]
                bo = self.bank(); po = self.PSB[bo]
                for mt in range(2):
                    P.op("pe", lambda e, mt=mt, pd=pd, w=w, pts=pts: e.matmul(pd[:, :w], lhsT=self.onesb[:], rhs=pts[mt][:, :w], start=(mt == 0), stop=(mt == 1)),
                         r=[("sq", mt), "onesb"], w=[("ps", bd)])
                for mt in range(2):
                    P.op("pe", lambda e, mt=mt, h=h, po=po, w=w, pts=pts: e.matmul(po[:, :w], lhsT=self.vv[:, mt, h * 128:(h + 1) * 128], rhs=pts[mt][:, :w], start=(mt == 0), stop=(mt == 1)),
                         r=[("sq", mt), "vv"], w=[("ps", bo)])
                si = self.uid() % 2
                rc = self.tmpA[si]
                P.op("dve", lambda e, pd=pd, rc=rc, w=w: e.reciprocal(out=rc[:, :w], in_=pd[:, :w]), r=[("ps", bd)], w=[("tmpA", si)])
                P.op("dve", lambda e, po=po, rc=rc, h=h, t0=t0, w=w: e.tensor_tensor(out=self.oT[:, h, t0:t0 + w], in0=po[:, :w], in1=rc[:, :w], op=ALU.mult),
                     r=[("ps", bo), ("tmpA", si)], w=[("oT", t0)])
        if half == 1 and "as" not in SKIP:
            self.attn_sample(l)

    def attn_sample(self, l):
        P = self.P
        sc = 128 ** -0.5
        kvbuf = self.Wco[:].rearrange("p a k r c -> p (a k r c)").bitcast(F32)
        KV = [kvbuf[:, j * 2048:(j + 1) * 2048].rearrange("p (x t c) -> p x t c", x=2, t=2) for j in range(2)]
        ck, cv = self.i["ck"], self.i["cv"]
        bo = 6; po = self.PSB[bo]
        bd = 7; pd = self.PSB[bd]
        Sx = self.sA[:, 0:128].rearrange("p (n t h) -> p n t h", n=NS, t=2)
        Pb = self.small[:, 512:576].bitcast(BF16).rearrange("p (n t h) -> p n t h", n=NS, t=2)
        qms = [(self.sq[0][0:NS, :], ("sq", 0)), (self.sq[1][0:NS, :], ("sq", 1))]
        vbs = [(self.tmpA[j][:, :].bitcast(BF16).rearrange("p (t c) -> p t c", t=2), ("tmpA", j)) for j in range(2)]
        pqs = {}

        def qbcast(n):
            kv = KV[n % 2]
            kres = ("kvs", n % 2)
            P.dma("sp", kv[:, 0], ck[l, n].rearrange("(t p) c -> p t c", p=128), w=[kres, "ssmw"], sem="kvs%d" % (n % 2))
            P.dma("sp", kv[:, 1], cv[l, n].rearrange("(t p) c -> p t c", p=128), w=[kres], sem="kvs%d" % (n % 2))
            bq = self.bank(); pq = self.PSB[bq]
            qm, qres = qms[n % 2]
            P.op("dve", lambda e: e.tensor_scalar(out=qm, in0=self.qtok[:, :], scalar1=self.cst[0:NS, n:n + 1], scalar2=None, op0=ALU.mult), r=["qtok", "cst"], w=[qres])
            P.op("pe", lambda e: e.matmul(pq[:, :], lhsT=self.onesb[0:NS, :], rhs=qm, start=True, stop=True), r=["onesb", qres], w=[("ps", bq)])
            vb, vres = vbs[n % 2]
            P.op("pool", lambda e: e.tensor_copy(out=vb, in_=kv[:, 1]), r=[kres], w=[vres])
            pqs[n] = (bq, pq)
        qbcast(0)
        for n in range(NS):
            kv = KV[n % 2]
            kres = ("kvs", n % 2)
            vb, vres = vbs[n % 2]
            bq, pq = pqs[n]
            for mt in range(2):
                si = self.uid() % 2
                t = self.tmpB[si]
                P.op("dve", lambda e, kv=kv, mt=mt, pq=pq, t=t: e.tensor_tensor(out=t[:, :], in0=kv[:, 0, mt, :], in1=pq[:, :], op=ALU.mult), r=[kres, ("ps", bq)], w=[("tmpB", si)])
                P.op("dve", lambda e, n=n, mt=mt, t=t: e.tensor_reduce(out=Sx[:, n, mt, :], in_=t[:, :].rearrange("p (h d) -> p h d", d=128), axis=AX.X, op=ALU.add),
                     r=[("tmpB", si)], w=[("Sx", n), "BU", "sAo"])
            P.op("act", lambda e, n=n: e.activation(out=Pb[:, n], in_=Sx[:, n], func=AF.Exp, scale=sc), r=[("Sx", n)], w=[("Pf", n)])
            if n + 1 < NS:
                qbcast(n + 1)
            P.op("pe", lambda e, n=n: e.matmul(pd[:, n * 8:(n + 1) * 8], lhsT=self.onesb[:], rhs=Pb[:, n].rearrange("p t h -> p (t h)"), start=True, stop=True, skip_group_check=True),
                 r=[("Pf", n), "onesb"], w=[("ps", bd)])
            for h in range(4):
                for mt in range(2):
                    P.op("pe", lambda e, n=n, h=h, mt=mt, vb=vb: e.matmul(po[:, h * NS + n: h * NS + n + 1], lhsT=vb[:, mt, h * 128:(h + 1) * 128], rhs=Pb[:, n, mt, h:h + 1],
                                                                        start=(mt == 0), stop=(mt == 1), skip_group_check=True),
                         r=[vres, ("Pf", n)], w=[("ps", bo)])
        den = self.sB[:, 0:64].rearrange("p (h n) -> p h n", n=NS)
        d8 = self.sB[:, 64:192]
        P.op("dve", lambda e: e.tensor_copy(out=d8, in_=pd[:, 0:128]), r=[("ps", bd)], w=["den", "sT", "hb", "sBo"])
        d84 = d8.rearrange("p (n t h) -> p h n t", n=NS, t=2)
        P.op("dve", lambda e: e.tensor_tensor(out=den, in0=d84[:, :, :, 0], in1=d84[:, :, :, 1], op=ALU.add), r=["den"], w=["den", "sT", "hb", "sBo"])
        P.op("dve", lambda e: e.reciprocal(out=den, in_=den), r=["den"], w=["den"])
        P.op("dve", lambda e: e.tensor_tensor(out=self.oT[:, :, HALF:HALF + NS], in0=po[:, 0:64].rearrange("p (h n) -> p h n", n=NS), in1=den, op=ALU.mult),
             r=[("ps", bo), "den"], w=[("oT", HALF)])

    def layer(self, l, half):
        P = self.P
        last = (half == self.nh - 1)
        if half == 1 and "ss" not in SKIP:
            self.ssm_sample_load(l)
        self.load_prep(l)
        self.norm_to_h(l, 0)
        if half == 1 and KSTOP <= 1:
            return
        self.dump("hT", self.hT[:], l, half)
        self.proj_u(l, 0)
        self.dump("u_ssm", self.uT, l, half)
        self.ssm_core(l, half)
        wtf = self.Wtp[:].rearrange("p g k c -> p (g k c)")
        P.dma("sp", wtf[:, 0:2048], self.scr["kv"][l], r=["scr"], w=["kT", "vv", "ssmw"], sem="kld")
        self.dump("ygelu", self.oT, l, half)
        self.dump("Hst", self.Hst[:], l, half)
        self.dump("Wtp", self.Wtp[:], l, half)
        self.dump("Wco", self.Wco[:], l, half)
        self.dump("Wsb", self.Wsb, l, half)
        self.glu(l)
        self.dump("o_ssm", self.uT, l, half)
        self.merge(l, 0, self.uT, "uT")
        self.dump("mg0", self.mg, l, half)
        if half == 1 and KSTOP <= 2:
            return

        def cap(ps, b, m, t0, w):
            if last and t0 == 512:
                P.op("dve", lambda e: e.tensor_copy(out=self.lastu[:, m, :], in_=ps[:, 496:512]), r=[("ps", b)], w=["lastu"])
            if t0 == HALF:
                P.op("dve", lambda e: e.tensor_copy(out=self.usam[:, m, :], in_=ps[:, 0:NS]), r=[("ps", b)], w=["usam"])
        self.proj_u(l, 512, extra=cap)
        self.dump("u_pool", self.uT, l, half)
        self.pool_core(l, half)
        self.dump("o_pool", self.oT, l, half)
        self.merge(l, 1, self.oT, "oT")
        self.dump("mg1", self.mg, l, half)
        if half == 1 and KSTOP <= 3:
            return
        if half == 1:
            def qextra(ps, b, m, t0, w):
                pass

            def qchunk(wb, wres, c):
                b = self.bank(); ps = self.PSB[b]
                for k in range(KT):
                    P.op("pe", lambda e, k=k, ps=ps: e.matmul(ps[0:NS, 0:256], lhsT=self.hT[:, k, HALF:HALF + NS], rhs=wb[:, k, :], start=(k == 0), stop=(k == KT - 1)),
                         r=[wres, ("hT", HALF)], w=[("ps", b)])
                P.op("dve", lambda e, ps=ps, c=c: e.tensor_copy(out=self.qtok[:, c * 256:(c + 1) * 256], in_=ps[0:NS, 0:256]), r=[("ps", b)], w=["qtok"])
            qextra.chunk = qchunk
            self.proj_u(l, 1024, extra=qextra)
        else:
            self.proj_u(l, 1024)
        self.dump("q", self.uT, l, half)
        self.attn_core(l, half)
        self.dump("o_mem", self.oT, l, half)
        self.merge(l, 2, self.oT, "oT")
        self.dump("mg2", self.mg, l, half)
        if half == 1 and KSTOP <= 4:
            return
        self.out_proj(l)
        self.dump("x_mix", self.xT[:], l, half)
        if half == 1 and KSTOP <= 5:
            return
        self.ffn(l)


_CACHE = {}


def _host_consts():
    cst = np.zeros((128, 128 + 8 + 64 + 16), np.float32)
    cst[:, 0:128] = np.eye(128, dtype=np.float32)
    p = np.arange(128)
    for e2 in range(2):
        cst[:, 128 + e2] = ((p // 64) == e2)
        cst[:, 130 + e2] = (((p // 16) % 2) == e2)
    for m in range(4):
        cst[:, 132 + m] = ((p // 32) == m)
    for gi, wdw in enumerate((2, 4, 8, 16)):
        cst[:, 136 + gi * 16:136 + (gi + 1) * 16] = 1.0 / np.minimum(np.arange(16) + 1, wdw)
    cst[:, 200:202] = -cst[:, 128:130]
    sel = np.zeros((NS, NS, 128), np.float32)
    for n in range(NS):
        sel[n, n, :] = 1.0
    return cst, sel.reshape(NS, NS * 128)


def _layout_params(inp):
    f = lambda a: np.asarray(a, dtype=np.float32)
    g = np.stack([f(inp[k]) for k in ("g_mix_pre", "g_mix_post", "g_ffn_pre", "g_ffn_post", "g_mem")])
    gvec = g.reshape(5, DEPTH, KT, 128).transpose(3, 0, 1, 2).reshape(128, 5 * DEPTH * KT)
    v4 = np.stack([f(inp[k]) for k in ("ssm_d", "ssm_b_glu", "pool_scale")])
    vec4 = v4.reshape(3, DEPTH, 4, 128).transpose(3, 0, 1, 2).reshape(128, 3 * DEPTH * 4)
    lr, li, ld = f(inp["ssm_lam_re"]), f(inp["ssm_lam_im"]), f(inp["ssm_log_dt"])
    ldb = np.broadcast_to(ld[:, :, None], lr.shape)
    def ps_l(a):
        return a.reshape(DEPTH, 16, 2, 64).transpose(0, 2, 3, 1).reshape(DEPTH, 128, 16)
    lamPS = np.concatenate([ps_l(lr), ps_l(li), ps_l(ldb)], axis=2)
    br, bi = f(inp["ssm_b_re"]), f(inp["ssm_b_im"])
    cr, ci = f(inp["ssm_c_re"]), f(inp["ssm_c_im"])
    def ps_b(a):
        return a.reshape(DEPTH, 16, 2, 64, 16).transpose(0, 2, 3, 1, 4).reshape(DEPTH, 128, 256)
    def ps_c(a):
        return a.reshape(DEPTH, 16, 2, 16, 64).transpose(0, 2, 4, 1, 3).reshape(DEPTH, 128, 256)
    bcPS = np.concatenate([ps_b(br), ps_b(bi), ps_c(cr), ps_c(ci)], axis=2)
    def lb_l(a):
        t = a.reshape(DEPTH, 4, 8, 64).transpose(0, 2, 1, 3)
        t = np.broadcast_to(t[:, :, None], (DEPTH, 8, 16, 4, 64))
        return t.reshape(DEPTH, 128, 256)
    lamLB = np.concatenate([lb_l(lr), lb_l(li), lb_l(ldb)], axis=2)
    def lb_b(a):
        return a.reshape(DEPTH, 4, 8, 64, 16).transpose(0, 2, 4, 1, 3).reshape(DEPTH, 128, 256)
    bLB = np.concatenate([lb_b(br), lb_b(bi)], axis=2)
    c = np.ascontiguousarray
    return dict(gvec=c(gvec), vec4=c(vec4), sprm=c(np.concatenate([lamPS, bcPS, lamLB, bLB], axis=2)))


def _split_out(flat):
    flat = np.asarray(flat, dtype=np.float32).reshape(-1)
    return {nm: flat[off:off + int(np.prod(shp))].reshape(shp) for nm, (off, shp) in OUT_LAYOUT.items()}


def _shared_inputs(inputs):
    f = lambda a: np.ascontiguousarray(np.asarray(a, dtype=np.float32))
    cst, _ = _host_consts()
    prm = _layout_params(inputs)
    call = np.ascontiguousarray(np.concatenate([cst, prm["gvec"], prm["vec4"]], axis=1))
    return dict(w_in=f(inputs["w_in"]), w_kv=f(inputs["w_kv"]), w_glu=f(inputs["ssm_w_glu"]), pool_w=f(inputs["pool_w"]),
                w_up=f(inputs["w_branch_up"]), w_out=f(inputs["w_out"]), w_f1=f(inputs["w_ffn_in"]), w_f2=f(inputs["w_ffn_out"]),
                call=call, sprm=prm["sprm"])


def kernel(**inputs):
    f = lambda a: np.ascontiguousarray(np.asarray(a, dtype=np.float32))
    if "nc" not in _CACHE:
        _CACHE["nc"] = Builder().build()
    nc = _CACHE["nc"]
    shared = _shared_inputs(inputs)
    xp = f(inputs["x_prompt"]); xs = f(inputs["x_sample"]); mem = f(inputs["mem_prompt"])
    ck = f(inputs["cache_mem_k"]).reshape(DEPTH, 128, NMEM, 512); cv = f(inputs["cache_mem_v"]).reshape(DEPTH, 128, NMEM, 512)
    sre = f(inputs["state_ssm_re"]).reshape(DEPTH, 128, 2048); sim = f(inputs["state_ssm_im"]).reshape(DEPTH, 128, 2048)
    spool = f(inputs["state_pool"])
    in_maps = []
    for c in range(8):
        sl = slice(c * NS, (c + 1) * NS)
        m = dict(shared)
        m.update(xp=xp[c], xs=f(xs[sl, 0, :]), mem=mem[c], ck=f(ck[:, sl]), cv=f(cv[:, sl]), sre=f(sre[:, sl]), sim=f(sim[:, sl]), spool=f(spool[:, sl]))
        in_maps.append(m)
    res = run_bass_kernel_spmd(nc, in_maps, core_ids=list(range(8)))
    R = [_split_out(r["out"]) for r in res.results]
    cat = lambda k, ax: np.concatenate([np.asarray(r[k], dtype=np.float32) for r in R], axis=ax)
    st = lambda k: np.stack([np.asarray(r[k], dtype=np.float32) for r in R], axis=1)
    yp = np.stack([np.asarray(r["yp"], dtype=np.float32) for r in R], axis=0)
    ys = cat("ys", 0).reshape(128, 1, D)
    re_p = st("re_p").reshape(DEPTH, 8, 32, 64); im_p = st("im_p").reshape(DEPTH, 8, 32, 64)
    pool_p = st("pool_p")
    mk_p = st("mk_p").reshape(DEPTH, 8, NMEM, 4, 128); mv_p = st("mv_p").reshape(DEPTH, 8, NMEM, 4, 128)
    re_s = cat("re_s", 1).reshape(DEPTH, 128, 32, 64); im_s = cat("im_s", 1).reshape(DEPTH, 128, 32, 64)
    pool_s = cat("pool_s", 1)
    return (yp, ys, re_p, im_p, pool_p, mk_p, mv_p, re_s, im_s, pool_s)
```

```python
import numpy as np
import concourse.bass as bass
import concourse.mybir as mybir
from concourse.bass_utils import run_bass_kernel_spmd

F32 = mybir.dt.float32
BF16 = mybir.dt.bfloat16
ALU = mybir.AluOpType
AF = mybir.ActivationFunctionType
AX = mybir.AxisListType

ENGS = ("pe", "act", "dve", "pool", "sp")
import os as _os
SKIP = _os.environ.get("KSKIP", "").split(",")
TAPS2D = bool(int(_os.environ.get("TAPS2D", "0")))
KSS = int(_os.environ.get("KSS", "9"))
KSTOP = int(_os.environ.get("KSTOP", "99"))
FFNBAR = bool(int(_os.environ.get("FFNBAR", "0")))
DQ = _os.environ.get("DQ", "sp")

DEPTH = 4
D = 1024
KT = 8
SEQ = 2048
HALF = 1024
NS = 16
NT = HALF + NS
LCH = 8
NCH = HALF // LCH
OUT_LAYOUT = {}
_off = 0
for _nm, _shp in (("yp", (2048, 1024)), ("ys", (16, 1024)), ("re_p", (4, 2048)), ("im_p", (4, 2048)), ("pool_p", (4, 15, 512)), ("mk_p", (4, 256, 512)),
                  ("mv_p", (4, 256, 512)), ("re_s", (4, 16, 2048)), ("im_s", (4, 16, 2048)), ("pool_s", (4, 16, 15, 512))):
    OUT_LAYOUT[_nm] = (_off, _shp)
    _off += int(np.prod(_shp))
OUT_TOTAL = _off
DFF = 2816
FT = 22
NMEM = 256
EPS = 1e-6
PAST = 16384


class Prog:
    def __init__(self, nc):
        self.nc = nc
        self.ops = {e: [] for e in ENGS}
        self.count = {e: 0 for e in ENGS}
        self.waited = {e: {} for e in ENGS}
        self.last_w = {}
        self.readers = {}
        self.dma_cum = {}
        self.sems = {}
        self.final_tokens = []
        self._ctx = []
        self.barrier_tok = {e: [] for e in ENGS}

    def sb(self, name, shape, dt):
        g = self.nc.sbuf_tensor("sb_" + name, list(shape), dt)
        t = g.__enter__()
        self._ctx.append(g)
        return t

    def ps(self, name, shape, dt=F32):
        g = self.nc.psum_tensor("ps_" + name, list(shape), dt)
        t = g.__enter__()
        self._ctx.append(g)
        return t

    def _need(self, eng, tok, waits):
        if tok is None:
            return
        key, val = tok
        if key == ("eng", eng) and eng in ("pe", "sp"):
            return
        if self.waited[eng].get(key, 0) >= val:
            return
        self.waited[eng][key] = val
        waits.append((key, val))

    def _deps(self, eng, r, w):
        waits = []
        for t in self.barrier_tok[eng]:
            self._need(eng, t, waits)
        self.barrier_tok[eng] = []
        for x in r:
            self._need(eng, self.last_w.get(x), waits)
        for x in w:
            self._need(eng, self.last_w.get(x), waits)
            for t in self.readers.get(x, ()):
                self._need(eng, t, waits)
        return waits

    def _commit(self, tok, r, w):
        for x in r:
            lst = self.readers.setdefault(x, [])
            lst[:] = [t for t in lst if t[0] != tok[0]] + [tok]
        for x in w:
            self.last_w[x] = tok
            self.readers[x] = []

    def op(self, eng, fn, r=(), w=()):
        psr = [x for x in r if isinstance(x, tuple) and x[0] == "ps" and x not in w]
        if psr:
            w = list(w) + psr
        waits = self._deps(eng, r, w)
        self.count[eng] += 1
        tok = (("eng", eng), self.count[eng])
        self.ops[eng].append((waits, fn, "op", None))
        self._commit(tok, r, w)
        return tok

    def dma(self, eng, out, in_, r=(), w=(), sem="dma", final=False, **kw):
        waits = self._deps(eng, r, w)
        key = ("dma", sem)
        self.dma_cum[key] = self.dma_cum.get(key, 0) + 16
        tok = (key, self.dma_cum[key])
        self.ops[eng].append((waits, lambda e: e.dma_start(out=out, in_=in_, **kw), "dma", key))
        self._commit(tok, r, w)
        if final:
            self.final_tokens.append(tok)
        return tok

    def barrier(self):
        toks = [(("eng", o), self.count[o]) for o in ENGS if o != "sp" and self.count[o] > 0]
        toks += [(k, v) for k, v in self.dma_cum.items()]
        for e in ENGS:
            self.barrier_tok[e] = list(toks)

    def finalize(self):
        nc = self.nc
        keys = [("eng", e) for e in ENGS if e != "sp"] + list(self.dma_cum.keys())
        guards = []
        for k in keys:
            g = nc.semaphore("s_" + "_".join(str(x) for x in k))
            self.sems[k] = g.__enter__()
            guards.append(g)
        seen = {}
        for key, val in self.final_tokens:
            seen[key] = max(seen.get(key, 0), val)
        final_waits = list(seen.items())
        blk = nc.Block()
        block = blk.__enter__()

        def runner(ename):
            def run(e):
                mysem = self.sems.get(("eng", ename))
                for waits, fn, kind, key in self.ops[ename]:
                    for k, v in waits:
                        e.wait_ge(self.sems[k], v)
                    inst = fn(e)
                    if kind == "dma":
                        inst.then_inc(self.sems[key], 16)
                    elif mysem is not None:
                        inst.then_inc(mysem, 1)
                if ename == "sp":
                    for k, v in final_waits:
                        e.wait_ge(self.sems[k], v)
            return run

        block.tensor(runner("pe"))
        block.scalar(runner("act"))
        block.vector(runner("dve"))
        block.gpsimd(runner("pool"))
        block.sync(runner("sp"))
        blk.__exit__(None, None, None)
        for g in reversed(guards):
            g.__exit__(None, None, None)
        for g in reversed(self._ctx):
            g.__exit__(None, None, None)


class DummyProg:
    def op(self, *a, **k):
        return None

    def dma(self, *a, **k):
        return None

    def barrier(self):
        pass


class Builder:
    def __init__(self, nlayers=DEPTH, nhalves=2, dbg=None):
        self.nl = nlayers
        self.nh = nhalves
        self.dbg = dbg
        nc = bass.Bass("TRN2", target_bir_lowering=False)
        self.nc = nc
        self.P = Prog(nc)
        self.din = {}
        self.dout = {}
        self.nsem = 0
        self._q = None
        self._xr = ()
        self._xw = ()

    def inp(self, name, shape):
        self.din[name] = self.nc.dram_tensor(name, list(shape), F32, kind="ExternalInput").ap()
        return self.din[name]

    def outp(self, name, shape):
        self.dout[name] = self.nc.dram_tensor(name, list(shape), F32, kind="ExternalOutput").ap()
        return self.dout[name]

    def build(self):
        P = self.P
        nc = self.nc
        i = self.inp
        xp = i("xp", [SEQ, D]); xs = i("xs", [NS, D]); mem = i("mem", [NMEM, D])
        ck = i("ck", [DEPTH, NS, NMEM, 512]); cv = i("cv", [DEPTH, NS, NMEM, 512])
        sre = i("sre", [DEPTH, NS, 2048]); sim = i("sim", [DEPTH, NS, 2048])
        spool = i("spool", [DEPTH, NS, 15, 512])
        self.w_in = i("w_in", [DEPTH, D, 4608]); self.w_kv = i("w_kv", [DEPTH, D, 1024])
        self.w_glu = i("w_glu", [DEPTH, 512, 512]); self.pool_w = i("pool_w", [DEPTH, 4, 128, 128])
        self.w_up = i("w_up", [DEPTH, 3, 512, D]); self.w_out = i("w_out", [DEPTH, D, D])
        self.w_f1 = i("w_f1", [DEPTH, D, 2 * DFF]); self.w_f2 = i("w_f2", [DEPTH, DFF, D])
        call_d = i("call", [128, 216 + 160 + 48])
        cst_d = call_d[:, 0:216]; gvec_d = call_d[:, 216:376]; vec4_d = call_d[:, 376:424]
        sprm_d = i("sprm", [DEPTH, 128, 2352])
        out_d = self.outp("out", [OUT_TOTAL])
        ov = {}
        for nm, (off, shp) in OUT_LAYOUT.items():
            n = int(np.prod(shp))
            v = out_d[off:off + n]
            if len(shp) == 2:
                v = v.rearrange("(a b) -> a b", b=shp[1])
            elif len(shp) == 3:
                v = v.rearrange("(a b c) -> a b c", b=shp[1], c=shp[2])
            elif len(shp) == 4:
                v = v.rearrange("(a b c d) -> a b c d", b=shp[1], c=shp[2], d=shp[3])
            ov[nm] = v
        yp, ys, re_p, im_p, pool_p, mk_p, mv_p, re_s, im_s, pool_s = [ov[k] for k in ("yp", "ys", "re_p", "im_p", "pool_p", "mk_p", "mv_p", "re_s", "im_s", "pool_s")]
        self.o = dict(yp=yp, ys=ys, re_p=re_p, im_p=im_p, pool_p=pool_p, mk_p=mk_p, mv_p=mv_p,
                      re_s=re_s, im_s=im_s, pool_s=pool_s)
        self.i = dict(xp=xp, xs=xs, mem=mem, ck=ck, cv=cv, sre=sre, sim=sim, spool=spool,
                      sprm=sprm_d)

        self.xT = P.sb("xT", [128, KT, NT], F32)
        self.hT = P.sb("hT", [128, KT, NT], BF16)
        self.A = P.sb("arena", [128, FT * NT], BF16)
        A = self.A
        self.aT = A[:, 0:FT * NT].rearrange("p (k n) -> p k n", n=NT)
        self.mg = A[:, 0:8 * NT].rearrange("p (k n) -> p k n", n=NT)
        self.uT = A[:, 8 * NT:12 * NT].rearrange("p (k n) -> p k n", n=NT)
        self.oT = A[:, 12 * NT:16 * NT].rearrange("p (k n) -> p k n", n=NT)
        self.Hbf = A[:, 16 * NT:16 * NT + 16 * 2 * (NCH + 1)].rearrange("p (a r c) -> p a r c", a=16, r=2)
        self.Wsb = A[:, 0:8192].rearrange("p (g k r c) -> p g k r c", g=4, k=8, r=2)
        self.Tc = A[:, 12 * NT:16 * NT].bitcast(F32)[:, 0:2048].rearrange("p (a c) -> p a c", c=NCH)
        self.Ts = A[:, 16 * NT:16 * NT + 4128].bitcast(F32)[:, 0:2048].rearrange("p (a c) -> p a c", c=NCH)
        self.Hst = P.sb("Hst", [128, 16, 2, NCH + 2], F32)
        self.Wco = P.sb("Wco", [128, 16, 9, 2, 32], BF16)
        self.Wtp = P.sb("Wtp", [128, 4, 8, 128], BF16)
        self.zb = P.sb("zb", [128, KT, 512], F32)
        self.stg = [P.sb("stg%d" % j, [128, 2048], F32) for j in range(2)]
        self.wbf = [P.sb("wbf%d" % j, [128, 2048], BF16) for j in range(3)]
        self.gvec = P.sb("gvec", [128, 5, DEPTH, KT], F32)
        self.vec4 = P.sb("vec4", [128, 3, DEPTH, 4], F32)
        self.cst = P.sb("cst", [128, 128 + 8 + 64 + 16], F32)
        self.identb = P.sb("identb", [128, 128], BF16)
        self.onesb = P.sb("onesb", [128, 128], BF16)
        self.onesf = P.sb("onesf", [128, 128], F32)
        self.rstd = P.sb("rstd", [128, 512], F32)
        self.sq = [P.sb("sq%d" % j, [128, 512], BF16) for j in range(2)]
        self.tmpA = [P.sb("tmpA%d" % j, [128, 512], F32) for j in range(2)]
        self.tmpB = [P.sb("tmpB%d" % j, [128, 512], F32) for j in range(2)]
        self.carry = P.sb("carry", [128, DEPTH, 16, 2], F32)
        self.phist = P.sb("phist", [128, DEPTH, 4, 16], F32)
        wtf = self.Wtp[:].rearrange("p g k c -> p (g k c)")
        self.kT = wtf[:, 0:1024].rearrange("p (h m) -> p h m", m=NMEM)
        self.vv = wtf[:, 1024:2048].rearrange("p (t c) -> p t c", c=512)
        self.qtok = wtf[0:NS, 2048:3072].bitcast(F32)
        self.memn = P.sb("memn", [128, KT, NMEM], BF16)
        self.memh = A[:, 16 * NT + 4128:16 * NT + 4128 + 2048].rearrange("p (k m) -> p k m", m=NMEM)
        self.epsb = P.sb("epsb", [128, 2], F32)
        self.lastu = P.sb("lastu", [128, 4, 16], F32)
        self.usam = P.sb("usam", [128, 4, 16], F32)
        self.small = P.sb("small", [128, 768], F32)
        self.sA = P.sb("sA", [128, 512], F32)
        self.sB = P.sb("sB", [128, 512], F32)
        self.a1rho = P.sb("a1rho", [128, 48], F32)
        self.rho = self.a1rho[:, 32:48]
        self.A8 = P.sb("A8", [128, 2, 16], F32)
        self.A1 = self.a1rho[:, 0:32].rearrange("p (r a) -> p r a", r=2)
        self.pw = self.zb[:].rearrange("p k c -> p (k c)")[:, 1024:1024 + 2080]
        self.prm = self.Hst[:].rearrange("p a r c -> p (a r c)")[:, 0:2400]
        self.PSB = [P.ps("psb%d" % j, [128, 512]) for j in range(8)]
        self.ident = self.cst[:, 0:128]
        dt_ = lambda nm, shp, dt: self.nc.dram_tensor(nm, shp, dt, kind="Internal").ap()
        self.scr = dict(sb=dt_("scr_sb", [DEPTH, 128, 8192], BF16), co=dt_("scr_co", [DEPTH, 128, 9216], BF16),
                        tp=dt_("scr_tp", [DEPTH, 128, 4096], BF16), t=dt_("scr_t", [DEPTH, 128, 4144], F32),
                        kv=dt_("scr_kv", [DEPTH, 128, 2048], BF16))
        self.dd = dict(cst_d=cst_d, gvec_d=gvec_d, vec4_d=vec4_d)
        realP = self.P
        self.P = DummyProg()
        self.specs = None
        self.rec = []
        self.emit()
        self.specs = self.rec
        self.P = realP
        self.emit()
        self.P.finalize()
        return nc

    def emit(self):
        P = self.P
        self.rot = 0
        self.cnt = 0
        self.stg_i = 0
        self.wbf_i = 0
        self.wl_i = 0
        self.wl_issued = 0
        ident = self.ident
        cst_d, gvec_d, vec4_d = self.dd["cst_d"], self.dd["gvec_d"], self.dd["vec4_d"]
        P.dma("sp", self.cst[:], cst_d, w=["cst"], sem="c0")
        P.dma("sp", self.gvec[:].rearrange("p a l k -> p (a l k)"), gvec_d, w=["gvec"], sem="c1")
        P.dma("sp", self.vec4[:].rearrange("p a l k -> p (a l k)"), vec4_d, w=["vec4"], sem="c2")
        P.op("dve", lambda e: e.tensor_copy(out=self.identb[:], in_=ident), r=["cst"], w=["identb"])
        P.op("pool", lambda e: e.memset(self.onesb[:], 1.0), w=["onesb"])
        P.op("pool", lambda e: e.memset(self.onesf[:], 1.0), w=["onesf"])
        P.op("pool", lambda e: e.memset(self.epsb[:, 0:1], EPS), w=["epsb"])
        P.op("pool", lambda e: e.memset(self.epsb[:, 1:2], float(np.pi / 2)), w=["epsb"])
        P.op("pool", lambda e: e.memset(self.carry[:], 0.0), w=["carry"])
        P.op("pool", lambda e: e.memset(self.phist[:], 0.0), w=["phist"])

        self.prep_mem()
        P.barrier()
        kvst = self.A[:, 8 * NT:8 * NT + 2048]
        save_kv = (self.kT, self.vv)
        self.kT = kvst[:, 0:1024].rearrange("p (h m) -> p h m", m=NMEM)
        self.vv = kvst[:, 1024:2048].rearrange("p (t c) -> p t c", c=512)
        for l in range(self.nl):
            self.kv(l, 0)
            P.dma("sp", self.scr["kv"][l], kvst, r=["kT", "vv"], sem="kst")
            self.ssm_prep(l)
        self.kT, self.vv = save_kv
        P.barrier()
        P.op("dve", lambda e: e.memset(self.small[:, 760:761], 0.0), w=["scr"])
        for half in range(self.nh):
            self.half = half
            self.ntok = HALF + (NS if half == 1 else 0)
            self.tbs = [(0, 512), (512, 512)] + ([(HALF, NS)] if half == 1 else [])
            self.load_x(half)
            P.barrier()
            for l in range(self.nl):
                self.layer(l, half)
            P.barrier()
            if not (half == 1 and KSTOP <= 6):
                self.store_y(half)

    def bank(self):
        b = self.rot
        self.rot = (self.rot + 1) % 6
        return b

    def uid(self):
        self.cnt += 1
        return self.cnt

    def cast_eng(self):
        return "pool"

    def _issue(self, idx):
        P = self.P
        pieces, kt, ncols = self.specs[idx]
        si = idx % 2
        bi = idx % 3
        st = self.stg[si][:, 0:kt * ncols].rearrange("p (k c) -> p k c", c=ncols)
        wb = self.wbf[bi][:, 0:kt * ncols].rearrange("p (k c) -> p k c", c=ncols)
        c0 = 0
        for ap in pieces:
            c = ap.shape[-1]
            P.dma("sp", st[:, :, c0:c0 + c], ap.rearrange("(k p) c -> p k c", p=128), w=[("stg", si)], sem="stg%d" % si)
            c0 += c
        if idx % 2 == 0:
            P.op("act", lambda e: e.activation(out=wb, in_=st, func=AF.Copy), r=[("stg", si)], w=[("wbf", bi)])
        else:
            P.op("dve", lambda e: e.tensor_copy(out=wb, in_=st), r=[("stg", si)], w=[("wbf", bi)])

    def wload(self, pieces, kt, ncols):
        idx = self.wl_i
        self.wl_i += 1
        bi = idx % 3
        wb = self.wbf[bi][:, 0:kt * ncols].rearrange("p (k c) -> p k c", c=ncols)
        if self.specs is None:
            self.rec.append((pieces, kt, ncols))
            return wb, ("wbf", bi)
        while self.wl_issued <= min(idx + 1, len(self.specs) - 1):
            self._issue(self.wl_issued)
            self.wl_issued += 1
        return wb, ("wbf", bi)

    def norm_rstd(self, src_fn, src_res, ntiles, w, tag):
        P = self.P
        pb = 6 + (self.uid() % 2)
        ps = self.PSB[pb]
        for k in range(ntiles):
            sq = self.sq[k % 2]
            P.op("act", lambda e, k=k, sq=sq: e.activation(out=sq[:, :w], in_=src_fn(k), func=AF.Square),
                 r=[src_res(k)], w=[("sq", k % 2)])
            P.op("pe", lambda e, k=k, sq=sq: e.matmul(ps[:, :w], lhsT=self.onesb[:], rhs=sq[:, :w], start=(k == 0), stop=(k == ntiles - 1)),
                 r=[("sq", k % 2), "onesb"], w=[("ps", pb)])
        P.op("act", lambda e: e.activation(out=self.rstd[:, :w], in_=ps[:, :w], func=AF.Sqrt, scale=1.0 / D, bias=self.epsb[:, 0:1]),
             r=[("ps", pb), "epsb"], w=["rstd"])
        P.op("dve", lambda e: e.reciprocal(out=self.rstd[:, :w], in_=self.rstd[:, :w]), r=["rstd"], w=["rstd"])

    def dump(self, name, ap, l=0, half=0):
        if not self.dbg or l != 0 or half != 0 or isinstance(self.P, DummyProg):
            return
        shp = list(ap.shape)
        d = self.nc.dram_tensor("dbg_" + name, shp, ap.dtype, kind="ExternalOutput").ap()
        self.P.barrier()
        self.P.dma("sp", d, ap, sem="dbg_" + name, final=True)
        self.P.barrier()

    def load_x(self, half):
        P = self.P
        xp = self.i["xp"]
        zb4 = self.zb[:].rearrange("p k c -> p (k c)").rearrange("p (j c) -> p j c", c=D)
        for blk in range(2):
            t0 = blk * 512
            r0 = half * HALF + t0
            P.dma("sp", zb4, xp[r0:r0 + 512, :].rearrange("(j p) c -> p j c", p=128), w=["zb"], sem="zb")
            for k in range(KT):
                b = self.bank(); ps = self.PSB[b]
                for j in range(4):
                    P.op("pe", lambda e, j=j, k=k, ps=ps: e.transpose(ps[:, j * 128:(j + 1) * 128], zb4[:, j, k * 128:(k + 1) * 128], self.ident),
                         r=["zb", "cst"], w=[("ps", b)])
                eng = "act" if k % 2 else "dve"
                if eng == "act":
                    P.op("act", lambda e, k=k, ps=ps, t0=t0: e.activation(out=self.xT[:, k, t0:t0 + 512], in_=ps[:, :], func=AF.Copy), r=[("ps", b)], w=[("xT", t0)])
                else:
                    P.op("dve", lambda e, k=k, ps=ps, t0=t0: e.tensor_copy(out=self.xT[:, k, t0:t0 + 512], in_=ps[:, :]), r=[("ps", b)], w=[("xT", t0)])
        if half == 1:
            xs = self.i["xs"]
            P.dma("sp", zb4[0:NS, 0, :], xs, w=["zb"], sem="zb")
            b = self.bank(); ps = self.PSB[b]
            for k in range(KT):
                P.op("pe", lambda e, k=k, ps=ps: e.transpose(ps[:, k * NS:(k + 1) * NS], zb4[0:NS, 0, k * 128:(k + 1) * 128], self.ident[0:NS, 0:NS]),
                     r=["zb", "cst"], w=[("ps", b)])
            P.op("dve", lambda e, ps=ps: e.tensor_copy(out=self.xT[:, :, HALF:HALF + NS], in_=ps[:, 0:KT * NS].rearrange("p (k n) -> p k n", n=NS)),
                 r=[("ps", b)], w=[("xT", HALF)])

    def store_y(self, half):
        P = self.P
        yp = self.o["yp"]
        zb4 = self.zb[:].rearrange("p k c -> p (k c)").rearrange("p (j c) -> p j c", c=D)
        for blk in range(2):
            t0 = blk * 512
            r0 = half * HALF + t0
            for j in range(4):
                for kk in range(2):
                    b = self.bank(); ps = self.PSB[b]
                    for k4 in range(4):
                        k = kk * 4 + k4
                        P.op("pe", lambda e, j=j, k=k, k4=k4, ps=ps, t0=t0: e.transpose(ps[:, k4 * 128:(k4 + 1) * 128], self.xT[:, k, t0 + j * 128:t0 + (j + 1) * 128], self.ident),
                             r=[("xT", t0), "cst"], w=[("ps", b)])
                    if kk:
                        P.op("act", lambda e, j=j, kk=kk, ps=ps: e.activation(out=zb4[:, j, kk * 512:(kk + 1) * 512], in_=ps[:, :], func=AF.Copy), r=[("ps", b)], w=["zb"])
                    else:
                        P.op("dve", lambda e, j=j, kk=kk, ps=ps: e.tensor_copy(out=zb4[:, j, kk * 512:(kk + 1) * 512], in_=ps[:, :]), r=[("ps", b)], w=["zb"])
            P.dma("sp", yp[r0:r0 + 512, :].rearrange("(j p) c -> p j c", p=128), zb4, r=["zb"], sem="yout", final=True)
        if half == 1:
            ys = self.o["ys"]
            for kk in range(2):
                b = self.bank(); ps = self.PSB[b]
                for k4 in range(4):
                    k = kk * 4 + k4
                    P.op("pe", lambda e, k=k, k4=k4, ps=ps: e.transpose(ps[0:NS, k4 * 128:(k4 + 1) * 128], self.xT[:, k, HALF:HALF + NS], self.ident),
                         r=[("xT", HALF), "cst"], w=[("ps", b)])
                P.op("dve", lambda e, kk=kk, ps=ps: e.tensor_copy(out=zb4[0:NS, 0, kk * 512:(kk + 1) * 512], in_=ps[0:NS, :]), r=[("ps", b)], w=["zb"])
            P.dma("sp", ys, zb4[0:NS, 0, :], r=["zb"], sem="yout", final=True)

    def prep_mem(self):
        P = self.P
        mem = self.i["mem"]
        zb2 = self.zb[:].rearrange("p k c -> p (k c)").rearrange("p (j c) -> p j c", c=D)
        P.dma("sp", zb2[:, 0:2, :], mem.rearrange("(j p) c -> p j c", p=128), w=["zb"], sem="zb")
        ss = self.small[:, 0:2]
        for j in range(2):
            P.op("act", lambda e, j=j: e.activation(out=zb2[:, 2 + j, :], in_=zb2[:, j, :], func=AF.Square, accum_out=ss[:, j:j + 1]), r=["zb"], w=["zb", "small"])
        P.op("act", lambda e: e.activation(out=ss, in_=ss, func=AF.Sqrt, scale=1.0 / D, bias=self.epsb[:, 0:1]), r=["small", "epsb"], w=["small"])
        P.op("dve", lambda e: e.reciprocal(out=ss, in_=ss), r=["small"], w=["small"])
        for j in range(2):
            P.op("dve", lambda e, j=j: e.tensor_scalar(out=zb2[:, j, :], in0=zb2[:, j, :], scalar1=ss[:, j:j + 1], scalar2=None, op0=ALU.mult), r=["zb", "small"], w=["zb"])
        for k in range(KT):
            b = self.bank(); ps = self.PSB[b]
            for j in range(2):
                P.op("pe", lambda e, j=j, k=k, ps=ps: e.transpose(ps[:, j * 128:(j + 1) * 128], zb2[:, j, k * 128:(k + 1) * 128], self.ident), r=["zb", "cst"], w=[("ps", b)])
            P.op("dve", lambda e, k=k, ps=ps: e.tensor_copy(out=self.memn[:, k, :], in_=ps[:, 0:NMEM]), r=[("ps", b)], w=["memn"])

    def kv(self, l, half):
        P = self.P
        kT_, vv_ = self.kT, self.vv
        for k in range(KT):
            P.op("dve", lambda e, k=k: e.tensor_scalar(out=self.memh[:, k, :], in0=self.memn[:, k, :], scalar1=self.gvec[:, 4, l, k:k + 1], scalar2=None, op0=ALU.mult),
                 r=["memn", "gvec"], w=["memh"])
        for c in range(4):
            wb, wres = self.wload([self.w_kv[l][:, c * 256:(c + 1) * 256]], KT, 256)
            if c < 2:
                for hh in range(2):
                    b = self.bank(); ps = self.PSB[b]
                    for k in range(KT):
                        P.op("pe", lambda e, k=k, hh=hh, ps=ps, wb=wb: e.matmul(ps[:, 0:NMEM], lhsT=wb[:, k, hh * 128:(hh + 1) * 128], rhs=self.memh[:, k, :], start=(k == 0), stop=(k == KT - 1)),
                             r=[wres, "memh"], w=[("ps", b)])
                    P.op("act", lambda e, hh=hh, ps=ps, c=c: e.activation(out=kT_[:, 2 * c + hh, :], in_=ps[:, 0:NMEM], func=AF.Copy), r=[("ps", b)], w=["kT"])
            if c >= 2 or half == 0:
                for mt in range(2):
                    b = self.bank(); ps = self.PSB[b]
                    for k in range(KT):
                        P.op("pe", lambda e, k=k, mt=mt, ps=ps, wb=wb: e.matmul(ps[:, 0:256], lhsT=self.memh[:, k, mt * 128:(mt + 1) * 128], rhs=wb[:, k, :], start=(k == 0), stop=(k == KT - 1)),
                             r=[wres, "memh"], w=[("ps", b)])
                    if c >= 2:
                        P.op("act", lambda e, mt=mt, ps=ps, c=c: e.activation(out=vv_[:, mt, (c - 2) * 256:(c - 1) * 256], in_=ps[:, 0:256], func=AF.Copy), r=[("ps", b)], w=["vv"])
                    if half == 0:
                        t = self.tmpA[mt]
                        P.op("dve", lambda e, ps=ps, t=t: e.tensor_copy(out=t[:, 0:256], in_=ps[:, 0:256]), r=[("ps", b)], w=[("tmpA", mt)])
                        dst = self.o["mk_p"] if c < 2 else self.o["mv_p"]
                        cc = c % 2
                        P.dma("sp", dst[l][mt * 128:(mt + 1) * 128, cc * 256:(cc + 1) * 256], t[:, 0:256], r=[("tmpA", mt)], sem="kvout%d" % mt, final=True)

    def norm_to_h(self, l, kind):
        P = self.P
        for (t0, w) in self.tbs:
            self.norm_rstd(lambda k, t0=t0, w=w: self.xT[:, k, t0:t0 + w], lambda k, t0=t0: ("xT", t0), KT, w, "n")
            for k in range(KT):
                eng = "dve"
                if eng == "dve":
                    P.op("dve", lambda e, k=k, t0=t0, w=w: e.scalar_tensor_tensor(out=self.hT[:, k, t0:t0 + w], in0=self.xT[:, k, t0:t0 + w], scalar=self.gvec[:, kind, l, k:k + 1],
                                                                                 in1=self.rstd[:, :w], op0=ALU.mult, op1=ALU.mult),
                         r=[("xT", t0), "rstd", "gvec"], w=[("hT", t0)])
                else:
                    P.op("pool", lambda e, k=k, t0=t0, w=w: e.scalar_tensor_tensor(out=self.hT[:, k, t0:t0 + w], in0=self.xT[:, k, t0:t0 + w], scalar=self.gvec[:, kind, l, k:k + 1],
                                                                                  in1=self.rstd[:, :w], op0=ALU.mult, op1=ALU.mult),
                         r=[("xT", t0), "rstd", "gvec"], w=[("hT", t0)])

    def mm_fm(self, wb, wres, kt, nm, src, src_name, evac, m0=0):
        P = self.P
        for mi in range(nm):
            for (t0, w) in self.tbs:
                b = self.bank(); ps = self.PSB[b]
                for k in range(kt):
                    P.op("pe", lambda e, k=k, mi=mi, ps=ps, t0=t0, w=w: e.matmul(ps[:, :w], lhsT=wb[:, k, mi * 128:(mi + 1) * 128], rhs=src[:, k, t0:t0 + w],
                                                                                start=(k == 0), stop=(k == kt - 1)),
                         r=[wres, (src_name, t0)], w=[("ps", b)])
                evac(ps, b, m0 + mi, t0, w)

    def proj_u(self, l, col0, extra=None):
        P = self.P
        for c in range(2):
            wb, wres = self.wload([self.w_in[l][:, col0 + c * 256: col0 + (c + 1) * 256]], KT, 256)

            def evac(ps, b, m, t0, w):
                if (m + (t0 // 512)) % 2 == 0:
                    P.op("act", lambda e: e.activation(out=self.uT[:, m, t0:t0 + w], in_=ps[:, :w], func=AF.Copy), r=[("ps", b)], w=[("uT", t0)])
                else:
                    P.op("dve", lambda e: e.tensor_copy(out=self.uT[:, m, t0:t0 + w], in_=ps[:, :w]), r=[("ps", b)], w=[("uT", t0)])
                if extra is not None:
                    extra(ps, b, m, t0, w)
            self.mm_fm(wb, wres, KT, 2, self.hT, "hT", evac, m0=2 * c)
            if extra is not None and hasattr(extra, "chunk"):
                extra.chunk(wb, wres, c)

    def merge(self, l, br, src, src_name):
        P = self.P
        for c2 in range(2):
            for cg in range(2):
                ucol = c2 * 512 + cg * 256
                wu, wures = self.wload([self.w_up[l, br][:, ucol:ucol + 256]], 4, 256)
                gcol = 1536 + br * 1024 + ucol
                wg, wgres = self.wload([self.w_in[l][:, gcol:gcol + 256]], KT, 256)
                for mi in range(2):
                    m = c2 * 4 + cg * 2 + mi
                    for (t0, w) in self.tbs:
                        bg = self.bank(); pg = self.PSB[bg]
                        for k in range(KT):
                            P.op("pe", lambda e, k=k, mi=mi, pg=pg, t0=t0, w=w, wg=wg: e.matmul(pg[:, :w], lhsT=wg[:, k, mi * 128:(mi + 1) * 128], rhs=self.hT[:, k, t0:t0 + w],
                                                                                                start=(k == 0), stop=(k == KT - 1)),
                                 r=[wgres, ("hT", t0)], w=[("ps", bg)])
                        si = self.uid() % 2
                        sg = self.tmpB[si]
                        P.op("act", lambda e, pg=pg, sg=sg, w=w: e.activation(out=sg[:, :w], in_=pg[:, :w], func=AF.Sigmoid), r=[("ps", bg)], w=[("tmpB", si)])
                        bu = self.bank(); pu = self.PSB[bu]
                        mu = mi
                        for k in range(4):
                            P.op("pe", lambda e, k=k, mu=mu, pu=pu, t0=t0, w=w, wu=wu: e.matmul(pu[:, :w], lhsT=wu[:, k, mu * 128:(mu + 1) * 128], rhs=src[:, k, t0:t0 + w],
                                                                                                start=(k == 0), stop=(k == 3)),
                                 r=[wures, (src_name, t0)], w=[("ps", bu)])
                        if br == 0:
                            P.op("dve", lambda e, pu=pu, sg=sg, m=m, t0=t0, w=w: e.tensor_tensor(out=self.mg[:, m, t0:t0 + w], in0=pu[:, :w], in1=sg[:, :w], op=ALU.mult),
                                 r=[("ps", bu), ("tmpB", si)], w=[("mg", t0)])
                        else:
                            P.op("dve", lambda e, pu=pu, sg=sg, w=w: e.tensor_tensor(out=sg[:, :w], in0=pu[:, :w], in1=sg[:, :w], op=ALU.mult),
                                 r=[("ps", bu), ("tmpB", si)], w=[("tmpB", si)])
                            P.op("pool", lambda e, sg=sg, m=m, t0=t0, w=w: e.tensor_tensor(out=self.mg[:, m, t0:t0 + w], in0=self.mg[:, m, t0:t0 + w], in1=sg[:, :w], op=ALU.add),
                                 r=[("tmpB", si), ("mg", t0)], w=[("mg", t0)])

    def post_norm_add(self, l, kind, zT, zname):
        P = self.P
        for (t0, w) in self.tbs:
            self.norm_rstd(lambda k, t0=t0, w=w: zT[:, k, t0:t0 + w], lambda k, t0=t0: (zname, t0), KT, w, "p")
            for k in range(KT):
                si = self.uid() % 2
                t = self.tmpA[si]
                P.op("dve", lambda e, k=k, t=t, t0=t0, w=w: e.scalar_tensor_tensor(out=t[:, :w], in0=zT[:, k, t0:t0 + w], scalar=self.gvec[:, kind, l, k:k + 1], in1=self.rstd[:, :w],
                                                                                  op0=ALU.mult, op1=ALU.mult),
                     r=[(zname, t0), "rstd", "gvec"], w=[("tmpA", si)])
                P.op("pool", lambda e, k=k, t=t, t0=t0, w=w: e.tensor_tensor(out=self.xT[:, k, t0:t0 + w], in0=self.xT[:, k, t0:t0 + w], in1=t[:, :w], op=ALU.add),
                     r=[("tmpA", si), ("xT", t0)], w=[("xT", t0)])

    def out_proj(self, l):
        P = self.P
        zT = self.A[:, 8 * NT:16 * NT].rearrange("p (k n) -> p k n", n=NT)
        for c in range(4):
            wb, wres = self.wload([self.w_out[l][:, c * 256:(c + 1) * 256]], KT, 256)

            def evac(ps, b, m, t0, w):
                if (m + t0 // 512) % 2 == 0:
                    P.op("act", lambda e: e.activation(out=zT[:, m, t0:t0 + w], in_=ps[:, :w], func=AF.Copy), r=[("ps", b)], w=[("zT", t0), ("uT", t0), ("oT", t0)])
                else:
                    P.op("dve", lambda e: e.tensor_copy(out=zT[:, m, t0:t0 + w], in_=ps[:, :w]), r=[("ps", b)], w=[("zT", t0), ("uT", t0), ("oT", t0)])
            self.mm_fm(wb, wres, KT, 2, self.mg, "mg", evac, m0=2 * c)
        self.post_norm_add(l, 1, zT, "zT")

    def ffn(self, l):
        P = self.P
        self.norm_to_h(l, 2)
        for j in range(FT):
            wb, wres = self.wload([self.w_f1[l][:, j * 128:(j + 1) * 128], self.w_f1[l][:, DFF + j * 128:DFF + (j + 1) * 128]], KT, 256)
            for (t0, w) in self.tbs:
                ba = self.bank(); pa = self.PSB[ba]
                bb = self.bank(); pb = self.PSB[bb]
                for mi, (bx, px) in enumerate(((ba, pa), (bb, pb))):
                    for k in range(KT):
                        P.op("pe", lambda e, k=k, mi=mi, px=px, t0=t0, w=w, wb=wb: e.matmul(px[:, :w], lhsT=wb[:, k, mi * 128:(mi + 1) * 128], rhs=self.hT[:, k, t0:t0 + w],
                                                                                            start=(k == 0), stop=(k == KT - 1)),
                             r=[wres, ("hT", t0)], w=[("ps", bx)])
                si = self.uid() % 2
                sg = self.tmpB[si]
                P.op("act", lambda e, pa=pa, sg=sg, w=w: e.activation(out=sg[:, :w], in_=pa[:, :w], func=AF.Silu), r=[("ps", ba)], w=[("tmpB", si)])
                if j < 8:
                    al = [("mg", t0)]
                elif j < 16:
                    al = [("zT", t0), ("uT", t0), ("oT", t0)]
                else:
                    al = [("Hbf", 0), ("Hbf", 1), ("Hbf", 2), ("Hbf", 3), "memh", "ssmw"]
                P.op("dve", lambda e, pb=pb, sg=sg, j=j, t0=t0, w=w: e.tensor_tensor(out=self.aT[:, j, t0:t0 + w], in0=pb[:, :w], in1=sg[:, :w], op=ALU.mult),
                     r=[("ps", bb), ("tmpB", si)], w=[("aT", t0)] + al)
        zT = self.Hst[:].rearrange("p a r c -> p (a r c)").bitcast(BF16).rearrange("p (k n) -> p k n", n=NT)
        for m in range(KT):
            banks = [self.bank() for _ in self.tbs]
            for kh in range(2):
                wb, wres = self.wload([self.w_f2[l][kh * 1408:(kh + 1) * 1408, m * 128:(m + 1) * 128]], 11, 128)
                for ti, (t0, w) in enumerate(self.tbs):
                    b = banks[ti]; ps = self.PSB[b]
                    for k in range(11):
                        kk = kh * 11 + k
                        P.op("pe", lambda e, k=k, kk=kk, ps=ps, t0=t0, w=w, wb=wb: e.matmul(ps[:, :w], lhsT=wb[:, k, :], rhs=self.aT[:, kk, t0:t0 + w],
                                                                                            start=(kk == 0), stop=(kk == FT - 1)),
                             r=[wres, ("aT", t0)], w=[("ps", b)])
            for ti, (t0, w) in enumerate(self.tbs):
                b = banks[ti]; ps = self.PSB[b]
                if ti % 2 == 0:
                    P.op("act", lambda e, ps=ps, m=m, t0=t0, w=w: e.activation(out=zT[:, m, t0:t0 + w], in_=ps[:, :w], func=AF.Copy), r=[("ps", b)], w=[("zF", t0), "carry"] + [("Hst", g) for g in range(4)])
                else:
                    P.op("dve", lambda e, ps=ps, m=m, t0=t0, w=w: e.tensor_copy(out=zT[:, m, t0:t0 + w], in_=ps[:, :w]), r=[("ps", b)], w=[("zF", t0), "carry"] + [("Hst", g) for g in range(4)])
        self.post_norm_add(l, 3, zT, "zF")
        if FFNBAR:
            P.barrier()

    def _emit(self, eng, fn, r, w):
        r = list(r) + list(self._xr)
        w = list(w) + list(self._xw)
        if self._q is not None:
            self._q.append((eng, fn, r, w))
        else:
            self.P.op(eng, fn, r=r, w=w)

    def _flush(self, qa, qb):
        i = j = 0
        while i < len(qa) or j < len(qb):
            if i < len(qa):
                eng, fn, r, w = qa[i]; self.P.op(eng, fn, r=r, w=w); i += 1
            if j < len(qb):
                eng, fn, r, w = qb[j]; self.P.op(eng, fn, r=r, w=w); j += 1

    def _tt(self, out, a, b, op):
        ch = self._ch
        self._emit(self._eng, lambda e: e.tensor_tensor(out=out, in0=a, in1=b, op=op), [ch], [ch])

    def _ts(self, out, a, s1, op0, s2=None, op1=None, extra_r=()):
        ch = self._ch
        if op1 is None:
            self._emit(self._eng, lambda e: e.tensor_scalar(out=out, in0=a, scalar1=s1, scalar2=None, op0=op0), [ch] + list(extra_r), [ch])
        else:
            self._emit(self._eng, lambda e: e.tensor_scalar(out=out, in0=a, scalar1=s1, scalar2=s2, op0=op0, op1=op1), [ch] + list(extra_r), [ch])

    def _mask(self, out, a, mcol):
        ch = self._ch
        if self._eng == "pool":
            shp = list(out.shape)
            mc = mcol
            for d in range(2, len(shp)):
                mc = mc.unsqueeze(d)
            mb = mc.to_broadcast(shp)
            self._emit("pool", lambda e: e.tensor_tensor(out=out, in0=a, in1=mb, op=ALU.mult), [ch, "cst"], [ch])
        else:
            self._emit(self._eng, lambda e: e.tensor_scalar(out=out, in0=a, scalar1=mcol, scalar2=None, op0=ALU.mult), [ch, "cst"], [ch])

    def _cp(self, out, a):
        ch = self._ch
        self._emit(self._eng, lambda e: e.tensor_copy(out=out, in_=a), [ch], [ch])

    def _ms(self, out, val):
        ch = self._ch
        self._emit(self._eng, lambda e: e.memset(out, val), [ch], [ch])

    def _act(self, out, a, func, scale=1.0, bias=None):
        ch = self._ch
        if bias is None:
            self._emit("act", lambda e: e.activation(out=out, in_=a, func=func, scale=scale), [ch], [ch])
        else:
            self._emit("act", lambda e: e.activation(out=out, in_=a, func=func, scale=scale, bias=bias), [ch, "epsb"], [ch])

    def _rcp(self, out, a):
        ch = self._ch
        self._emit("dve", lambda e: e.reciprocal(out=out, in_=a), [ch], [ch])

    def _disc(self, F, lr, li, ldt, S, pre_rden=None):
        tt, ts, act = self._tt, self._ts, self._act
        s0, s1, s2, s3, s4, s5, s6 = S[:7]
        if pre_rden is not None:
            ch = self._ch
            self.P.op("dve", lambda e: e.tensor_tensor(out=pre_rden, in0=lr, in1=lr, op=ALU.mult), r=[ch], w=[ch])
            self.P.op("dve", lambda e: e.tensor_tensor(out=s6, in0=li, in1=li, op=ALU.mult), r=[ch], w=[ch])
            self.P.op("dve", lambda e: e.tensor_tensor(out=pre_rden, in0=pre_rden, in1=s6, op=ALU.add), r=[ch], w=[ch])
            self.P.op("dve", lambda e: e.reciprocal(out=pre_rden, in_=pre_rden), r=[ch], w=[ch])
        act(s0, ldt, AF.Exp)
        tt(s1, lr, s0, ALU.mult)
        tt(s2, li, s0, ALU.mult)
        act(s0, s1, AF.Exp)
        act(s1, s2, AF.Sin, scale=1.0 / 32, bias=self.epsb[:, 1:2])
        act(s3, s2, AF.Sin, scale=1.0 / 32)
        for _ in range(5):
            tt(s2, s1, s1, ALU.mult)
            tt(s4, s3, s3, ALU.mult)
            tt(s5, s1, s3, ALU.mult)
            tt(s1, s2, s4, ALU.subtract)
            ts(s3, s5, 2.0, ALU.mult)
        tt(s2, s0, s1, ALU.mult)
        tt(s4, s0, s3, ALU.mult)
        if pre_rden is None:
            tt(s0, lr, lr, ALU.mult)
            tt(s1, li, li, ALU.mult)
            tt(s0, s0, s1, ALU.add)
            self._rcp(s0, s0)
        else:
            s0 = pre_rden
        ts(s1, s2, -1.0, ALU.add)
        tt(s3, s1, lr, ALU.mult)
        tt(s5, s4, li, ALU.mult)
        tt(s3, s3, s5, ALU.add)
        tt(s3, s3, s0, ALU.mult)
        tt(s5, s4, lr, ALU.mult)
        tt(s6, s1, li, ALU.mult)
        tt(s5, s5, s6, ALU.subtract)
        tt(s5, s5, s0, ALU.mult)
        return s2, s4, s3, s5

    def ssm_prep(self, l):
        P = self.P
        tt, ts = self._tt, self._ts
        pw = self.prm
        P.dma("sp", pw[:, 0:2352], self.i["sprm"][l], w=["pP", "pL"], sem="prm")
        zf = self.zb[:].rearrange("p k c -> p (k c)")
        self._eng, self._ch = "dve", "pP"
        mE = self.cst[:, 128:130]; mL = self.cst[:, 130:132]; mM = self.cst[:, 132:136]; nmE = self.cst[:, 200:202]
        F = 256
        self._eng, self._ch = "pool", "pL"
        zf = self.hT[:].rearrange("p k n -> p (k n)").bitcast(F32)
        S = [zf[:, j * F:(j + 1) * F] for j in range(7)]
        ar, ai, fr, fi = self._disc(F, pw[:, 1072:1328], pw[:, 1328:1584], pw[:, 1584:1840], S, pre_rden=zf[:, 3840:4096])
        o = 7 * F
        brL = pw[:, 1840:2096]; biL = pw[:, 2096:2352]

        def t1(j):
            return zf[:, o + j * 256:o + (j + 1) * 256]
        bb_r, bb_i, cur_r, cur_i, n_r, n_i, u1, u2 = [t1(j) for j in range(8)]
        x1, x2 = S[0], S[1]
        tt(u1, brL, fr, ALU.mult); tt(u2, biL, fi, ALU.mult); tt(bb_r, u1, u2, ALU.subtract)
        tt(u1, biL, fr, ALU.mult); tt(u2, brL, fi, ALU.mult); tt(bb_i, u1, u2, ALU.add)
        self._xw = [("curL", 0), ("curL", 1), "pLa", "pLb"]
        self._ms(cur_r, 1.0)
        self._ms(cur_i, 0.0)
        self._xw = ()
        for k in range(8):
            s = 7 - k
            par = k % 2
            self._q = qa = []
            self._ch = "pLa"; self._xr = [("curL", par), "pL"]; self._xw = ()
            tt(u1, cur_r, bb_r, ALU.mult); tt(u2, cur_i, bb_i, ALU.mult); tt(u1, u1, u2, ALU.subtract)
            for e2 in range(2):
                self._mask(self.Wsb[:, :, s, 0, e2 * 64:(e2 + 1) * 64], u1.rearrange("p (g q) -> p g q", q=64), mL[:, e2:e2 + 1])
            tt(u1, cur_r, bb_i, ALU.mult); tt(u2, cur_i, bb_r, ALU.mult); tt(u1, u1, u2, ALU.add)
            for e2 in range(2):
                self._mask(self.Wsb[:, :, s, 1, e2 * 64:(e2 + 1) * 64], u1.rearrange("p (g q) -> p g q", q=64), mL[:, e2:e2 + 1])
            self._q = qb = []
            if k < 7:
                self._ch = "pLb"; self._xr = [("curL", par), "pL"]; self._xw = ()
                tt(x1, cur_r, ar, ALU.mult); tt(x2, cur_i, ai, ALU.mult)
                self._xw = [("curL", 1 - par)]
                tt(n_r, x1, x2, ALU.subtract)
                self._xw = ()
                tt(x1, cur_r, ai, ALU.mult); tt(x2, cur_i, ar, ALU.mult)
                self._xw = [("curL", 1 - par)]
                tt(n_i, x1, x2, ALU.add)
                self._xw = ()
            self._q = None
            self._flush(qa, qb)
            if k < 7:
                cur_r, n_r = n_r, cur_r
                cur_i, n_i = n_i, cur_i
        self._xr = (); self._xw = ()
        self._ch = "pL"
        self.P.op("pool", lambda e: e.memset(self.small[:, 761:762], 0.0), r=["pLa", "pLb", ("curL", 0), ("curL", 1)], w=["pL"])
        zf = self.zb[:].rearrange("p k c -> p (k c)")
        self._eng, self._ch = "dve", "pP"
        F = 16
        S = [zf[:, j * F:(j + 1) * F] for j in range(7)]
        ar, ai, fr, fi = self._disc(F, pw[:, 0:16], pw[:, 16:32], pw[:, 32:48], S)
        o = 7 * F
        br = pw[:, 48:304].rearrange("p (a h) -> p a h", h=16); bi = pw[:, 304:560].rearrange("p (a h) -> p a h", h=16)
        cr = pw[:, 560:816].rearrange("p (a h) -> p a h", h=16); ci = pw[:, 816:1072].rearrange("p (a h) -> p a h", h=16)

        def t3(j):
            return zf[:, o + j * 256:o + (j + 1) * 256].rearrange("p (a h) -> p a h", h=16)
        bb_r, bb_i, u1, u2 = t3(0), t3(1), t3(2), t3(3)
        frb = fr.unsqueeze(2).to_broadcast([128, 16, 16]); fib = fi.unsqueeze(2).to_broadcast([128, 16, 16])
        tt(u1, br, frb, ALU.mult); tt(u2, bi, fib, ALU.mult); tt(bb_r, u1, u2, ALU.subtract)
        tt(u1, bi, frb, ALU.mult); tt(u2, br, fib, ALU.mult); tt(bb_i, u1, u2, ALU.add)
        Bp = self.small[:, 0:512].bitcast(BF16)[:, 0:1024].rearrange("p (a r c) -> p a r c", a=16, r=2)
        for ri, bb in enumerate((bb_r, bb_i)):
            for e2 in range(2):
                self._mask(Bp[:, :, ri, e2 * 16:(e2 + 1) * 16], bb, mE[:, e2:e2 + 1])
        o2 = o + 4 * 256
        cur_r = zf[:, o2:o2 + 16]; cur_i = zf[:, o2 + 16:o2 + 32]; n_r = zf[:, o2 + 32:o2 + 48]; n_i = zf[:, o2 + 48:o2 + 64]
        w1 = zf[:, o2 + 64:o2 + 80]; w2 = zf[:, o2 + 80:o2 + 96]
        self._xw = [("curP", 0), ("curP", 1), "pPa", "pPb"]
        self._ms(cur_r, 1.0)
        self._ms(cur_i, 0.0)
        self._xw = ()
        for k in range(9):
            par = k % 2
            crb = cur_r.unsqueeze(2).to_broadcast([128, 16, 16]); cib = cur_i.unsqueeze(2).to_broadcast([128, 16, 16])
            self._q = qa = []
            self._ch = "pPa"; self._xr = [("curP", par), "pP"]; self._xw = ()
            tt(u1, cr, crb, ALU.mult); tt(u2, ci, cib, ALU.mult); tt(u1, u1, u2, ALU.subtract)
            for e2 in range(2):
                self._mask(self.Wco[:, :, k, 0, e2 * 16:(e2 + 1) * 16], u1, mE[:, e2:e2 + 1])
            tt(u1, cr, cib, ALU.mult); tt(u2, ci, crb, ALU.mult); tt(u1, u1, u2, ALU.add)
            for e2 in range(2):
                self._mask(self.Wco[:, :, k, 1, e2 * 16:(e2 + 1) * 16], u1, nmE[:, e2:e2 + 1])
            if k == 1:
                self._cp(self.A1[:, 0, :], cur_r)
                self._cp(self.A1[:, 1, :], cur_i)
            if k == 8:
                self._cp(self.A8[:, 0, :], cur_r)
                self._cp(self.A8[:, 1, :], cur_i)
            self._q = qb = []
            if k < 8:
                self._ch = "pPb"; self._xr = [("curP", par), "pP"]; self._xw = ()
                tt(w1, cur_r, ar, ALU.mult); tt(w2, cur_i, ai, ALU.mult)
                self._xw = [("curP", 1 - par)]
                tt(n_r, w1, w2, ALU.subtract)
                self._xw = ()
                tt(w1, cur_r, ai, ALU.mult); tt(w2, cur_i, ar, ALU.mult)
                self._xw = [("curP", 1 - par)]
                tt(n_i, w1, w2, ALU.add)
                self._xw = ()
            self._q = None
            self._flush(qa, qb)
            if k < 8:
                cur_r, n_r = n_r, cur_r
                cur_i, n_i = n_i, cur_i
        self._xr = (); self._xw = ()
        self._ch = "pP"
        self.P.op("dve", lambda e: e.memset(self.small[:, 762:763], 0.0), r=["pPa", "pPb", ("curP", 0), ("curP", 1)], w=["pP"])
        A8r, A8i = self.A8[:, 0, :], self.A8[:, 1, :]
        r2 = zf[:, o2 + 96:o2 + 112]; r3 = zf[:, o2 + 112:o2 + 128]; Ur = zf[:, o2 + 128:o2 + 144]; Ui = zf[:, o2 + 144:o2 + 160]
        tt(r2, A8r, A8r, ALU.mult); tt(r3, A8i, A8i, ALU.mult); tt(r2, r2, r3, ALU.add)
        self._act(self.rho, r2, AF.Sqrt)
        self._rcp(r3, self.rho)
        tt(Ur, A8r, r3, ALU.mult); tt(Ui, A8i, r3, ALU.mult)
        Tc, Ts = self.Tc, self.Ts
        self._cp(Tc[:, :, 0], Ur); self._cp(Ts[:, :, 0], Ui)
        g1 = zf[:, 2048:3072]; g2 = zf[:, 3072:4096]
        Pr, Pi = Ur, Ui
        q1 = zf[:, o2 + 160:o2 + 176]; q2 = zf[:, o2 + 176:o2 + 192]; q3 = zf[:, o2 + 192:o2 + 208]; q4 = zf[:, o2 + 208:o2 + 224]
        nxt = [(q1, q2), (q3, q4)]
        n = 1
        lvl = 0
        while n < NCH:
            a1 = g1[:, 0:16 * n].rearrange("p (a j) -> p a j", j=n); a2 = g2[:, 0:16 * n].rearrange("p (a j) -> p a j", j=n)
            Prb = Pr.unsqueeze(2).to_broadcast([128, 16, n]); Pib = Pi.unsqueeze(2).to_broadcast([128, 16, n])
            pres = "pP" if lvl == 0 else ("Pq", (lvl - 1) % 2)
            self._q = qa = []
            self._ch = "pTa"; self._xr = [pres, "pP"]; self._xw = ()
            tt(a1, Tc[:, :, 0:n], Prb, ALU.mult); tt(a2, Ts[:, :, 0:n], Pib, ALU.mult); tt(Tc[:, :, n:2 * n], a1, a2, ALU.subtract)
            tt(a1, Tc[:, :, 0:n], Pib, ALU.mult); tt(a2, Ts[:, :, 0:n], Prb, ALU.mult); tt(Ts[:, :, n:2 * n], a1, a2, ALU.add)
            self._q = qb = []
            if 2 * n < NCH:
                nr, ni = nxt[lvl % 2]
                b1 = zf[:, o2 + 224:o2 + 240]; b2 = zf[:, o2 + 240:o2 + 256]
                self._ch = "pTb"; self._xr = [pres, "pP"]; self._xw = ()
                tt(b1, Pr, Pr, ALU.mult); tt(b2, Pi, Pi, ALU.mult); tt(b1, b1, b2, ALU.subtract)
                tt(b2, Pr, Pi, ALU.mult)
                self._xw = [("Pq", lvl % 2)]
                ts(ni, b2, 2.0, ALU.mult); self._cp(nr, b1)
                self._xw = ()
            self._q = None
            self._flush(qa, qb)
            if 2 * n < NCH:
                Pr, Pi = nr, ni
            n *= 2
            lvl += 1
        self._xr = (); self._xw = ()
        self._ch = "pP"
        self.P.op("dve", lambda e: e.memset(self.small[:, 763:764], 0.0), r=["pTa", "pTb", ("Pq", 0), ("Pq", 1)], w=["pP"])
        for gt in range(4):
            pb = 6 + gt % 2
            ps = self.PSB[pb]
            for m in range(4):
                pair = 4 * gt + m
                for ri in range(2):
                    P.op("pe", lambda e, m=m, pair=pair, ri=ri, ps=ps: e.matmul(ps[32 * m:32 * m + 32, 0:256], lhsT=Bp[:, pair, ri, :],
                                                                              rhs=self.Wco[:, pair, 0:8, ri, :], start=(ri == 0), stop=(ri == 1),
                                                                              skip_group_check=True, tile_position=(0, 32 * m)),
                         r=["pP"], w=[("ps", pb)])
            for m2 in range(4):
                P.op("dve", lambda e, gt=gt, m2=m2, ps=ps: e.tensor_scalar(out=self.Wtp[:, gt, :, 32 * m2:32 * m2 + 32], in0=ps[:, 0:256].rearrange("p (k c) -> p k c", c=32),
                                                                          scalar1=mM[:, m2:m2 + 1], scalar2=None, op0=ALU.mult),
                     r=[("ps", pb), "cst"], w=["Wtp"])
        sc = self.scr
        P.dma("sp", sc["sb"][l], self.Wsb.rearrange("p g k r c -> p (g k r c)"), r=["pL"], sem="pst1")
        P.dma("sp", sc["co"][l], self.Wco[:].rearrange("p a k r c -> p (a k r c)"), r=["pP"], sem="pst2")
        P.dma("sp", sc["tp"][l], self.Wtp[:].rearrange("p g k c -> p (g k c)"), r=["Wtp"], sem="pst3")
        P.dma("sp", sc["t"][l][:, 0:2048], self.Tc.rearrange("p a c -> p (a c)"), r=["pP"], sem="pst4")
        P.dma("sp", sc["t"][l][:, 2048:4096], self.Ts.rearrange("p a c -> p (a c)"), r=["pP"], sem="pst5")
        P.dma("sp", sc["t"][l][:, 4096:4144], self.a1rho[:, :], r=["pP"], sem="pst6")

    def load_prep(self, l):
        P = self.P
        sc = self.scr
        T3 = (0, 512, HALF)
        aA = [("aT", t) for t in T3]
        P.dma("sp", self.Wsb.rearrange("p g k r c -> p (g k r c)"), sc["sb"][l], r=["scr"], w=["ssmw"] + aA + [("mg", t) for t in T3], sem="pld")
        P.dma("sp", self.Wco[:].rearrange("p a k r c -> p (a k r c)"), sc["co"][l], r=["scr"], w=["ssmw", ("kvs", 0), ("kvs", 1)], sem="pld")
        P.dma("sp", self.Wtp[:].rearrange("p g k c -> p (g k c)"), sc["tp"][l], r=["scr"], w=["ssmw", "kT", "vv", "qtok"], sem="pld")
        P.dma("sp", self.Tc.rearrange("p a c -> p (a c)"), sc["t"][l][:, 0:2048], r=["scr"], w=["ssmw"] + aA + [("oT", t) for t in T3] + [("zT", t) for t in T3], sem="pld")
        P.dma("sp", self.Ts.rearrange("p a c -> p (a c)"), sc["t"][l][:, 2048:4096], r=["scr"], w=["ssmw"] + aA + [("Hbf", g) for g in range(4)], sem="pld")
        P.dma("sp", self.a1rho[:, :], sc["t"][l][:, 4096:4144], r=["scr"], w=["ssmw"], sem="pld")

    def gelu_to(self, y, yres, dst, dres, w):
        P = self.P
        si = self.uid() % 2
        t = self.tmpB[si]
        P.op("pool", lambda e: e.tensor_tensor(out=t[:, :w], in0=y, in1=y, op=ALU.mult), r=[yres], w=[("tmpB", si)])
        P.op("pool", lambda e: e.tensor_scalar(out=t[:, :w], in0=t[:, :w], scalar1=0.044715, scalar2=1.0, op0=ALU.mult, op1=ALU.add), r=[("tmpB", si)], w=[("tmpB", si)])
        P.op("pool", lambda e: e.tensor_tensor(out=t[:, :w], in0=t[:, :w], in1=y, op=ALU.mult), r=[("tmpB", si), yres], w=[("tmpB", si)])
        P.op("act", lambda e: e.activation(out=t[:, :w], in_=t[:, :w], func=AF.Sigmoid, scale=1.5957691216057308), r=[("tmpB", si)], w=[("tmpB", si)])
        P.op("dve", lambda e: e.tensor_tensor(out=dst, in0=y, in1=t[:, :w], op=ALU.mult), r=[("tmpB", si), yres], w=[dres])

    def ssm_core(self, l, half):
        P = self.P
        Hst, Hbf, Wsb, Wco, Wtp, uT = self.Hst, self.Hbf, self.Wsb, self.Wco, self.Wtp, self.uT
        u8 = uT[:, :, 0:HALF].rearrange("p g (c j) -> p g c j", j=LCH)
        HG = [("Hst", g) for g in range(4)]
        P.op("dve", lambda e: e.tensor_copy(out=Hst[:, :, :, 0], in_=self.carry[:, l]), r=["carry"], w=HG)
        if half == 1 and "ss" not in SKIP:
            self.ssm_sample_h0(l)
        Tc, Ts = self.Tc, self.Ts
        tmps = [(self.tmpA[0], ("tmpA", 0)), (self.tmpA[1], ("tmpA", 1)), (self.tmpB[0], ("tmpB", 0)), (self.tmpB[1], ("tmpB", 1))]
        v = lambda t: t[:, :].rearrange("p (a c) -> p a c", c=NCH)

        def rot(sign, qd):
            hres = ("Hst", qd)
            sl = slice(4 * qd, 4 * qd + 4)
            Xr = Hst[:, sl, 0, 1:NCH + 1]; Xi = Hst[:, sl, 1, 1:NCH + 1]
            C = Tc[:, sl, :]; S_ = Ts[:, sl, :]
            (t1, r1), (t2, r2), (t3, r3), (t4, r4) = tmps
            P.op("dve", lambda e: e.tensor_tensor(out=v(t1), in0=Xr, in1=C, op=ALU.mult), r=[hres, "ssmw"], w=[r1])
            P.op("pool", lambda e: e.tensor_tensor(out=v(t2), in0=Xi, in1=S_, op=ALU.mult), r=[hres, "ssmw"], w=[r2])
            P.op("dve", lambda e: e.tensor_tensor(out=v(t3), in0=Xi, in1=C, op=ALU.mult), r=[hres, "ssmw"], w=[r3])
            P.op("pool", lambda e: e.tensor_tensor(out=v(t4), in0=Xr, in1=S_, op=ALU.mult), r=[hres, "ssmw"], w=[r4])
            if sign < 0:
                P.op("dve", lambda e: e.tensor_tensor(out=Xr, in0=v(t1), in1=v(t2), op=ALU.add), r=[r1, r2], w=[hres])
                P.op("pool", lambda e: e.tensor_tensor(out=Xi, in0=v(t3), in1=v(t4), op=ALU.subtract), r=[r3, r4], w=[hres])
            else:
                P.op("dve", lambda e: e.tensor_tensor(out=Xr, in0=v(t1), in1=v(t2), op=ALU.subtract), r=[r1, r2], w=[hres])
                P.op("pool", lambda e: e.tensor_tensor(out=Xi, in0=v(t3), in1=v(t4), op=ALU.add), r=[r3, r4], w=[hres])
        def state_build(gt):
            off = (gt % 2) * 256
            for ri in range(2):
                for k in range(LCH):
                    for m in range(4):
                        ps = self.PSB[m]
                        P.op("pe", lambda e, m=m, ri=ri, k=k, ps=ps: e.matmul(ps[:, off + ri * 128: off + (ri + 1) * 128], lhsT=Wsb[32 * m:32 * m + 32, gt, k, ri, :],
                                                                               rhs=u8[32 * m:32 * m + 32, gt, :, k], start=(k == 0), stop=(k == LCH - 1),
                                                                               skip_group_check=True, tile_position=(32 * m, 0)),
                             r=["ssmw", ("uT", 0), ("uT", 512)], w=[("ps", m)])
            for m in range(4):
                pair = 4 * gt + m
                ps = self.PSB[m]
                if pair % 2:
                    P.op("act", lambda e, pair=pair, ps=ps: e.activation(out=Hst[:, pair, :, 1:NCH + 1], in_=ps[:, off:off + 256].rearrange("p (r c) -> p r c", r=2), func=AF.Copy),
                         r=[("ps", m)], w=[("Hst", gt)])
                else:
                    P.op("dve", lambda e, pair=pair, ps=ps: e.tensor_copy(out=Hst[:, pair, :, 1:NCH + 1], in_=ps[:, off:off + 256].rearrange("p (r c) -> p r c", r=2)),
                         r=[("ps", m)], w=[("Hst", gt)])

        for qd in range(4):
            state_build(qd)
            rot(-1, qd)
            for pair in range(4 * qd, 4 * qd + 4):
                for ri in range(2):
                    P.op("dve", lambda e, pair=pair, ri=ri: e.tensor_tensor_scan(out=Hst[:, pair, ri, 1:NCH + 1], data0=self.rho[:, pair:pair + 1].to_broadcast([128, NCH]),
                                                                                data1=Hst[:, pair, ri, 1:NCH + 1], initial=Hst[:, pair, ri, 0:1], op0=ALU.mult, op1=ALU.add),
                         r=[("Hst", qd), "ssmw"], w=[("Hst", qd)])
            rot(+1, qd)
        for qd in range(4):
            P.op("act", lambda e, qd=qd: e.activation(out=Hbf[:, 4 * qd:4 * qd + 4, :, 0:NCH], in_=Hst[:, 4 * qd:4 * qd + 4, :, 0:NCH], func=AF.Copy), r=HG, w=[("Hbf", qd)])
        P.op("dve", lambda e: e.tensor_copy(out=self.carry[:, l], in_=Hst[:, :, :, NCH]), r=HG, w=["carry"])
        if half == self.nh - 1:
            for ri, nm in enumerate(("re_p", "im_p")):
                t = self.small[:, 640 + 16 * ri: 656 + 16 * ri]
                P.op("act", lambda e, ri=ri, t=t: e.activation(out=t, in_=Hst[:, :, ri, NCH], func=AF.Copy), r=HG, w=[("fin", ri)])
                P.dma("sp", self.o[nm][l].rearrange("(a q) -> q a", q=128), t, r=[("fin", ri)], sem="fin%d" % ri, final=True, allow_slow_non_contiguous=True)
        for gt in range(4):
            for bi_, (t0, w) in enumerate(self.tbs[:2]):
                b = self.bank(); ps = self.PSB[b]
                y3 = ps[:, :].rearrange("p (c j) -> p c j", j=LCH)
                u3 = uT[:, gt, t0:t0 + 512].rearrange("p (c j) -> p c j", j=LCH)
                if TAPS2D:
                    for k in range(LCH):
                        for j2 in range(k, LCH):
                            P.op("pe", lambda e, gt=gt, k=k, j2=j2, y3=y3, u3=u3: e.matmul(y3[:, :, j2], lhsT=Wtp[:, gt, k, :], rhs=u3[:, :, j2 - k], start=(k == 0 and j2 == 0), stop=False, skip_group_check=True),
                                 r=["ssmw", ("uT", t0)], w=[("ps", b)])
                else:
                    for k in range(LCH):
                        P.op("pe", lambda e, gt=gt, k=k, y3=y3, u3=u3: e.matmul(y3[:, :, k:LCH], lhsT=Wtp[:, gt, k, :], rhs=u3[:, :, 0:LCH - k], start=(k == 0), stop=False, skip_group_check=True),
                             r=["ssmw", ("uT", t0)], w=[("ps", b)])
                c0 = bi_ * 64
                for m in range(4):
                    pair = 4 * gt + m
                    for j in range(LCH):
                        for ri in range(2):
                            last = (m == 3 and j == LCH - 1 and ri == 1)
                            P.op("pe", lambda e, m=m, pair=pair, j=j, ri=ri, y3=y3, c0=c0, last=last: e.matmul(y3[32 * m:32 * m + 32, :, j], lhsT=Wco[:, pair, j + 1, ri, :],
                                                                                                           rhs=Hbf[:, pair, ri, c0:c0 + 64], start=False, stop=last,
                                                                                                           skip_group_check=True, tile_position=(0, 32 * m)),
                                 r=["ssmw", ("Hbf", gt)], w=[("ps", b)])
                self.ssm_post(l, gt, ps, b, t0, 512)
        if half == 1 and "ss" not in SKIP:
            self.ssm_sample(l)

    def ssm_post(self, l, gt, ps, b, t0, w):
        P = self.P
        si = self.uid() % 2
        y = self.tmpA[si]
        P.op("dve", lambda e: e.scalar_tensor_tensor(out=y[:, :w], in0=self.uT[:, gt, t0:t0 + w], scalar=self.vec4[:, 0, l, gt:gt + 1], in1=ps[:, :w], op0=ALU.mult, op1=ALU.add),
             r=[("ps", b), ("uT", t0), "vec4"], w=[("tmpA", si)])
        self.gelu_to(y[:, :w], ("tmpA", si), self.oT[:, gt, t0:t0 + w], ("oT", t0), w)

    def glu(self, l):
        P = self.P
        wb, wres = self.wload([self.w_glu[l]], 4, 512)

        def evac(ps, b, m, t0, w):
            si = self.uid() % 2
            sg = self.tmpB[si]
            P.op("act", lambda e: e.activation(out=sg[:, :w], in_=ps[:, :w], func=AF.Sigmoid, bias=self.vec4[:, 1, l, m:m + 1]), r=[("ps", b), "vec4"], w=[("tmpB", si)])
            P.op("dve", lambda e: e.tensor_tensor(out=self.uT[:, m, t0:t0 + w], in0=self.oT[:, m, t0:t0 + w], in1=sg[:, :w], op=ALU.mult), r=[("tmpB", si), ("oT", t0)], w=[("uT", t0)])
        self.mm_fm(wb, wres, 4, 4, self.oT, "oT", evac)

    def ssm_sample_load(self, l):
        P = self.P
        zs = self.zb[:].rearrange("p k c -> p (k c)")[0:NS, :].rearrange("p (r c) -> p r c", r=2)
        P.dma("sp", zs[:, 0, :], self.i["sre"][l], w=["zb", "pw0", "pw1", "hist", ("zbs", 0)], sem="zbs0")
        P.dma("sp", zs[:, 1, :], self.i["sim"][l], w=[("zbs", 1)], sem="zbs1")

    def ssm_sample_h0(self, l):
        P = self.P
        zs = self.zb[:].rearrange("p k c -> p (k c)")[0:NS, :].rearrange("p (r c) -> p r c", r=2)
        b = self.bank(); ps = self.PSB[b]
        ps4 = ps[:, :].rearrange("p (a r n) -> p a r n", a=16, r=2)
        for pair in range(16):
            for ri in range(2):
                P.op("pe", lambda e, pair=pair, ri=ri, ps4=ps4: e.transpose(ps4[:, pair, ri, :], zs[:, ri, pair * 128:(pair + 1) * 128], self.ident[0:NS, 0:NS]),
                     r=["zb", ("zbs", 0), ("zbs", 1), "cst"], w=[("ps", b)])
        H0 = self.small[:, 0:512].rearrange("p (a r n) -> p a r n", a=16, r=2)
        P.op("dve", lambda e: e.tensor_copy(out=H0, in_=ps4), r=[("ps", b)], w=["H0"])

    def ssm_sample(self, l):
        P = self.P
        Wsb, Wco, uT = self.Wsb, self.Wco, self.uT
        H0 = self.small[:, 0:512].rearrange("p (a r n) -> p a r n", a=16, r=2)
        if KSS < 2:
            return
        BU = self.sA[:, :].rearrange("p (a r n) -> p a r n", a=16, r=2)
        BUg = self.sA[:, :].rearrange("p (g m r n) -> p g m r n", g=4, m=4, r=2)
        for gt in range(4):
            for ri in range(2):
                c0 = (gt * 2 + ri) * NS
                for m in range(4):
                    psm = self.PSB[m]
                    P.op("pe", lambda e, m=m, gt=gt, ri=ri, psm=psm, c0=c0: e.matmul(psm[:, c0:c0 + NS], lhsT=Wsb[32 * m:32 * m + 32, gt, 7, ri, :], rhs=uT[32 * m:32 * m + 32, gt, HALF:HALF + NS],
                                                                                   start=True, stop=True, skip_group_check=True, tile_position=(32 * m, 0)),
                         r=["ssmw", ("uT", HALF)], w=[("ps", m)])
        for m in range(4):
            psm = self.PSB[m]
            P.op("dve" if m % 2 else "act", (lambda e, m=m, psm=psm: e.tensor_copy(out=BUg[:, :, m], in_=psm[:, 0:128].rearrange("p (g r n) -> p g r n", g=4, r=2))) if m % 2 else
                 (lambda e, m=m, psm=psm: e.activation(out=BUg[:, :, m], in_=psm[:, 0:128].rearrange("p (g r n) -> p g r n", g=4, r=2), func=AF.Copy)), r=[("ps", m)], w=["BU"])
        if KSS < 3:
            return
        A1r = self.A1[:, 0, :].unsqueeze(2).to_broadcast([128, 16, NS]); A1i = self.A1[:, 1, :].unsqueeze(2).to_broadcast([128, 16, NS])
        T = self.sB[:, 0:256].rearrange("p (a n) -> p a n", n=NS)
        seq = [(0, A1r, 0, ALU.add), (1, A1i, 0, ALU.subtract), (0, A1i, 1, ALU.add), (1, A1r, 1, ALU.add)]
        for (hs, Ax, dst, op) in seq:
            P.op("pool", lambda e, hs=hs, Ax=Ax: e.tensor_tensor(out=T, in0=H0[:, :, hs, :], in1=Ax, op=ALU.mult), r=["H0", "ssmw"], w=["sT"])
            P.op("pool", lambda e, dst=dst, op=op: e.tensor_tensor(out=BU[:, :, dst, :], in0=BU[:, :, dst, :], in1=T, op=op), r=["sT", "BU"], w=["BU"])
        if KSS < 4:
            return
        hb = self.sB[:, 256:512].bitcast(BF16).rearrange("p (a r n) -> p a r n", a=16, r=2)
        P.op("act", lambda e: e.activation(out=hb, in_=BU, func=AF.Copy), r=["BU"], w=["hb"])
        for gt in range(4):
            b = self.bank(); ps = self.PSB[b]
            for m in range(4):
                pair = 4 * gt + m
                for ri in range(2):
                    P.op("pe", lambda e, m=m, pair=pair, ri=ri, ps=ps: e.matmul(ps[32 * m:32 * m + 32, 0:NS], lhsT=Wco[:, pair, 0, ri, :], rhs=hb[:, pair, ri, :],
                                                                              start=(ri == 0), stop=(ri == 1), skip_group_check=True, tile_position=(0, 32 * m)),
                         r=["ssmw", "hb"], w=[("ps", b)])
            self.ssm_post(l, gt, ps, b, HALF, NS)
        if KSS < 5:
            return
        for ri, nm in enumerate(("re_s", "im_s")):
            for q in range(4):
                psq = self.PSB[q]
                for a4 in range(4):
                    pair = 4 * q + a4
                    P.op("pe", lambda e, pair=pair, ri=ri, a4=a4, psq=psq: e.transpose(psq[0:NS, a4 * 128:(a4 + 1) * 128], BU[:, pair, ri, :], self.ident),
                         r=["BU", "cst"], w=[("ps", q)])
                si = self.uid() % 2
                t = self.tmpA[si]
                P.op("dve", lambda e, psq=psq, t=t: e.tensor_copy(out=t[0:NS, :], in_=psq[0:NS, :]), r=[("ps", q)], w=[("tmpA", si)])
                if KSS == 7:
                    if not hasattr(self, "dbgres"):
                        self.dbgres = self.nc.dram_tensor("dbg_res", [2, NS, 2048], F32, kind="ExternalOutput").ap()
                    P.dma("sp", self.dbgres[ri][:, q * 512:(q + 1) * 512], t[0:NS, :], r=[("tmpA", si)], sem="sso%d" % si, final=True)
                elif KSS >= 6:
                    P.dma(DQ, self.o[nm][l][:, q * 512:(q + 1) * 512], t[0:NS, :], r=[("tmpA", si)], sem="sso%d" % si, final=True)

    def pool_core(self, l, half):
        P = self.P
        uT = self.uT
        wb, wres = self.wload([self.pool_w[l, gi] for gi in range(4)], 1, 512)
        W = HALF + 16
        bufs = [(self.pw[:, 0:W], "pw0"), (self.pw[:, W:2 * W], "pw1")]
        invc = self.cst[:, 136:200].rearrange("p (g t) -> p g t", t=16)
        for gi, win in enumerate((2, 4, 8, 16)):
            (src, sres), (dst, dres) = bufs
            weng = "pool" if gi in (0, 3) else "dve"
            P.op(weng, lambda e, a=src, gi=gi: e.tensor_copy(out=a[:, 0:16], in_=self.phist[:, l, gi, :]), r=["phist"], w=[sres, "zb", "hist"])
            P.op(weng, lambda e, a=src, gi=gi: e.tensor_copy(out=a[:, 16:W], in_=uT[:, gi, 0:HALF]), r=[("uT", 0), ("uT", 512)], w=[sres])
            P.op(weng, lambda e, gi=gi: e.tensor_copy(out=self.phist[:, l, gi, :], in_=uT[:, gi, HALF - 16:HALF]), r=[("uT", 512), sres], w=["phist"])
            if win >= 4:
                P.op("dve", lambda e, src=src, dst=dst: e.tensor_tensor_scan(out=dst[:, 0:W], data0=self.onesf[:, 0:1].to_broadcast([128, W]), data1=src[:, 0:W], initial=0.0,
                                                                            op0=ALU.mult, op1=ALU.add), r=[sres, "onesf"], w=[dres])
                P.op("dve", lambda e, src=src, dst=dst, win=win: e.tensor_tensor(out=src[:, 16:W], in0=dst[:, 16:W], in1=dst[:, 16 - win:W - win], op=ALU.subtract), r=[dres], w=[sres])
            else:
                step = 1
                while step < win:
                    P.op(weng, lambda e, src=src, dst=dst, step=step: e.tensor_tensor(out=dst[:, step:W], in0=src[:, step:W], in1=src[:, 0:W - step], op=ALU.add), r=[sres], w=[dres])
                    P.op(weng, lambda e, src=src, dst=dst, step=step: e.tensor_copy(out=dst[:, 0:step], in_=src[:, 0:step]), r=[sres], w=[dres])
                    src, dst = dst, src
                    sres, dres = dres, sres
                    step *= 2
            pl = dst[:, 0:HALF // 2 + 8].bitcast(BF16)[:, 0:HALF]
            P.op("dve", lambda e, src=src, pl=pl, gi=gi, win=win: e.scalar_tensor_tensor(out=pl, in0=src[:, 16:W], scalar=1.0 / win, in1=uT[:, gi, 0:HALF], op0=ALU.mult, op1=ALU.subtract),
                 r=[sres, ("uT", 0), ("uT", 512)], w=[dres])
            if half == 0:
                t = self.small[:, 672:688]
                P.op("dve", lambda e, src=src, gi=gi, t=t: e.tensor_tensor(out=t, in0=src[:, 16:32], in1=invc[:, gi, :], op=ALU.mult), r=[sres, "cst"], w=["pfix"])
                P.op("dve", lambda e, pl=pl, gi=gi, t=t: e.tensor_tensor(out=pl[:, 0:16], in0=t, in1=uT[:, gi, 0:16], op=ALU.subtract), r=["pfix", ("uT", 0), dres], w=[dres])
            for (t0, w) in self.tbs[:2]:
                b = self.bank(); ps = self.PSB[b]
                P.op("pe", lambda e, gi=gi, ps=ps, pl=pl, t0=t0, w=w: e.matmul(ps[:, :w], lhsT=wb[:, 0, gi * 128:(gi + 1) * 128], rhs=pl[:, t0:t0 + w], start=True, stop=True),
                     r=[wres, dres], w=[("ps", b)])
                P.op("act", lambda e, gi=gi, ps=ps, t0=t0, w=w: e.activation(out=self.oT[:, gi, t0:t0 + w], in_=ps[:, :w], func=AF.Copy, scale=self.vec4[:, 2, l, gi:gi + 1]),
                     r=[("ps", b), "vec4"], w=[("oT", t0)])
        if half == 1 and "ps" not in SKIP:
            self.pool_sample(l, wb, wres)
        if half == self.nh - 1:
            b = self.bank(); ps = self.PSB[b]
            for gi in range(4):
                P.op("pe", lambda e, gi=gi, ps=ps: e.transpose(ps[0:16, gi * 128:(gi + 1) * 128], self.lastu[:, gi, :], self.ident), r=["lastu", "cst"], w=[("ps", b)])
            t = self.sB
            P.op("dve", lambda e, ps=ps: e.tensor_copy(out=t[0:16, :], in_=ps[0:16, :]), r=[("ps", b)], w=["sBo", "sT", "hb", "den"])
            P.dma("sp", self.o["pool_p"][l], t[1:16, :], r=["sBo"], sem="sBo", final=True)

    def pool_sample(self, l, wb, wres):
        P = self.P
        uT = self.uT
        sp = self.i["spool"][l].rearrange("n r c -> (n r) c")
        zt = self.zb[:].rearrange("p k c -> p (k c)")
        hist = self.pw[:, 0:960].rearrange("p (g x) -> p g x", g=4)
        for j in range(2):
            P.dma("sp", zt[0:120, j * 512:(j + 1) * 512], sp[j * 120:(j + 1) * 120, :], w=["zb"], sem="zb")
        P.dma("sp", self.o["pool_s"][l][:, 0:14, :], self.i["spool"][l][:, 1:15, :], sem="pcopy", final=True)
        for gi in range(4):
            b = self.bank(); ps = self.PSB[b]
            for j in range(2):
                P.op("pe", lambda e, gi=gi, j=j, ps=ps: e.transpose(ps[:, j * 120:(j + 1) * 120], zt[0:120, j * 512 + gi * 128: j * 512 + (gi + 1) * 128], self.ident[0:120, 0:120]),
                     r=["zb", "cst"], w=[("ps", b)])
            P.op("dve", lambda e, gi=gi, ps=ps: e.tensor_copy(out=hist[:, gi, :], in_=ps[:, 0:240]), r=[("ps", b)], w=["hist", "pw0"])
        h4 = self.pw[:, 0:960].rearrange("p (g n r) -> p g n r", g=4, r=15)
        red = self.small[:, 688:704]
        plb = self.small[:, 720:752].bitcast(BF16).rearrange("p (g n) -> p g n", g=4)
        for gi, win in enumerate((2, 4, 8, 16)):
            P.op("dve", lambda e, gi=gi, win=win: e.tensor_reduce(out=red, in_=h4[:, gi, :, 16 - win:15], axis=AX.X, op=ALU.add), r=["hist"], w=["red"])
            P.op("dve", lambda e, gi=gi: e.tensor_tensor(out=red, in0=red, in1=self.usam[:, gi, :], op=ALU.add), r=["red", "usam"], w=["red"])
            P.op("dve", lambda e, gi=gi, win=win: e.scalar_tensor_tensor(out=plb[:, gi, :], in0=red, scalar=1.0 / win, in1=self.usam[:, gi, :], op0=ALU.mult, op1=ALU.subtract),
                 r=["red", "usam"], w=["plb"])
            b = self.bank(); ps = self.PSB[b]
            P.op("pe", lambda e, gi=gi, ps=ps: e.matmul(ps[:, 0:NS], lhsT=wb[:, 0, gi * 128:(gi + 1) * 128], rhs=plb[:, gi, :], start=True, stop=True), r=[wres, "plb"], w=[("ps", b)])
            P.op("act", lambda e, gi=gi, ps=ps: e.activation(out=self.oT[:, gi, HALF:HALF + NS], in_=ps[:, 0:NS], func=AF.Copy, scale=self.vec4[:, 2, l, gi:gi + 1]),
                 r=[("ps", b), "vec4"], w=[("oT", HALF)])
        b = self.bank(); ps = self.PSB[b]
        for gi in range(4):
            P.op("pe", lambda e, gi=gi, ps=ps: e.transpose(ps[0:NS, gi * 128:(gi + 1) * 128], self.usam[:, gi, :], self.ident), r=["usam", "cst"], w=[("ps", b)])
        t = self.sA
        P.op("dve", lambda e, ps=ps: e.tensor_copy(out=t[0:NS, :], in_=ps[0:NS, :]), r=[("ps", b)], w=["sAo", "BU"])
        P.dma("sp", self.o["pool_s"][l][:, 14, :], t[0:NS, :], r=["sAo"], sem="sAo", final=True)

    def attn_core(self, l, half):
        P = self.P
        qT = self.uT
        sc = 128 ** -0.5
        for (t0, w) in self.tbs[:2]:
            for h in range(4):
                pts = []
                for mt in range(2):
                    b = self.bank(); ps = self.PSB[b]
                    P.op("pe", lambda e, h=h, mt=mt, ps=ps, t0=t0, w=w: e.matmul(ps[:, :w], lhsT=self.kT[:, h, mt * 128:(mt + 1) * 128], rhs=qT[:, h, t0:t0 + w], start=True, stop=True),
                         r=["kT", ("uT", t0)], w=[("ps", b)])
                    pt = self.sq[mt]
                    P.op("act", lambda e, ps=ps, pt=pt, w=w: e.activation(out=pt[:, :w], in_=ps[:, :w], func=AF.Exp, scale=sc), r=[("ps", b)], w=[("sq", mt)])
                    pts.append(pt)
                bd = self.bank(); pd = self.PSB[bd]
                bo = self.bank(); po = self.PSB[bo]
                for mt in range(2):
                    P.op("pe", lambda e, mt=mt, pd=pd, w=w, pts=pts: e.matmul(pd[:, :w], lhsT=self.onesb[:], rhs=pts[mt][:, :w], start=(mt == 0), stop=(mt == 1)),
                         r=[("sq", mt), "onesb"], w=[("ps", bd)])
                for mt in range(2):
                    P.op("pe", lambda e, mt=mt, h=h, po=po, w=w, pts=pts: e.matmul(po[:, :w], lhsT=self.vv[:, mt, h * 128:(h + 1) * 128], rhs=pts[mt][:, :w], start=(mt == 0), stop=(mt == 1)),
                         r=[("sq", mt), "vv"], w=[("ps", bo)])
                si = self.uid() % 2
                rc = self.tmpA[si]
                P.op("dve", lambda e, pd=pd, rc=rc, w=w: e.reciprocal(out=rc[:, :w], in_=pd[:, :w]), r=[("ps", bd)], w=[("tmpA", si)])
                P.op("dve", lambda e, po=po, rc=rc, h=h, t0=t0, w=w: e.tensor_tensor(out=self.oT[:, h, t0:t0 + w], in0=po[:, :w], in1=rc[:, :w], op=ALU.mult),
                     r=[("ps", bo), ("tmpA", si)], w=[("oT", t0)])
        if half == 1 and "as" not in SKIP:
            self.attn_sample(l)

    def attn_sample(self, l):
        P = self.P
        sc = 128 ** -0.5
        kvbuf = self.Wco[:].rearrange("p a k r c -> p (a k r c)").bitcast(F32)
        KV = [kvbuf[:, j * 2048:(j + 1) * 2048].rearrange("p (x t c) -> p x t c", x=2, t=2) for j in range(2)]
        ck, cv = self.i["ck"], self.i["cv"]
        bo = 6; po = self.PSB[bo]
        bd = 7; pd = self.PSB[bd]
        Sx = self.sA[:, 0:128].rearrange("p (n t h) -> p n t h", n=NS, t=2)
        Pb = self.small[:, 512:576].bitcast(BF16).rearrange("p (n t h) -> p n t h", n=NS, t=2)
        qms = [(self.sq[0][0:NS, :], ("sq", 0)), (self.sq[1][0:NS, :], ("sq", 1))]
        vbs = [(self.tmpA[j][:, :].bitcast(BF16).rearrange("p (t c) -> p t c", t=2), ("tmpA", j)) for j in range(2)]
        pqs = {}

        def qbcast(n):
            kv = KV[n % 2]
            kres = ("kvs", n % 2)
            P.dma("sp", kv[:, 0], ck[l, n].rearrange("(t p) c -> p t c", p=128), w=[kres, "ssmw"], sem="kvs%d" % (n % 2))
            P.dma("sp", kv[:, 1], cv[l, n].rearrange("(t p) c -> p t c", p=128), w=[kres], sem="kvs%d" % (n % 2))
            bq = self.bank(); pq = self.PSB[bq]
            qm, qres = qms[n % 2]
            P.op("dve", lambda e: e.tensor_scalar(out=qm, in0=self.qtok[:, :], scalar1=self.cst[0:NS, n:n + 1], scalar2=None, op0=ALU.mult), r=["qtok", "cst"], w=[qres])
            P.op("pe", lambda e: e.matmul(pq[:, :], lhsT=self.onesb[0:NS, :], rhs=qm, start=True, stop=True), r=["onesb", qres], w=[("ps", bq)])
            vb, vres = vbs[n % 2]
            P.op("pool", lambda e: e.tensor_copy(out=vb, in_=kv[:, 1]), r=[kres], w=[vres])
            pqs[n] = (bq, pq)
        qbcast(0)
        for n in range(NS):
            kv = KV[n % 2]
            kres = ("kvs", n % 2)
            vb, vres = vbs[n % 2]
            bq, pq = pqs[n]
            for mt in range(2):
                si = self.uid() % 2
                t = self.tmpB[si]
                P.op("dve", lambda e, kv=kv, mt=mt, pq=pq, t=t: e.tensor_tensor(out=t[:, :], in0=kv[:, 0, mt, :], in1=pq[:, :], op=ALU.mult), r=[kres, ("ps", bq)], w=[("tmpB", si)])
                P.op("dve", lambda e, n=n, mt=mt, t=t: e.tensor_reduce(out=Sx[:, n, mt, :], in_=t[:, :].rearrange("p (h d) -> p h d", d=128), axis=AX.X, op=ALU.add),
                     r=[("tmpB", si)], w=[("Sx", n), "BU", "sAo"])
            P.op("act", lambda e, n=n: e.activation(out=Pb[:, n], in_=Sx[:, n], func=AF.Exp, scale=sc), r=[("Sx", n)], w=[("Pf", n)])
            if n + 1 < NS:
                qbcast(n + 1)
            P.op("pe", lambda e, n=n: e.matmul(pd[:, n * 8:(n + 1) * 8], lhsT=self.onesb[:], rhs=Pb[:, n].rearrange("p t h -> p (t h)"), start=True, stop=True, skip_group_check=True),
                 r=[("Pf", n), "onesb"], w=[("ps", bd)])
            for h in range(4):
                for mt in range(2):
                    P.op("pe", lambda e, n=n, h=h, mt=mt, vb=vb: e.matmul(po[:, h * NS + n: h * NS + n + 1], lhsT=vb[:, mt, h * 128:(h + 1) * 128], rhs=Pb[:, n, mt, h:h + 1],
                                                                        start=(mt == 0), stop=(mt == 1), skip_group_check=True),
                         r=[vres, ("Pf", n)], w=[("ps", bo)])
        den = self.sB[:, 0:64].rearrange("p (h n) -> p h n", n=NS)
        d8 = self.sB[:, 64:192]
        P.op("dve", lambda e: e.tensor_copy(out=d8, in_=pd[:, 0:128]), r=[("ps", bd)], w=["den", "sT", "hb", "sBo"])
        d84 = d8.rearrange("p (n t h) -> p h n t", n=NS, t=2)
        P.op("dve", lambda e: e.tensor_tensor(out=den, in0=d84[:, :, :, 0], in1=d84[:, :, :, 1], op=ALU.add), r=["den"], w=["den", "sT", "hb", "sBo"])
        P.op("dve", lambda e: e.reciprocal(out=den, in_=den), r=["den"], w=["den"])
        P.op("dve", lambda e: e.tensor_tensor(out=self.oT[:, :, HALF:HALF + NS], in0=po[:, 0:64].rearrange("p (h n) -> p h n", n=NS), in1=den, op=ALU.mult),
             r=[("ps", bo), "den"], w=[("oT", HALF)])

    def layer(self, l, half):
        P = self.P
        last = (half == self.nh - 1)
        if half == 1 and "ss" not in SKIP:
            self.ssm_sample_load(l)
        self.load_prep(l)
        self.norm_to_h(l, 0)
        if half == 1 and KSTOP <= 1:
            return
        self.dump("hT", self.hT[:], l, half)
        self.proj_u(l, 0)
        self.dump("u_ssm", self.uT, l, half)
        self.ssm_core(l, half)
        wtf = self.Wtp[:].rearrange("p g k c -> p (g k c)")
        P.dma("sp", wtf[:, 0:2048], self.scr["kv"][l], r=["scr"], w=["kT", "vv", "ssmw"], sem="kld")
        self.dump("ygelu", self.oT, l, half)
        self.dump("Hst", self.Hst[:], l, half)
        self.dump("Wtp", self.Wtp[:], l, half)
        self.dump("Wco", self.Wco[:], l, half)
        self.dump("Wsb", self.Wsb, l, half)
        self.glu(l)
        self.dump("o_ssm", self.uT, l, half)
        self.merge(l, 0, self.uT, "uT")
        self.dump("mg0", self.mg, l, half)
        if half == 1 and KSTOP <= 2:
            return

        def cap(ps, b, m, t0, w):
            if last and t0 == 512:
                P.op("dve", lambda e: e.tensor_copy(out=self.lastu[:, m, :], in_=ps[:, 496:512]), r=[("ps", b)], w=["lastu"])
            if t0 == HALF:
                P.op("dve", lambda e: e.tensor_copy(out=self.usam[:, m, :], in_=ps[:, 0:NS]), r=[("ps", b)], w=["usam"])
        self.proj_u(l, 512, extra=cap)
        self.dump("u_pool", self.uT, l, half)
        self.pool_core(l, half)
        self.dump("o_pool", self.oT, l, half)
        self.merge(l, 1, self.oT, "oT")
        self.dump("mg1", self.mg, l, half)
        if half == 1 and KSTOP <= 3:
            return
        if half == 1:
            def qextra(ps, b, m, t0, w):
                pass

            def qchunk(wb, wres, c):
                b = self.bank(); ps = self.PSB[b]
                for k in range(KT):
                    P.op("pe", lambda e, k=k, ps=ps: e.matmul(ps[0:NS, 0:256], lhsT=self.hT[:, k, HALF:HALF + NS], rhs=wb[:, k, :], start=(k == 0), stop=(k == KT - 1)),
                         r=[wres, ("hT", HALF)], w=[("ps", b)])
                P.op("dve", lambda e, ps=ps, c=c: e.tensor_copy(out=self.qtok[:, c * 256:(c + 1) * 256], in_=ps[0:NS, 0:256]), r=[("ps", b)], w=["qtok"])
            qextra.chunk = qchunk
            self.proj_u(l, 1024, extra=qextra)
        else:
            self.proj_u(l, 1024)
        self.dump("q", self.uT, l, half)
        self.attn_core(l, half)
        self.dump("o_mem", self.oT, l, half)
        self.merge(l, 2, self.oT, "oT")
        self.dump("mg2", self.mg, l, half)
        if half == 1 and KSTOP <= 4:
            return
        self.out_proj(l)
        self.dump("x_mix", self.xT[:], l, half)
        if half == 1 and KSTOP <= 5:
            return
        self.ffn(l)


_CACHE = {}


def _host_consts():
    cst = np.zeros((128, 128 + 8 + 64 + 16), np.float32)
    cst[:, 0:128] = np.eye(128, dtype=np.float32)
    p = np.arange(128)
    for e2 in range(2):
        cst[:, 128 + e2] = ((p // 64) == e2)
        cst[:, 130 + e2] = (((p // 16) % 2) == e2)
    for m in range(4):
        cst[:, 132 + m] = ((p // 32) == m)
    for gi, wdw in enumerate((2, 4, 8, 16)):
        cst[:, 136 + gi * 16:136 + (gi + 1) * 16] = 1.0 / np.minimum(np.arange(16) + 1, wdw)
    cst[:, 200:202] = -cst[:, 128:130]
    sel = np.zeros((NS, NS, 128), np.float32)
    for n in range(NS):
        sel[n, n, :] = 1.0
    return cst, sel.reshape(NS, NS * 128)


def _layout_params(inp):
    f = lambda a: np.asarray(a, dtype=np.float32)
    g = np.stack([f(inp[k]) for k in ("g_mix_pre", "g_mix_post", "g_ffn_pre", "g_ffn_post", "g_mem")])
    gvec = g.reshape(5, DEPTH, KT, 128).transpose(3, 0, 1, 2).reshape(128, 5 * DEPTH * KT)
    v4 = np.stack([f(inp[k]) for k in ("ssm_d", "ssm_b_glu", "pool_scale")])
    vec4 = v4.reshape(3, DEPTH, 4, 128).transpose(3, 0, 1, 2).reshape(128, 3 * DEPTH * 4)
    lr, li, ld = f(inp["ssm_lam_re"]), f(inp["ssm_lam_im"]), f(inp["ssm_log_dt"])
    ldb = np.broadcast_to(ld[:, :, None], lr.shape)
    def ps_l(a):
        return a.reshape(DEPTH, 16, 2, 64).transpose(0, 2, 3, 1).reshape(DEPTH, 128, 16)
    lamPS = np.concatenate([ps_l(lr), ps_l(li), ps_l(ldb)], axis=2)
    br, bi = f(inp["ssm_b_re"]), f(inp["ssm_b_im"])
    cr, ci = f(inp["ssm_c_re"]), f(inp["ssm_c_im"])
    def ps_b(a):
        return a.reshape(DEPTH, 16, 2, 64, 16).transpose(0, 2, 3, 1, 4).reshape(DEPTH, 128, 256)
    def ps_c(a):
        return a.reshape(DEPTH, 16, 2, 16, 64).transpose(0, 2, 4, 1, 3).reshape(DEPTH, 128, 256)
    bcPS = np.concatenate([ps_b(br), ps_b(bi), ps_c(cr), ps_c(ci)], axis=2)
    def lb_l(a):
        t = a.reshape(DEPTH, 4, 8, 64).transpose(0, 2, 1, 3)
        t = np.broadcast_to(t[:, :, None], (DEPTH, 8, 16, 4, 64))
        return t.reshape(DEPTH, 128, 256)
    lamLB = np.concatenate([lb_l(lr), lb_l(li), lb_l(ldb)], axis=2)
    def lb_b(a):
        return a.reshape(DEPTH, 4, 8, 64, 16).transpose(0, 2, 4, 1, 3).reshape(DEPTH, 128, 256)
    bLB = np.concatenate([lb_b(br), lb_b(bi)], axis=2)
    c = np.ascontiguousarray
    return dict(gvec=c(gvec), vec4=c(vec4), sprm=c(np.concatenate([lamPS, bcPS, lamLB, bLB], axis=2)))


def _split_out(flat):
    flat = np.asarray(flat, dtype=np.float32).reshape(-1)
    return {nm: flat[off:off + int(np.prod(shp))].reshape(shp) for nm, (off, shp) in OUT_LAYOUT.items()}


def _shared_inputs(inputs):
    f = lambda a: np.ascontiguousarray(np.asarray(a, dtype=np.float32))
    cst, _ = _host_consts()
    prm = _layout_params(inputs)
    call = np.ascontiguousarray(np.concatenate([cst, prm["gvec"], prm["vec4"]], axis=1))
    return dict(w_in=f(inputs["w_in"]), w_kv=f(inputs["w_kv"]), w_glu=f(inputs["ssm_w_glu"]), pool_w=f(inputs["pool_w"]),
                w_up=f(inputs["w_branch_up"]), w_out=f(inputs["w_out"]), w_f1=f(inputs["w_ffn_in"]), w_f2=f(inputs["w_ffn_out"]),
                call=call, sprm=prm["sprm"])


def kernel(**inputs):
    f = lambda a: np.ascontiguousarray(np.asarray(a, dtype=np.float32))
    if "nc" not in _CACHE:
        _CACHE["nc"] = Builder().build()
    nc = _CACHE["nc"]
    shared = _shared_inputs(inputs)
    xp = f(inputs["x_prompt"]); xs = f(inputs["x_sample"]); mem = f(inputs["mem_prompt"])
    ck = f(inputs["cache_mem_k"]).reshape(DEPTH, 128, NMEM, 512); cv = f(inputs["cache_mem_v"]).reshape(DEPTH, 128, NMEM, 512)
    sre = f(inputs["state_ssm_re"]).reshape(DEPTH, 128, 2048); sim = f(inputs["state_ssm_im"]).reshape(DEPTH, 128, 2048)
    spool = f(inputs["state_pool"])
    in_maps = []
    for c in range(8):
        sl = slice(c * NS, (c + 1) * NS)
        m = dict(shared)
        m.update(xp=xp[c], xs=f(xs[sl, 0, :]), mem=mem[c], ck=f(ck[:, sl]), cv=f(cv[:, sl]), sre=f(sre[:, sl]), sim=f(sim[:, sl]), spool=f(spool[:, sl]))
        in_maps.append(m)
    res = run_bass_kernel_spmd(nc, in_maps, core_ids=list(range(8)))
    R = [_split_out(r["out"]) for r in res.results]
    cat = lambda k, ax: np.concatenate([np.asarray(r[k], dtype=np.float32) for r in R], axis=ax)
    st = lambda k: np.stack([np.asarray(r[k], dtype=np.float32) for r in R], axis=1)
    yp = np.stack([np.asarray(r["yp"], dtype=np.float32) for r in R], axis=0)
    ys = cat("ys", 0).reshape(128, 1, D)
    re_p = st("re_p").reshape(DEPTH, 8, 32, 64); im_p = st("im_p").reshape(DEPTH, 8, 32, 64)
    pool_p = st("pool_p")
    mk_p = st("mk_p").reshape(DEPTH, 8, NMEM, 4, 128); mv_p = st("mv_p").reshape(DEPTH, 8, NMEM, 4, 128)
    re_s = cat("re_s", 1).reshape(DEPTH, 128, 32, 64); im_s = cat("im_s", 1).reshape(DEPTH, 128, 32, 64)
    pool_s = cat("pool_s", 1)
    return (yp, ys, re_p, im_p, pool_p, mk_p, mv_p, re_s, im_s, pool_s)
```

```python
import numpy as np
import concourse.bass as bass
import concourse.mybir as mybir
from concourse.bass_utils import run_bass_kernel_spmd

F32 = mybir.dt.float32
BF16 = mybir.dt.bfloat16
ALU = mybir.AluOpType
AF = mybir.ActivationFunctionType
AX = mybir.AxisListType

ENGS = ("pe", "act", "dve", "pool", "sp")
import os as _os
SKIP = _os.environ.get("KSKIP", "").split(",")
TAPS2D = bool(int(_os.environ.get("TAPS2D", "0")))
KSS = int(_os.environ.get("KSS", "9"))
KSTOP = int(_os.environ.get("KSTOP", "99"))
FFNBAR = bool(int(_os.environ.get("FFNBAR", "0")))
DQ = _os.environ.get("DQ", "sp")

DEPTH = 4
D = 1024
KT = 8
SEQ = 2048
HALF = 1024
NS = 16
NT = HALF + NS
LCH = 8
NCH = HALF // LCH
OUT_LAYOUT = {}
_off = 0
for _nm, _shp in (("yp", (2048, 1024)), ("ys", (16, 1024)), ("re_p", (4, 2048)), ("im_p", (4, 2048)), ("pool_p", (4, 15, 512)), ("mk_p", (4, 256, 512)),
                  ("mv_p", (4, 256, 512)), ("re_s", (4, 16, 2048)), ("im_s", (4, 16, 2048)), ("pool_s", (4, 16, 15, 512))):
    OUT_LAYOUT[_nm] = (_off, _shp)
    _off += int(np.prod(_shp))
OUT_TOTAL = _off
DFF = 2816
FT = 22
NMEM = 256
EPS = 1e-6
PAST = 16384


class Prog:
    def __init__(self, nc):
        self.nc = nc
        self.ops = {e: [] for e in ENGS}
        self.count = {e: 0 for e in ENGS}
        self.waited = {e: {} for e in ENGS}
        self.last_w = {}
        self.readers = {}
        self.dma_cum = {}
        self.sems = {}
        self.final_tokens = []
        self._ctx = []
        self.barrier_tok = {e: [] for e in ENGS}

    def sb(self, name, shape, dt):
        g = self.nc.sbuf_tensor("sb_" + name, list(shape), dt)
        t = g.__enter__()
        self._ctx.append(g)
        return t

    def ps(self, name, shape, dt=F32):
        g = self.nc.psum_tensor("ps_" + name, list(shape), dt)
        t = g.__enter__()
        self._ctx.append(g)
        return t

    def _need(self, eng, tok, waits):
        if tok is None:
            return
        key, val = tok
        if key == ("eng", eng) and eng in ("pe", "sp"):
            return
        if self.waited[eng].get(key, 0) >= val:
            return
        self.waited[eng][key] = val
        waits.append((key, val))

    def _deps(self, eng, r, w):
        waits = []
        for t in self.barrier_tok[eng]:
            self._need(eng, t, waits)
        self.barrier_tok[eng] = []
        for x in r:
            self._need(eng, self.last_w.get(x), waits)
        for x in w:
            self._need(eng, self.last_w.get(x), waits)
            for t in self.readers.get(x, ()):
                self._need(eng, t, waits)
        return waits

    def _commit(self, tok, r, w):
        for x in r:
            lst = self.readers.setdefault(x, [])
            lst[:] = [t for t in lst if t[0] != tok[0]] + [tok]
        for x in w:
            self.last_w[x] = tok
            self.readers[x] = []

    def op(self, eng, fn, r=(), w=()):
        psr = [x for x in r if isinstance(x, tuple) and x[0] == "ps" and x not in w]
        if psr:
            w = list(w) + psr
        waits = self._deps(eng, r, w)
        self.count[eng] += 1
        tok = (("eng", eng), self.count[eng])
        self.ops[eng].append((waits, fn, "op", None))
        self._commit(tok, r, w)
        return tok

    def dma(self, eng, out, in_, r=(), w=(), sem="dma", final=False, **kw):
        waits = self._deps(eng, r, w)
        key = ("dma", sem)
        self.dma_cum[key] = self.dma_cum.get(key, 0) + 16
        tok = (key, self.dma_cum[key])
        self.ops[eng].append((waits, lambda e: e.dma_start(out=out, in_=in_, **kw), "dma", key))
        self._commit(tok, r, w)
        if final:
            self.final_tokens.append(tok)
        return tok

    def barrier(self):
        toks = [(("eng", o), self.count[o]) for o in ENGS if o != "sp" and self.count[o] > 0]
        toks += [(k, v) for k, v in self.dma_cum.items()]
        for e in ENGS:
            self.barrier_tok[e] = list(toks)

    def finalize(self):
        nc = self.nc
        keys = [("eng", e) for e in ENGS if e != "sp"] + list(self.dma_cum.keys())
        guards = []
        for k in keys:
            g = nc.semaphore("s_" + "_".join(str(x) for x in k))
            self.sems[k] = g.__enter__()
            guards.append(g)
        seen = {}
        for key, val in self.final_tokens:
            seen[key] = max(seen.get(key, 0), val)
        final_waits = list(seen.items())
        blk = nc.Block()
        block = blk.__enter__()

        def runner(ename):
            def run(e):
                mysem = self.sems.get(("eng", ename))
                for waits, fn, kind, key in self.ops[ename]:
                    for k, v in waits:
                        e.wait_ge(self.sems[k], v)
                    inst = fn(e)
                    if kind == "dma":
                        inst.then_inc(self.sems[key], 16)
                    elif mysem is not None:
                        inst.then_inc(mysem, 1)
                if ename == "sp":
                    for k, v in final_waits:
                        e.wait_ge(self.sems[k], v)
            return run

        block.tensor(runner("pe"))
        block.scalar(runner("act"))
        block.vector(runner("dve"))
        block.gpsimd(runner("pool"))
        block.sync(runner("sp"))
        blk.__exit__(None, None, None)
        for g in reversed(guards):
            g.__exit__(None, None, None)
        for g in reversed(self._ctx):
            g.__exit__(None, None, None)


class DummyProg:
    def op(self, *a, **k):
        return None

    def dma(self, *a, **k):
        return None

    def barrier(self):
        pass


class Builder:
    def __init__(self, nlayers=DEPTH, nhalves=2, dbg=None):
        self.nl = nlayers
        self.nh = nhalves
        self.dbg = dbg
        nc = bass.Bass("TRN2", target_bir_lowering=False)
        self.nc = nc
        self.P = Prog(nc)
        self.din = {}
        self.dout = {}
        self.nsem = 0
        self._q = None
        self._xr = ()
        self._xw = ()

    def inp(self, name, shape):
        self.din[name] = self.nc.dram_tensor(name, list(shape), F32, kind="ExternalInput").ap()
        return self.din[name]

    def outp(self, name, shape):
        self.dout[name] = self.nc.dram_tensor(name, list(shape), F32, kind="ExternalOutput").ap()
        return self.dout[name]

    def build(self):
        P = self.P
        nc = self.nc
        i = self.inp
        xp = i("xp", [SEQ, D]); xs = i("xs", [NS, D]); mem = i("mem", [NMEM, D])
        ck = i("ck", [DEPTH, NS, NMEM, 512]); cv = i("cv", [DEPTH, NS, NMEM, 512])
        sre = i("sre", [DEPTH, NS, 2048]); sim = i("sim", [DEPTH, NS, 2048])
        spool = i("spool", [DEPTH, NS, 15, 512])
        self.w_in = i("w_in", [DEPTH, D, 4608]); self.w_kv = i("w_kv", [DEPTH, D, 1024])
        self.w_glu = i("w_glu", [DEPTH, 512, 512]); self.pool_w = i("pool_w", [DEPTH, 4, 128, 128])
        self.w_up = i("w_up", [DEPTH, 3, 512, D]); self.w_out = i("w_out", [DEPTH, D, D])
        self.w_f1 = i("w_f1", [DEPTH, D, 2 * DFF]); self.w_f2 = i("w_f2", [DEPTH, DFF, D])
        call_d = i("call", [128, 216 + 160 + 48])
        cst_d = call_d[:, 0:216]; gvec_d = call_d[:, 216:376]; vec4_d = call_d[:, 376:424]
        sprm_d = i("sprm", [DEPTH, 128, 2352])
        out_d = self.outp("out", [OUT_TOTAL])
        ov = {}
        for nm, (off, shp) in OUT_LAYOUT.items():
            n = int(np.prod(shp))
            v = out_d[off:off + n]
            if len(shp) == 2:
                v = v.rearrange("(a b) -> a b", b=shp[1])
            elif len(shp) == 3:
                v = v.rearrange("(a b c) -> a b c", b=shp[1], c=shp[2])
            elif len(shp) == 4:
                v = v.rearrange("(a b c d) -> a b c d", b=shp[1], c=shp[2], d=shp[3])
            ov[nm] = v
        yp, ys, re_p, im_p, pool_p, mk_p, mv_p, re_s, im_s, pool_s = [ov[k] for k in ("yp", "ys", "re_p", "im_p", "pool_p", "mk_p", "mv_p", "re_s", "im_s", "pool_s")]
        self.o = dict(yp=yp, ys=ys, re_p=re_p, im_p=im_p, pool_p=pool_p, mk_p=mk_p, mv_p=mv_p,
                      re_s=re_s, im_s=im_s, pool_s=pool_s)
        self.i = dict(xp=xp, xs=xs, mem=mem, ck=ck, cv=cv, sre=sre, sim=sim, spool=spool,
                      sprm=sprm_d)

        self.xT = P.sb("xT", [128, KT, NT], F32)
        self.hT = P.sb("hT", [128, KT, NT], BF16)
        self.A = P.sb("arena", [128, FT * NT], BF16)
        A = self.A
        self.aT = A[:, 0:FT * NT].rearrange("p (k n) -> p k n", n=NT)
        self.mg = A[:, 0:8 * NT].rearrange("p (k n) -> p k n", n=NT)
        self.uT = A[:, 8 * NT:12 * NT].rearrange("p (k n) -> p k n", n=NT)
        self.oT = A[:, 12 * NT:16 * NT].rearrange("p (k n) -> p k n", n=NT)
        self.Hbf = A[:, 16 * NT:16 * NT + 16 * 2 * (NCH + 1)].rearrange("p (a r c) -> p a r c", a=16, r=2)
        self.Wsb = A[:, 0:8192].rearrange("p (g k r c) -> p g k r c", g=4, k=8, r=2)
        self.Tc = A[:, 12 * NT:16 * NT].bitcast(F32)[:, 0:2048].rearrange("p (a c) -> p a c", c=NCH)
        self.Ts = A[:, 16 * NT:16 * NT + 4128].bitcast(F32)[:, 0:2048].rearrange("p (a c) -> p a c", c=NCH)
        self.Hst = P.sb("Hst", [128, 16, 2, NCH + 2], F32)
        self.Wco = P.sb("Wco", [128, 16, 9, 2, 32], BF16)
        self.Wtp = P.sb("Wtp", [128, 4, 8, 128], BF16)
        self.zb = P.sb("zb", [128, KT, 512], F32)
        self.stg = [P.sb("stg%d" % j, [128, 2048], F32) for j in range(2)]
        self.wbf = [P.sb("wbf%d" % j, [128, 2048], BF16) for j in range(3)]
        self.gvec = P.sb("gvec", [128, 5, DEPTH, KT], F32)
        self.vec4 = P.sb("vec4", [128, 3, DEPTH, 4], F32)
        self.cst = P.sb("cst", [128, 128 + 8 + 64 + 16], F32)
        self.identb = P.sb("identb", [128, 128], BF16)
        self.onesb = P.sb("onesb", [128, 128], BF16)
        self.onesf = P.sb("onesf", [128, 128], F32)
        self.rstd = P.sb("rstd", [128, 512], F32)
        self.sq = [P.sb("sq%d" % j, [128, 512], BF16) for j in range(2)]
        self.tmpA = [P.sb("tmpA%d" % j, [128, 512], F32) for j in range(2)]
        self.tmpB = [P.sb("tmpB%d" % j, [128, 512], F32) for j in range(2)]
        self.carry = P.sb("carry", [128, DEPTH, 16, 2], F32)
        self.phist = P.sb("phist", [128, DEPTH, 4, 16], F32)
        wtf = self.Wtp[:].rearrange("p g k c -> p (g k c)")
        self.kT = wtf[:, 0:1024].rearrange("p (h m) -> p h m", m=NMEM)
        self.vv = wtf[:, 1024:2048].rearrange("p (t c) -> p t c", c=512)
        self.qtok = wtf[0:NS, 2048:3072].bitcast(F32)
        self.memn = P.sb("memn", [128, KT, NMEM], BF16)
        self.memh = A[:, 16 * NT + 4128:16 * NT + 4128 + 2048].rearrange("p (k m) -> p k m", m=NMEM)
        self.epsb = P.sb("epsb", [128, 2], F32)
        self.lastu = P.sb("lastu", [128, 4, 16], F32)
        self.usam = P.sb("usam", [128, 4, 16], F32)
        self.small = P.sb("small", [128, 768], F32)
        self.sA = P.sb("sA", [128, 512], F32)
        self.sB = P.sb("sB", [128, 512], F32)
        self.a1rho = P.sb("a1rho", [128, 48], F32)
        self.rho = self.a1rho[:, 32:48]
        self.A8 = P.sb("A8", [128, 2, 16], F32)
        self.A1 = self.a1rho[:, 0:32].rearrange("p (r a) -> p r a", r=2)
        self.pw = self.zb[:].rearrange("p k c -> p (k c)")[:, 1024:1024 + 2080]
        self.prm = self.Hst[:].rearrange("p a r c -> p (a r c)")[:, 0:2400]
        self.PSB = [P.ps("psb%d" % j, [128, 512]) for j in range(8)]
        self.ident = self.cst[:, 0:128]
        dt_ = lambda nm, shp, dt: self.nc.dram_tensor(nm, shp, dt, kind="Internal").ap()
        self.scr = dict(sb=dt_("scr_sb", [DEPTH, 128, 8192], BF16), co=dt_("scr_co", [DEPTH, 128, 9216], BF16),
                        tp=dt_("scr_tp", [DEPTH, 128, 4096], BF16), t=dt_("scr_t", [DEPTH, 128, 4144], F32),
                        kv=dt_("scr_kv", [DEPTH, 128, 2048], BF16))
        self.dd = dict(cst_d=cst_d, gvec_d=gvec_d, vec4_d=vec4_d)
        realP = self.P
        self.P = DummyProg()
        self.specs = None
        self.rec = []
        self.emit()
        self.specs = self.rec
        self.P = realP
        self.emit()
        self.P.finalize()
        return nc

    def emit(self):
        P = self.P
        self.rot = 0
        self.cnt = 0
        self.stg_i = 0
        self.wbf_i = 0
        self.wl_i = 0
        self.wl_issued = 0
        ident = self.ident
        cst_d, gvec_d, vec4_d = self.dd["cst_d"], self.dd["gvec_d"], self.dd["vec4_d"]
        P.dma("sp", self.cst[:], cst_d, w=["cst"], sem="c0")
        P.dma("sp", self.gvec[:].rearrange("p a l k -> p (a l k)"), gvec_d, w=["gvec"], sem="c1")
        P.dma("sp", self.vec4[:].rearrange("p a l k -> p (a l k)"), vec4_d, w=["vec4"], sem="c2")
        P.op("dve", lambda e: e.tensor_copy(out=self.identb[:], in_=ident), r=["cst"], w=["identb"])
        P.op("pool", lambda e: e.memset(self.onesb[:], 1.0), w=["onesb"])
        P.op("pool", lambda e: e.memset(self.onesf[:], 1.0), w=["onesf"])
        P.op("pool", lambda e: e.memset(self.epsb[:, 0:1], EPS), w=["epsb"])
        P.op("pool", lambda e: e.memset(self.epsb[:, 1:2], float(np.pi / 2)), w=["epsb"])
        P.op("pool", lambda e: e.memset(self.carry[:], 0.0), w=["carry"])
        P.op("pool", lambda e: e.memset(self.phist[:], 0.0), w=["phist"])

        self.prep_mem()
        P.barrier()
        kvst = self.A[:, 8 * NT:8 * NT + 2048]
        save_kv = (self.kT, self.vv)
        self.kT = kvst[:, 0:1024].rearrange("p (h m) -> p h m", m=NMEM)
        self.vv = kvst[:, 1024:2048].rearrange("p (t c) -> p t c", c=512)
        for l in range(self.nl):
            self.kv(l, 0)
            P.dma("sp", self.scr["kv"][l], kvst, r=["kT", "vv"], sem="kst")
            self.ssm_prep(l)
        self.kT, self.vv = save_kv
        P.barrier()
        P.op("dve", lambda e: e.memset(self.small[:, 760:761], 0.0), w=["scr"])
        for half in range(self.nh):
            self.half = half
            self.ntok = HALF + (NS if half == 1 else 0)
            self.tbs = [(0, 512), (512, 512)] + ([(HALF, NS)] if half == 1 else [])
            self.load_x(half)
            P.barrier()
            for l in range(self.nl):
                self.layer(l, half)
            P.barrier()
            if not (half == 1 and KSTOP <= 6):
                self.store_y(half)

    def bank(self):
        b = self.rot
        self.rot = (self.rot + 1) % 6
        return b

    def uid(self):
        self.cnt += 1
        return self.cnt

    def cast_eng(self):
        return "pool"

    def _issue(self, idx):
        P = self.P
        pieces, kt, ncols = self.specs[idx]
        si = idx % 2
        bi = idx % 3
        st = self.stg[si][:, 0:kt * ncols].rearrange("p (k c) -> p k c", c=ncols)
        wb = self.wbf[bi][:, 0:kt * ncols].rearrange("p (k c) -> p k c", c=ncols)
        c0 = 0
        for ap in pieces:
            c = ap.shape[-1]
            P.dma("sp", st[:, :, c0:c0 + c], ap.rearrange("(k p) c -> p k c", p=128), w=[("stg", si)], sem="stg%d" % si)
            c0 += c
        if idx % 2 == 0:
            P.op("act", lambda e: e.activation(out=wb, in_=st, func=AF.Copy), r=[("stg", si)], w=[("wbf", bi)])
        else:
            P.op("dve", lambda e: e.tensor_copy(out=wb, in_=st), r=[("stg", si)], w=[("wbf", bi)])

    def wload(self, pieces, kt, ncols):
        idx = self.wl_i
        self.wl_i += 1
        bi = idx % 3
        wb = self.wbf[bi][:, 0:kt * ncols].rearrange("p (k c) -> p k c", c=ncols)
        if self.specs is None:
            self.rec.append((pieces, kt, ncols))
            return wb, ("wbf", bi)
        while self.wl_issued <= min(idx + 1, len(self.specs) - 1):
            self._issue(self.wl_issued)
            self.wl_issued += 1
        return wb, ("wbf", bi)

    def norm_rstd(self, src_fn, src_res, ntiles, w, tag):
        P = self.P
        pb = 6 + (self.uid() % 2)
        ps = self.PSB[pb]
        for k in range(ntiles):
            sq = self.sq[k % 2]
            P.op("act", lambda e, k=k, sq=sq: e.activation(out=sq[:, :w], in_=src_fn(k), func=AF.Square),
                 r=[src_res(k)], w=[("sq", k % 2)])
            P.op("pe", lambda e, k=k, sq=sq: e.matmul(ps[:, :w], lhsT=self.onesb[:], rhs=sq[:, :w], start=(k == 0), stop=(k == ntiles - 1)),
                 r=[("sq", k % 2), "onesb"], w=[("ps", pb)])
        P.op("act", lambda e: e.activation(out=self.rstd[:, :w], in_=ps[:, :w], func=AF.Sqrt, scale=1.0 / D, bias=self.epsb[:, 0:1]),
             r=[("ps", pb), "epsb"], w=["rstd"])
        P.op("dve", lambda e: e.reciprocal(out=self.rstd[:, :w], in_=self.rstd[:, :w]), r=["rstd"], w=["rstd"])

    def dump(self, name, ap, l=0, half=0):
        if not self.dbg or l != 0 or half != 0 or isinstance(self.P, DummyProg):
            return
        shp = list(ap.shape)
        d = self.nc.dram_tensor("dbg_" + name, shp, ap.dtype, kind="ExternalOutput").ap()
        self.P.barrier()
        self.P.dma("sp", d, ap, sem="dbg_" + name, final=True)
        self.P.barrier()

    def load_x(self, half):
        P = self.P
        xp = self.i["xp"]
        zb4 = self.zb[:].rearrange("p k c -> p (k c)").rearrange("p (j c) -> p j c", c=D)
        for blk in range(2):
            t0 = blk * 512
            r0 = half * HALF + t0
            P.dma("sp", zb4, xp[r0:r0 + 512, :].rearrange("(j p) c -> p j c", p=128), w=["zb"], sem="zb")
            for k in range(KT):
                b = self.bank(); ps = self.PSB[b]
                for j in range(4):
                    P.op("pe", lambda e, j=j, k=k, ps=ps: e.transpose(ps[:, j * 128:(j + 1) * 128], zb4[:, j, k * 128:(k + 1) * 128], self.ident),
                         r=["zb", "cst"], w=[("ps", b)])
                eng = "act" if k % 2 else "dve"
                if eng == "act":
                    P.op("act", lambda e, k=k, ps=ps, t0=t0: e.activation(out=self.xT[:, k, t0:t0 + 512], in_=ps[:, :], func=AF.Copy), r=[("ps", b)], w=[("xT", t0)])
                else:
                    P.op("dve", lambda e, k=k, ps=ps, t0=t0: e.tensor_copy(out=self.xT[:, k, t0:t0 + 512], in_=ps[:, :]), r=[("ps", b)], w=[("xT", t0)])
        if half == 1:
            xs = self.i["xs"]
            P.dma("sp", zb4[0:NS, 0, :], xs, w=["zb"], sem="zb")
            b = self.bank(); ps = self.PSB[b]
            for k in range(KT):
                P.op("pe", lambda e, k=k, ps=ps: e.transpose(ps[:, k * NS:(k + 1) * NS], zb4[0:NS, 0, k * 128:(k + 1) * 128], self.ident[0:NS, 0:NS]),
                     r=["zb", "cst"], w=[("ps", b)])
            P.op("dve", lambda e, ps=ps: e.tensor_copy(out=self.xT[:, :, HALF:HALF + NS], in_=ps[:, 0:KT * NS].rearrange("p (k n) -> p k n", n=NS)),
                 r=[("ps", b)], w=[("xT", HALF)])

    def store_y(self, half):
        P = self.P
        yp = self.o["yp"]
        zb4 = self.zb[:].rearrange("p k c -> p (k c)").rearrange("p (j c) -> p j c", c=D)
        for blk in range(2):
            t0 = blk * 512
            r0 = half * HALF + t0
            for j in range(4):
                for kk in range(2):
                    b = self.bank(); ps = self.PSB[b]
                    for k4 in range(4):
                        k = kk * 4 + k4
                        P.op("pe", lambda e, j=j, k=k, k4=k4, ps=ps, t0=t0: e.transpose(ps[:, k4 * 128:(k4 + 1) * 128], self.xT[:, k, t0 + j * 128:t0 + (j + 1) * 128], self.ident),
                             r=[("xT", t0), "cst"], w=[("ps", b)])
                    if kk:
                        P.op("act", lambda e, j=j, kk=kk, ps=ps: e.activation(out=zb4[:, j, kk * 512:(kk + 1) * 512], in_=ps[:, :], func=AF.Copy), r=[("ps", b)], w=["zb"])
                    else:
                        P.op("dve", lambda e, j=j, kk=kk, ps=ps: e.tensor_copy(out=zb4[:, j, kk * 512:(kk + 1) * 512], in_=ps[:, :]), r=[("ps", b)], w=["zb"])
            P.dma("sp", yp[r0:r0 + 512, :].rearrange("(j p) c -> p j c", p=128), zb4, r=["zb"], sem="yout", final=True)
        if half == 1:
            ys = self.o["ys"]
            for kk in range(2):
                b = self.bank(); ps = self.PSB[b]
                for k4 in range(4):
                    k = kk * 4 + k4
                    P.op("pe", lambda e, k=k, k4=k4, ps=ps: e.transpose(ps[0:NS, k4 * 128:(k4 + 1) * 128], self.xT[:, k, HALF:HALF + NS], self.ident),
                         r=[("xT", HALF), "cst"], w=[("ps", b)])
                P.op("dve", lambda e, kk=kk, ps=ps: e.tensor_copy(out=zb4[0:NS, 0, kk * 512:(kk + 1) * 512], in_=ps[0:NS, :]), r=[("ps", b)], w=["zb"])
            P.dma("sp", ys, zb4[0:NS, 0, :], r=["zb"], sem="yout", final=True)

    def prep_mem(self):
        P = self.P
        mem = self.i["mem"]
        zb2 = self.zb[:].rearrange("p k c -> p (k c)").rearrange("p (j c) -> p j c", c=D)
        P.dma("sp", zb2[:, 0:2, :], mem.rearrange("(j p) c -> p j c", p=128), w=["zb"], sem="zb")
        ss = self.small[:, 0:2]
        for j in range(2):
            P.op("act", lambda e, j=j: e.activation(out=zb2[:, 2 + j, :], in_=zb2[:, j, :], func=AF.Square, accum_out=ss[:, j:j + 1]), r=["zb"], w=["zb", "small"])
        P.op("act", lambda e: e.activation(out=ss, in_=ss, func=AF.Sqrt, scale=1.0 / D, bias=self.epsb[:, 0:1]), r=["small", "epsb"], w=["small"])
        P.op("dve", lambda e: e.reciprocal(out=ss, in_=ss), r=["small"], w=["small"])
        for j in range(2):
            P.op("dve", lambda e, j=j: e.tensor_scalar(out=zb2[:, j, :], in0=zb2[:, j, :], scalar1=ss[:, j:j + 1], scalar2=None, op0=ALU.mult), r=["zb", "small"], w=["zb"])
        for k in range(KT):
            b = self.bank(); ps = self.PSB[b]
            for j in range(2):
                P.op("pe", lambda e, j=j, k=k, ps=ps: e.transpose(ps[:, j * 128:(j + 1) * 128], zb2[:, j, k * 128:(k + 1) * 128], self.ident), r=["zb", "cst"], w=[("ps", b)])
            P.op("dve", lambda e, k=k, ps=ps: e.tensor_copy(out=self.memn[:, k, :], in_=ps[:, 0:NMEM]), r=[("ps", b)], w=["memn"])

    def kv(self, l, half):
        P = self.P
        kT_, vv_ = self.kT, self.vv
        for k in range(KT):
            P.op("dve", lambda e, k=k: e.tensor_scalar(out=self.memh[:, k, :], in0=self.memn[:, k, :], scalar1=self.gvec[:, 4, l, k:k + 1], scalar2=None, op0=ALU.mult),
                 r=["memn", "gvec"], w=["memh"])
        for c in range(4):
            wb, wres = self.wload([self.w_kv[l][:, c * 256:(c + 1) * 256]], KT, 256)
            if c < 2:
                for hh in range(2):
                    b = self.bank(); ps = self.PSB[b]
                    for k in range(KT):
                        P.op("pe", lambda e, k=k, hh=hh, ps=ps, wb=wb: e.matmul(ps[:, 0:NMEM], lhsT=wb[:, k, hh * 128:(hh + 1) * 128], rhs=self.memh[:, k, :], start=(k == 0), stop=(k == KT - 1)),
                             r=[wres, "memh"], w=[("ps", b)])
                    P.op("act", lambda e, hh=hh, ps=ps, c=c: e.activation(out=kT_[:, 2 * c + hh, :], in_=ps[:, 0:NMEM], func=AF.Copy), r=[("ps", b)], w=["kT"])
            if c >= 2 or half == 0:
                for mt in range(2):
                    b = self.bank(); ps = self.PSB[b]
                    for k in range(KT):
                        P.op("pe", lambda e, k=k, mt=mt, ps=ps, wb=wb: e.matmul(ps[:, 0:256], lhsT=self.memh[:, k, mt * 128:(mt + 1) * 128], rhs=wb[:, k, :], start=(k == 0), stop=(k == KT - 1)),
                             r=[wres, "memh"], w=[("ps", b)])
                    if c >= 2:
                        P.op("act", lambda e, mt=mt, ps=ps, c=c: e.activation(out=vv_[:, mt, (c - 2) * 256:(c - 1) * 256], in_=ps[:, 0:256], func=AF.Copy), r=[("ps", b)], w=["vv"])
                    if half == 0:
                        t = self.tmpA[mt]
                        P.op("dve", lambda e, ps=ps, t=t: e.tensor_copy(out=t[:, 0:256], in_=ps[:, 0:256]), r=[("ps", b)], w=[("tmpA", mt)])
                        dst = self.o["mk_p"] if c < 2 else self.o["mv_p"]
                        cc = c % 2
                        P.dma("sp", dst[l][mt * 128:(mt + 1) * 128, cc * 256:(cc + 1) * 256], t[:, 0:256], r=[("tmpA", mt)], sem="kvout%d" % mt, final=True)

    def norm_to_h(self, l, kind):
        P = self.P
        for (t0, w) in self.tbs:
            self.norm_rstd(lambda k, t0=t0, w=w: self.xT[:, k, t0:t0 + w], lambda k, t0=t0: ("xT", t0), KT, w, "n")
            for k in range(KT):
                eng = "dve"
                if eng == "dve":
                    P.op("dve", lambda e, k=k, t0=t0, w=w: e.scalar_tensor_tensor(out=self.hT[:, k, t0:t0 + w], in0=self.xT[:, k, t0:t0 + w], scalar=self.gvec[:, kind, l, k:k + 1],
                                                                                 in1=self.rstd[:, :w], op0=ALU.mult, op1=ALU.mult),
                         r=[("xT", t0), "rstd", "gvec"], w=[("hT", t0)])
                else:
                    P.op("pool", lambda e, k=k, t0=t0, w=w: e.scalar_tensor_tensor(out=self.hT[:, k, t0:t0 + w], in0=self.xT[:, k, t0:t0 + w], scalar=self.gvec[:, kind, l, k:k + 1],
                                                                                  in1=self.rstd[:, :w], op0=ALU.mult, op1=ALU.mult),
                         r=[("xT", t0), "rstd", "gvec"], w=[("hT", t0)])

    def mm_fm(self, wb, wres, kt, nm, src, src_name, evac, m0=0):
        P = self.P
        for mi in range(nm):
            for (t0, w) in self.tbs:
                b = self.bank(); ps = self.PSB[b]
                for k in range(kt):
                    P.op("pe", lambda e, k=k, mi=mi, ps=ps, t0=t0, w=w: e.matmul(ps[:, :w], lhsT=wb[:, k, mi * 128:(mi + 1) * 128], rhs=src[:, k, t0:t0 + w],
                                                                                start=(k == 0), stop=(k == kt - 1)),
                         r=[wres, (src_name, t0)], w=[("ps", b)])
                evac(ps, b, m0 + mi, t0, w)

    def proj_u(self, l, col0, extra=None):
        P = self.P
        for c in range(2):
            wb, wres = self.wload([self.w_in[l][:, col0 + c * 256: col0 + (c + 1) * 256]], KT, 256)

            def evac(ps, b, m, t0, w):
                if (m + (t0 // 512)) % 2 == 0:
                    P.op("act", lambda e: e.activation(out=self.uT[:, m, t0:t0 + w], in_=ps[:, :w], func=AF.Copy), r=[("ps", b)], w=[("uT", t0)])
                else:
                    P.op("dve", lambda e: e.tensor_copy(out=self.uT[:, m, t0:t0 + w], in_=ps[:, :w]), r=[("ps", b)], w=[("uT", t0)])
                if extra is not None:
                    extra(ps, b, m, t0, w)
            self.mm_fm(wb, wres, KT, 2, self.hT, "hT", evac, m0=2 * c)
            if extra is not None and hasattr(extra, "chunk"):
                extra.chunk(wb, wres, c)

    def merge(self, l, br, src, src_name):
        P = self.P
        for c2 in range(2):
            for cg in range(2):
                ucol = c2 * 512 + cg * 256
                wu, wures = self.wload([self.w_up[l, br][:, ucol:ucol + 256]], 4, 256)
                gcol = 1536 + br * 1024 + ucol
                wg, wgres = self.wload([self.w_in[l][:, gcol:gcol + 256]], KT, 256)
                for mi in range(2):
                    m = c2 * 4 + cg * 2 + mi
                    for (t0, w) in self.tbs:
                        bg = self.bank(); pg = self.PSB[bg]
                        for k in range(KT):
                            P.op("pe", lambda e, k=k, mi=mi, pg=pg, t0=t0, w=w, wg=wg: e.matmul(pg[:, :w], lhsT=wg[:, k, mi * 128:(mi + 1) * 128], rhs=self.hT[:, k, t0:t0 + w],
                                                                                                start=(k == 0), stop=(k == KT - 1)),
                                 r=[wgres, ("hT", t0)], w=[("ps", bg)])
                        si = self.uid() % 2
                        sg = self.tmpB[si]
                        P.op("act", lambda e, pg=pg, sg=sg, w=w: e.activation(out=sg[:, :w], in_=pg[:, :w], func=AF.Sigmoid), r=[("ps", bg)], w=[("tmpB", si)])
                        bu = self.bank(); pu = self.PSB[bu]
                        mu = mi
                        for k in range(4):
                            P.op("pe", lambda e, k=k, mu=mu, pu=pu, t0=t0, w=w, wu=wu: e.matmul(pu[:, :w], lhsT=wu[:, k, mu * 128:(mu + 1) * 128], rhs=src[:, k, t0:t0 + w],
                                                                                                start=(k == 0), stop=(k == 3)),
                                 r=[wures, (src_name, t0)], w=[("ps", bu)])
                        if br == 0:
                            P.op("dve", lambda e, pu=pu, sg=sg, m=m, t0=t0, w=w: e.tensor_tensor(out=self.mg[:, m, t0:t0 + w], in0=pu[:, :w], in1=sg[:, :w], op=ALU.mult),
                                 r=[("ps", bu), ("tmpB", si)], w=[("mg", t0)])
                        else:
                            P.op("dve", lambda e, pu=pu, sg=sg, w=w: e.tensor_tensor(out=sg[:, :w], in0=pu[:, :w], in1=sg[:, :w], op=ALU.mult),
                                 r=[("ps", bu), ("tmpB", si)], w=[("tmpB", si)])
                            P.op("pool", lambda e, sg=sg, m=m, t0=t0, w=w: e.tensor_tensor(out=self.mg[:, m, t0:t0 + w], in0=self.mg[:, m, t0:t0 + w], in1=sg[:, :w], op=ALU.add),
                                 r=[("tmpB", si), ("mg", t0)], w=[("mg", t0)])

    def post_norm_add(self, l, kind, zT, zname):
        P = self.P
        for (t0, w) in self.tbs:
            self.norm_rstd(lambda k, t0=t0, w=w: zT[:, k, t0:t0 + w], lambda k, t0=t0: (zname, t0), KT, w, "p")
            for k in range(KT):
                si = self.uid() % 2
                t = self.tmpA[si]
                P.op("dve", lambda e, k=k, t=t, t0=t0, w=w: e.scalar_tensor_tensor(out=t[:, :w], in0=zT[:, k, t0:t0 + w], scalar=self.gvec[:, kind, l, k:k + 1], in1=self.rstd[:, :w],
                                                                                  op0=ALU.mult, op1=ALU.mult),
                     r=[(zname, t0), "rstd", "gvec"], w=[("tmpA", si)])
                P.op("pool", lambda e, k=k, t=t, t0=t0, w=w: e.tensor_tensor(out=self.xT[:, k, t0:t0 + w], in0=self.xT[:, k, t0:t0 + w], in1=t[:, :w], op=ALU.add),
                     r=[("tmpA", si), ("xT", t0)], w=[("xT", t0)])

    def out_proj(self, l):
        P = self.P
        zT = self.A[:, 8 * NT:16 * NT].rearrange("p (k n) -> p k n", n=NT)
        for c in range(4):
            wb, wres = self.wload([self.w_out[l][:, c * 256:(c + 1) * 256]], KT, 256)

            def evac(ps, b, m, t0, w):
                if (m + t0 // 512) % 2 == 0:
                    P.op("act", lambda e: e.activation(out=zT[:, m, t0:t0 + w], in_=ps[:, :w], func=AF.Copy), r=[("ps", b)], w=[("zT", t0), ("uT", t0), ("oT", t0)])
                else:
                    P.op("dve", lambda e: e.tensor_copy(out=zT[:, m, t0:t0 + w], in_=ps[:, :w]), r=[("ps", b)], w=[("zT", t0), ("uT", t0), ("oT", t0)])
            self.mm_fm(wb, wres, KT, 2, self.mg, "mg", evac, m0=2 * c)
        self.post_norm_add(l, 1, zT, "zT")

    def ffn(self, l):
        P = self.P
        self.norm_to_h(l, 2)
        for j in range(FT):
            wb, wres = self.wload([self.w_f1[l][:, j * 128:(j + 1) * 128], self.w_f1[l][:, DFF + j * 128:DFF + (j + 1) * 128]], KT, 256)
            for (t0, w) in self.tbs:
                ba = self.bank(); pa = self.PSB[ba]
                bb = self.bank(); pb = self.PSB[bb]
                for mi, (bx, px) in enumerate(((ba, pa), (bb, pb))):
                    for k in range(KT):
                        P.op("pe", lambda e, k=k, mi=mi, px=px, t0=t0, w=w, wb=wb: e.matmul(px[:, :w], lhsT=wb[:, k, mi * 128:(mi + 1) * 128], rhs=self.hT[:, k, t0:t0 + w],
                                                                                            start=(k == 0), stop=(k == KT - 1)),
                             r=[wres, ("hT", t0)], w=[("ps", bx)])
                si = self.uid() % 2
                sg = self.tmpB[si]
                P.op("act", lambda e, pa=pa, sg=sg, w=w: e.activation(out=sg[:, :w], in_=pa[:, :w], func=AF.Silu), r=[("ps", ba)], w=[("tmpB", si)])
                if j < 8:
                    al = [("mg", t0)]
                elif j < 16:
                    al = [("zT", t0), ("uT", t0), ("oT", t0)]
                else:
                    al = [("Hbf", 0), ("Hbf", 1), ("Hbf", 2), ("Hbf", 3), "memh", "ssmw"]
                P.op("dve", lambda e, pb=pb, sg=sg, j=j, t0=t0, w=w: e.tensor_tensor(out=self.aT[:, j, t0:t0 + w], in0=pb[:, :w], in1=sg[:, :w], op=ALU.mult),
                     r=[("ps", bb), ("tmpB", si)], w=[("aT", t0)] + al)
        zT = self.Hst[:].rearrange("p a r c -> p (a r c)").bitcast(BF16).rearrange("p (k n) -> p k n", n=NT)
        for m in range(KT):
            banks = [self.bank() for _ in self.tbs]
            for kh in range(2):
                wb, wres = self.wload([self.w_f2[l][kh * 1408:(kh + 1) * 1408, m * 128:(m + 1) * 128]], 11, 128)
                for ti, (t0, w) in enumerate(self.tbs):
                    b = banks[ti]; ps = self.PSB[b]
                    for k in range(11):
                        kk = kh * 11 + k
                        P.op("pe", lambda e, k=k, kk=kk, ps=ps, t0=t0, w=w, wb=wb: e.matmul(ps[:, :w], lhsT=wb[:, k, :], rhs=self.aT[:, kk, t0:t0 + w],
                                                                                            start=(kk == 0), stop=(kk == FT - 1)),
                             r=[wres, ("aT", t0)], w=[("ps", b)])
            for ti, (t0, w) in enumerate(self.tbs):
                b = banks[ti]; ps = self.PSB[b]
                if ti % 2 == 0:
                    P.op("act", lambda e, ps=ps, m=m, t0=t0, w=w: e.activation(out=zT[:, m, t0:t0 + w], in_=ps[:, :w], func=AF.Copy), r=[("ps", b)], w=[("zF", t0), "carry"] + [("Hst", g) for g in range(4)])
                else:
                    P.op("dve", lambda e, ps=ps, m=m, t0=t0, w=w: e.tensor_copy(out=zT[:, m, t0:t0 + w], in_=ps[:, :w]), r=[("ps", b)], w=[("zF", t0), "carry"] + [("Hst", g) for g in range(4)])
        self.post_norm_add(l, 3, zT, "zF")
        if FFNBAR:
            P.barrier()

    def _emit(self, eng, fn, r, w):
        r = list(r) + list(self._xr)
        w = list(w) + list(self._xw)
        if self._q is not None:
            self._q.append((eng, fn, r, w))
        else:
            self.P.op(eng, fn, r=r, w=w)

    def _flush(self, qa, qb):
        i = j = 0
        while i < len(qa) or j < len(qb):
            if i < len(qa):
                eng, fn, r, w = qa[i]; self.P.op(eng, fn, r=r, w=w); i += 1
            if j < len(qb):
                eng, fn, r, w = qb[j]; self.P.op(eng, fn, r=r, w=w); j += 1

    def _tt(self, out, a, b, op):
        ch = self._ch
        self._emit(self._eng, lambda e: e.tensor_tensor(out=out, in0=a, in1=b, op=op), [ch], [ch])

    def _ts(self, out, a, s1, op0, s2=None, op1=None, extra_r=()):
        ch = self._ch
        if op1 is None:
            self._emit(self._eng, lambda e: e.tensor_scalar(out=out, in0=a, scalar1=s1, scalar2=None, op0=op0), [ch] + list(extra_r), [ch])
        else:
            self._emit(self._eng, lambda e: e.tensor_scalar(out=out, in0=a, scalar1=s1, scalar2=s2, op0=op0, op1=op1), [ch] + list(extra_r), [ch])

    def _mask(self, out, a, mcol):
        ch = self._ch
        if self._eng == "pool":
            shp = list(out.shape)
            mc = mcol
            for d in range(2, len(shp)):
                mc = mc.unsqueeze(d)
            mb = mc.to_broadcast(shp)
            self._emit("pool", lambda e: e.tensor_tensor(out=out, in0=a, in1=mb, op=ALU.mult), [ch, "cst"], [ch])
        else:
            self._emit(self._eng, lambda e: e.tensor_scalar(out=out, in0=a, scalar1=mcol, scalar2=None, op0=ALU.mult), [ch, "cst"], [ch])

    def _cp(self, out, a):
        ch = self._ch
        self._emit(self._eng, lambda e: e.tensor_copy(out=out, in_=a), [ch], [ch])

    def _ms(self, out, val):
        ch = self._ch
        self._emit(self._eng, lambda e: e.memset(out, val), [ch], [ch])

    def _act(self, out, a, func, scale=1.0, bias=None):
        ch = self._ch
        if bias is None:
            self._emit("act", lambda e: e.activation(out=out, in_=a, func=func, scale=scale), [ch], [ch])
        else:
            self._emit("act", lambda e: e.activation(out=out, in_=a, func=func, scale=scale, bias=bias), [ch, "epsb"], [ch])

    def _rcp(self, out, a):
        ch = self._ch
        self._emit("dve", lambda e: e.reciprocal(out=out, in_=a), [ch], [ch])

    def _disc(self, F, lr, li, ldt, S, pre_rden=None):
        tt, ts, act = self._tt, self._ts, self._act
        s0, s1, s2, s3, s4, s5, s6 = S[:7]
        if pre_rden is not None:
            ch = self._ch
            self.P.op("dve", lambda e: e.tensor_tensor(out=pre_rden, in0=lr, in1=lr, op=ALU.mult), r=[ch], w=[ch])
            self.P.op("dve", lambda e: e.tensor_tensor(out=s6, in0=li, in1=li, op=ALU.mult), r=[ch], w=[ch])
            self.P.op("dve", lambda e: e.tensor_tensor(out=pre_rden, in0=pre_rden, in1=s6, op=ALU.add), r=[ch], w=[ch])
            self.P.op("dve", lambda e: e.reciprocal(out=pre_rden, in_=pre_rden), r=[ch], w=[ch])
        act(s0, ldt, AF.Exp)
        tt(s1, lr, s0, ALU.mult)
        tt(s2, li, s0, ALU.mult)
        act(s0, s1, AF.Exp)
        act(s1, s2, AF.Sin, scale=1.0 / 32, bias=self.epsb[:, 1:2])
        act(s3, s2, AF.Sin, scale=1.0 / 32)
        for _ in range(5):
            tt(s2, s1, s1, ALU.mult)
            tt(s4, s3, s3, ALU.mult)
            tt(s5, s1, s3, ALU.mult)
            tt(s1, s2, s4, ALU.subtract)
            ts(s3, s5, 2.0, ALU.mult)
        tt(s2, s0, s1, ALU.mult)
        tt(s4, s0, s3, ALU.mult)
        if pre_rden is None:
            tt(s0, lr, lr, ALU.mult)
            tt(s1, li, li, ALU.mult)
            tt(s0, s0, s1, ALU.add)
            self._rcp(s0, s0)
        else:
            s0 = pre_rden
        ts(s1, s2, -1.0, ALU.add)
        tt(s3, s1, lr, ALU.mult)
        tt(s5, s4, li, ALU.mult)
        tt(s3, s3, s5, ALU.add)
        tt(s3, s3, s0, ALU.mult)
        tt(s5, s4, lr, ALU.mult)
        tt(s6, s1, li, ALU.mult)
        tt(s5, s5, s6, ALU.subtract)
        tt(s5, s5, s0, ALU.mult)
        return s2, s4, s3, s5

    def ssm_prep(self, l):
        P = self.P
        tt, ts = self._tt, self._ts
        pw = self.prm
        P.dma("sp", pw[:, 0:2352], self.i["sprm"][l], w=["pP", "pL"], sem="prm")
        zf = self.zb[:].rearrange("p k c -> p (k c)")
        self._eng, self._ch = "dve", "pP"
        mE = self.cst[:, 128:130]; mL = self.cst[:, 130:132]; mM = self.cst[:, 132:136]; nmE = self.cst[:, 200:202]
        F = 256
        self._eng, self._ch = "pool", "pL"
        zf = self.hT[:].rearrange("p k n -> p (k n)").bitcast(F32)
        S = [zf[:, j * F:(j + 1) * F] for j in range(7)]
        ar, ai, fr, fi = self._disc(F, pw[:, 1072:1328], pw[:, 1328:1584], pw[:, 1584:1840], S, pre_rden=zf[:, 3840:4096])
        o = 7 * F
        brL = pw[:, 1840:2096]; biL = pw[:, 2096:2352]

        def t1(j):
            return zf[:, o + j * 256:o + (j + 1) * 256]
        bb_r, bb_i, cur_r, cur_i, n_r, n_i, u1, u2 = [t1(j) for j in range(8)]
        x1, x2 = S[0], S[1]
        tt(u1, brL, fr, ALU.mult); tt(u2, biL, fi, ALU.mult); tt(bb_r, u1, u2, ALU.subtract)
        tt(u1, biL, fr, ALU.mult); tt(u2, brL, fi, ALU.mult); tt(bb_i, u1, u2, ALU.add)
        self._xw = [("curL", 0), ("curL", 1), "pLa", "pLb"]
        self._ms(cur_r, 1.0)
        self._ms(cur_i, 0.0)
        self._xw = ()
        for k in range(8):
            s = 7 - k
            par = k % 2
            self._q = qa = []
            self._ch = "pLa"; self._xr = [("curL", par), "pL"]; self._xw = ()
            tt(u1, cur_r, bb_r, ALU.mult); tt(u2, cur_i, bb_i, ALU.mult); tt(u1, u1, u2, ALU.subtract)
            for e2 in range(2):
                self._mask(self.Wsb[:, :, s, 0, e2 * 64:(e2 + 1) * 64], u1.rearrange("p (g q) -> p g q", q=64), mL[:, e2:e2 + 1])
            tt(u1, cur_r, bb_i, ALU.mult); tt(u2, cur_i, bb_r, ALU.mult); tt(u1, u1, u2, ALU.add)
            for e2 in range(2):
                self._mask(self.Wsb[:, :, s, 1, e2 * 64:(e2 + 1) * 64], u1.rearrange("p (g q) -> p g q", q=64), mL[:, e2:e2 + 1])
            self._q = qb = []
            if k < 7:
                self._ch = "pLb"; self._xr = [("curL", par), "pL"]; self._xw = ()
                tt(x1, cur_r, ar, ALU.mult); tt(x2, cur_i, ai, ALU.mult)
                self._xw = [("curL", 1 - par)]
                tt(n_r, x1, x2, ALU.subtract)
                self._xw = ()
                tt(x1, cur_r, ai, ALU.mult); tt(x2, cur_i, ar, ALU.mult)
                self._xw = [("curL", 1 - par)]
                tt(n_i, x1, x2, ALU.add)
                self._xw = ()
            self._q = None
            self._flush(qa, qb)
            if k < 7:
                cur_r, n_r = n_r, cur_r
                cur_i, n_i = n_i, cur_i
        self._xr = (); self._xw = ()
        self._ch = "pL"
        self.P.op("pool", lambda e: e.memset(self.small[:, 761:762], 0.0), r=["pLa", "pLb", ("curL", 0), ("curL", 1)], w=["pL"])
        zf = self.zb[:].rearrange("p k c -> p (k c)")
        self._eng, self._ch = "dve", "pP"
        F = 16
        S = [zf[:, j * F:(j + 1) * F] for j in range(7)]
        ar, ai, fr, fi = self._disc(F, pw[:, 0:16], pw[:, 16:32], pw[:, 32:48], S)
        o = 7 * F
        br = pw[:, 48:304].rearrange("p (a h) -> p a h", h=16); bi = pw[:, 304:560].rearrange("p (a h) -> p a h", h=16)
        cr = pw[:, 560:816].rearrange("p (a h) -> p a h", h=16); ci = pw[:, 816:1072].rearrange("p (a h) -> p a h", h=16)

        def t3(j):
            return zf[:, o + j * 256:o + (j + 1) * 256].rearrange("p (a h) -> p a h", h=16)
        bb_r, bb_i, u1, u2 = t3(0), t3(1), t3(2), t3(3)
        frb = fr.unsqueeze(2).to_broadcast([128, 16, 16]); fib = fi.unsqueeze(2).to_broadcast([128, 16, 16])
        tt(u1, br, frb, ALU.mult); tt(u2, bi, fib, ALU.mult); tt(bb_r, u1, u2, ALU.subtract)
        tt(u1, bi, frb, ALU.mult); tt(u2, br, fib, ALU.mult); tt(bb_i, u1, u2, ALU.add)
        Bp = self.small[:, 0:512].bitcast(BF16)[:, 0:1024].rearrange("p (a r c) -> p a r c", a=16, r=2)
        for ri, bb in enumerate((bb_r, bb_i)):
            for e2 in range(2):
                self._mask(Bp[:, :, ri, e2 * 16:(e2 + 1) * 16], bb, mE[:, e2:e2 + 1])
        o2 = o + 4 * 256
        cur_r = zf[:, o2:o2 + 16]; cur_i = zf[:, o2 + 16:o2 + 32]; n_r = zf[:, o2 + 32:o2 + 48]; n_i = zf[:, o2 + 48:o2 + 64]
        w1 = zf[:, o2 + 64:o2 + 80]; w2 = zf[:, o2 + 80:o2 + 96]
        self._xw = [("curP", 0), ("curP", 1), "pPa", "pPb"]
        self._ms(cur_r, 1.0)
        self._ms(cur_i, 0.0)
        self._xw = ()
        for k in range(9):
            par = k % 2
            crb = cur_r.unsqueeze(2).to_broadcast([128, 16, 16]); cib = cur_i.unsqueeze(2).to_broadcast([128, 16, 16])
            self._q = qa = []
            self._ch = "pPa"; self._xr = [("curP", par), "pP"]; self._xw = ()
            tt(u1, cr, crb, ALU.mult); tt(u2, ci, cib, ALU.mult); tt(u1, u1, u2, ALU.subtract)
            for e2 in range(2):
                self._mask(self.Wco[:, :, k, 0, e2 * 16:(e2 + 1) * 16], u1, mE[:, e2:e2 + 1])
            tt(u1, cr, cib, ALU.mult); tt(u2, ci, crb, ALU.mult); tt(u1, u1, u2, ALU.add)
            for e2 in range(2):
                self._mask(self.Wco[:, :, k, 1, e2 * 16:(e2 + 1) * 16], u1, nmE[:, e2:e2 + 1])
            if k == 1:
                self._cp(self.A1[:, 0, :], cur_r)
                self._cp(self.A1[:, 1, :], cur_i)
            if k == 8:
                self._cp(self.A8[:, 0, :], cur_r)
                self._cp(self.A8[:, 1, :], cur_i)
            self._q = qb = []
            if k < 8:
                self._ch = "pPb"; self._xr = [("curP", par), "pP"]; self._xw = ()
                tt(w1, cur_r, ar, ALU.mult); tt(w2, cur_i, ai, ALU.mult)
                self._xw = [("curP", 1 - par)]
                tt(n_r, w1, w2, ALU.subtract)
                self._xw = ()
                tt(w1, cur_r, ai, ALU.mult); tt(w2, cur_i, ar, ALU.mult)
                self._xw = [("curP", 1 - par)]
                tt(n_i, w1, w2, ALU.add)
                self._xw = ()
            self._q = None
            self._flush(qa, qb)
            if k < 8:
                cur_r, n_r = n_r, cur_r
                cur_i, n_i = n_i, cur_i
        self._xr = (); self._xw = ()
        self._ch = "pP"
        self.P.op("dve", lambda e: e.memset(self.small[:, 762:763], 0.0), r=["pPa", "pPb", ("curP", 0), ("curP", 1)], w=["pP"])
        A8r, A8i = self.A8[:, 0, :], self.A8[:, 1, :]
        r2 = zf[:, o2 + 96:o2 + 112]; r3 = zf[:, o2 + 112:o2 + 128]; Ur = zf[:, o2 + 128:o2 + 144]; Ui = zf[:, o2 + 144:o2 + 160]
        tt(r2, A8r, A8r, ALU.mult); tt(r3, A8i, A8i, ALU.mult); tt(r2, r2, r3, ALU.add)
        self._act(self.rho, r2, AF.Sqrt)
        self._rcp(r3, self.rho)
        tt(Ur, A8r, r3, ALU.mult); tt(Ui, A8i, r3, ALU.mult)
        Tc, Ts = self.Tc, self.Ts
        self._cp(Tc[:, :, 0], Ur); self._cp(Ts[:, :, 0], Ui)
        g1 = zf[:, 2048:3072]; g2 = zf[:, 3072:4096]
        Pr, Pi = Ur, Ui
        q1 = zf[:, o2 + 160:o2 + 176]; q2 = zf[:, o2 + 176:o2 + 192]; q3 = zf[:, o2 + 192:o2 + 208]; q4 = zf[:, o2 + 208:o2 + 224]
        nxt = [(q1, q2), (q3, q4)]
        n = 1
        lvl = 0
        while n < NCH:
            a1 = g1[:, 0:16 * n].rearrange("p (a j) -> p a j", j=n); a2 = g2[:, 0:16 * n].rearrange("p (a j) -> p a j", j=n)
            Prb = Pr.unsqueeze(2).to_broadcast([128, 16, n]); Pib = Pi.unsqueeze(2).to_broadcast([128, 16, n])
            pres = "pP" if lvl == 0 else ("Pq", (lvl - 1) % 2)
            self._q = qa = []
            self._ch = "pTa"; self._xr = [pres, "pP"]; self._xw = ()
            tt(a1, Tc[:, :, 0:n], Prb, ALU.mult); tt(a2, Ts[:, :, 0:n], Pib, ALU.mult); tt(Tc[:, :, n:2 * n], a1, a2, ALU.subtract)
            tt(a1, Tc[:, :, 0:n], Pib, ALU.mult); tt(a2, Ts[:, :, 0:n], Prb, ALU.mult); tt(Ts[:, :, n:2 * n], a1, a2, ALU.add)
            self._q = qb = []
            if 2 * n < NCH:
                nr, ni = nxt[lvl % 2]
                b1 = zf[:, o2 + 224:o2 + 240]; b2 = zf[:, o2 + 240:o2 + 256]
                self._ch = "pTb"; self._xr = [pres, "pP"]; self._xw = ()
                tt(b1, Pr, Pr, ALU.mult); tt(b2, Pi, Pi, ALU.mult); tt(b1, b1, b2, ALU.subtract)
                tt(b2, Pr, Pi, ALU.mult)
                self._xw = [("Pq", lvl % 2)]
                ts(ni, b2, 2.0, ALU.mult); self._cp(nr, b1)
                self._xw = ()
            self._q = None
            self._flush(qa, qb)
            if 2 * n < NCH:
                Pr, Pi = nr, ni
            n *= 2
            lvl += 1
        self._xr = (); self._xw = ()
        self._ch = "pP"
        self.P.op("dve", lambda e: e.memset(self.small[:, 763:764], 0.0), r=["pTa", "pTb", ("Pq", 0), ("Pq", 1)], w=["pP"])
        for gt in range(4):
            pb = 6 + gt % 2
            ps = self.PSB[pb]
            for m in range(4):
                pair = 4 * gt + m
                for ri in range(2):
                    P.op("pe", lambda e, m=m, pair=pair, ri=ri, ps=ps: e.matmul(ps[32 * m:32 * m + 32, 0:256], lhsT=Bp[:, pair, ri, :],
                                                                              rhs=self.Wco[:, pair, 0:8, ri, :], start=(ri == 0), stop=(ri == 1),
                                                                              skip_group_check=True, tile_position=(0, 32 * m)),
                         r=["pP"], w=[("ps", pb)])
            for m2 in range(4):
                P.op("dve", lambda e, gt=gt, m2=m2, ps=ps: e.tensor_scalar(out=self.Wtp[:, gt, :, 32 * m2:32 * m2 + 32], in0=ps[:, 0:256].rearrange("p (k c) -> p k c", c=32),
                                                                          scalar1=mM[:, m2:m2 + 1], scalar2=None, op0=ALU.mult),
                     r=[("ps", pb), "cst"], w=["Wtp"])
        sc = self.scr
        P.dma("sp", sc["sb"][l], self.Wsb.rearrange("p g k r c -> p (g k r c)"), r=["pL"], sem="pst1")
        P.dma("sp", sc["co"][l], self.Wco[:].rearrange("p a k r c -> p (a k r c)"), r=["pP"], sem="pst2")
        P.dma("sp", sc["tp"][l], self.Wtp[:].rearrange("p g k c -> p (g k c)"), r=["Wtp"], sem="pst3")
        P.dma("sp", sc["t"][l][:, 0:2048], self.Tc.rearrange("p a c -> p (a c)"), r=["pP"], sem="pst4")
        P.dma("sp", sc["t"][l][:, 2048:4096], self.Ts.rearrange("p a c -> p (a c)"), r=["pP"], sem="pst5")
        P.dma("sp", sc["t"][l][:, 4096:4144], self.a1rho[:, :], r=["pP"], sem="pst6")

    def load_prep(self, l):
        P = self.P
        sc = self.scr
        T3 = (0, 512, HALF)
        aA = [("aT", t) for t in T3]
        P.dma("sp", self.Wsb.rearrange("p g k r c -> p (g k r c)"), sc["sb"][l], r=["scr"], w=["ssmw"] + aA + [("mg", t) for t in T3], sem="pld")
        P.dma("sp", self.Wco[:].rearrange("p a k r c -> p (a k r c)"), sc["co"][l], r=["scr"], w=["ssmw", ("kvs", 0), ("kvs", 1)], sem="pld")
        P.dma("sp", self.Wtp[:].rearrange("p g k c -> p (g k c)"), sc["tp"][l], r=["scr"], w=["ssmw", "kT", "vv", "qtok"], sem="pld")
        P.dma("sp", self.Tc.rearrange("p a c -> p (a c)"), sc["t"][l][:, 0:2048], r=["scr"], w=["ssmw"] + aA + [("oT", t) for t in T3] + [("zT", t) for t in T3], sem="pld")
        P.dma("sp", self.Ts.rearrange("p a c -> p (a c)"), sc["t"][l][:, 2048:4096], r=["scr"], w=["ssmw"] + aA + [("Hbf", g) for g in range(4)], sem="pld")
        P.dma("sp", self.a1rho[:, :], sc["t"][l][:, 4096:4144], r=["scr"], w=["ssmw"], sem="pld")

    def gelu_to(self, y, yres, dst, dres, w):
        P = self.P
        si = self.uid() % 2
        t = self.tmpB[si]
        P.op("dve", lambda e: e.scalar_tensor_tensor(out=t[:, :w], in0=y, scalar=0.044715, in1=y, op0=ALU.mult, op1=ALU.mult), r=[yres], w=[("tmpB", si)])
        P.op("dve", lambda e: e.scalar_tensor_tensor(out=t[:, :w], in0=t[:, :w], scalar=1.0, in1=y, op0=ALU.add, op1=ALU.mult), r=[("tmpB", si), yres], w=[("tmpB", si)])
        P.op("act", lambda e: e.activation(out=t[:, :w], in_=t[:, :w], func=AF.Sigmoid, scale=1.5957691216057308), r=[("tmpB", si)], w=[("tmpB", si)])
        P.op("dve", lambda e: e.tensor_tensor(out=dst, in0=y, in1=t[:, :w], op=ALU.mult), r=[("tmpB", si), yres], w=[dres])

    def ssm_core(self, l, half):
        P = self.P
        Hst, Hbf, Wsb, Wco, Wtp, uT = self.Hst, self.Hbf, self.Wsb, self.Wco, self.Wtp, self.uT
        u8 = uT[:, :, 0:HALF].rearrange("p g (c j) -> p g c j", j=LCH)
        HG = [("Hst", g) for g in range(4)]
        P.op("dve", lambda e: e.tensor_copy(out=Hst[:, :, :, 0], in_=self.carry[:, l]), r=["carry"], w=HG)
        if half == 1 and "ss" not in SKIP:
            self.ssm_sample_h0(l)
        Tc, Ts = self.Tc, self.Ts
        tmps = [(self.tmpA[0], ("tmpA", 0)), (self.tmpA[1], ("tmpA", 1)), (self.tmpB[0], ("tmpB", 0)), (self.tmpB[1], ("tmpB", 1))]
        v = lambda t: t[:, :].rearrange("p (a c) -> p a c", c=NCH)

        def rot(sign, qd):
            hres = ("Hst", qd)
            sl = slice(4 * qd, 4 * qd + 4)
            Xr = Hst[:, sl, 0, 1:NCH + 1]; Xi = Hst[:, sl, 1, 1:NCH + 1]
            C = Tc[:, sl, :]; S_ = Ts[:, sl, :]
            (t1, r1), (t2, r2), (t3, r3), (t4, r4) = tmps
            P.op("dve", lambda e: e.tensor_tensor(out=v(t1), in0=Xr, in1=C, op=ALU.mult), r=[hres, "ssmw"], w=[r1])
            P.op("pool", lambda e: e.tensor_tensor(out=v(t2), in0=Xi, in1=S_, op=ALU.mult), r=[hres, "ssmw"], w=[r2])
            P.op("dve", lambda e: e.tensor_tensor(out=v(t3), in0=Xi, in1=C, op=ALU.mult), r=[hres, "ssmw"], w=[r3])
            P.op("pool", lambda e: e.tensor_tensor(out=v(t4), in0=Xr, in1=S_, op=ALU.mult), r=[hres, "ssmw"], w=[r4])
            if sign < 0:
                P.op("dve", lambda e: e.tensor_tensor(out=Xr, in0=v(t1), in1=v(t2), op=ALU.add), r=[r1, r2], w=[hres])
                P.op("pool", lambda e: e.tensor_tensor(out=Xi, in0=v(t3), in1=v(t4), op=ALU.subtract), r=[r3, r4], w=[hres])
            else:
                P.op("dve", lambda e: e.tensor_tensor(out=Xr, in0=v(t1), in1=v(t2), op=ALU.subtract), r=[r1, r2], w=[hres])
                P.op("pool", lambda e: e.tensor_tensor(out=Xi, in0=v(t3), in1=v(t4), op=ALU.add), r=[r3, r4], w=[hres])
        def state_build(gt):
            off = (gt % 2) * 256
            for ri in range(2):
                for k in range(LCH):
                    for m in range(4):
                        ps = self.PSB[m]
                        P.op("pe", lambda e, m=m, ri=ri, k=k, ps=ps: e.matmul(ps[:, off + ri * 128: off + (ri + 1) * 128], lhsT=Wsb[32 * m:32 * m + 32, gt, k, ri, :],
                                                                               rhs=u8[32 * m:32 * m + 32, gt, :, k], start=(k == 0), stop=(k == LCH - 1),
                                                                               skip_group_check=True, tile_position=(32 * m, 0)),
                             r=["ssmw", ("uT", 0), ("uT", 512)], w=[("ps", m)])
            for m in range(4):
                pair = 4 * gt + m
                ps = self.PSB[m]
                if pair % 2:
                    P.op("act", lambda e, pair=pair, ps=ps: e.activation(out=Hst[:, pair, :, 1:NCH + 1], in_=ps[:, off:off + 256].rearrange("p (r c) -> p r c", r=2), func=AF.Copy),
                         r=[("ps", m)], w=[("Hst", gt)])
                else:
                    P.op("dve", lambda e, pair=pair, ps=ps: e.tensor_copy(out=Hst[:, pair, :, 1:NCH + 1], in_=ps[:, off:off + 256].rearrange("p (r c) -> p r c", r=2)),
                         r=[("ps", m)], w=[("Hst", gt)])

        for qd in range(4):
            state_build(qd)
            rot(-1, qd)
            for pair in range(4 * qd, 4 * qd + 4):
                for ri in range(2):
                    P.op("dve", lambda e, pair=pair, ri=ri: e.tensor_tensor_scan(out=Hst[:, pair, ri, 1:NCH + 1], data0=self.rho[:, pair:pair + 1].to_broadcast([128, NCH]),
                                                                                data1=Hst[:, pair, ri, 1:NCH + 1], initial=Hst[:, pair, ri, 0:1], op0=ALU.mult, op1=ALU.add),
                         r=[("Hst", qd), "ssmw"], w=[("Hst", qd)])
            rot(+1, qd)
        for qd in range(4):
            P.op("act", lambda e, qd=qd: e.activation(out=Hbf[:, 4 * qd:4 * qd + 4, :, 0:NCH], in_=Hst[:, 4 * qd:4 * qd + 4, :, 0:NCH], func=AF.Copy), r=HG, w=[("Hbf", qd)])
        P.op("dve", lambda e: e.tensor_copy(out=self.carry[:, l], in_=Hst[:, :, :, NCH]), r=HG, w=["carry"])
        if half == self.nh - 1:
            for ri, nm in enumerate(("re_p", "im_p")):
                t = self.small[:, 640 + 16 * ri: 656 + 16 * ri]
                P.op("act", lambda e, ri=ri, t=t: e.activation(out=t, in_=Hst[:, :, ri, NCH], func=AF.Copy), r=HG, w=[("fin", ri)])
                P.dma("sp", self.o[nm][l].rearrange("(a q) -> q a", q=128), t, r=[("fin", ri)], sem="fin%d" % ri, final=True, allow_slow_non_contiguous=True)
        for gt in range(4):
            for bi_, (t0, w) in enumerate(self.tbs[:2]):
                b = self.bank(); ps = self.PSB[b]
                y3 = ps[:, :].rearrange("p (c j) -> p c j", j=LCH)
                u3 = uT[:, gt, t0:t0 + 512].rearrange("p (c j) -> p c j", j=LCH)
                if TAPS2D:
                    for k in range(LCH):
                        for j2 in range(k, LCH):
                            P.op("pe", lambda e, gt=gt, k=k, j2=j2, y3=y3, u3=u3: e.matmul(y3[:, :, j2], lhsT=Wtp[:, gt, k, :], rhs=u3[:, :, j2 - k], start=(k == 0 and j2 == 0), stop=False, skip_group_check=True),
                                 r=["ssmw", ("uT", t0)], w=[("ps", b)])
                else:
                    for k in range(LCH):
                        P.op("pe", lambda e, gt=gt, k=k, y3=y3, u3=u3: e.matmul(y3[:, :, k:LCH], lhsT=Wtp[:, gt, k, :], rhs=u3[:, :, 0:LCH - k], start=(k == 0), stop=False, skip_group_check=True),
                             r=["ssmw", ("uT", t0)], w=[("ps", b)])
                c0 = bi_ * 64
                for m in range(4):
                    pair = 4 * gt + m
                    for j in range(LCH):
                        for ri in range(2):
                            last = (m == 3 and j == LCH - 1 and ri == 1)
                            P.op("pe", lambda e, m=m, pair=pair, j=j, ri=ri, y3=y3, c0=c0, last=last: e.matmul(y3[32 * m:32 * m + 32, :, j], lhsT=Wco[:, pair, j + 1, ri, :],
                                                                                                           rhs=Hbf[:, pair, ri, c0:c0 + 64], start=False, stop=last,
                                                                                                           skip_group_check=True, tile_position=(0, 32 * m)),
                                 r=["ssmw", ("Hbf", gt)], w=[("ps", b)])
                self.ssm_post(l, gt, ps, b, t0, 512)
        if half == 1 and "ss" not in SKIP:
            self.ssm_sample(l)

    def ssm_post(self, l, gt, ps, b, t0, w):
        P = self.P
        si = self.uid() % 2
        y = self.tmpA[si]
        P.op("dve", lambda e: e.scalar_tensor_tensor(out=y[:, :w], in0=self.uT[:, gt, t0:t0 + w], scalar=self.vec4[:, 0, l, gt:gt + 1], in1=ps[:, :w], op0=ALU.mult, op1=ALU.add),
             r=[("ps", b), ("uT", t0), "vec4"], w=[("tmpA", si)])
        self.gelu_to(y[:, :w], ("tmpA", si), self.oT[:, gt, t0:t0 + w], ("oT", t0), w)

    def glu(self, l):
        P = self.P
        wb, wres = self.wload([self.w_glu[l]], 4, 512)

        def evac(ps, b, m, t0, w):
            si = self.uid() % 2
            sg = self.tmpB[si]
            P.op("act", lambda e: e.activation(out=sg[:, :w], in_=ps[:, :w], func=AF.Sigmoid, bias=self.vec4[:, 1, l, m:m + 1]), r=[("ps", b), "vec4"], w=[("tmpB", si)])
            P.op("dve", lambda e: e.tensor_tensor(out=self.uT[:, m, t0:t0 + w], in0=self.oT[:, m, t0:t0 + w], in1=sg[:, :w], op=ALU.mult), r=[("tmpB", si), ("oT", t0)], w=[("uT", t0)])
        self.mm_fm(wb, wres, 4, 4, self.oT, "oT", evac)

    def ssm_sample_load(self, l):
        P = self.P
        zs = self.zb[:].rearrange("p k c -> p (k c)")[0:NS, :].rearrange("p (r c) -> p r c", r=2)
        P.dma("sp", zs[:, 0, :], self.i["sre"][l], w=["zb", "pw0", "pw1", "hist", ("zbs", 0)], sem="zbs0")
        P.dma("sp", zs[:, 1, :], self.i["sim"][l], w=[("zbs", 1)], sem="zbs1")

    def ssm_sample_h0(self, l):
        P = self.P
        zs = self.zb[:].rearrange("p k c -> p (k c)")[0:NS, :].rearrange("p (r c) -> p r c", r=2)
        b = self.bank(); ps = self.PSB[b]
        ps4 = ps[:, :].rearrange("p (a r n) -> p a r n", a=16, r=2)
        for pair in range(16):
            for ri in range(2):
                P.op("pe", lambda e, pair=pair, ri=ri, ps4=ps4: e.transpose(ps4[:, pair, ri, :], zs[:, ri, pair * 128:(pair + 1) * 128], self.ident[0:NS, 0:NS]),
                     r=["zb", ("zbs", 0), ("zbs", 1), "cst"], w=[("ps", b)])
        H0 = self.small[:, 0:512].rearrange("p (a r n) -> p a r n", a=16, r=2)
        P.op("dve", lambda e: e.tensor_copy(out=H0, in_=ps4), r=[("ps", b)], w=["H0"])

    def ssm_sample(self, l):
        P = self.P
        Wsb, Wco, uT = self.Wsb, self.Wco, self.uT
        H0 = self.small[:, 0:512].rearrange("p (a r n) -> p a r n", a=16, r=2)
        if KSS < 2:
            return
        BU = self.sA[:, :].rearrange("p (a r n) -> p a r n", a=16, r=2)
        BUg = self.sA[:, :].rearrange("p (g m r n) -> p g m r n", g=4, m=4, r=2)
        for gt in range(4):
            for ri in range(2):
                c0 = (gt * 2 + ri) * NS
                for m in range(4):
                    psm = self.PSB[m]
                    P.op("pe", lambda e, m=m, gt=gt, ri=ri, psm=psm, c0=c0: e.matmul(psm[:, c0:c0 + NS], lhsT=Wsb[32 * m:32 * m + 32, gt, 7, ri, :], rhs=uT[32 * m:32 * m + 32, gt, HALF:HALF + NS],
                                                                                   start=True, stop=True, skip_group_check=True, tile_position=(32 * m, 0)),
                         r=["ssmw", ("uT", HALF)], w=[("ps", m)])
        for m in range(4):
            psm = self.PSB[m]
            P.op("dve" if m % 2 else "act", (lambda e, m=m, psm=psm: e.tensor_copy(out=BUg[:, :, m], in_=psm[:, 0:128].rearrange("p (g r n) -> p g r n", g=4, r=2))) if m % 2 else
                 (lambda e, m=m, psm=psm: e.activation(out=BUg[:, :, m], in_=psm[:, 0:128].rearrange("p (g r n) -> p g r n", g=4, r=2), func=AF.Copy)), r=[("ps", m)], w=["BU"])
        if KSS < 3:
            return
        A1r = self.A1[:, 0, :].unsqueeze(2).to_broadcast([128, 16, NS]); A1i = self.A1[:, 1, :].unsqueeze(2).to_broadcast([128, 16, NS])
        T = self.sB[:, 0:256].rearrange("p (a n) -> p a n", n=NS)
        seq = [(0, A1r, 0, ALU.add), (1, A1i, 0, ALU.subtract), (0, A1i, 1, ALU.add), (1, A1r, 1, ALU.add)]
        for (hs, Ax, dst, op) in seq:
            P.op("pool", lambda e, hs=hs, Ax=Ax: e.tensor_tensor(out=T, in0=H0[:, :, hs, :], in1=Ax, op=ALU.mult), r=["H0", "ssmw"], w=["sT"])
            P.op("pool", lambda e, dst=dst, op=op: e.tensor_tensor(out=BU[:, :, dst, :], in0=BU[:, :, dst, :], in1=T, op=op), r=["sT", "BU"], w=["BU"])
        if KSS < 4:
            return
        hb = self.sB[:, 256:512].bitcast(BF16).rearrange("p (a r n) -> p a r n", a=16, r=2)
        P.op("act", lambda e: e.activation(out=hb, in_=BU, func=AF.Copy), r=["BU"], w=["hb"])
        for gt in range(4):
            b = self.bank(); ps = self.PSB[b]
            for m in range(4):
                pair = 4 * gt + m
                for ri in range(2):
                    P.op("pe", lambda e, m=m, pair=pair, ri=ri, ps=ps: e.matmul(ps[32 * m:32 * m + 32, 0:NS], lhsT=Wco[:, pair, 0, ri, :], rhs=hb[:, pair, ri, :],
                                                                              start=(ri == 0), stop=(ri == 1), skip_group_check=True, tile_position=(0, 32 * m)),
                         r=["ssmw", "hb"], w=[("ps", b)])
            self.ssm_post(l, gt, ps, b, HALF, NS)
        if KSS < 5:
            return
        for ri, nm in enumerate(("re_s", "im_s")):
            for q in range(4):
                psq = self.PSB[q]
                for a4 in range(4):
                    pair = 4 * q + a4
                    P.op("pe", lambda e, pair=pair, ri=ri, a4=a4, psq=psq: e.transpose(psq[0:NS, a4 * 128:(a4 + 1) * 128], BU[:, pair, ri, :], self.ident),
                         r=["BU", "cst"], w=[("ps", q)])
                si = self.uid() % 2
                t = self.tmpA[si]
                P.op("dve", lambda e, psq=psq, t=t: e.tensor_copy(out=t[0:NS, :], in_=psq[0:NS, :]), r=[("ps", q)], w=[("tmpA", si)])
                if KSS == 7:
                    if not hasattr(self, "dbgres"):
                        self.dbgres = self.nc.dram_tensor("dbg_res", [2, NS, 2048], F32, kind="ExternalOutput").ap()
                    P.dma("sp", self.dbgres[ri][:, q * 512:(q + 1) * 512], t[0:NS, :], r=[("tmpA", si)], sem="sso%d" % si, final=True)
                elif KSS >= 6:
                    P.dma(DQ, self.o[nm][l][:, q * 512:(q + 1) * 512], t[0:NS, :], r=[("tmpA", si)], sem="sso%d" % si, final=True)

    def pool_core(self, l, half):
        P = self.P
        uT = self.uT
        wb, wres = self.wload([self.pool_w[l, gi] for gi in range(4)], 1, 512)
        W = HALF + 16
        bufs = [(self.pw[:, 0:W], "pw0"), (self.pw[:, W:2 * W], "pw1")]
        invc = self.cst[:, 136:200].rearrange("p (g t) -> p g t", t=16)
        for gi, win in enumerate((2, 4, 8, 16)):
            (src, sres), (dst, dres) = bufs
            weng = "pool" if gi in (0, 3) else "dve"
            P.op(weng, lambda e, a=src, gi=gi: e.tensor_copy(out=a[:, 0:16], in_=self.phist[:, l, gi, :]), r=["phist"], w=[sres, "zb", "hist"])
            P.op(weng, lambda e, a=src, gi=gi: e.tensor_copy(out=a[:, 16:W], in_=uT[:, gi, 0:HALF]), r=[("uT", 0), ("uT", 512)], w=[sres])
            P.op(weng, lambda e, gi=gi: e.tensor_copy(out=self.phist[:, l, gi, :], in_=uT[:, gi, HALF - 16:HALF]), r=[("uT", 512), sres], w=["phist"])
            if win >= 4:
                P.op("dve", lambda e, src=src, dst=dst: e.tensor_tensor_scan(out=dst[:, 0:W], data0=self.onesf[:, 0:1].to_broadcast([128, W]), data1=src[:, 0:W], initial=0.0,
                                                                            op0=ALU.mult, op1=ALU.add), r=[sres, "onesf"], w=[dres])
                P.op("dve", lambda e, src=src, dst=dst, win=win: e.tensor_tensor(out=src[:, 16:W], in0=dst[:, 16:W], in1=dst[:, 16 - win:W - win], op=ALU.subtract), r=[dres], w=[sres])
            else:
                step = 1
                while step < win:
                    P.op(weng, lambda e, src=src, dst=dst, step=step: e.tensor_tensor(out=dst[:, step:W], in0=src[:, step:W], in1=src[:, 0:W - step], op=ALU.add), r=[sres], w=[dres])
                    P.op(weng, lambda e, src=src, dst=dst, step=step: e.tensor_copy(out=dst[:, 0:step], in_=src[:, 0:step]), r=[sres], w=[dres])
                    src, dst = dst, src
                    sres, dres = dres, sres
                    step *= 2
            pl = dst[:, 0:HALF // 2 + 8].bitcast(BF16)[:, 0:HALF]
            P.op("dve", lambda e, src=src, pl=pl, gi=gi, win=win: e.scalar_tensor_tensor(out=pl, in0=src[:, 16:W], scalar=1.0 / win, in1=uT[:, gi, 0:HALF], op0=ALU.mult, op1=ALU.subtract),
                 r=[sres, ("uT", 0), ("uT", 512)], w=[dres])
            if half == 0:
                t = self.small[:, 672:688]
                P.op("dve", lambda e, src=src, gi=gi, t=t: e.tensor_tensor(out=t, in0=src[:, 16:32], in1=invc[:, gi, :], op=ALU.mult), r=[sres, "cst"], w=["pfix"])
                P.op("dve", lambda e, pl=pl, gi=gi, t=t: e.tensor_tensor(out=pl[:, 0:16], in0=t, in1=uT[:, gi, 0:16], op=ALU.subtract), r=["pfix", ("uT", 0), dres], w=[dres])
            for (t0, w) in self.tbs[:2]:
                b = self.bank(); ps = self.PSB[b]
                P.op("pe", lambda e, gi=gi, ps=ps, pl=pl, t0=t0, w=w: e.matmul(ps[:, :w], lhsT=wb[:, 0, gi * 128:(gi + 1) * 128], rhs=pl[:, t0:t0 + w], start=True, stop=True),
                     r=[wres, dres], w=[("ps", b)])
                P.op("act", lambda e, gi=gi, ps=ps, t0=t0, w=w: e.activation(out=self.oT[:, gi, t0:t0 + w], in_=ps[:, :w], func=AF.Copy, scale=self.vec4[:, 2, l, gi:gi + 1]),
                     r=[("ps", b), "vec4"], w=[("oT", t0)])
        if half == 1 and "ps" not in SKIP:
            self.pool_sample(l, wb, wres)
        if half == self.nh - 1:
            b = self.bank(); ps = self.PSB[b]
            for gi in range(4):
                P.op("pe", lambda e, gi=gi, ps=ps: e.transpose(ps[0:16, gi * 128:(gi + 1) * 128], self.lastu[:, gi, :], self.ident), r=["lastu", "cst"], w=[("ps", b)])
            t = self.sB
            P.op("dve", lambda e, ps=ps: e.tensor_copy(out=t[0:16, :], in_=ps[0:16, :]), r=[("ps", b)], w=["sBo", "sT", "hb", "den"])
            P.dma("sp", self.o["pool_p"][l], t[1:16, :], r=["sBo"], sem="sBo", final=True)

    def pool_sample(self, l, wb, wres):
        P = self.P
        uT = self.uT
        sp = self.i["spool"][l].rearrange("n r c -> (n r) c")
        zt = self.zb[:].rearrange("p k c -> p (k c)")
        hist = self.pw[:, 0:960].rearrange("p (g x) -> p g x", g=4)
        for j in range(2):
            P.dma("sp", zt[0:120, j * 512:(j + 1) * 512], sp[j * 120:(j + 1) * 120, :], w=["zb"], sem="zb")
        P.dma("sp", self.o["pool_s"][l][:, 0:14, :], self.i["spool"][l][:, 1:15, :], sem="pcopy", final=True)
        for gi in range(4):
            b = self.bank(); ps = self.PSB[b]
            for j in range(2):
                P.op("pe", lambda e, gi=gi, j=j, ps=ps: e.transpose(ps[:, j * 120:(j + 1) * 120], zt[0:120, j * 512 + gi * 128: j * 512 + (gi + 1) * 128], self.ident[0:120, 0:120]),
                     r=["zb", "cst"], w=[("ps", b)])
            P.op("dve", lambda e, gi=gi, ps=ps: e.tensor_copy(out=hist[:, gi, :], in_=ps[:, 0:240]), r=[("ps", b)], w=["hist", "pw0"])
        h4 = self.pw[:, 0:960].rearrange("p (g n r) -> p g n r", g=4, r=15)
        red = self.small[:, 688:704]
        plb = self.small[:, 720:752].bitcast(BF16).rearrange("p (g n) -> p g n", g=4)
        for gi, win in enumerate((2, 4, 8, 16)):
            P.op("dve", lambda e, gi=gi, win=win: e.tensor_reduce(out=red, in_=h4[:, gi, :, 16 - win:15], axis=AX.X, op=ALU.add), r=["hist"], w=["red"])
            P.op("dve", lambda e, gi=gi: e.tensor_tensor(out=red, in0=red, in1=self.usam[:, gi, :], op=ALU.add), r=["red", "usam"], w=["red"])
            P.op("dve", lambda e, gi=gi, win=win: e.scalar_tensor_tensor(out=plb[:, gi, :], in0=red, scalar=1.0 / win, in1=self.usam[:, gi, :], op0=ALU.mult, op1=ALU.subtract),
                 r=["red", "usam"], w=["plb"])
            b = self.bank(); ps = self.PSB[b]
            P.op("pe", lambda e, gi=gi, ps=ps: e.matmul(ps[:, 0:NS], lhsT=wb[:, 0, gi * 128:(gi + 1) * 128], rhs=plb[:, gi, :], start=True, stop=True), r=[wres, "plb"], w=[("ps", b)])
            P.op("act", lambda e, gi=gi, ps=ps: e.activation(out=self.oT[:, gi, HALF:HALF + NS], in_=ps[:, 0:NS], func=AF.Copy, scale=self.vec4[:, 2, l, gi:gi + 1]),
                 r=[("ps", b), "vec4"], w=[("oT", HALF)])
        b = self.bank(); ps = self.PSB[b]
        for gi in range(4):
            P.op("pe", lambda e, gi=gi, ps=ps: e.transpose(ps[0:NS, gi * 128:(gi + 1) * 128], self.usam[:, gi, :], self.ident), r=["usam", "cst"], w=[("ps", b)])
        t = self.sA
        P.op("dve", lambda e, ps=ps: e.tensor_copy(out=t[0:NS, :], in_=ps[0:NS, :]), r=[("ps", b)], w=["sAo", "BU"])
        P.dma("sp", self.o["pool_s"][l][:, 14, :], t[0:NS, :], r=["sAo"], sem="sAo", final=True)

    def attn_core(self, l, half):
        P = self.P
        qT = self.uT
        sc = 128 ** -0.5
        for (t0, w) in self.tbs[:2]:
            for h in range(4):
                pts = []
                for mt in range(2):
                    b = self.bank(); ps = self.PSB[b]
                    P.op("pe", lambda e, h=h, mt=mt, ps=ps, t0=t0, w=w: e.matmul(ps[:, :w], lhsT=self.kT[:, h, mt * 128:(mt + 1) * 128], rhs=qT[:, h, t0:t0 + w], start=True, stop=True),
                         r=["kT", ("uT", t0)], w=[("ps", b)])
                    pt = self.sq[mt]
                    P.op("act", lambda e, ps=ps, pt=pt, w=w: e.activation(out=pt[:, :w], in_=ps[:, :w], func=AF.Exp, scale=sc), r=[("ps", b)], w=[("sq", mt)])
                    pts.append(pt)
                bd = self.bank(); pd = self.PSB[bd]
                bo = self.bank(); po = self.PSB[bo]
                for mt in range(2):
                    P.op("pe", lambda e, mt=mt, pd=pd, w=w, pts=pts: e.matmul(pd[:, :w], lhsT=self.onesb[:], rhs=pts[mt][:, :w], start=(mt == 0), stop=(mt == 1)),
                         r=[("sq", mt), "onesb"], w=[("ps", bd)])
                for mt in range(2):
                    P.op("pe", lambda e, mt=mt, h=h, po=po, w=w, pts=pts: e.matmul(po[:, :w], lhsT=self.vv[:, mt, h * 128:(h + 1) * 128], rhs=pts[mt][:, :w], start=(mt == 0), stop=(mt == 1)),
                         r=[("sq", mt), "vv"], w=[("ps", bo)])
                si = self.uid() % 2
                rc = self.tmpA[si]
                P.op("dve", lambda e, pd=pd, rc=rc, w=w: e.reciprocal(out=rc[:, :w], in_=pd[:, :w]), r=[("ps", bd)], w=[("tmpA", si)])
                P.op("dve", lambda e, po=po, rc=rc, h=h, t0=t0, w=w: e.tensor_tensor(out=self.oT[:, h, t0:t0 + w], in0=po[:, :w], in1=rc[:, :w], op=ALU.mult),
                     r=[("ps", bo), ("tmpA", si)], w=[("oT", t0)])
        if half == 1 and "as" not in SKIP:
            self.attn_sample(l)

    def attn_sample(self, l):
        P = self.P
        sc = 128 ** -0.5
        kvbuf = self.Wco[:].rearrange("p a k r c -> p (a k r c)").bitcast(F32)
        KV = [kvbuf[:, j * 2048:(j + 1) * 2048].rearrange("p (x t c) -> p x t c", x=2, t=2) for j in range(2)]
        ck, cv = self.i["ck"], self.i["cv"]
        bo = 6; po = self.PSB[bo]
        bd = 7; pd = self.PSB[bd]
        Sx = self.sA[:, 0:128].rearrange("p (n t h) -> p n t h", n=NS, t=2)
        Pb = self.small[:, 512:576].bitcast(BF16).rearrange("p (n t h) -> p n t h", n=NS, t=2)
        qms = [(self.sq[0][0:NS, :], ("sq", 0)), (self.sq[1][0:NS, :], ("sq", 1))]
        vbs = [(self.tmpA[j][:, :].bitcast(BF16).rearrange("p (t c) -> p t c", t=2), ("tmpA", j)) for j in range(2)]
        pqs = {}

        def qbcast(n):
            kv = KV[n % 2]
            kres = ("kvs", n % 2)
            P.dma("sp", kv[:, 0], ck[l, n].rearrange("(t p) c -> p t c", p=128), w=[kres, "ssmw"], sem="kvs%d" % (n % 2))
            P.dma("sp", kv[:, 1], cv[l, n].rearrange("(t p) c -> p t c", p=128), w=[kres], sem="kvs%d" % (n % 2))
            bq = self.bank(); pq = self.PSB[bq]
            qm, qres = qms[n % 2]
            P.op("dve", lambda e: e.tensor_scalar(out=qm, in0=self.qtok[:, :], scalar1=self.cst[0:NS, n:n + 1], scalar2=None, op0=ALU.mult), r=["qtok", "cst"], w=[qres])
            P.op("pe", lambda e: e.matmul(pq[:, :], lhsT=self.onesb[0:NS, :], rhs=qm, start=True, stop=True), r=["onesb", qres], w=[("ps", bq)])
            vb, vres = vbs[n % 2]
            P.op("pool", lambda e: e.tensor_copy(out=vb, in_=kv[:, 1]), r=[kres], w=[vres])
            pqs[n] = (bq, pq)
        qbcast(0)
        for n in range(NS):
            kv = KV[n % 2]
            kres = ("kvs", n % 2)
            vb, vres = vbs[n % 2]
            bq, pq = pqs[n]
            for mt in range(2):
                si = self.uid() % 2
                t = self.tmpB[si]
                P.op("dve", lambda e, kv=kv, mt=mt, pq=pq, t=t: e.tensor_tensor(out=t[:, :], in0=kv[:, 0, mt, :], in1=pq[:, :], op=ALU.mult), r=[kres, ("ps", bq)], w=[("tmpB", si)])
                P.op("dve", lambda e, n=n, mt=mt, t=t: e.tensor_reduce(out=Sx[:, n, mt, :], in_=t[:, :].rearrange("p (h d) -> p h d", d=128), axis=AX.X, op=ALU.add),
                     r=[("tmpB", si)], w=[("Sx", n), "BU", "sAo"])
            P.op("act", lambda e, n=n: e.activation(out=Pb[:, n], in_=Sx[:, n], func=AF.Exp, scale=sc), r=[("Sx", n)], w=[("Pf", n)])
            if n + 1 < NS:
                qbcast(n + 1)
            P.op("pe", lambda e, n=n: e.matmul(pd[:, n * 8:(n + 1) * 8], lhsT=self.onesb[:], rhs=Pb[:, n].rearrange("p t h -> p (t h)"), start=True, stop=True, skip_group_check=True),
                 r=[("Pf", n), "onesb"], w=[("ps", bd)])
            for h in range(4):
                for mt in range(2):
                    P.op("pe", lambda e, n=n, h=h, mt=mt, vb=vb: e.matmul(po[:, h * NS + n: h * NS + n + 1], lhsT=vb[:, mt, h * 128:(h + 1) * 128], rhs=Pb[:, n, mt, h:h + 1],
                                                                        start=(mt == 0), stop=(mt == 1), skip_group_check=True),
                         r=[vres, ("Pf", n)], w=[("ps", bo)])
        den = self.sB[:, 0:64].rearrange("p (h n) -> p h n", n=NS)
        d8 = self.sB[:, 64:192]
        P.op("dve", lambda e: e.tensor_copy(out=d8, in_=pd[:, 0:128]), r=[("ps", bd)], w=["den", "sT", "hb", "sBo"])
        d84 = d8.rearrange("p (n t h) -> p h n t", n=NS, t=2)
        P.op("dve", lambda e: e.tensor_tensor(out=den, in0=d84[:, :, :, 0], in1=d84[:, :, :, 1], op=ALU.add), r=["den"], w=["den", "sT", "hb", "sBo"])
        P.op("dve", lambda e: e.reciprocal(out=den, in_=den), r=["den"], w=["den"])
        P.op("dve", lambda e: e.tensor_tensor(out=self.oT[:, :, HALF:HALF + NS], in0=po[:, 0:64].rearrange("p (h n) -> p h n", n=NS), in1=den, op=ALU.mult),
             r=[("ps", bo), "den"], w=[("oT", HALF)])

    def layer(self, l, half):
        P = self.P
        last = (half == self.nh - 1)
        if half == 1 and "ss" not in SKIP:
            self.ssm_sample_load(l)
        self.load_prep(l)
        self.norm_to_h(l, 0)
        if half == 1 and KSTOP <= 1:
            return
        self.dump("hT", self.hT[:], l, half)
        self.proj_u(l, 0)
        self.dump("u_ssm", self.uT, l, half)
        self.ssm_core(l, half)
        wtf = self.Wtp[:].rearrange("p g k c -> p (g k c)")
        P.dma("sp", wtf[:, 0:2048], self.scr["kv"][l], r=["scr"], w=["kT", "vv", "ssmw"], sem="kld")
        self.dump("ygelu", self.oT, l, half)
        self.dump("Hst", self.Hst[:], l, half)
        self.dump("Wtp", self.Wtp[:], l, half)
        self.dump("Wco", self.Wco[:], l, half)
        self.dump("Wsb", self.Wsb, l, half)
        self.glu(l)
        self.dump("o_ssm", self.uT, l, half)
        self.merge(l, 0, self.uT, "uT")
        self.dump("mg0", self.mg, l, half)
        if half == 1 and KSTOP <= 2:
            return

        def cap(ps, b, m, t0, w):
            if last and t0 == 512:
                P.op("dve", lambda e: e.tensor_copy(out=self.lastu[:, m, :], in_=ps[:, 496:512]), r=[("ps", b)], w=["lastu"])
            if t0 == HALF:
                P.op("dve", lambda e: e.tensor_copy(out=self.usam[:, m, :], in_=ps[:, 0:NS]), r=[("ps", b)], w=["usam"])
        self.proj_u(l, 512, extra=cap)
        self.dump("u_pool", self.uT, l, half)
        self.pool_core(l, half)
        self.dump("o_pool", self.oT, l, half)
        self.merge(l, 1, self.oT, "oT")
        self.dump("mg1", self.mg, l, half)
        if half == 1 and KSTOP <= 3:
            return
        if half == 1:
            def qextra(ps, b, m, t0, w):
                pass

            def qchunk(wb, wres, c):
                b = self.bank(); ps = self.PSB[b]
                for k in range(KT):
                    P.op("pe", lambda e, k=k, ps=ps: e.matmul(ps[0:NS, 0:256], lhsT=self.hT[:, k, HALF:HALF + NS], rhs=wb[:, k, :], start=(k == 0), stop=(k == KT - 1)),
                         r=[wres, ("hT", HALF)], w=[("ps", b)])
                P.op("dve", lambda e, ps=ps, c=c: e.tensor_copy(out=self.qtok[:, c * 256:(c + 1) * 256], in_=ps[0:NS, 0:256]), r=[("ps", b)], w=["qtok"])
            qextra.chunk = qchunk
            self.proj_u(l, 1024, extra=qextra)
        else:
            self.proj_u(l, 1024)
        self.dump("q", self.uT, l, half)
        self.attn_core(l, half)
        self.dump("o_mem", self.oT, l, half)
        self.merge(l, 2, self.oT, "oT")
        self.dump("mg2", self.mg, l, half)
        if half == 1 and KSTOP <= 4:
            return
        self.out_proj(l)
        self.dump("x_mix", self.xT[:], l, half)
        if half == 1 and KSTOP <= 5:
            return
        self.ffn(l)


_CACHE = {}


def _host_consts():
    cst = np.zeros((128, 128 + 8 + 64 + 16), np.float32)
    cst[:, 0:128] = np.eye(128, dtype=np.float32)
    p = np.arange(128)
    for e2 in range(2):
        cst[:, 128 + e2] = ((p // 64) == e2)
        cst[:, 130 + e2] = (((p // 16) % 2) == e2)
    for m in range(4):
        cst[:, 132 + m] = ((p // 32) == m)
    for gi, wdw in enumerate((2, 4, 8, 16)):
        cst[:, 136 + gi * 16:136 + (gi + 1) * 16] = 1.0 / np.minimum(np.arange(16) + 1, wdw)
    cst[:, 200:202] = -cst[:, 128:130]
    sel = np.zeros((NS, NS, 128), np.float32)
    for n in range(NS):
        sel[n, n, :] = 1.0
    return cst, sel.reshape(NS, NS * 128)


def _layout_params(inp):
    f = lambda a: np.asarray(a, dtype=np.float32)
    g = np.stack([f(inp[k]) for k in ("g_mix_pre", "g_mix_post", "g_ffn_pre", "g_ffn_post", "g_mem")])
    gvec = g.reshape(5, DEPTH, KT, 128).transpose(3, 0, 1, 2).reshape(128, 5 * DEPTH * KT)
    v4 = np.stack([f(inp[k]) for k in ("ssm_d", "ssm_b_glu", "pool_scale")])
    vec4 = v4.reshape(3, DEPTH, 4, 128).transpose(3, 0, 1, 2).reshape(128, 3 * DEPTH * 4)
    lr, li, ld = f(inp["ssm_lam_re"]), f(inp["ssm_lam_im"]), f(inp["ssm_log_dt"])
    ldb = np.broadcast_to(ld[:, :, None], lr.shape)
    def ps_l(a):
        return a.reshape(DEPTH, 16, 2, 64).transpose(0, 2, 3, 1).reshape(DEPTH, 128, 16)
    lamPS = np.concatenate([ps_l(lr), ps_l(li), ps_l(ldb)], axis=2)
    br, bi = f(inp["ssm_b_re"]), f(inp["ssm_b_im"])
    cr, ci = f(inp["ssm_c_re"]), f(inp["ssm_c_im"])
    def ps_b(a):
        return a.reshape(DEPTH, 16, 2, 64, 16).transpose(0, 2, 3, 1, 4).reshape(DEPTH, 128, 256)
    def ps_c(a):
        return a.reshape(DEPTH, 16, 2, 16, 64).transpose(0, 2, 4, 1, 3).reshape(DEPTH, 128, 256)
    bcPS = np.concatenate([ps_b(br), ps_b(bi), ps_c(cr), ps_c(ci)], axis=2)
    def lb_l(a):
        t = a.reshape(DEPTH, 4, 8, 64).transpose(0, 2, 1, 3)
        t = np.broadcast_to(t[:, :, None], (DEPTH, 8, 16, 4, 64))
        return t.reshape(DEPTH, 128, 256)
    lamLB = np.concatenate([lb_l(lr), lb_l(li), lb_l(ldb)], axis=2)
    def lb_b(a):
        return a.reshape(DEPTH, 4, 8, 64, 16).transpose(0, 2, 4, 1, 3).reshape(DEPTH, 128, 256)
    bLB = np.concatenate([lb_b(br), lb_b(bi)], axis=2)
    c = np.ascontiguousarray
    return dict(gvec=c(gvec), vec4=c(vec4), sprm=c(np.concatenate([lamPS, bcPS, lamLB, bLB], axis=2)))


def _split_out(flat):
    flat = np.asarray(flat, dtype=np.float32).reshape(-1)
    return {nm: flat[off:off + int(np.prod(shp))].reshape(shp) for nm, (off, shp) in OUT_LAYOUT.items()}


def _shared_inputs(inputs):
    f = lambda a: np.ascontiguousarray(np.asarray(a, dtype=np.float32))
    cst, _ = _host_consts()
    prm = _layout_params(inputs)
    call = np.ascontiguousarray(np.concatenate([cst, prm["gvec"], prm["vec4"]], axis=1))
    return dict(w_in=f(inputs["w_in"]), w_kv=f(inputs["w_kv"]), w_glu=f(inputs["ssm_w_glu"]), pool_w=f(inputs["pool_w"]),
                w_up=f(inputs["w_branch_up"]), w_out=f(inputs["w_out"]), w_f1=f(inputs["w_ffn_in"]), w_f2=f(inputs["w_ffn_out"]),
                call=call, sprm=prm["sprm"])


def kernel(**inputs):
    f = lambda a: np.ascontiguousarray(np.asarray(a, dtype=np.float32))
    if "nc" not in _CACHE:
        _CACHE["nc"] = Builder().build()
    nc = _CACHE["nc"]
    shared = _shared_inputs(inputs)
    xp = f(inputs["x_prompt"]); xs = f(inputs["x_sample"]); mem = f(inputs["mem_prompt"])
    ck = f(inputs["cache_mem_k"]).reshape(DEPTH, 128, NMEM, 512); cv = f(inputs["cache_mem_v"]).reshape(DEPTH, 128, NMEM, 512)
    sre = f(inputs["state_ssm_re"]).reshape(DEPTH, 128, 2048); sim = f(inputs["state_ssm_im"]).reshape(DEPTH, 128, 2048)
    spool = f(inputs["state_pool"])
    in_maps = []
    for c in range(8):
        sl = slice(c * NS, (c + 1) * NS)
        m = dict(shared)
        m.update(xp=xp[c], xs=f(xs[sl, 0, :]), mem=mem[c], ck=f(ck[:, sl]), cv=f(cv[:, sl]), sre=f(sre[:, sl]), sim=f(sim[:, sl]), spool=f(spool[:, sl]))
        in_maps.append(m)
    res = run_bass_kernel_spmd(nc, in_maps, core_ids=list(range(8)))
    R = [_split_out(r["out"]) for r in res.results]
    cat = lambda k, ax: np.concatenate([np.asarray(r[k], dtype=np.float32) for r in R], axis=ax)
    st = lambda k: np.stack([np.asarray(r[k], dtype=np.float32) for r in R], axis=1)
    yp = np.stack([np.asarray(r["yp"], dtype=np.float32) for r in R], axis=0)
    ys = cat("ys", 0).reshape(128, 1, D)
    re_p = st("re_p").reshape(DEPTH, 8, 32, 64); im_p = st("im_p").reshape(DEPTH, 8, 32, 64)
    pool_p = st("pool_p")
    mk_p = st("mk_p").reshape(DEPTH, 8, NMEM, 4, 128); mv_p = st("mv_p").reshape(DEPTH, 8, NMEM, 4, 128)
    re_s = cat("re_s", 1).reshape(DEPTH, 128, 32, 64); im_s = cat("im_s", 1).reshape(DEPTH, 128, 32, 64)
    pool_s = cat("pool_s", 1)
    return (yp, ys, re_p, im_p, pool_p, mk_p, mv_p, re_s, im_s, pool_s)
```

```python
import numpy as np
import concourse.bass as bass
import concourse.mybir as mybir
from concourse.bass_utils import run_bass_kernel_spmd

F32 = mybir.dt.float32
BF16 = mybir.dt.bfloat16
ALU = mybir.AluOpType
AF = mybir.ActivationFunctionType
AX = mybir.AxisListType

ENGS = ("pe", "act", "dve", "pool", "sp")
import os as _os
SKIP = _os.environ.get("KSKIP", "").split(",")
TAPS2D = bool(int(_os.environ.get("TAPS2D", "0")))
KSS = int(_os.environ.get("KSS", "9"))
KSTOP = int(_os.environ.get("KSTOP", "99"))
FFNBAR = bool(int(_os.environ.get("FFNBAR", "0")))
DQ = _os.environ.get("DQ", "sp")

DEPTH = 4
D = 1024
KT = 8
SEQ = 2048
HALF = 1024
NS = 16
NT = HALF + NS
LCH = 8
NCH = HALF // LCH
OUT_LAYOUT = {}
_off = 0
for _nm, _shp in (("yp", (2048, 1024)), ("ys", (16, 1024)), ("re_p", (4, 2048)), ("im_p", (4, 2048)), ("pool_p", (4, 15, 512)), ("mk_p", (4, 256, 512)),
                  ("mv_p", (4, 256, 512)), ("re_s", (4, 16, 2048)), ("im_s", (4, 16, 2048)), ("pool_s", (4, 16, 15, 512))):
    OUT_LAYOUT[_nm] = (_off, _shp)
    _off += int(np.prod(_shp))
OUT_TOTAL = _off
DFF = 2816
FT = 22
NMEM = 256
EPS = 1e-6
PAST = 16384


class Prog:
    def __init__(self, nc):
        self.nc = nc
        self.ops = {e: [] for e in ENGS}
        self.count = {e: 0 for e in ENGS}
        self.waited = {e: {} for e in ENGS}
        self.last_w = {}
        self.readers = {}
        self.dma_cum = {}
        self.sems = {}
        self.final_tokens = []
        self._ctx = []
        self.barrier_tok = {e: [] for e in ENGS}

    def sb(self, name, shape, dt):
        g = self.nc.sbuf_tensor("sb_" + name, list(shape), dt)
        t = g.__enter__()
        self._ctx.append(g)
        return t

    def ps(self, name, shape, dt=F32):
        g = self.nc.psum_tensor("ps_" + name, list(shape), dt)
        t = g.__enter__()
        self._ctx.append(g)
        return t

    def _need(self, eng, tok, waits):
        if tok is None:
            return
        key, val = tok
        if key == ("eng", eng) and eng in ("pe", "sp"):
            return
        if self.waited[eng].get(key, 0) >= val:
            return
        self.waited[eng][key] = val
        waits.append((key, val))

    def _deps(self, eng, r, w):
        waits = []
        for t in self.barrier_tok[eng]:
            self._need(eng, t, waits)
        self.barrier_tok[eng] = []
        for x in r:
            self._need(eng, self.last_w.get(x), waits)
        for x in w:
            self._need(eng, self.last_w.get(x), waits)
            for t in self.readers.get(x, ()):
                self._need(eng, t, waits)
        return waits

    def _commit(self, tok, r, w):
        for x in r:
            lst = self.readers.setdefault(x, [])
            lst[:] = [t for t in lst if t[0] != tok[0]] + [tok]
        for x in w:
            self.last_w[x] = tok
            self.readers[x] = []

    def op(self, eng, fn, r=(), w=()):
        psr = [x for x in r if isinstance(x, tuple) and x[0] == "ps" and x not in w]
        if psr:
            w = list(w) + psr
        waits = self._deps(eng, r, w)
        self.count[eng] += 1
        tok = (("eng", eng), self.count[eng])
        self.ops[eng].append((waits, fn, "op", None))
        self._commit(tok, r, w)
        return tok

    def dma(self, eng, out, in_, r=(), w=(), sem="dma", final=False, **kw):
        waits = self._deps(eng, r, w)
        key = ("dma", sem)
        self.dma_cum[key] = self.dma_cum.get(key, 0) + 16
        tok = (key, self.dma_cum[key])
        self.ops[eng].append((waits, lambda e: e.dma_start(out=out, in_=in_, **kw), "dma", key))
        self._commit(tok, r, w)
        if final:
            self.final_tokens.append(tok)
        return tok

    def barrier(self):
        toks = [(("eng", o), self.count[o]) for o in ENGS if o != "sp" and self.count[o] > 0]
        toks += [(k, v) for k, v in self.dma_cum.items()]
        for e in ENGS:
            self.barrier_tok[e] = list(toks)

    def finalize(self):
        nc = self.nc
        keys = [("eng", e) for e in ENGS if e != "sp"] + list(self.dma_cum.keys())
        guards = []
        for k in keys:
            g = nc.semaphore("s_" + "_".join(str(x) for x in k))
            self.sems[k] = g.__enter__()
            guards.append(g)
        seen = {}
        for key, val in self.final_tokens:
            seen[key] = max(seen.get(key, 0), val)
        final_waits = list(seen.items())
        blk = nc.Block()
        block = blk.__enter__()

        def runner(ename):
            def run(e):
                mysem = self.sems.get(("eng", ename))
                for waits, fn, kind, key in self.ops[ename]:
                    for k, v in waits:
                        e.wait_ge(self.sems[k], v)
                    inst = fn(e)
                    if kind == "dma":
                        inst.then_inc(self.sems[key], 16)
                    elif mysem is not None:
                        inst.then_inc(mysem, 1)
                if ename == "sp":
                    for k, v in final_waits:
                        e.wait_ge(self.sems[k], v)
            return run

        block.tensor(runner("pe"))
        block.scalar(runner("act"))
        block.vector(runner("dve"))
        block.gpsimd(runner("pool"))
        block.sync(runner("sp"))
        blk.__exit__(None, None, None)
        for g in reversed(guards):
            g.__exit__(None, None, None)
        for g in reversed(self._ctx):
            g.__exit__(None, None, None)


class DummyProg:
    def op(self, *a, **k):
        return None

    def dma(self, *a, **k):
        return None

    def barrier(self):
        pass


class Builder:
    def __init__(self, nlayers=DEPTH, nhalves=2, dbg=None):
        self.nl = nlayers
        self.nh = nhalves
        self.dbg = dbg
        nc = bass.Bass("TRN2", target_bir_lowering=False)
        self.nc = nc
        self.P = Prog(nc)
        self.din = {}
        self.dout = {}
        self.nsem = 0
        self._q = None
        self._xr = ()
        self._xw = ()

    def inp(self, name, shape):
        self.din[name] = self.nc.dram_tensor(name, list(shape), F32, kind="ExternalInput").ap()
        return self.din[name]

    def outp(self, name, shape):
        self.dout[name] = self.nc.dram_tensor(name, list(shape), F32, kind="ExternalOutput").ap()
        return self.dout[name]

    def build(self):
        P = self.P
        nc = self.nc
        i = self.inp
        xp = i("xp", [SEQ, D]); xs = i("xs", [NS, D]); mem = i("mem", [NMEM, D])
        ck = i("ck", [DEPTH, NS, NMEM, 512]); cv = i("cv", [DEPTH, NS, NMEM, 512])
        sre = i("sre", [DEPTH, NS, 2048]); sim = i("sim", [DEPTH, NS, 2048])
        spool = i("spool", [DEPTH, NS, 15, 512])
        self.w_in = i("w_in", [DEPTH, D, 4608]); self.w_kv = i("w_kv", [DEPTH, D, 1024])
        self.w_glu = i("w_glu", [DEPTH, 512, 512]); self.pool_w = i("pool_w", [DEPTH, 4, 128, 128])
        self.w_up = i("w_up", [DEPTH, 3, 512, D]); self.w_out = i("w_out", [DEPTH, D, D])
        self.w_f1 = i("w_f1", [DEPTH, D, 2 * DFF]); self.w_f2 = i("w_f2", [DEPTH, DFF, D])
        call_d = i("call", [128, 216 + 160 + 48])
        cst_d = call_d[:, 0:216]; gvec_d = call_d[:, 216:376]; vec4_d = call_d[:, 376:424]
        sprm_d = i("sprm", [DEPTH, 128, 2352])
        out_d = self.outp("out", [OUT_TOTAL])
        ov = {}
        for nm, (off, shp) in OUT_LAYOUT.items():
            n = int(np.prod(shp))
            v = out_d[off:off + n]
            if len(shp) == 2:
                v = v.rearrange("(a b) -> a b", b=shp[1])
            elif len(shp) == 3:
                v = v.rearrange("(a b c) -> a b c", b=shp[1], c=shp[2])
            elif len(shp) == 4:
                v = v.rearrange("(a b c d) -> a b c d", b=shp[1], c=shp[2], d=shp[3])
            ov[nm] = v
        yp, ys, re_p, im_p, pool_p, mk_p, mv_p, re_s, im_s, pool_s = [ov[k] for k in ("yp", "ys", "re_p", "im_p", "pool_p", "mk_p", "mv_p", "re_s", "im_s", "pool_s")]
        self.o = dict(yp=yp, ys=ys, re_p=re_p, im_p=im_p, pool_p=pool_p, mk_p=mk_p, mv_p=mv_p,
                      re_s=re_s, im_s=im_s, pool_s=pool_s)
        self.i = dict(xp=xp, xs=xs, mem=mem, ck=ck, cv=cv, sre=sre, sim=sim, spool=spool,
                      sprm=sprm_d)

        self.xT = P.sb("xT", [128, KT, NT], F32)
        self.hT = P.sb("hT", [128, KT, NT], BF16)
        self.A = P.sb("arena", [128, FT * NT], BF16)
        A = self.A
        self.aT = A[:, 0:FT * NT].rearrange("p (k n) -> p k n", n=NT)
        self.mg = A[:, 0:8 * NT].rearrange("p (k n) -> p k n", n=NT)
        self.uT = A[:, 8 * NT:12 * NT].rearrange("p (k n) -> p k n", n=NT)
        self.oT = A[:, 12 * NT:16 * NT].rearrange("p (k n) -> p k n", n=NT)
        self.Hbf = A[:, 16 * NT:16 * NT + 16 * 2 * (NCH + 1)].rearrange("p (a r c) -> p a r c", a=16, r=2)
        self.Wsb = A[:, 0:8192].rearrange("p (g k r c) -> p g k r c", g=4, k=8, r=2)
        self.Tc = A[:, 12 * NT:16 * NT].bitcast(F32)[:, 0:2048].rearrange("p (a c) -> p a c", c=NCH)
        self.Ts = A[:, 16 * NT:16 * NT + 4128].bitcast(F32)[:, 0:2048].rearrange("p (a c) -> p a c", c=NCH)
        self.Hst = P.sb("Hst", [128, 16, 2, NCH + 2], F32)
        self.Wco = P.sb("Wco", [128, 16, 9, 2, 32], BF16)
        self.Wtp = P.sb("Wtp", [128, 4, 8, 128], BF16)
        self.zb = P.sb("zb", [128, KT, 512], F32)
        self.stg = [P.sb("stg%d" % j, [128, 2048], F32) for j in range(2)]
        self.wbf = [P.sb("wbf%d" % j, [128, 2048], BF16) for j in range(3)]
        self.gvec = P.sb("gvec", [128, 5, DEPTH, KT], F32)
        self.vec4 = P.sb("vec4", [128, 3, DEPTH, 4], F32)
        self.cst = P.sb("cst", [128, 128 + 8 + 64 + 16], F32)
        self.identb = P.sb("identb", [128, 128], BF16)
        self.onesb = P.sb("onesb", [128, 128], BF16)
        self.onesf = P.sb("onesf", [128, 128], F32)
        self.rstd = P.sb("rstd", [128, 512], F32)
        self.sq = [P.sb("sq%d" % j, [128, 512], BF16) for j in range(2)]
        self.tmpA = [P.sb("tmpA%d" % j, [128, 512], F32) for j in range(2)]
        self.tmpB = [P.sb("tmpB%d" % j, [128, 512], F32) for j in range(2)]
        self.carry = P.sb("carry", [128, DEPTH, 16, 2], F32)
        self.phist = P.sb("phist", [128, DEPTH, 4, 16], F32)
        wtf = self.Wtp[:].rearrange("p g k c -> p (g k c)")
        self.kT = wtf[:, 0:1024].rearrange("p (h m) -> p h m", m=NMEM)
        self.vv = wtf[:, 1024:2048].rearrange("p (t c) -> p t c", c=512)
        self.qtok = wtf[0:NS, 2048:3072].bitcast(F32)
        self.memn = P.sb("memn", [128, KT, NMEM], BF16)
        self.memh = A[:, 16 * NT + 4128:16 * NT + 4128 + 2048].rearrange("p (k m) -> p k m", m=NMEM)
        self.epsb = P.sb("epsb", [128, 2], F32)
        self.lastu = P.sb("lastu", [128, 4, 16], F32)
        self.usam = P.sb("usam", [128, 4, 16], F32)
        self.small = P.sb("small", [128, 768], F32)
        self.sA = P.sb("sA", [128, 512], F32)
        self.sB = P.sb("sB", [128, 512], F32)
        self.a1rho = P.sb("a1rho", [128, 48], F32)
        self.rho = self.a1rho[:, 32:48]
        self.A8 = P.sb("A8", [128, 2, 16], F32)
        self.A1 = self.a1rho[:, 0:32].rearrange("p (r a) -> p r a", r=2)
        self.pw = self.zb[:].rearrange("p k c -> p (k c)")[:, 1024:1024 + 2080]
        self.prm = self.Hst[:].rearrange("p a r c -> p (a r c)")[:, 0:2400]
        self.PSB = [P.ps("psb%d" % j, [128, 512]) for j in range(8)]
        self.ident = self.cst[:, 0:128]
        dt_ = lambda nm, shp, dt: self.nc.dram_tensor(nm, shp, dt, kind="Internal").ap()
        self.scr = dict(sb=dt_("scr_sb", [DEPTH, 128, 8192], BF16), co=dt_("scr_co", [DEPTH, 128, 9216], BF16),
                        tp=dt_("scr_tp", [DEPTH, 128, 4096], BF16), t=dt_("scr_t", [DEPTH, 128, 4144], F32),
                        kv=dt_("scr_kv", [DEPTH, 128, 2048], BF16))
        self.dd = dict(cst_d=cst_d, gvec_d=gvec_d, vec4_d=vec4_d)
        realP = self.P
        self.P = DummyProg()
        self.specs = None
        self.rec = []
        self.emit()
        self.specs = self.rec
        self.P = realP
        self.emit()
        self.P.finalize()
        return nc

    def emit(self):
        P = self.P
        self.rot = 0
        self.cnt = 0
        self.stg_i = 0
        self.wbf_i = 0
        self.wl_i = 0
        self.wl_issued = 0
        ident = self.ident
        cst_d, gvec_d, vec4_d = self.dd["cst_d"], self.dd["gvec_d"], self.dd["vec4_d"]
        P.dma("sp", self.cst[:], cst_d, w=["cst"], sem="c0")
        P.dma("sp", self.gvec[:].rearrange("p a l k -> p (a l k)"), gvec_d, w=["gvec"], sem="c1")
        P.dma("sp", self.vec4[:].rearrange("p a l k -> p (a l k)"), vec4_d, w=["vec4"], sem="c2")
        P.op("dve", lambda e: e.tensor_copy(out=self.identb[:], in_=ident), r=["cst"], w=["identb"])
        P.op("pool", lambda e: e.memset(self.onesb[:], 1.0), w=["onesb"])
        P.op("pool", lambda e: e.memset(self.onesf[:], 1.0), w=["onesf"])
        P.op("pool", lambda e: e.memset(self.epsb[:, 0:1], EPS), w=["epsb"])
        P.op("pool", lambda e: e.memset(self.epsb[:, 1:2], float(np.pi / 2)), w=["epsb"])
        P.op("pool", lambda e: e.memset(self.carry[:], 0.0), w=["carry"])
        P.op("pool", lambda e: e.memset(self.phist[:], 0.0), w=["phist"])

        self.prep_mem()
        P.barrier()
        kvst = self.A[:, 8 * NT:8 * NT + 2048]
        save_kv = (self.kT, self.vv)
        self.kT = kvst[:, 0:1024].rearrange("p (h m) -> p h m", m=NMEM)
        self.vv = kvst[:, 1024:2048].rearrange("p (t c) -> p t c", c=512)
        for l in range(self.nl):
            self.kv(l, 0)
            P.dma("sp", self.scr["kv"][l], kvst, r=["kT", "vv"], sem="kst")
            self.ssm_prep(l)
        self.kT, self.vv = save_kv
        P.barrier()
        P.op("dve", lambda e: e.memset(self.small[:, 760:761], 0.0), w=["scr"])
        for half in range(self.nh):
            self.half = half
            self.ntok = HALF + (NS if half == 1 else 0)
            self.tbs = [(0, 512), (512, 512)] + ([(HALF, NS)] if half == 1 else [])
            self.load_x(half)
            P.barrier()
            for l in range(self.nl):
                self.layer(l, half)
            P.barrier()
            if not (half == 1 and KSTOP <= 6):
                self.store_y(half)

    def bank(self):
        b = self.rot
        self.rot = (self.rot + 1) % 6
        return b

    def uid(self):
        self.cnt += 1
        return self.cnt

    def cast_eng(self):
        return "pool"

    def _issue(self, idx):
        P = self.P
        pieces, kt, ncols = self.specs[idx]
        si = idx % 2
        bi = idx % 3
        st = self.stg[si][:, 0:kt * ncols].rearrange("p (k c) -> p k c", c=ncols)
        wb = self.wbf[bi][:, 0:kt * ncols].rearrange("p (k c) -> p k c", c=ncols)
        c0 = 0
        for ap in pieces:
            c = ap.shape[-1]
            P.dma("sp", st[:, :, c0:c0 + c], ap.rearrange("(k p) c -> p k c", p=128), w=[("stg", si)], sem="stg%d" % si)
            c0 += c
        if idx % 2 == 0:
            P.op("act", lambda e: e.activation(out=wb, in_=st, func=AF.Copy), r=[("stg", si)], w=[("wbf", bi)])
        else:
            P.op("dve", lambda e: e.tensor_copy(out=wb, in_=st), r=[("stg", si)], w=[("wbf", bi)])

    def wload(self, pieces, kt, ncols):
        idx = self.wl_i
        self.wl_i += 1
        bi = idx % 3
        wb = self.wbf[bi][:, 0:kt * ncols].rearrange("p (k c) -> p k c", c=ncols)
        if self.specs is None:
            self.rec.append((pieces, kt, ncols))
            return wb, ("wbf", bi)
        while self.wl_issued <= min(idx + 1, len(self.specs) - 1):
            self._issue(self.wl_issued)
            self.wl_issued += 1
        return wb, ("wbf", bi)

    def norm_rstd(self, src_fn, src_res, ntiles, w, tag):
        P = self.P
        pb = 6 + (self.uid() % 2)
        ps = self.PSB[pb]
        for k in range(ntiles):
            sq = self.sq[k % 2]
            P.op("act", lambda e, k=k, sq=sq: e.activation(out=sq[:, :w], in_=src_fn(k), func=AF.Square),
                 r=[src_res(k)], w=[("sq", k % 2)])
            P.op("pe", lambda e, k=k, sq=sq: e.matmul(ps[:, :w], lhsT=self.onesb[:], rhs=sq[:, :w], start=(k == 0), stop=(k == ntiles - 1)),
                 r=[("sq", k % 2), "onesb"], w=[("ps", pb)])
        P.op("act", lambda e: e.activation(out=self.rstd[:, :w], in_=ps[:, :w], func=AF.Sqrt, scale=1.0 / D, bias=self.epsb[:, 0:1]),
             r=[("ps", pb), "epsb"], w=["rstd"])
        P.op("dve", lambda e: e.reciprocal(out=self.rstd[:, :w], in_=self.rstd[:, :w]), r=["rstd"], w=["rstd"])

    def dump(self, name, ap, l=0, half=0):
        if not self.dbg or l != 0 or half != 0 or isinstance(self.P, DummyProg):
            return
        shp = list(ap.shape)
        d = self.nc.dram_tensor("dbg_" + name, shp, ap.dtype, kind="ExternalOutput").ap()
        self.P.barrier()
        self.P.dma("sp", d, ap, sem="dbg_" + name, final=True)
        self.P.barrier()

    def load_x(self, half):
        P = self.P
        xp = self.i["xp"]
        zb4 = self.zb[:].rearrange("p k c -> p (k c)").rearrange("p (j c) -> p j c", c=D)
        for blk in range(2):
            t0 = blk * 512
            r0 = half * HALF + t0
            P.dma("sp", zb4, xp[r0:r0 + 512, :].rearrange("(j p) c -> p j c", p=128), w=["zb"], sem="zb")
            for k in range(KT):
                b = self.bank(); ps = self.PSB[b]
                for j in range(4):
                    P.op("pe", lambda e, j=j, k=k, ps=ps: e.transpose(ps[:, j * 128:(j + 1) * 128], zb4[:, j, k * 128:(k + 1) * 128], self.ident),
                         r=["zb", "cst"], w=[("ps", b)])
                eng = "act" if k % 2 else "dve"
                if eng == "act":
                    P.op("act", lambda e, k=k, ps=ps, t0=t0: e.activation(out=self.xT[:, k, t0:t0 + 512], in_=ps[:, :], func=AF.Copy), r=[("ps", b)], w=[("xT", t0)])
                else:
                    P.op("dve", lambda e, k=k, ps=ps, t0=t0: e.tensor_copy(out=self.xT[:, k, t0:t0 + 512], in_=ps[:, :]), r=[("ps", b)], w=[("xT", t0)])
        if half == 1:
            xs = self.i["xs"]
            P.dma("sp", zb4[0:NS, 0, :], xs, w=["zb"], sem="zb")
            b = self.bank(); ps = self.PSB[b]
            for k in range(KT):
                P.op("pe", lambda e, k=k, ps=ps: e.transpose(ps[:, k * NS:(k + 1) * NS], zb4[0:NS, 0, k * 128:(k + 1) * 128], self.ident[0:NS, 0:NS]),
                     r=["zb", "cst"], w=[("ps", b)])
            P.op("dve", lambda e, ps=ps: e.tensor_copy(out=self.xT[:, :, HALF:HALF + NS], in_=ps[:, 0:KT * NS].rearrange("p (k n) -> p k n", n=NS)),
                 r=[("ps", b)], w=[("xT", HALF)])

    def store_y(self, half):
        P = self.P
        yp = self.o["yp"]
        zb4 = self.zb[:].rearrange("p k c -> p (k c)").rearrange("p (j c) -> p j c", c=D)
        for blk in range(2):
            t0 = blk * 512
            r0 = half * HALF + t0
            for j in range(4):
                for kk in range(2):
                    b = self.bank(); ps = self.PSB[b]
                    for k4 in range(4):
                        k = kk * 4 + k4
                        P.op("pe", lambda e, j=j, k=k, k4=k4, ps=ps, t0=t0: e.transpose(ps[:, k4 * 128:(k4 + 1) * 128], self.xT[:, k, t0 + j * 128:t0 + (j + 1) * 128], self.ident),
                             r=[("xT", t0), "cst"], w=[("ps", b)])
                    if kk:
                        P.op("act", lambda e, j=j, kk=kk, ps=ps: e.activation(out=zb4[:, j, kk * 512:(kk + 1) * 512], in_=ps[:, :], func=AF.Copy), r=[("ps", b)], w=["zb"])
                    else:
                        P.op("dve", lambda e, j=j, kk=kk, ps=ps: e.tensor_copy(out=zb4[:, j, kk * 512:(kk + 1) * 512], in_=ps[:, :]), r=[("ps", b)], w=["zb"])
            P.dma("sp", yp[r0:r0 + 512, :].rearrange("(j p) c -> p j c", p=128), zb4, r=["zb"], sem="yout", final=True)
        if half == 1:
            ys = self.o["ys"]
            for kk in range(2):
                b = self.bank(); ps = self.PSB[b]
                for k4 in range(4):
                    k = kk * 4 + k4
                    P.op("pe", lambda e, k=k, k4=k4, ps=ps: e.transpose(ps[0:NS, k4 * 128:(k4 + 1) * 128], self.xT[:, k, HALF:HALF + NS], self.ident),
                         r=[("xT", HALF), "cst"], w=[("ps", b)])
                P.op("dve", lambda e, kk=kk, ps=ps: e.tensor_copy(out=zb4[0:NS, 0, kk * 512:(kk + 1) * 512], in_=ps[0:NS, :]), r=[("ps", b)], w=["zb"])
            P.dma("sp", ys, zb4[0:NS, 0, :], r=["zb"], sem="yout", final=True)

    def prep_mem(self):
        P = self.P
        mem = self.i["mem"]
        zb2 = self.zb[:].rearrange("p k c -> p (k c)").rearrange("p (j c) -> p j c", c=D)
        P.dma("sp", zb2[:, 0:2, :], mem.rearrange("(j p) c -> p j c", p=128), w=["zb"], sem="zb")
        ss = self.small[:, 0:2]
        for j in range(2):
            P.op("act", lambda e, j=j: e.activation(out=zb2[:, 2 + j, :], in_=zb2[:, j, :], func=AF.Square, accum_out=ss[:, j:j + 1]), r=["zb"], w=["zb", "small"])
        P.op("act", lambda e: e.activation(out=ss, in_=ss, func=AF.Sqrt, scale=1.0 / D, bias=self.epsb[:, 0:1]), r=["small", "epsb"], w=["small"])
        P.op("dve", lambda e: e.reciprocal(out=ss, in_=ss), r=["small"], w=["small"])
        for j in range(2):
            P.op("dve", lambda e, j=j: e.tensor_scalar(out=zb2[:, j, :], in0=zb2[:, j, :], scalar1=ss[:, j:j + 1], scalar2=None, op0=ALU.mult), r=["zb", "small"], w=["zb"])
        for k in range(KT):
            b = self.bank(); ps = self.PSB[b]
            for j in range(2):
                P.op("pe", lambda e, j=j, k=k, ps=ps: e.transpose(ps[:, j * 128:(j + 1) * 128], zb2[:, j, k * 128:(k + 1) * 128], self.ident), r=["zb", "cst"], w=[("ps", b)])
            P.op("dve", lambda e, k=k, ps=ps: e.tensor_copy(out=self.memn[:, k, :], in_=ps[:, 0:NMEM]), r=[("ps", b)], w=["memn"])

    def kv(self, l, half):
        P = self.P
        kT_, vv_ = self.kT, self.vv
        for k in range(KT):
            P.op("dve", lambda e, k=k: e.tensor_scalar(out=self.memh[:, k, :], in0=self.memn[:, k, :], scalar1=self.gvec[:, 4, l, k:k + 1], scalar2=None, op0=ALU.mult),
                 r=["memn", "gvec"], w=["memh"])
        for c in range(4):
            wb, wres = self.wload([self.w_kv[l][:, c * 256:(c + 1) * 256]], KT, 256)
            if c < 2:
                for hh in range(2):
                    b = self.bank(); ps = self.PSB[b]
                    for k in range(KT):
                        P.op("pe", lambda e, k=k, hh=hh, ps=ps, wb=wb: e.matmul(ps[:, 0:NMEM], lhsT=wb[:, k, hh * 128:(hh + 1) * 128], rhs=self.memh[:, k, :], start=(k == 0), stop=(k == KT - 1)),
                             r=[wres, "memh"], w=[("ps", b)])
                    P.op("act", lambda e, hh=hh, ps=ps, c=c: e.activation(out=kT_[:, 2 * c + hh, :], in_=ps[:, 0:NMEM], func=AF.Copy), r=[("ps", b)], w=["kT"])
            if c >= 2 or half == 0:
                for mt in range(2):
                    b = self.bank(); ps = self.PSB[b]
                    for k in range(KT):
                        P.op("pe", lambda e, k=k, mt=mt, ps=ps, wb=wb: e.matmul(ps[:, 0:256], lhsT=self.memh[:, k, mt * 128:(mt + 1) * 128], rhs=wb[:, k, :], start=(k == 0), stop=(k == KT - 1)),
                             r=[wres, "memh"], w=[("ps", b)])
                    if c >= 2:
                        P.op("act", lambda e, mt=mt, ps=ps, c=c: e.activation(out=vv_[:, mt, (c - 2) * 256:(c - 1) * 256], in_=ps[:, 0:256], func=AF.Copy), r=[("ps", b)], w=["vv"])
                    if half == 0:
                        t = self.tmpA[mt]
                        P.op("dve", lambda e, ps=ps, t=t: e.tensor_copy(out=t[:, 0:256], in_=ps[:, 0:256]), r=[("ps", b)], w=[("tmpA", mt)])
                        dst = self.o["mk_p"] if c < 2 else self.o["mv_p"]
                        cc = c % 2
                        P.dma("sp", dst[l][mt * 128:(mt + 1) * 128, cc * 256:(cc + 1) * 256], t[:, 0:256], r=[("tmpA", mt)], sem="kvout%d" % mt, final=True)

    def norm_to_h(self, l, kind):
        P = self.P
        for (t0, w) in self.tbs:
            self.norm_rstd(lambda k, t0=t0, w=w: self.xT[:, k, t0:t0 + w], lambda k, t0=t0: ("xT", t0), KT, w, "n")
            for k in range(KT):
                eng = "dve"
                if eng == "dve":
                    P.op("dve", lambda e, k=k, t0=t0, w=w: e.scalar_tensor_tensor(out=self.hT[:, k, t0:t0 + w], in0=self.xT[:, k, t0:t0 + w], scalar=self.gvec[:, kind, l, k:k + 1],
                                                                                 in1=self.rstd[:, :w], op0=ALU.mult, op1=ALU.mult),
                         r=[("xT", t0), "rstd", "gvec"], w=[("hT", t0)])
                else:
                    P.op("pool", lambda e, k=k, t0=t0, w=w: e.scalar_tensor_tensor(out=self.hT[:, k, t0:t0 + w], in0=self.xT[:, k, t0:t0 + w], scalar=self.gvec[:, kind, l, k:k + 1],
                                                                                  in1=self.rstd[:, :w], op0=ALU.mult, op1=ALU.mult),
                         r=[("xT", t0), "rstd", "gvec"], w=[("hT", t0)])

    def mm_fm(self, wb, wres, kt, nm, src, src_name, evac, m0=0):
        P = self.P
        for mi in range(nm):
            for (t0, w) in self.tbs:
                b = self.bank(); ps = self.PSB[b]
                for k in range(kt):
                    P.op("pe", lambda e, k=k, mi=mi, ps=ps, t0=t0, w=w: e.matmul(ps[:, :w], lhsT=wb[:, k, mi * 128:(mi + 1) * 128], rhs=src[:, k, t0:t0 + w],
                                                                                start=(k == 0), stop=(k == kt - 1)),
                         r=[wres, (src_name, t0)], w=[("ps", b)])
                evac(ps, b, m0 + mi, t0, w)

    def proj_u(self, l, col0, extra=None):
        P = self.P
        for c in range(2):
            wb, wres = self.wload([self.w_in[l][:, col0 + c * 256: col0 + (c + 1) * 256]], KT, 256)

            def evac(ps, b, m, t0, w):
                if (m + (t0 // 512)) % 2 == 0:
                    P.op("act", lambda e: e.activation(out=self.uT[:, m, t0:t0 + w], in_=ps[:, :w], func=AF.Copy), r=[("ps", b)], w=[("uT", t0)])
                else:
                    P.op("dve", lambda e: e.tensor_copy(out=self.uT[:, m, t0:t0 + w], in_=ps[:, :w]), r=[("ps", b)], w=[("uT", t0)])
                if extra is not None:
                    extra(ps, b, m, t0, w)
            self.mm_fm(wb, wres, KT, 2, self.hT, "hT", evac, m0=2 * c)
            if extra is not None and hasattr(extra, "chunk"):
                extra.chunk(wb, wres, c)

    def merge(self, l, br, src, src_name):
        P = self.P
        for c2 in range(2):
            for cg in range(2):
                ucol = c2 * 512 + cg * 256
                wu, wures = self.wload([self.w_up[l, br][:, ucol:ucol + 256]], 4, 256)
                gcol = 1536 + br * 1024 + ucol
                wg, wgres = self.wload([self.w_in[l][:, gcol:gcol + 256]], KT, 256)
                for mi in range(2):
                    m = c2 * 4 + cg * 2 + mi
                    for (t0, w) in self.tbs:
                        bg = self.bank(); pg = self.PSB[bg]
                        for k in range(KT):
                            P.op("pe", lambda e, k=k, mi=mi, pg=pg, t0=t0, w=w, wg=wg: e.matmul(pg[:, :w], lhsT=wg[:, k, mi * 128:(mi + 1) * 128], rhs=self.hT[:, k, t0:t0 + w],
                                                                                                start=(k == 0), stop=(k == KT - 1)),
                                 r=[wgres, ("hT", t0)], w=[("ps", bg)])
                        si = self.uid() % 2
                        sg = self.tmpB[si]
                        P.op("act", lambda e, pg=pg, sg=sg, w=w: e.activation(out=sg[:, :w], in_=pg[:, :w], func=AF.Sigmoid), r=[("ps", bg)], w=[("tmpB", si)])
                        bu = self.bank(); pu = self.PSB[bu]
                        mu = mi
                        for k in range(4):
                            P.op("pe", lambda e, k=k, mu=mu, pu=pu, t0=t0, w=w, wu=wu: e.matmul(pu[:, :w], lhsT=wu[:, k, mu * 128:(mu + 1) * 128], rhs=src[:, k, t0:t0 + w],
                                                                                                start=(k == 0), stop=(k == 3)),
                                 r=[wures, (src_name, t0)], w=[("ps", bu)])
                        if br == 0:
                            P.op("dve", lambda e, pu=pu, sg=sg, m=m, t0=t0, w=w: e.tensor_tensor(out=self.mg[:, m, t0:t0 + w], in0=pu[:, :w], in1=sg[:, :w], op=ALU.mult),
                                 r=[("ps", bu), ("tmpB", si)], w=[("mg", t0)])
                        else:
                            P.op("dve", lambda e, pu=pu, sg=sg, w=w: e.tensor_tensor(out=sg[:, :w], in0=pu[:, :w], in1=sg[:, :w], op=ALU.mult),
                                 r=[("ps", bu), ("tmpB", si)], w=[("tmpB", si)])
                            P.op("pool", lambda e, sg=sg, m=m, t0=t0, w=w: e.tensor_tensor(out=self.mg[:, m, t0:t0 + w], in0=self.mg[:, m, t0:t0 + w], in1=sg[:, :w], op=ALU.add),
                                 r=[("tmpB", si), ("mg", t0)], w=[("mg", t0)])

    def post_norm_add(self, l, kind, zT, zname):
        P = self.P
        for (t0, w) in self.tbs:
            self.norm_rstd(lambda k, t0=t0, w=w: zT[:, k, t0:t0 + w], lambda k, t0=t0: (zname, t0), KT, w, "p")
            for k in range(KT):
                si = self.uid() % 2
                t = self.tmpA[si]
                P.op("dve", lambda e, k=k, t=t, t0=t0, w=w: e.scalar_tensor_tensor(out=t[:, :w], in0=zT[:, k, t0:t0 + w], scalar=self.gvec[:, kind, l, k:k + 1], in1=self.rstd[:, :w],
                                                                                  op0=ALU.mult, op1=ALU.mult),
                     r=[(zname, t0), "rstd", "gvec"], w=[("tmpA", si)])
                P.op("pool", lambda e, k=k, t=t, t0=t0, w=w: e.tensor_tensor(out=self.xT[:, k, t0:t0 + w], in0=self.xT[:, k, t0:t0 + w], in1=t[:, :w], op=ALU.add),
                     r=[("tmpA", si), ("xT", t0)], w=[("xT", t0)])

    def out_proj(self, l):
        P = self.P
        zT = self.A[:, 8 * NT:16 * NT].rearrange("p (k n) -> p k n", n=NT)
        for c in range(4):
            wb, wres = self.wload([self.w_out[l][:, c * 256:(c + 1) * 256]], KT, 256)

            def evac(ps, b, m, t0, w):
                if (m + t0 // 512) % 2 == 0:
                    P.op("act", lambda e: e.activation(out=zT[:, m, t0:t0 + w], in_=ps[:, :w], func=AF.Copy), r=[("ps", b)], w=[("zT", t0), ("uT", t0), ("oT", t0)])
                else:
                    P.op("dve", lambda e: e.tensor_copy(out=zT[:, m, t0:t0 + w], in_=ps[:, :w]), r=[("ps", b)], w=[("zT", t0), ("uT", t0), ("oT", t0)])
            self.mm_fm(wb, wres, KT, 2, self.mg, "mg", evac, m0=2 * c)
        self.post_norm_add(l, 1, zT, "zT")

    def ffn(self, l):
        P = self.P
        self.norm_to_h(l, 2)
        for j in range(FT):
            wb, wres = self.wload([self.w_f1[l][:, j * 128:(j + 1) * 128], self.w_f1[l][:, DFF + j * 128:DFF + (j + 1) * 128]], KT, 256)
            for (t0, w) in self.tbs:
                ba = self.bank(); pa = self.PSB[ba]
                bb = self.bank(); pb = self.PSB[bb]
                for mi, (bx, px) in enumerate(((ba, pa), (bb, pb))):
                    for k in range(KT):
                        P.op("pe", lambda e, k=k, mi=mi, px=px, t0=t0, w=w, wb=wb: e.matmul(px[:, :w], lhsT=wb[:, k, mi * 128:(mi + 1) * 128], rhs=self.hT[:, k, t0:t0 + w],
                                                                                            start=(k == 0), stop=(k == KT - 1)),
                             r=[wres, ("hT", t0)], w=[("ps", bx)])
                si = self.uid() % 2
                sg = self.tmpB[si]
                P.op("act", lambda e, pa=pa, sg=sg, w=w: e.activation(out=sg[:, :w], in_=pa[:, :w], func=AF.Silu), r=[("ps", ba)], w=[("tmpB", si)])
                if j < 8:
                    al = [("mg", t0)]
                elif j < 16:
                    al = [("zT", t0), ("uT", t0), ("oT", t0)]
                else:
                    al = [("Hbf", 0), ("Hbf", 1), ("Hbf", 2), ("Hbf", 3), "memh", "ssmw"]
                P.op("dve", lambda e, pb=pb, sg=sg, j=j, t0=t0, w=w: e.tensor_tensor(out=self.aT[:, j, t0:t0 + w], in0=pb[:, :w], in1=sg[:, :w], op=ALU.mult),
                     r=[("ps", bb), ("tmpB", si)], w=[("aT", t0)] + al)
        zT = self.Hst[:].rearrange("p a r c -> p (a r c)").bitcast(BF16).rearrange("p (k n) -> p k n", n=NT)
        for m in range(KT):
            banks = [self.bank() for _ in self.tbs]
            for kh in range(2):
                wb, wres = self.wload([self.w_f2[l][kh * 1408:(kh + 1) * 1408, m * 128:(m + 1) * 128]], 11, 128)
                for ti, (t0, w) in enumerate(self.tbs):
                    b = banks[ti]; ps = self.PSB[b]
                    for k in range(11):
                        kk = kh * 11 + k
                        P.op("pe", lambda e, k=k, kk=kk, ps=ps, t0=t0, w=w, wb=wb: e.matmul(ps[:, :w], lhsT=wb[:, k, :], rhs=self.aT[:, kk, t0:t0 + w],
                                                                                            start=(kk == 0), stop=(kk == FT - 1)),
                             r=[wres, ("aT", t0)], w=[("ps", b)])
            for ti, (t0, w) in enumerate(self.tbs):
                b = banks[ti]; ps = self.PSB[b]
                if ti % 2 == 0:
                    P.op("act", lambda e, ps=ps, m=m, t0=t0, w=w: e.activation(out=zT[:, m, t0:t0 + w], in_=ps[:, :w], func=AF.Copy), r=[("ps", b)], w=[("zF", t0), "carry"] + [("Hst", g) for g in range(4)])
                else:
                    P.op("dve", lambda e, ps=ps, m=m, t0=t0, w=w: e.tensor_copy(out=zT[:, m, t0:t0 + w], in_=ps[:, :w]), r=[("ps", b)], w=[("zF", t0), "carry"] + [("Hst", g) for g in range(4)])
        self.post_norm_add(l, 3, zT, "zF")
        if FFNBAR:
            P.barrier()

    def _emit(self, eng, fn, r, w):
        r = list(r) + list(self._xr)
        w = list(w) + list(self._xw)
        if self._q is not None:
            self._q.append((eng, fn, r, w))
        else:
            self.P.op(eng, fn, r=r, w=w)

    def _flush(self, qa, qb):
        i = j = 0
        while i < len(qa) or j < len(qb):
            if i < len(qa):
                eng, fn, r, w = qa[i]; self.P.op(eng, fn, r=r, w=w); i += 1
            if j < len(qb):
                eng, fn, r, w = qb[j]; self.P.op(eng, fn, r=r, w=w); j += 1

    def _tt(self, out, a, b, op):
        ch = self._ch
        self._emit(self._eng, lambda e: e.tensor_tensor(out=out, in0=a, in1=b, op=op), [ch], [ch])

    def _ts(self, out, a, s1, op0, s2=None, op1=None, extra_r=()):
        ch = self._ch
        if op1 is None:
            self._emit(self._eng, lambda e: e.tensor_scalar(out=out, in0=a, scalar1=s1, scalar2=None, op0=op0), [ch] + list(extra_r), [ch])
        else:
            self._emit(self._eng, lambda e: e.tensor_scalar(out=out, in0=a, scalar1=s1, scalar2=s2, op0=op0, op1=op1), [ch] + list(extra_r), [ch])

    def _mask(self, out, a, mcol):
        ch = self._ch
        if self._eng == "pool":
            shp = list(out.shape)
            mc = mcol
            for d in range(2, len(shp)):
                mc = mc.unsqueeze(d)
            mb = mc.to_broadcast(shp)
            self._emit("pool", lambda e: e.tensor_tensor(out=out, in0=a, in1=mb, op=ALU.mult), [ch, "cst"], [ch])
        else:
            self._emit(self._eng, lambda e: e.tensor_scalar(out=out, in0=a, scalar1=mcol, scalar2=None, op0=ALU.mult), [ch, "cst"], [ch])

    def _cp(self, out, a):
        ch = self._ch
        self._emit(self._eng, lambda e: e.tensor_copy(out=out, in_=a), [ch], [ch])

    def _ms(self, out, val):
        ch = self._ch
        self._emit(self._eng, lambda e: e.memset(out, val), [ch], [ch])

    def _act(self, out, a, func, scale=1.0, bias=None):
        ch = self._ch
        if bias is None:
            self._emit("act", lambda e: e.activation(out=out, in_=a, func=func, scale=scale), [ch], [ch])
        else:
            self._emit("act", lambda e: e.activation(out=out, in_=a, func=func, scale=scale, bias=bias), [ch, "epsb"], [ch])

    def _rcp(self, out, a):
        ch = self._ch
        self._emit("dve", lambda e: e.reciprocal(out=out, in_=a), [ch], [ch])

    def _disc(self, F, lr, li, ldt, S, pre_rden=None):
        tt, ts, act = self._tt, self._ts, self._act
        s0, s1, s2, s3, s4, s5, s6 = S[:7]
        if pre_rden is not None:
            ch = self._ch
            self.P.op("dve", lambda e: e.tensor_tensor(out=pre_rden, in0=lr, in1=lr, op=ALU.mult), r=[ch], w=[ch])
            self.P.op("dve", lambda e: e.tensor_tensor(out=s6, in0=li, in1=li, op=ALU.mult), r=[ch], w=[ch])
            self.P.op("dve", lambda e: e.tensor_tensor(out=pre_rden, in0=pre_rden, in1=s6, op=ALU.add), r=[ch], w=[ch])
            self.P.op("dve", lambda e: e.reciprocal(out=pre_rden, in_=pre_rden), r=[ch], w=[ch])
        act(s0, ldt, AF.Exp)
        tt(s1, lr, s0, ALU.mult)
        tt(s2, li, s0, ALU.mult)
        act(s0, s1, AF.Exp)
        act(s1, s2, AF.Sin, scale=1.0 / 32, bias=self.epsb[:, 1:2])
        act(s3, s2, AF.Sin, scale=1.0 / 32)
        for _ in range(5):
            tt(s2, s1, s1, ALU.mult)
            tt(s4, s3, s3, ALU.mult)
            tt(s5, s1, s3, ALU.mult)
            tt(s1, s2, s4, ALU.subtract)
            ts(s3, s5, 2.0, ALU.mult)
        tt(s2, s0, s1, ALU.mult)
        tt(s4, s0, s3, ALU.mult)
        if pre_rden is None:
            tt(s0, lr, lr, ALU.mult)
            tt(s1, li, li, ALU.mult)
            tt(s0, s0, s1, ALU.add)
            self._rcp(s0, s0)
        else:
            s0 = pre_rden
        ts(s1, s2, -1.0, ALU.add)
        tt(s3, s1, lr, ALU.mult)
        tt(s5, s4, li, ALU.mult)
        tt(s3, s3, s5, ALU.add)
        tt(s3, s3, s0, ALU.mult)
        tt(s5, s4, lr, ALU.mult)
        tt(s6, s1, li, ALU.mult)
        tt(s5, s5, s6, ALU.subtract)
        tt(s5, s5, s0, ALU.mult)
        return s2, s4, s3, s5

    def ssm_prep(self, l):
        P = self.P
        tt, ts = self._tt, self._ts
        pw = self.prm
        P.dma("sp", pw[:, 0:2352], self.i["sprm"][l], w=["pP", "pL"], sem="prm")
        zf = self.zb[:].rearrange("p k c -> p (k c)")
        self._eng, self._ch = "dve", "pP"
        mE = self.cst[:, 128:130]; mL = self.cst[:, 130:132]; mM = self.cst[:, 132:136]; nmE = self.cst[:, 200:202]
        F = 256
        self._eng, self._ch = "pool", "pL"
        zf = self.hT[:].rearrange("p k n -> p (k n)").bitcast(F32)
        S = [zf[:, j * F:(j + 1) * F] for j in range(7)]
        ar, ai, fr, fi = self._disc(F, pw[:, 1072:1328], pw[:, 1328:1584], pw[:, 1584:1840], S, pre_rden=zf[:, 3840:4096])
        o = 7 * F
        brL = pw[:, 1840:2096]; biL = pw[:, 2096:2352]

        def t1(j):
            return zf[:, o + j * 256:o + (j + 1) * 256]
        bb_r, bb_i, cur_r, cur_i, n_r, n_i, u1, u2 = [t1(j) for j in range(8)]
        x1, x2 = S[0], S[1]
        tt(u1, brL, fr, ALU.mult); tt(u2, biL, fi, ALU.mult); tt(bb_r, u1, u2, ALU.subtract)
        tt(u1, biL, fr, ALU.mult); tt(u2, brL, fi, ALU.mult); tt(bb_i, u1, u2, ALU.add)
        self._xw = [("curL", 0), ("curL", 1), "pLa", "pLb"]
        self._ms(cur_r, 1.0)
        self._ms(cur_i, 0.0)
        self._xw = ()
        for k in range(8):
            s = 7 - k
            par = k % 2
            self._q = qa = []
            self._ch = "pLa"; self._xr = [("curL", par), "pL"]; self._xw = ()
            tt(u1, cur_r, bb_r, ALU.mult); tt(u2, cur_i, bb_i, ALU.mult); tt(u1, u1, u2, ALU.subtract)
            for e2 in range(2):
                self._mask(self.Wsb[:, :, s, 0, e2 * 64:(e2 + 1) * 64], u1.rearrange("p (g q) -> p g q", q=64), mL[:, e2:e2 + 1])
            tt(u1, cur_r, bb_i, ALU.mult); tt(u2, cur_i, bb_r, ALU.mult); tt(u1, u1, u2, ALU.add)
            for e2 in range(2):
                self._mask(self.Wsb[:, :, s, 1, e2 * 64:(e2 + 1) * 64], u1.rearrange("p (g q) -> p g q", q=64), mL[:, e2:e2 + 1])
            self._q = qb = []
            if k < 7:
                self._ch = "pLb"; self._xr = [("curL", par), "pL"]; self._xw = ()
                tt(x1, cur_r, ar, ALU.mult); tt(x2, cur_i, ai, ALU.mult)
                self._xw = [("curL", 1 - par)]
                tt(n_r, x1, x2, ALU.subtract)
                self._xw = ()
                tt(x1, cur_r, ai, ALU.mult); tt(x2, cur_i, ar, ALU.mult)
                self._xw = [("curL", 1 - par)]
                tt(n_i, x1, x2, ALU.add)
                self._xw = ()
            self._q = None
            self._flush(qa, qb)
            if k < 7:
                cur_r, n_r = n_r, cur_r
                cur_i, n_i = n_i, cur_i
        self._xr = (); self._xw = ()
        self._ch = "pL"
        self.P.op("pool", lambda e: e.memset(self.small[:, 761:762], 0.0), r=["pLa", "pLb", ("curL", 0), ("curL", 1)], w=["pL"])
        zf = self.zb[:].rearrange("p k c -> p (k c)")
        self._eng, self._ch = "dve", "pP"
        F = 16
        S = [zf[:, j * F:(j + 1) * F] for j in range(7)]
        ar, ai, fr, fi = self._disc(F, pw[:, 0:16], pw[:, 16:32], pw[:, 32:48], S)
        o = 7 * F
        br = pw[:, 48:304].rearrange("p (a h) -> p a h", h=16); bi = pw[:, 304:560].rearrange("p (a h) -> p a h", h=16)
        cr = pw[:, 560:816].rearrange("p (a h) -> p a h", h=16); ci = pw[:, 816:1072].rearrange("p (a h) -> p a h", h=16)

        def t3(j):
            return zf[:, o + j * 256:o + (j + 1) * 256].rearrange("p (a h) -> p a h", h=16)
        bb_r, bb_i, u1, u2 = t3(0), t3(1), t3(2), t3(3)
        frb = fr.unsqueeze(2).to_broadcast([128, 16, 16]); fib = fi.unsqueeze(2).to_broadcast([128, 16, 16])
        tt(u1, br, frb, ALU.mult); tt(u2, bi, fib, ALU.mult); tt(bb_r, u1, u2, ALU.subtract)
        tt(u1, bi, frb, ALU.mult); tt(u2, br, fib, ALU.mult); tt(bb_i, u1, u2, ALU.add)
        Bp = self.small[:, 0:512].bitcast(BF16)[:, 0:1024].rearrange("p (a r c) -> p a r c", a=16, r=2)
        for ri, bb in enumerate((bb_r, bb_i)):
            for e2 in range(2):
                self._mask(Bp[:, :, ri, e2 * 16:(e2 + 1) * 16], bb, mE[:, e2:e2 + 1])
        o2 = o + 4 * 256
        cur_r = zf[:, o2:o2 + 16]; cur_i = zf[:, o2 + 16:o2 + 32]; n_r = zf[:, o2 + 32:o2 + 48]; n_i = zf[:, o2 + 48:o2 + 64]
        w1 = zf[:, o2 + 64:o2 + 80]; w2 = zf[:, o2 + 80:o2 + 96]
        self._xw = [("curP", 0), ("curP", 1), "pPa", "pPb"]
        self._ms(cur_r, 1.0)
        self._ms(cur_i, 0.0)
        self._xw = ()
        for k in range(9):
            par = k % 2
            crb = cur_r.unsqueeze(2).to_broadcast([128, 16, 16]); cib = cur_i.unsqueeze(2).to_broadcast([128, 16, 16])
            self._q = qa = []
            self._ch = "pPa"; self._xr = [("curP", par), "pP"]; self._xw = ()
            tt(u1, cr, crb, ALU.mult); tt(u2, ci, cib, ALU.mult); tt(u1, u1, u2, ALU.subtract)
            for e2 in range(2):
                self._mask(self.Wco[:, :, k, 0, e2 * 16:(e2 + 1) * 16], u1, mE[:, e2:e2 + 1])
            tt(u1, cr, cib, ALU.mult); tt(u2, ci, crb, ALU.mult); tt(u1, u1, u2, ALU.add)
            for e2 in range(2):
                self._mask(self.Wco[:, :, k, 1, e2 * 16:(e2 + 1) * 16], u1, nmE[:, e2:e2 + 1])
            if k == 1:
                self._cp(self.A1[:, 0, :], cur_r)
                self._cp(self.A1[:, 1, :], cur_i)
            if k == 8:
                self._cp(self.A8[:, 0, :], cur_r)
                self._cp(self.A8[:, 1, :], cur_i)
            self._q = qb = []
            if k < 8:
                self._ch = "pPb"; self._xr = [("curP", par), "pP"]; self._xw = ()
                tt(w1, cur_r, ar, ALU.mult); tt(w2, cur_i, ai, ALU.mult)
                self._xw = [("curP", 1 - par)]
                tt(n_r, w1, w2, ALU.subtract)
                self._xw = ()
                tt(w1, cur_r, ai, ALU.mult); tt(w2, cur_i, ar, ALU.mult)
                self._xw = [("curP", 1 - par)]
                tt(n_i, w1, w2, ALU.add)
                self._xw = ()
            self._q = None
            self._flush(qa, qb)
            if k < 8:
                cur_r, n_r = n_r, cur_r
                cur_i, n_i = n_i, cur_i
        self._xr = (); self._xw = ()
        self._ch = "pP"
        self.P.op("dve", lambda e: e.memset(self.small[:, 762:763], 0.0), r=["pPa", "pPb", ("curP", 0), ("curP", 1)], w=["pP"])
        A8r, A8i = self.A8[:, 0, :], self.A8[:, 1, :]
        r2 = zf[:, o2 + 96:o2 + 112]; r3 = zf[:, o2 + 112:o2 + 128]; Ur = zf[:, o2 + 128:o2 + 144]; Ui = zf[:, o2 + 144:o2 + 160]
        tt(r2, A8r, A8r, ALU.mult); tt(r3, A8i, A8i, ALU.mult); tt(r2, r2, r3, ALU.add)
        self._act(self.rho, r2, AF.Sqrt)
        self._rcp(r3, self.rho)
        tt(Ur, A8r, r3, ALU.mult); tt(Ui, A8i, r3, ALU.mult)
        Tc, Ts = self.Tc, self.Ts
        self._cp(Tc[:, :, 0], Ur); self._cp(Ts[:, :, 0], Ui)
        g1 = zf[:, 2048:3072]; g2 = zf[:, 3072:4096]
        Pr, Pi = Ur, Ui
        q1 = zf[:, o2 + 160:o2 + 176]; q2 = zf[:, o2 + 176:o2 + 192]; q3 = zf[:, o2 + 192:o2 + 208]; q4 = zf[:, o2 + 208:o2 + 224]
        nxt = [(q1, q2), (q3, q4)]
        n = 1
        lvl = 0
        while n < NCH:
            a1 = g1[:, 0:16 * n].rearrange("p (a j) -> p a j", j=n); a2 = g2[:, 0:16 * n].rearrange("p (a j) -> p a j", j=n)
            Prb = Pr.unsqueeze(2).to_broadcast([128, 16, n]); Pib = Pi.unsqueeze(2).to_broadcast([128, 16, n])
            pres = "pP" if lvl == 0 else ("Pq", (lvl - 1) % 2)
            self._q = qa = []
            self._ch = "pTa"; self._xr = [pres, "pP"]; self._xw = ()
            tt(a1, Tc[:, :, 0:n], Prb, ALU.mult); tt(a2, Ts[:, :, 0:n], Pib, ALU.mult); tt(Tc[:, :, n:2 * n], a1, a2, ALU.subtract)
            tt(a1, Tc[:, :, 0:n], Pib, ALU.mult); tt(a2, Ts[:, :, 0:n], Prb, ALU.mult); tt(Ts[:, :, n:2 * n], a1, a2, ALU.add)
            self._q = qb = []
            if 2 * n < NCH:
                nr, ni = nxt[lvl % 2]
                b1 = zf[:, o2 + 224:o2 + 240]; b2 = zf[:, o2 + 240:o2 + 256]
                self._ch = "pTb"; self._xr = [pres, "pP"]; self._xw = ()
                tt(b1, Pr, Pr, ALU.mult); tt(b2, Pi, Pi, ALU.mult); tt(b1, b1, b2, ALU.subtract)
                tt(b2, Pr, Pi, ALU.mult)
                self._xw = [("Pq", lvl % 2)]
                ts(ni, b2, 2.0, ALU.mult); self._cp(nr, b1)
                self._xw = ()
            self._q = None
            self._flush(qa, qb)
            if 2 * n < NCH:
                Pr, Pi = nr, ni
            n *= 2
            lvl += 1
        self._xr = (); self._xw = ()
        self._ch = "pP"
        self.P.op("dve", lambda e: e.memset(self.small[:, 763:764], 0.0), r=["pTa", "pTb", ("Pq", 0), ("Pq", 1)], w=["pP"])
        for gt in range(4):
            pb = 6 + gt % 2
            ps = self.PSB[pb]
            for m in range(4):
                pair = 4 * gt + m
                for ri in range(2):
                    P.op("pe", lambda e, m=m, pair=pair, ri=ri, ps=ps: e.matmul(ps[32 * m:32 * m + 32, 0:256], lhsT=Bp[:, pair, ri, :],
                                                                              rhs=self.Wco[:, pair, 0:8, ri, :], start=(ri == 0), stop=(ri == 1),
                                                                              skip_group_check=True, tile_position=(0, 32 * m)),
                         r=["pP"], w=[("ps", pb)])
            for m2 in range(4):
                P.op("dve", lambda e, gt=gt, m2=m2, ps=ps: e.tensor_scalar(out=self.Wtp[:, gt, :, 32 * m2:32 * m2 + 32], in0=ps[:, 0:256].rearrange("p (k c) -> p k c", c=32),
                                                                          scalar1=mM[:, m2:m2 + 1], scalar2=None, op0=ALU.mult),
                     r=[("ps", pb), "cst"], w=["Wtp"])
        sc = self.scr
        P.dma("sp", sc["sb"][l], self.Wsb.rearrange("p g k r c -> p (g k r c)"), r=["pL"], sem="pst1")
        P.dma("sp", sc["co"][l], self.Wco[:].rearrange("p a k r c -> p (a k r c)"), r=["pP"], sem="pst2")
        P.dma("sp", sc["tp"][l], self.Wtp[:].rearrange("p g k c -> p (g k c)"), r=["Wtp"], sem="pst3")
        P.dma("sp", sc["t"][l][:, 0:2048], self.Tc.rearrange("p a c -> p (a c)"), r=["pP"], sem="pst4")
        P.dma("sp", sc["t"][l][:, 2048:4096], self.Ts.rearrange("p a c -> p (a c)"), r=["pP"], sem="pst5")
        P.dma("sp", sc["t"][l][:, 4096:4144], self.a1rho[:, :], r=["pP"], sem="pst6")

    def load_prep(self, l):
        P = self.P
        sc = self.scr
        T3 = (0, 512, HALF)
        aA = [("aT", t) for t in T3]
        P.dma("sp", self.Wsb.rearrange("p g k r c -> p (g k r c)"), sc["sb"][l], r=["scr"], w=["ssmw"] + aA + [("mg", t) for t in T3], sem="pld")
        P.dma("sp", self.Wco[:].rearrange("p a k r c -> p (a k r c)"), sc["co"][l], r=["scr"], w=["ssmw", ("kvs", 0), ("kvs", 1)], sem="pld")
        P.dma("sp", self.Wtp[:].rearrange("p g k c -> p (g k c)"), sc["tp"][l], r=["scr"], w=["ssmw", "kT", "vv", "qtok"], sem="pld")
        P.dma("sp", self.Tc.rearrange("p a c -> p (a c)"), sc["t"][l][:, 0:2048], r=["scr"], w=["ssmw"] + aA + [("oT", t) for t in T3] + [("zT", t) for t in T3], sem="pld")
        P.dma("sp", self.Ts.rearrange("p a c -> p (a c)"), sc["t"][l][:, 2048:4096], r=["scr"], w=["ssmw"] + aA + [("Hbf", g) for g in range(4)], sem="pld")
        P.dma("sp", self.a1rho[:, :], sc["t"][l][:, 4096:4144], r=["scr"], w=["ssmw"], sem="pld")

    def gelu_to(self, y, yres, dst, dres, w):
        P = self.P
        si = self.uid() % 2
        t = self.tmpB[si]
        P.op("dve", lambda e: e.scalar_tensor_tensor(out=t[:, :w], in0=y, scalar=0.044715, in1=y, op0=ALU.mult, op1=ALU.mult), r=[yres], w=[("tmpB", si)])
        P.op("dve", lambda e: e.scalar_tensor_tensor(out=t[:, :w], in0=t[:, :w], scalar=1.0, in1=y, op0=ALU.add, op1=ALU.mult), r=[("tmpB", si), yres], w=[("tmpB", si)])
        P.op("act", lambda e: e.activation(out=t[:, :w], in_=t[:, :w], func=AF.Sigmoid, scale=1.5957691216057308), r=[("tmpB", si)], w=[("tmpB", si)])
        P.op("dve", lambda e: e.tensor_tensor(out=dst, in0=y, in1=t[:, :w], op=ALU.mult), r=[("tmpB", si), yres], w=[dres])

    def ssm_core(self, l, half):
        P = self.P
        Hst, Hbf, Wsb, Wco, Wtp, uT = self.Hst, self.Hbf, self.Wsb, self.Wco, self.Wtp, self.uT
        u8 = uT[:, :, 0:HALF].rearrange("p g (c j) -> p g c j", j=LCH)
        HG = [("Hst", g) for g in range(4)]
        P.op("dve", lambda e: e.tensor_copy(out=Hst[:, :, :, 0], in_=self.carry[:, l]), r=["carry"], w=HG)
        if half == 1 and "ss" not in SKIP:
            self.ssm_sample_h0(l)
        Tc, Ts = self.Tc, self.Ts
        tmps = [(self.tmpA[0], ("tmpA", 0)), (self.tmpA[1], ("tmpA", 1)), (self.tmpB[0], ("tmpB", 0)), (self.tmpB[1], ("tmpB", 1))]
        v = lambda t: t[:, :].rearrange("p (a c) -> p a c", c=NCH)

        def rot(sign, qd):
            hres = ("Hst", qd)
            sl = slice(4 * qd, 4 * qd + 4)
            Xr = Hst[:, sl, 0, 1:NCH + 1]; Xi = Hst[:, sl, 1, 1:NCH + 1]
            C = Tc[:, sl, :]; S_ = Ts[:, sl, :]
            (t1, r1), (t2, r2), (t3, r3), (t4, r4) = tmps
            P.op("dve", lambda e: e.tensor_tensor(out=v(t1), in0=Xr, in1=C, op=ALU.mult), r=[hres, "ssmw"], w=[r1])
            P.op("pool", lambda e: e.tensor_tensor(out=v(t2), in0=Xi, in1=S_, op=ALU.mult), r=[hres, "ssmw"], w=[r2])
            P.op("dve", lambda e: e.tensor_tensor(out=v(t3), in0=Xi, in1=C, op=ALU.mult), r=[hres, "ssmw"], w=[r3])
            P.op("pool", lambda e: e.tensor_tensor(out=v(t4), in0=Xr, in1=S_, op=ALU.mult), r=[hres, "ssmw"], w=[r4])
            if sign < 0:
                P.op("dve", lambda e: e.tensor_tensor(out=Xr, in0=v(t1), in1=v(t2), op=ALU.add), r=[r1, r2], w=[hres])
                P.op("pool", lambda e: e.tensor_tensor(out=Xi, in0=v(t3), in1=v(t4), op=ALU.subtract), r=[r3, r4], w=[hres])
            else:
                P.op("dve", lambda e: e.tensor_tensor(out=Xr, in0=v(t1), in1=v(t2), op=ALU.subtract), r=[r1, r2], w=[hres])
                P.op("pool", lambda e: e.tensor_tensor(out=Xi, in0=v(t3), in1=v(t4), op=ALU.add), r=[r3, r4], w=[hres])
        def state_build(gt):
            off = (gt % 2) * 256
            for ri in range(2):
                for k in range(LCH):
                    for m in range(4):
                        ps = self.PSB[m]
                        P.op("pe", lambda e, m=m, ri=ri, k=k, ps=ps: e.matmul(ps[:, off + ri * 128: off + (ri + 1) * 128], lhsT=Wsb[32 * m:32 * m + 32, gt, k, ri, :],
                                                                               rhs=u8[32 * m:32 * m + 32, gt, :, k], start=(k == 0), stop=(k == LCH - 1),
                                                                               skip_group_check=True, tile_position=(32 * m, 0)),
                             r=["ssmw", ("uT", 0), ("uT", 512)], w=[("ps", m)])
            for m in range(4):
                pair = 4 * gt + m
                ps = self.PSB[m]
                if True:
                    P.op("act", lambda e, pair=pair, ps=ps: e.activation(out=Hst[:, pair, :, 1:NCH + 1], in_=ps[:, off:off + 256].rearrange("p (r c) -> p r c", r=2), func=AF.Copy),
                         r=[("ps", m)], w=[("Hst", gt)])
                else:
                    P.op("dve", lambda e, pair=pair, ps=ps: e.tensor_copy(out=Hst[:, pair, :, 1:NCH + 1], in_=ps[:, off:off + 256].rearrange("p (r c) -> p r c", r=2)),
                         r=[("ps", m)], w=[("Hst", gt)])

        for qd in range(4):
            state_build(qd)
            rot(-1, qd)
            for pair in range(4 * qd, 4 * qd + 4):
                for ri in range(2):
                    P.op("dve", lambda e, pair=pair, ri=ri: e.tensor_tensor_scan(out=Hst[:, pair, ri, 1:NCH + 1], data0=self.rho[:, pair:pair + 1].to_broadcast([128, NCH]),
                                                                                data1=Hst[:, pair, ri, 1:NCH + 1], initial=Hst[:, pair, ri, 0:1], op0=ALU.mult, op1=ALU.add),
                         r=[("Hst", qd), "ssmw"], w=[("Hst", qd)])
            rot(+1, qd)
        for qd in range(4):
            P.op("act", lambda e, qd=qd: e.activation(out=Hbf[:, 4 * qd:4 * qd + 4, :, 0:NCH], in_=Hst[:, 4 * qd:4 * qd + 4, :, 0:NCH], func=AF.Copy), r=HG, w=[("Hbf", qd)])
        P.op("dve", lambda e: e.tensor_copy(out=self.carry[:, l], in_=Hst[:, :, :, NCH]), r=HG, w=["carry"])
        if half == self.nh - 1:
            for ri, nm in enumerate(("re_p", "im_p")):
                t = self.small[:, 640 + 16 * ri: 656 + 16 * ri]
                P.op("act", lambda e, ri=ri, t=t: e.activation(out=t, in_=Hst[:, :, ri, NCH], func=AF.Copy), r=HG, w=[("fin", ri)])
                P.dma("sp", self.o[nm][l].rearrange("(a q) -> q a", q=128), t, r=[("fin", ri)], sem="fin%d" % ri, final=True, allow_slow_non_contiguous=True)
        for gt in range(4):
            for bi_, (t0, w) in enumerate(self.tbs[:2]):
                b = self.bank(); ps = self.PSB[b]
                y3 = ps[:, :].rearrange("p (c j) -> p c j", j=LCH)
                u3 = uT[:, gt, t0:t0 + 512].rearrange("p (c j) -> p c j", j=LCH)
                if TAPS2D:
                    for k in range(LCH):
                        for j2 in range(k, LCH):
                            P.op("pe", lambda e, gt=gt, k=k, j2=j2, y3=y3, u3=u3: e.matmul(y3[:, :, j2], lhsT=Wtp[:, gt, k, :], rhs=u3[:, :, j2 - k], start=(k == 0 and j2 == 0), stop=False, skip_group_check=True),
                                 r=["ssmw", ("uT", t0)], w=[("ps", b)])
                else:
                    for k in range(LCH):
                        P.op("pe", lambda e, gt=gt, k=k, y3=y3, u3=u3: e.matmul(y3[:, :, k:LCH], lhsT=Wtp[:, gt, k, :], rhs=u3[:, :, 0:LCH - k], start=(k == 0), stop=False, skip_group_check=True),
                             r=["ssmw", ("uT", t0)], w=[("ps", b)])
                c0 = bi_ * 64
                for m in range(4):
                    pair = 4 * gt + m
                    for j in range(LCH):
                        for ri in range(2):
                            last = (m == 3 and j == LCH - 1 and ri == 1)
                            P.op("pe", lambda e, m=m, pair=pair, j=j, ri=ri, y3=y3, c0=c0, last=last: e.matmul(y3[32 * m:32 * m + 32, :, j], lhsT=Wco[:, pair, j + 1, ri, :],
                                                                                                           rhs=Hbf[:, pair, ri, c0:c0 + 64], start=False, stop=last,
                                                                                                           skip_group_check=True, tile_position=(0, 32 * m)),
                                 r=["ssmw", ("Hbf", gt)], w=[("ps", b)])
                self.ssm_post(l, gt, ps, b, t0, 512)
        if half == 1 and "ss" not in SKIP:
            self.ssm_sample(l)

    def ssm_post(self, l, gt, ps, b, t0, w):
        P = self.P
        si = self.uid() % 2
        y = self.tmpA[si]
        P.op("dve", lambda e: e.scalar_tensor_tensor(out=y[:, :w], in0=self.uT[:, gt, t0:t0 + w], scalar=self.vec4[:, 0, l, gt:gt + 1], in1=ps[:, :w], op0=ALU.mult, op1=ALU.add),
             r=[("ps", b), ("uT", t0), "vec4"], w=[("tmpA", si)])
        self.gelu_to(y[:, :w], ("tmpA", si), self.oT[:, gt, t0:t0 + w], ("oT", t0), w)

    def glu(self, l):
        P = self.P
        wb, wres = self.wload([self.w_glu[l]], 4, 512)

        def evac(ps, b, m, t0, w):
            si = self.uid() % 2
            sg = self.tmpB[si]
            P.op("act", lambda e: e.activation(out=sg[:, :w], in_=ps[:, :w], func=AF.Sigmoid, bias=self.vec4[:, 1, l, m:m + 1]), r=[("ps", b), "vec4"], w=[("tmpB", si)])
            P.op("dve", lambda e: e.tensor_tensor(out=self.uT[:, m, t0:t0 + w], in0=self.oT[:, m, t0:t0 + w], in1=sg[:, :w], op=ALU.mult), r=[("tmpB", si), ("oT", t0)], w=[("uT", t0)])
        self.mm_fm(wb, wres, 4, 4, self.oT, "oT", evac)

    def ssm_sample_load(self, l):
        P = self.P
        zs = self.zb[:].rearrange("p k c -> p (k c)")[0:NS, :].rearrange("p (r c) -> p r c", r=2)
        P.dma("sp", zs[:, 0, :], self.i["sre"][l], w=["zb", "pw0", "pw1", "hist", ("zbs", 0)], sem="zbs0")
        P.dma("sp", zs[:, 1, :], self.i["sim"][l], w=[("zbs", 1)], sem="zbs1")

    def ssm_sample_h0(self, l):
        P = self.P
        zs = self.zb[:].rearrange("p k c -> p (k c)")[0:NS, :].rearrange("p (r c) -> p r c", r=2)
        b = self.bank(); ps = self.PSB[b]
        ps4 = ps[:, :].rearrange("p (a r n) -> p a r n", a=16, r=2)
        for pair in range(16):
            for ri in range(2):
                P.op("pe", lambda e, pair=pair, ri=ri, ps4=ps4: e.transpose(ps4[:, pair, ri, :], zs[:, ri, pair * 128:(pair + 1) * 128], self.ident[0:NS, 0:NS]),
                     r=["zb", ("zbs", 0), ("zbs", 1), "cst"], w=[("ps", b)])
        H0 = self.small[:, 0:512].rearrange("p (a r n) -> p a r n", a=16, r=2)
        P.op("dve", lambda e: e.tensor_copy(out=H0, in_=ps4), r=[("ps", b)], w=["H0"])

    def ssm_sample(self, l):
        P = self.P
        Wsb, Wco, uT = self.Wsb, self.Wco, self.uT
        H0 = self.small[:, 0:512].rearrange("p (a r n) -> p a r n", a=16, r=2)
        if KSS < 2:
            return
        BU = self.sA[:, :].rearrange("p (a r n) -> p a r n", a=16, r=2)
        BUg = self.sA[:, :].rearrange("p (g m r n) -> p g m r n", g=4, m=4, r=2)
        for gt in range(4):
            for ri in range(2):
                c0 = (gt * 2 + ri) * NS
                for m in range(4):
                    psm = self.PSB[m]
                    P.op("pe", lambda e, m=m, gt=gt, ri=ri, psm=psm, c0=c0: e.matmul(psm[:, c0:c0 + NS], lhsT=Wsb[32 * m:32 * m + 32, gt, 7, ri, :], rhs=uT[32 * m:32 * m + 32, gt, HALF:HALF + NS],
                                                                                   start=True, stop=True, skip_group_check=True, tile_position=(32 * m, 0)),
                         r=["ssmw", ("uT", HALF)], w=[("ps", m)])
        for m in range(4):
            psm = self.PSB[m]
            P.op("dve" if m % 2 else "act", (lambda e, m=m, psm=psm: e.tensor_copy(out=BUg[:, :, m], in_=psm[:, 0:128].rearrange("p (g r n) -> p g r n", g=4, r=2))) if m % 2 else
                 (lambda e, m=m, psm=psm: e.activation(out=BUg[:, :, m], in_=psm[:, 0:128].rearrange("p (g r n) -> p g r n", g=4, r=2), func=AF.Copy)), r=[("ps", m)], w=["BU"])
        if KSS < 3:
            return
        A1r = self.A1[:, 0, :].unsqueeze(2).to_broadcast([128, 16, NS]); A1i = self.A1[:, 1, :].unsqueeze(2).to_broadcast([128, 16, NS])
        T = self.sB[:, 0:256].rearrange("p (a n) -> p a n", n=NS)
        seq = [(0, A1r, 0, ALU.add), (1, A1i, 0, ALU.subtract), (0, A1i, 1, ALU.add), (1, A1r, 1, ALU.add)]
        for (hs, Ax, dst, op) in seq:
            P.op("pool", lambda e, hs=hs, Ax=Ax: e.tensor_tensor(out=T, in0=H0[:, :, hs, :], in1=Ax, op=ALU.mult), r=["H0", "ssmw"], w=["sT"])
            P.op("pool", lambda e, dst=dst, op=op: e.tensor_tensor(out=BU[:, :, dst, :], in0=BU[:, :, dst, :], in1=T, op=op), r=["sT", "BU"], w=["BU"])
        if KSS < 4:
            return
        hb = self.sB[:, 256:512].bitcast(BF16).rearrange("p (a r n) -> p a r n", a=16, r=2)
        P.op("act", lambda e: e.activation(out=hb, in_=BU, func=AF.Copy), r=["BU"], w=["hb"])
        for gt in range(4):
            b = self.bank(); ps = self.PSB[b]
            for m in range(4):
                pair = 4 * gt + m
                for ri in range(2):
                    P.op("pe", lambda e, m=m, pair=pair, ri=ri, ps=ps: e.matmul(ps[32 * m:32 * m + 32, 0:NS], lhsT=Wco[:, pair, 0, ri, :], rhs=hb[:, pair, ri, :],
                                                                              start=(ri == 0), stop=(ri == 1), skip_group_check=True, tile_position=(0, 32 * m)),
                         r=["ssmw", "hb"], w=[("ps", b)])
            self.ssm_post(l, gt, ps, b, HALF, NS)
        if KSS < 5:
            return
        for ri, nm in enumerate(("re_s", "im_s")):
            for q in range(4):
                psq = self.PSB[q]
                for a4 in range(4):
                    pair = 4 * q + a4
                    P.op("pe", lambda e, pair=pair, ri=ri, a4=a4, psq=psq: e.transpose(psq[0:NS, a4 * 128:(a4 + 1) * 128], BU[:, pair, ri, :], self.ident),
                         r=["BU", "cst"], w=[("ps", q)])
                si = self.uid() % 2
                t = self.tmpA[si]
                P.op("dve", lambda e, psq=psq, t=t: e.tensor_copy(out=t[0:NS, :], in_=psq[0:NS, :]), r=[("ps", q)], w=[("tmpA", si)])
                if KSS == 7:
                    if not hasattr(self, "dbgres"):
                        self.dbgres = self.nc.dram_tensor("dbg_res", [2, NS, 2048], F32, kind="ExternalOutput").ap()
                    P.dma("sp", self.dbgres[ri][:, q * 512:(q + 1) * 512], t[0:NS, :], r=[("tmpA", si)], sem="sso%d" % si, final=True)
                elif KSS >= 6:
                    P.dma(DQ, self.o[nm][l][:, q * 512:(q + 1) * 512], t[0:NS, :], r=[("tmpA", si)], sem="sso%d" % si, final=True)

    def pool_core(self, l, half):
        P = self.P
        uT = self.uT
        wb, wres = self.wload([self.pool_w[l, gi] for gi in range(4)], 1, 512)
        W = HALF + 16
        bufs = [(self.pw[:, 0:W], "pw0"), (self.pw[:, W:2 * W], "pw1")]
        invc = self.cst[:, 136:200].rearrange("p (g t) -> p g t", t=16)
        for gi, win in enumerate((2, 4, 8, 16)):
            (src, sres), (dst, dres) = bufs
            weng = "pool" if gi in (0, 3) else "dve"
            P.op(weng, lambda e, a=src, gi=gi: e.tensor_copy(out=a[:, 0:16], in_=self.phist[:, l, gi, :]), r=["phist"], w=[sres, "zb", "hist"])
            P.op(weng, lambda e, a=src, gi=gi: e.tensor_copy(out=a[:, 16:W], in_=uT[:, gi, 0:HALF]), r=[("uT", 0), ("uT", 512)], w=[sres])
            P.op(weng, lambda e, gi=gi: e.tensor_copy(out=self.phist[:, l, gi, :], in_=uT[:, gi, HALF - 16:HALF]), r=[("uT", 512), sres], w=["phist"])
            if win >= 4:
                P.op("dve", lambda e, src=src, dst=dst: e.tensor_tensor_scan(out=dst[:, 0:W], data0=self.onesf[:, 0:1].to_broadcast([128, W]), data1=src[:, 0:W], initial=0.0,
                                                                            op0=ALU.mult, op1=ALU.add), r=[sres, "onesf"], w=[dres])
                P.op("dve", lambda e, src=src, dst=dst, win=win: e.tensor_tensor(out=src[:, 16:W], in0=dst[:, 16:W], in1=dst[:, 16 - win:W - win], op=ALU.subtract), r=[dres], w=[sres])
            else:
                step = 1
                while step < win:
                    P.op(weng, lambda e, src=src, dst=dst, step=step: e.tensor_tensor(out=dst[:, step:W], in0=src[:, step:W], in1=src[:, 0:W - step], op=ALU.add), r=[sres], w=[dres])
                    P.op(weng, lambda e, src=src, dst=dst, step=step: e.tensor_copy(out=dst[:, 0:step], in_=src[:, 0:step]), r=[sres], w=[dres])
                    src, dst = dst, src
                    sres, dres = dres, sres
                    step *= 2
            pl = dst[:, 0:HALF // 2 + 8].bitcast(BF16)[:, 0:HALF]
            P.op("dve", lambda e, src=src, pl=pl, gi=gi, win=win: e.scalar_tensor_tensor(out=pl, in0=src[:, 16:W], scalar=1.0 / win, in1=uT[:, gi, 0:HALF], op0=ALU.mult, op1=ALU.subtract),
                 r=[sres, ("uT", 0), ("uT", 512)], w=[dres])
            if half == 0:
                t = self.small[:, 672:688]
                P.op("dve", lambda e, src=src, gi=gi, t=t: e.tensor_tensor(out=t, in0=src[:, 16:32], in1=invc[:, gi, :], op=ALU.mult), r=[sres, "cst"], w=["pfix"])
                P.op("dve", lambda e, pl=pl, gi=gi, t=t: e.tensor_tensor(out=pl[:, 0:16], in0=t, in1=uT[:, gi, 0:16], op=ALU.subtract), r=["pfix", ("uT", 0), dres], w=[dres])
            for (t0, w) in self.tbs[:2]:
                b = self.bank(); ps = self.PSB[b]
                P.op("pe", lambda e, gi=gi, ps=ps, pl=pl, t0=t0, w=w: e.matmul(ps[:, :w], lhsT=wb[:, 0, gi * 128:(gi + 1) * 128], rhs=pl[:, t0:t0 + w], start=True, stop=True),
                     r=[wres, dres], w=[("ps", b)])
                P.op("act", lambda e, gi=gi, ps=ps, t0=t0, w=w: e.activation(out=self.oT[:, gi, t0:t0 + w], in_=ps[:, :w], func=AF.Copy, scale=self.vec4[:, 2, l, gi:gi + 1]),
                     r=[("ps", b), "vec4"], w=[("oT", t0)])
        if half == 1 and "ps" not in SKIP:
            self.pool_sample(l, wb, wres)
        if half == self.nh - 1:
            b = self.bank(); ps = self.PSB[b]
            for gi in range(4):
                P.op("pe", lambda e, gi=gi, ps=ps: e.transpose(ps[0:16, gi * 128:(gi + 1) * 128], self.lastu[:, gi, :], self.ident), r=["lastu", "cst"], w=[("ps", b)])
            t = self.sB
            P.op("dve", lambda e, ps=ps: e.tensor_copy(out=t[0:16, :], in_=ps[0:16, :]), r=[("ps", b)], w=["sBo", "sT", "hb", "den"])
            P.dma("sp", self.o["pool_p"][l], t[1:16, :], r=["sBo"], sem="sBo", final=True)

    def pool_sample(self, l, wb, wres):
        P = self.P
        uT = self.uT
        sp = self.i["spool"][l].rearrange("n r c -> (n r) c")
        zt = self.zb[:].rearrange("p k c -> p (k c)")
        hist = self.pw[:, 0:960].rearrange("p (g x) -> p g x", g=4)
        for j in range(2):
            P.dma("sp", zt[0:120, j * 512:(j + 1) * 512], sp[j * 120:(j + 1) * 120, :], w=["zb"], sem="zb")
        P.dma("sp", self.o["pool_s"][l][:, 0:14, :], self.i["spool"][l][:, 1:15, :], sem="pcopy", final=True)
        for gi in range(4):
            b = self.bank(); ps = self.PSB[b]
            for j in range(2):
                P.op("pe", lambda e, gi=gi, j=j, ps=ps: e.transpose(ps[:, j * 120:(j + 1) * 120], zt[0:120, j * 512 + gi * 128: j * 512 + (gi + 1) * 128], self.ident[0:120, 0:120]),
                     r=["zb", "cst"], w=[("ps", b)])
            P.op("dve", lambda e, gi=gi, ps=ps: e.tensor_copy(out=hist[:, gi, :], in_=ps[:, 0:240]), r=[("ps", b)], w=["hist", "pw0"])
        h4 = self.pw[:, 0:960].rearrange("p (g n r) -> p g n r", g=4, r=15)
        red = self.small[:, 688:704]
        plb = self.small[:, 720:752].bitcast(BF16).rearrange("p (g n) -> p g n", g=4)
        for gi, win in enumerate((2, 4, 8, 16)):
            P.op("dve", lambda e, gi=gi, win=win: e.tensor_reduce(out=red, in_=h4[:, gi, :, 16 - win:15], axis=AX.X, op=ALU.add), r=["hist"], w=["red"])
            P.op("dve", lambda e, gi=gi: e.tensor_tensor(out=red, in0=red, in1=self.usam[:, gi, :], op=ALU.add), r=["red", "usam"], w=["red"])
            P.op("dve", lambda e, gi=gi, win=win: e.scalar_tensor_tensor(out=plb[:, gi, :], in0=red, scalar=1.0 / win, in1=self.usam[:, gi, :], op0=ALU.mult, op1=ALU.subtract),
                 r=["red", "usam"], w=["plb"])
            b = self.bank(); ps = self.PSB[b]
            P.op("pe", lambda e, gi=gi, ps=ps: e.matmul(ps[:, 0:NS], lhsT=wb[:, 0, gi * 128:(gi + 1) * 128], rhs=plb[:, gi, :], start=True, stop=True), r=[wres, "plb"], w=[("ps", b)])
            P.op("act", lambda e, gi=gi, ps=ps: e.activation(out=self.oT[:, gi, HALF:HALF + NS], in_=ps[:, 0:NS], func=AF.Copy, scale=self.vec4[:, 2, l, gi:gi + 1]),
                 r=[("ps", b), "vec4"], w=[("oT", HALF)])
        b = self.bank(); ps = self.PSB[b]
        for gi in range(4):
            P.op("pe", lambda e, gi=gi, ps=ps: e.transpose(ps[0:NS, gi * 128:(gi + 1) * 128], self.usam[:, gi, :], self.ident), r=["usam", "cst"], w=[("ps", b)])
        t = self.sA
        P.op("dve", lambda e, ps=ps: e.tensor_copy(out=t[0:NS, :], in_=ps[0:NS, :]), r=[("ps", b)], w=["sAo", "BU"])
        P.dma("sp", self.o["pool_s"][l][:, 14, :], t[0:NS, :], r=["sAo"], sem="sAo", final=True)

    def attn_core(self, l, half):
        P = self.P
        qT = self.uT
        sc = 128 ** -0.5
        for (t0, w) in self.tbs[:2]:
            for h in range(4):
                pts = []
                for mt in range(2):
                    b = self.bank(); ps = self.PSB[b]
                    P.op("pe", lambda e, h=h, mt=mt, ps=ps, t0=t0, w=w: e.matmul(ps[:, :w], lhsT=self.kT[:, h, mt * 128:(mt + 1) * 128], rhs=qT[:, h, t0:t0 + w], start=True, stop=True),
                         r=["kT", ("uT", t0)], w=[("ps", b)])
                    pt = self.sq[mt]
                    P.op("act", lambda e, ps=ps, pt=pt, w=w: e.activation(out=pt[:, :w], in_=ps[:, :w], func=AF.Exp, scale=sc), r=[("ps", b)], w=[("sq", mt)])
                    pts.append(pt)
                bd = self.bank(); pd = self.PSB[bd]
                bo = self.bank(); po = self.PSB[bo]
                for mt in range(2):
                    P.op("pe", lambda e, mt=mt, pd=pd, w=w, pts=pts: e.matmul(pd[:, :w], lhsT=self.onesb[:], rhs=pts[mt][:, :w], start=(mt == 0), stop=(mt == 1)),
                         r=[("sq", mt), "onesb"], w=[("ps", bd)])
                for mt in range(2):
                    P.op("pe", lambda e, mt=mt, h=h, po=po, w=w, pts=pts: e.matmul(po[:, :w], lhsT=self.vv[:, mt, h * 128:(h + 1) * 128], rhs=pts[mt][:, :w], start=(mt == 0), stop=(mt == 1)),
                         r=[("sq", mt), "vv"], w=[("ps", bo)])
                si = self.uid() % 2
                rc = self.tmpA[si]
                P.op("dve", lambda e, pd=pd, rc=rc, w=w: e.reciprocal(out=rc[:, :w], in_=pd[:, :w]), r=[("ps", bd)], w=[("tmpA", si)])
                P.op("dve", lambda e, po=po, rc=rc, h=h, t0=t0, w=w: e.tensor_tensor(out=self.oT[:, h, t0:t0 + w], in0=po[:, :w], in1=rc[:, :w], op=ALU.mult),
                     r=[("ps", bo), ("tmpA", si)], w=[("oT", t0)])
        if half == 1 and "as" not in SKIP:
            self.attn_sample(l)

    def attn_sample(self, l):
        P = self.P
        sc = 128 ** -0.5
        kvbuf = self.Wco[:].rearrange("p a k r c -> p (a k r c)").bitcast(F32)
        KV = [kvbuf[:, j * 2048:(j + 1) * 2048].rearrange("p (x t c) -> p x t c", x=2, t=2) for j in range(2)]
        ck, cv = self.i["ck"], self.i["cv"]
        bo = 6; po = self.PSB[bo]
        bd = 7; pd = self.PSB[bd]
        Sx = self.sA[:, 0:128].rearrange("p (n t h) -> p n t h", n=NS, t=2)
        Pb = self.small[:, 512:576].bitcast(BF16).rearrange("p (n t h) -> p n t h", n=NS, t=2)
        qms = [(self.sq[0][0:NS, :], ("sq", 0)), (self.sq[1][0:NS, :], ("sq", 1))]
        vbs = [(self.tmpA[j][:, :].bitcast(BF16).rearrange("p (t c) -> p t c", t=2), ("tmpA", j)) for j in range(2)]
        pqs = {}

        def qbcast(n):
            kv = KV[n % 2]
            kres = ("kvs", n % 2)
            P.dma("sp", kv[:, 0], ck[l, n].rearrange("(t p) c -> p t c", p=128), w=[kres, "ssmw"], sem="kvs%d" % (n % 2))
            P.dma("sp", kv[:, 1], cv[l, n].rearrange("(t p) c -> p t c", p=128), w=[kres], sem="kvs%d" % (n % 2))
            bq = self.bank(); pq = self.PSB[bq]
            qm, qres = qms[n % 2]
            P.op("dve", lambda e: e.tensor_scalar(out=qm, in0=self.qtok[:, :], scalar1=self.cst[0:NS, n:n + 1], scalar2=None, op0=ALU.mult), r=["qtok", "cst"], w=[qres])
            P.op("pe", lambda e: e.matmul(pq[:, :], lhsT=self.onesb[0:NS, :], rhs=qm, start=True, stop=True), r=["onesb", qres], w=[("ps", bq)])
            vb, vres = vbs[n % 2]
            P.op("pool", lambda e: e.tensor_copy(out=vb, in_=kv[:, 1]), r=[kres], w=[vres])
            pqs[n] = (bq, pq)
        qbcast(0)
        for n in range(NS):
            kv = KV[n % 2]
            kres = ("kvs", n % 2)
            vb, vres = vbs[n % 2]
            bq, pq = pqs[n]
            for mt in range(2):
                si = self.uid() % 2
                t = self.tmpB[si]
                P.op("dve", lambda e, kv=kv, mt=mt, pq=pq, t=t: e.tensor_tensor(out=t[:, :], in0=kv[:, 0, mt, :], in1=pq[:, :], op=ALU.mult), r=[kres, ("ps", bq)], w=[("tmpB", si)])
                P.op("dve", lambda e, n=n, mt=mt, t=t: e.tensor_reduce(out=Sx[:, n, mt, :], in_=t[:, :].rearrange("p (h d) -> p h d", d=128), axis=AX.X, op=ALU.add),
                     r=[("tmpB", si)], w=[("Sx", n), "BU", "sAo"])
            P.op("act", lambda e, n=n: e.activation(out=Pb[:, n], in_=Sx[:, n], func=AF.Exp, scale=sc), r=[("Sx", n)], w=[("Pf", n)])
            if n + 1 < NS:
                qbcast(n + 1)
            P.op("pe", lambda e, n=n: e.matmul(pd[:, n * 8:(n + 1) * 8], lhsT=self.onesb[:], rhs=Pb[:, n].rearrange("p t h -> p (t h)"), start=True, stop=True, skip_group_check=True),
                 r=[("Pf", n), "onesb"], w=[("ps", bd)])
            for h in range(4):
                for mt in range(2):
                    P.op("pe", lambda e, n=n, h=h, mt=mt, vb=vb: e.matmul(po[:, h * NS + n: h * NS + n + 1], lhsT=vb[:, mt, h * 128:(h + 1) * 128], rhs=Pb[:, n, mt, h:h + 1],
                                                                        start=(mt == 0), stop=(mt == 1), skip_group_check=True),
                         r=[vres, ("Pf", n)], w=[("ps", bo)])
        den = self.sB[:, 0:64].rearrange("p (h n) -> p h n", n=NS)
        d8 = self.sB[:, 64:192]
        P.op("dve", lambda e: e.tensor_copy(out=d8, in_=pd[:, 0:128]), r=[("ps", bd)], w=["den", "sT", "hb", "sBo"])
        d84 = d8.rearrange("p (n t h) -> p h n t", n=NS, t=2)
        P.op("dve", lambda e: e.tensor_tensor(out=den, in0=d84[:, :, :, 0], in1=d84[:, :, :, 1], op=ALU.add), r=["den"], w=["den", "sT", "hb", "sBo"])
        P.op("dve", lambda e: e.reciprocal(out=den, in_=den), r=["den"], w=["den"])
        P.op("dve", lambda e: e.tensor_tensor(out=self.oT[:, :, HALF:HALF + NS], in0=po[:, 0:64].rearrange("p (h n) -> p h n", n=NS), in1=den, op=ALU.mult),
             r=[("ps", bo), "den"], w=[("oT", HALF)])

    def layer(self, l, half):
        P = self.P
        last = (half == self.nh - 1)
        if half == 1 and "ss" not in SKIP:
            self.ssm_sample_load(l)
        self.load_prep(l)
        self.norm_to_h(l, 0)
        if half == 1 and KSTOP <= 1:
            return
        self.dump("hT", self.hT[:], l, half)
        self.proj_u(l, 0)
        self.dump("u_ssm", self.uT, l, half)
        self.ssm_core(l, half)
        wtf = self.Wtp[:].rearrange("p g k c -> p (g k c)")
        P.dma("sp", wtf[:, 0:2048], self.scr["kv"][l], r=["scr"], w=["kT", "vv", "ssmw"], sem="kld")
        self.dump("ygelu", self.oT, l, half)
        self.dump("Hst", self.Hst[:], l, half)
        self.dump("Wtp", self.Wtp[:], l, half)
        self.dump("Wco", self.Wco[:], l, half)
        self.dump("Wsb", self.Wsb, l, half)
        self.glu(l)
        self.dump("o_ssm", self.uT, l, half)
        self.merge(l, 0, self.uT, "uT")
        self.dump("mg0", self.mg, l, half)
        if half == 1 and KSTOP <= 2:
            return

        def cap(ps, b, m, t0, w):
            if last and t0 == 512:
                P.op("dve", lambda e: e.tensor_copy(out=self.lastu[:, m, :], in_=ps[:, 496:512]), r=[("ps", b)], w=["lastu"])
            if t0 == HALF:
                P.op("dve", lambda e: e.tensor_copy(out=self.usam[:, m, :], in_=ps[:, 0:NS]), r=[("ps", b)], w=["usam"])
        self.proj_u(l, 512, extra=cap)
        self.dump("u_pool", self.uT, l, half)
        self.pool_core(l, half)
        self.dump("o_pool", self.oT, l, half)
        self.merge(l, 1, self.oT, "oT")
        self.dump("mg1", self.mg, l, half)
        if half == 1 and KSTOP <= 3:
            return
        if half == 1:
            def qextra(ps, b, m, t0, w):
                pass

            def qchunk(wb, wres, c):
                b = self.bank(); ps = self.PSB[b]
                for k in range(KT):
                    P.op("pe", lambda e, k=k, ps=ps: e.matmul(ps[0:NS, 0:256], lhsT=self.hT[:, k, HALF:HALF + NS], rhs=wb[:, k, :], start=(k == 0), stop=(k == KT - 1)),
                         r=[wres, ("hT", HALF)], w=[("ps", b)])
                P.op("dve", lambda e, ps=ps, c=c: e.tensor_copy(out=self.qtok[:, c * 256:(c + 1) * 256], in_=ps[0:NS, 0:256]), r=[("ps", b)], w=["qtok"])
            qextra.chunk = qchunk
            self.proj_u(l, 1024, extra=qextra)
        else:
            self.proj_u(l, 1024)
        self.dump("q", self.uT, l, half)
        self.attn_core(l, half)
        self.dump("o_mem", self.oT, l, half)
        self.merge(l, 2, self.oT, "oT")
        self.dump("mg2", self.mg, l, half)
        if half == 1 and KSTOP <= 4:
            return
        self.out_proj(l)
        self.dump("x_mix", self.xT[:], l, half)
        if half == 1 and KSTOP <= 5:
            return
        self.ffn(l)


_CACHE = {}


def _host_consts():
    cst = np.zeros((128, 128 + 8 + 64 + 16), np.float32)
    cst[:, 0:128] = np.eye(128, dtype=np.float32)
    p = np.arange(128)
    for e2 in range(2):
        cst[:, 128 + e2] = ((p // 64) == e2)
        cst[:, 130 + e2] = (((p // 16) % 2) == e2)
    for m in range(4):
        cst[:, 132 + m] = ((p // 32) == m)
    for gi, wdw in enumerate((2, 4, 8, 16)):
        cst[:, 136 + gi * 16:136 + (gi + 1) * 16] = 1.0 / np.minimum(np.arange(16) + 1, wdw)
    cst[:, 200:202] = -cst[:, 128:130]
    sel = np.zeros((NS, NS, 128), np.float32)
    for n in range(NS):
        sel[n, n, :] = 1.0
    return cst, sel.reshape(NS, NS * 128)


def _layout_params(inp):
    f = lambda a: np.asarray(a, dtype=np.float32)
    g = np.stack([f(inp[k]) for k in ("g_mix_pre", "g_mix_post", "g_ffn_pre", "g_ffn_post", "g_mem")])
    gvec = g.reshape(5, DEPTH, KT, 128).transpose(3, 0, 1, 2).reshape(128, 5 * DEPTH * KT)
    v4 = np.stack([f(inp[k]) for k in ("ssm_d", "ssm_b_glu", "pool_scale")])
    vec4 = v4.reshape(3, DEPTH, 4, 128).transpose(3, 0, 1, 2).reshape(128, 3 * DEPTH * 4)
    lr, li, ld = f(inp["ssm_lam_re"]), f(inp["ssm_lam_im"]), f(inp["ssm_log_dt"])
    ldb = np.broadcast_to(ld[:, :, None], lr.shape)
    def ps_l(a):
        return a.reshape(DEPTH, 16, 2, 64).transpose(0, 2, 3, 1).reshape(DEPTH, 128, 16)
    lamPS = np.concatenate([ps_l(lr), ps_l(li), ps_l(ldb)], axis=2)
    br, bi = f(inp["ssm_b_re"]), f(inp["ssm_b_im"])
    cr, ci = f(inp["ssm_c_re"]), f(inp["ssm_c_im"])
    def ps_b(a):
        return a.reshape(DEPTH, 16, 2, 64, 16).transpose(0, 2, 3, 1, 4).reshape(DEPTH, 128, 256)
    def ps_c(a):
        return a.reshape(DEPTH, 16, 2, 16, 64).transpose(0, 2, 4, 1, 3).reshape(DEPTH, 128, 256)
    bcPS = np.concatenate([ps_b(br), ps_b(bi), ps_c(cr), ps_c(ci)], axis=2)
    def lb_l(a):
        t = a.reshape(DEPTH, 4, 8, 64).transpose(0, 2, 1, 3)
        t = np.broadcast_to(t[:, :, None], (DEPTH, 8, 16, 4, 64))
        return t.reshape(DEPTH, 128, 256)
    lamLB = np.concatenate([lb_l(lr), lb_l(li), lb_l(ldb)], axis=2)
    def lb_b(a):
        return a.reshape(DEPTH, 4, 8, 64, 16).transpose(0, 2, 4, 1, 3).reshape(DEPTH, 128, 256)
    bLB = np.concatenate([lb_b(br), lb_b(bi)], axis=2)
    c = np.ascontiguousarray
    return dict(gvec=c(gvec), vec4=c(vec4), sprm=c(np.concatenate([lamPS, bcPS, lamLB, bLB], axis=2)))


def _split_out(flat):
    flat = np.asarray(flat, dtype=np.float32).reshape(-1)
    return {nm: flat[off:off + int(np.prod(shp))].reshape(shp) for nm, (off, shp) in OUT_LAYOUT.items()}


def _shared_inputs(inputs):
    f = lambda a: np.ascontiguousarray(np.asarray(a, dtype=np.float32))
    cst, _ = _host_consts()
    prm = _layout_params(inputs)
    call = np.ascontiguousarray(np.concatenate([cst, prm["gvec"], prm["vec4"]], axis=1))
    return dict(w_in=f(inputs["w_in"]), w_kv=f(inputs["w_kv"]), w_glu=f(inputs["ssm_w_glu"]), pool_w=f(inputs["pool_w"]),
                w_up=f(inputs["w_branch_up"]), w_out=f(inputs["w_out"]), w_f1=f(inputs["w_ffn_in"]), w_f2=f(inputs["w_ffn_out"]),
                call=call, sprm=prm["sprm"])


def kernel(**inputs):
    f = lambda a: np.ascontiguousarray(np.asarray(a, dtype=np.float32))
    if "nc" not in _CACHE:
        _CACHE["nc"] = Builder().build()
    nc = _CACHE["nc"]
    shared = _shared_inputs(inputs)
    xp = f(inputs["x_prompt"]); xs = f(inputs["x_sample"]); mem = f(inputs["mem_prompt"])
    ck = f(inputs["cache_mem_k"]).reshape(DEPTH, 128, NMEM, 512); cv = f(inputs["cache_mem_v"]).reshape(DEPTH, 128, NMEM, 512)
    sre = f(inputs["state_ssm_re"]).reshape(DEPTH, 128, 2048); sim = f(inputs["state_ssm_im"]).reshape(DEPTH, 128, 2048)
    spool = f(inputs["state_pool"])
    in_maps = []
    for c in range(8):
        sl = slice(c * NS, (c + 1) * NS)
        m = dict(shared)
        m.update(xp=xp[c], xs=f(xs[sl, 0, :]), mem=mem[c], ck=f(ck[:, sl]), cv=f(cv[:, sl]), sre=f(sre[:, sl]), sim=f(sim[:, sl]), spool=f(spool[:, sl]))
        in_maps.append(m)
    res = run_bass_kernel_spmd(nc, in_maps, core_ids=list(range(8)))
    R = [_split_out(r["out"]) for r in res.results]
    cat = lambda k, ax: np.concatenate([np.asarray(r[k], dtype=np.float32) for r in R], axis=ax)
    st = lambda k: np.stack([np.asarray(r[k], dtype=np.float32) for r in R], axis=1)
    yp = np.stack([np.asarray(r["yp"], dtype=np.float32) for r in R], axis=0)
    ys = cat("ys", 0).reshape(128, 1, D)
    re_p = st("re_p").reshape(DEPTH, 8, 32, 64); im_p = st("im_p").reshape(DEPTH, 8, 32, 64)
    pool_p = st("pool_p")
    mk_p = st("mk_p").reshape(DEPTH, 8, NMEM, 4, 128); mv_p = st("mv_p").reshape(DEPTH, 8, NMEM, 4, 128)
    re_s = cat("re_s", 1).reshape(DEPTH, 128, 32, 64); im_s = cat("im_s", 1).reshape(DEPTH, 128, 32, 64)
    pool_s = cat("pool_s", 1)
    return (yp, ys, re_p, im_p, pool_p, mk_p, mv_p, re_s, im_s, pool_s)
```
